# Optimizing a Trainium2 kernel written in Bass

```python
import math
import numpy as np
import jax
import jax.numpy as jnp
from jax import lax

D_MODEL = 1024
BATCH = 32
SEQ = 2048
DEPTH = 2
DEC_BATCH = 16
DEC_SEQ = 2048
PAST_LEN = 128

GRID_W = 64

NA_HEADS = 4
NA_HEAD_DIM = 64
NA_ROWS = 8
NA_COLS = 16
NA_QCOLS = 16
NA_WIDTH = NA_HEADS * NA_HEAD_DIM

RW_HEADS = 4
RW_HEAD_DIM = 64
RW_WIDTH = RW_HEADS * RW_HEAD_DIM
RW_DECAY_LORA = 64
RW_ICLR_LORA = 64
RW_LNX_EPS = 64e-5

DF_HEADS = 4
DF_HEAD_DIM = 32
DF_WIDTH = 2 * DF_HEADS * DF_HEAD_DIM
DF_QBLOCK = 128
DF_EPS = 1e-5

DL_HEADS = 4
DL_HEAD_DIM = 64
DL_WIDTH = DL_HEADS * DL_HEAD_DIM
DL_GROUPS = ((128, 1), (512, 4), (2048, 16))
DL_NGROUPS = 3

N_BRANCH = 4
BRANCH_WIDTH = 256
A_COLS = 4 * NA_WIDTH
B_COLS = 4 * RW_WIDTH + 2 * RW_DECAY_LORA + 2 * RW_ICLR_LORA
C_COLS = 4 * DF_WIDTH
D_COLS = 3 * DL_NGROUPS * DL_WIDTH + DL_WIDTH
MERGE_COLS = N_BRANCH * D_MODEL
IN_COLS = A_COLS + B_COLS + C_COLS + D_COLS + MERGE_COLS

ROPE_THETA = 10000.0
LN_EPS = 1e-5
DEEPNORM_ALPHA = (2 * DEPTH) ** 0.25
DEEPNORM_BETA = (8 * DEPTH) ** -0.25
NEG_INF = -1e30

kernel_name = 'hybrid_bidir_encoder_gated_branches'


def split_cols(x, sizes):
    return jnp.split(x, np.cumsum(sizes)[:-1].tolist(), axis=-1)


def layer_norm(x, g, b):
    xf = x.astype(jnp.float32)
    mu = jnp.mean(xf, -1, keepdims=True)
    var = jnp.mean(jnp.square(xf - mu), -1, keepdims=True)
    return (xf - mu) * lax.rsqrt(var + LN_EPS) * g + b


def rotary(x):
    T, d = x.shape[1], x.shape[-1]
    half = d // 2
    inv_freq = jnp.power(ROPE_THETA, -jnp.arange(half, dtype=jnp.float32) / half)
    ang = jnp.arange(T, dtype=jnp.float32)[:, None] * inv_freq[None, :]
    cos = jnp.cos(ang)[None, :, None, :]
    sin = jnp.sin(ang)[None, :, None, :]
    xf = x.astype(jnp.float32)
    x1, x2 = xf[..., :half], xf[..., half:]
    return jnp.concatenate([x1 * cos - x2 * sin, x2 * cos + x1 * sin], axis=-1)


def neighbourhood_attention(q, k, v, rpb):
    B, T, H, dh = q.shape
    rows = T // GRID_W
    kr = min(NA_ROWS, rows)
    n_cb = GRID_W // NA_QCOLS
    span = NA_QCOLS + NA_COLS
    cb_start = np.clip(np.arange(n_cb) * NA_QCOLS - NA_COLS // 2, 0, GRID_W - span)
    key_cols = cb_start[:, None] + np.arange(span)[None, :]
    q_cols = np.arange(GRID_W).reshape(n_cb, NA_QCOLS)
    c_start = np.clip(q_cols - NA_COLS // 2, 0, GRID_W - NA_COLS)
    kc = key_cols[:, None, :]
    col_ok = (kc >= c_start[..., None]) & (kc < c_start[..., None] + NA_COLS)
    dc_idx = np.clip(kc - q_cols[..., None] + NA_COLS - 1, 0, 2 * NA_COLS - 2)
    rpb_cols = rpb[:, :, dc_idx].astype(jnp.float32)
    qg = (q.astype(jnp.float32) * dh ** -0.5).reshape(B, rows, n_cb, NA_QCOLS, H, dh)
    kg = k.astype(jnp.float32).reshape(B, rows, GRID_W, H, dh)[:, :, key_cols]
    vg = v.astype(jnp.float32).reshape(B, rows, GRID_W, H, dh)[:, :, key_cols]
    mask = col_ok[:, :, None, :]

    def one_row(r):
        rs = jnp.clip(r - kr // 2, 0, rows - kr)
        k_r = lax.dynamic_slice_in_dim(kg, rs, kr, axis=1)
        v_r = lax.dynamic_slice_in_dim(vg, rs, kr, axis=1)
        q_r = lax.dynamic_index_in_dim(qg, r, axis=1, keepdims=False)
        s = jnp.einsum('bjqhd,brjchd->bhjqrc', q_r, k_r)
        dr_idx = rs + jnp.arange(kr) - r + NA_ROWS - 1
        bias = jnp.take(rpb_cols, dr_idx, axis=1).transpose(0, 2, 3, 1, 4)
        s = jnp.where(mask, s + bias, NEG_INF)
        p = jax.nn.softmax(s, axis=(-2, -1))
        return jnp.einsum('bhjqrc,brjchd->bjqhd', p, v_r)

    out = lax.map(one_row, jnp.arange(rows))
    return out.transpose(1, 0, 2, 3, 4, 5).reshape(B, T, H * dh)


def centred_shift(u, mu):
    up = jnp.pad(u, ((0, 0), (1, 1), (0, 0)))
    return u + mu * (0.5 * (up[:, :-2] + up[:, 2:]) - u)


def rwkv7_bidirectional(u, mu, w0, w2, a0, a2, k_k, k_a, r_k, lnx_g, lnx_b):
    B, T, _ = u.shape
    H, N, C = RW_HEADS, RW_HEAD_DIM, RW_WIDTH
    u = centred_shift(u, mu).astype(jnp.float32)
    r, k, v, g, wl, al = split_cols(u, [C, C, C, C, 2 * RW_DECAY_LORA, 2 * RW_ICLR_LORA])
    wl = wl.reshape(B, T, 2, RW_DECAY_LORA)
    al = al.reshape(B, T, 2, RW_ICLR_LORA)
    w_raw = w0 + jnp.einsum('btdl,dlc->btdc', jnp.tanh(wl), w2)
    decay = jnp.exp(-jnp.exp(-jax.nn.softplus(-w_raw) - 0.5))
    a = jax.nn.sigmoid(a0 + jnp.einsum('btdl,dlc->btdc', al, a2))
    kk = (k * k_k).reshape(B, T, H, N)
    kk = (kk * lax.rsqrt(jnp.maximum(jnp.sum(kk * kk, -1, keepdims=True), 1e-24))).reshape(B, T, C)
    kd = k[:, :, None, :] * (1.0 + (a - 1.0) * k_a)

    def per_dir(z):
        z = jnp.stack([z[:, :, 0], jnp.flip(z[:, :, 1], axis=1)], axis=0)
        return z.reshape(2, B, T, H, N).transpose(2, 0, 1, 3, 4)

    def shared(z):
        return per_dir(jnp.stack([z, z], axis=2))

    def step(S, inp):
        r_t, w_t, k_t, v_t, kk_t, a_t = inp
        sa = jnp.einsum('dbhvk,dbhk->dbhv', S, kk_t)
        S = S * w_t[..., None, :] - sa[..., :, None] * (kk_t * a_t)[..., None, :] + v_t[..., :, None] * k_t[..., None, :]
        return S, jnp.einsum('dbhvk,dbhk->dbhv', S, r_t)

    S0 = jnp.zeros((2, B, H, N, N), jnp.float32)
    _, ys = lax.scan(step, S0, (shared(r), per_dir(decay), per_dir(kd), shared(v), shared(kk), per_dir(a)))
    y = (ys[:, 0] + jnp.flip(ys[:, 1], axis=0)).transpose(1, 0, 2, 3)
    mu_y = jnp.mean(y, -1, keepdims=True)
    var_y = jnp.mean(jnp.square(y - mu_y), -1, keepdims=True)
    y = ((y - mu_y) * lax.rsqrt(var_y + RW_LNX_EPS)).reshape(B, T, C) * lnx_g + lnx_b
    bonus = jnp.sum((r[:, :, None, :] * kd * r_k.reshape(C)).reshape(B, T, 2, H, N), axis=(2, 4))
    y = y + (bonus[..., None] * v.reshape(B, T, H, N)).reshape(B, T, C)
    return y * jax.nn.silu(g)


def diff_attention(q, k, v, lam, subln_g, lam_init):
    B, T, _ = q.shape
    H, d = DF_HEADS, DF_HEAD_DIM
    q = rotary(q.reshape(B, T, 2 * H, d)).reshape(B, T, H, 2, d) * d ** -0.5
    k = rotary(k.reshape(B, T, 2 * H, d)).reshape(B, T, H, 2, d)
    v = v.reshape(B, T, H, 2 * d).astype(jnp.float32)
    lamf = lam.astype(jnp.float32)
    lam_full = jnp.exp(jnp.sum(lamf[0] * lamf[1])) - jnp.exp(jnp.sum(lamf[2] * lamf[3])) + lam_init
    nb = T // DF_QBLOCK
    q_blocks = q.reshape(B, nb, DF_QBLOCK, H, 2, d).transpose(1, 0, 2, 3, 4, 5)

    def attend_block(qb):
        s = jnp.einsum('bqhid,bkhid->bhiqk', qb, k)
        p = jax.nn.softmax(s, axis=-1)
        w = p[:, :, 0] - lam_full * p[:, :, 1]
        return jnp.einsum('bhqk,bkhe->bqhe', w, v)

    o = lax.map(attend_block, q_blocks)
    o = o.transpose(1, 0, 2, 3, 4).reshape(B, T, H, 2 * d)
    o = o * lax.rsqrt(jnp.mean(jnp.square(o), -1, keepdims=True) + DF_EPS) * subln_g * (1.0 - lam_init)
    return o.reshape(B, T, H * 2 * d)


def band_attention(q, k, v, half):
    lead = q.shape[:-2]
    L, dh = q.shape[-2], q.shape[-1]
    nb = -(-L // half)
    Lp = nb * half
    padw = [(0, 0)] * len(lead)
    qp = jnp.pad(q, padw + [(0, Lp - L), (0, 0)]).reshape(*lead, nb, half, dh)

    def windows(x):
        xp = jnp.pad(x, padw + [(half, Lp - L + half), (0, 0)]).reshape(*lead, nb + 2, half, dh)
        return jnp.concatenate([xp[..., :-2, :, :], xp[..., 1:-1, :, :], xp[..., 2:, :, :]], axis=-2)

    kw, vw = windows(k), windows(v)
    qpos = np.arange(nb)[:, None, None] * half + np.arange(half)[None, :, None]
    kpos = (np.arange(nb)[:, None, None] - 1) * half + np.arange(3 * half)[None, None, :]
    ok = (np.abs(kpos - qpos) <= half) & (kpos >= 0) & (kpos < L)
    s = jnp.where(ok, jnp.einsum('...nqd,...nkd->...nqk', qp, kw), NEG_INF)
    m = jnp.max(s, -1, keepdims=True)
    p = jnp.exp(s - m)
    den = jnp.sum(p, -1, keepdims=True)
    o = jnp.einsum('...nqk,...nkd->...nqd', p, vw) / den
    lse = (m + jnp.log(den))[..., 0]
    return o.reshape(*lead, Lp, dh)[..., :L, :], lse.reshape(*lead, Lp)[..., :L]


def dilated_attention(qkv):
    B, T, _ = qkv.shape
    H, dh = DL_HEADS, DL_HEAD_DIM
    qkv = qkv.reshape(B, T, DL_NGROUPS, 3, H, dh)
    outs, lses = [], []
    for gi, (window, dil) in enumerate(DL_GROUPS):
        half = window // (2 * dil)
        L = T // dil

        def residues(x):
            return x.astype(jnp.float32).reshape(B, L, dil, H, dh).transpose(0, 2, 3, 1, 4)

        q = residues(rotary(qkv[:, :, gi, 0])) * dh ** -0.5
        k = residues(rotary(qkv[:, :, gi, 1]))
        v = residues(qkv[:, :, gi, 2])
        o, lse = band_attention(q, k, v, half)
        outs.append(o.transpose(0, 3, 1, 2, 4).reshape(B, T, H, dh))
        lses.append(lse.transpose(0, 3, 1, 2).reshape(B, T, H))
    wts = jax.nn.softmax(jnp.stack(lses, 0), axis=0)
    out = jnp.sum(wts[..., None] * jnp.stack(outs, 0), axis=0)
    return out.reshape(B, T, H * dh)


def encode(x, ln0_g, ln0_b, w_in, b_in, na_rpb, rw_mu, rw_w0, rw_w2, rw_a0, rw_a2, rw_kk, rw_ka,
           rw_rk, rw_lnx_g, rw_lnx_b, df_lam, df_subln_g, w_branch, w_out, b_out, ln_g, ln_b):
    dtype = x.dtype
    x = layer_norm(x, ln0_g, ln0_b).astype(dtype)
    for l in range(DEPTH):
        B, T, _ = x.shape
        h = x @ w_in[l] + b_in[l]
        ua, ub, uc, ud, ug = split_cols(h, [A_COLS, B_COLS, C_COLS, D_COLS, MERGE_COLS])
        aq, ak, av, ag = split_cols(ua, [NA_WIDTH] * 4)
        heads = lambda z: z.reshape(B, T, NA_HEADS, NA_HEAD_DIM)
        y_a = neighbourhood_attention(heads(aq), heads(ak), heads(av), na_rpb[l]) * jax.nn.silu(ag)
        y_b = rwkv7_bidirectional(ub, rw_mu[l], rw_w0[l], rw_w2[l], rw_a0[l], rw_a2[l], rw_kk[l],
                                  rw_ka[l], rw_rk[l], rw_lnx_g[l], rw_lnx_b[l])
        cq, ck, cv, cg = split_cols(uc, [DF_WIDTH] * 4)
        lam_init = 0.8 - 0.6 * math.exp(-0.3 * l)
        y_c = diff_attention(cq, ck, cv, df_lam[l], df_subln_g[l], lam_init) * jax.nn.silu(cg)
        dqkv, dg = split_cols(ud, [3 * DL_NGROUPS * DL_WIDTH, DL_WIDTH])
        y_d = dilated_attention(dqkv) * jax.nn.silu(dg)
        gates = jax.nn.sigmoid(ug.astype(jnp.float32)).reshape(B, T, N_BRANCH, D_MODEL)
        merged = (gates[:, :, 0] * (y_a @ w_branch[l, 0]) + gates[:, :, 1] * (y_b @ w_branch[l, 1])
                  + gates[:, :, 2] * (y_c @ w_branch[l, 2]) + gates[:, :, 3] * (y_d @ w_branch[l, 3]))
        y = merged @ w_out[l] + b_out[l]
        x = layer_norm(DEEPNORM_ALPHA * x + y, ln_g[l], ln_b[l]).astype(dtype)
    return x


def setup_inputs(seed: int = 0) -> dict:
    key = jax.random.key(seed)
    ks = jax.random.split(key, 26)
    f32 = jnp.float32

    def nrm(k, shape, s):
        return jax.random.normal(k, shape, f32) * s

    return {
        'x_prompt': nrm(ks[0], (BATCH, SEQ, D_MODEL), 1.0),
        'x_sample': nrm(ks[1], (DEC_BATCH, DEC_SEQ, D_MODEL), 1.0),
        'ln0_g': 1.0 + nrm(ks[2], (D_MODEL,), 0.02),
        'ln0_b': nrm(ks[3], (D_MODEL,), 0.02),
        'w_in': nrm(ks[4], (DEPTH, D_MODEL, IN_COLS), D_MODEL ** -0.5),
        'b_in': nrm(ks[5], (DEPTH, IN_COLS), 0.02),
        'na_rpb': nrm(ks[6], (DEPTH, NA_HEADS, 2 * NA_ROWS - 1, 2 * NA_COLS - 1), 0.1),
        'rw_mu': jax.random.uniform(ks[7], (DEPTH, B_COLS), f32, 0.0, 1.0),
        'rw_w0': jax.random.uniform(ks[8], (DEPTH, 2, RW_WIDTH), f32, -5.0, 1.0),
        'rw_w2': nrm(ks[9], (DEPTH, 2, RW_DECAY_LORA, RW_WIDTH), 0.1 * RW_DECAY_LORA ** -0.5),
        'rw_a0': nrm(ks[10], (DEPTH, 2, RW_WIDTH), 0.1),
        'rw_a2': nrm(ks[11], (DEPTH, 2, RW_ICLR_LORA, RW_WIDTH), 0.1 * RW_ICLR_LORA ** -0.5),
        'rw_kk': 0.85 + nrm(ks[12], (DEPTH, RW_WIDTH), 0.02),
        'rw_ka': 1.0 + nrm(ks[13], (DEPTH, RW_WIDTH), 0.02),
        'rw_rk': nrm(ks[14], (DEPTH, RW_HEADS, RW_HEAD_DIM), 0.1),
        'rw_lnx_g': 1.0 + nrm(ks[15], (DEPTH, RW_WIDTH), 0.02),
        'rw_lnx_b': nrm(ks[16], (DEPTH, RW_WIDTH), 0.02),
        'df_lam': nrm(ks[17], (DEPTH, 4, DF_HEAD_DIM), 0.1),
        'df_subln_g': 1.0 + nrm(ks[18], (DEPTH, 2 * DF_HEAD_DIM), 0.02),
        'w_branch': nrm(ks[19], (DEPTH, N_BRANCH, BRANCH_WIDTH, D_MODEL), DEEPNORM_BETA * BRANCH_WIDTH ** -0.5),
        'w_out': nrm(ks[20], (DEPTH, D_MODEL, D_MODEL), DEEPNORM_BETA * D_MODEL ** -0.5),
        'b_out': nrm(ks[21], (DEPTH, D_MODEL), 0.02),
        'ln_g': 1.0 + nrm(ks[22], (DEPTH, D_MODEL), 0.02),
        'ln_b': nrm(ks[23], (DEPTH, D_MODEL), 0.02),
    }


def reference(x_prompt, x_sample, ln0_g, ln0_b, w_in, b_in, na_rpb, rw_mu, rw_w0, rw_w2, rw_a0, rw_a2,
              rw_kk, rw_ka, rw_rk, rw_lnx_g, rw_lnx_b, df_lam, df_subln_g, w_branch, w_out, b_out,
              ln_g, ln_b):
    weights = (ln0_g, ln0_b, w_in, b_in, na_rpb, rw_mu, rw_w0, rw_w2, rw_a0, rw_a2, rw_kk, rw_ka,
               rw_rk, rw_lnx_g, rw_lnx_b, df_lam, df_subln_g, w_branch, w_out, b_out, ln_g, ln_b)
    y_prompt = encode(x_prompt, *weights)
    y_sample = encode(x_sample, *weights)
    return (y_prompt, y_sample)
```

```python
import math
import numpy as np
import ml_dtypes
import concourse.bass as bass
import concourse.mybir as mybir
from concourse.bass_utils import run_bass_kernel_spmd

F32 = mybir.dt.float32
BF16 = mybir.dt.bfloat16
AF = mybir.ActivationFunctionType
ALU = mybir.AluOpType

T = 2048
DM = 1024
NTB = 4
IN_COLS = 9984
ROT_COLS = 2048
WCOLS = IN_COLS + ROT_COLS
ALPHA = (2 * 2) ** 0.25
LN_EPS = 1e-5
ARENA_F32 = 27136


class Tr:
    __slots__ = ("w", "r")

    def __init__(self):
        self.w = None
        self.r = {}


class V:
    __slots__ = ("ap", "trs")

    def __init__(self, ap, trs):
        self.ap = ap
        self.trs = trs


class Buf:
    def __init__(self, h, nslots=1):
        self.h = h
        self.trs = [Tr() for _ in range(nslots)]

    def __getitem__(self, idx):
        return V(self.h[idx], self.trs)

    def s(self, slot, idx):
        return V(self.h[idx], [self.trs[slot]])


class FW:
    LIMIT = 30000

    def __init__(self, nc, ndma=24):
        self.nc = nc
        self.eng = {"pe": nc.tensor, "act": nc.scalar, "dve": nc.vector, "pool": nc.gpsimd, "sp": nc.sync}
        self.sems = []
        self.cur = {}
        self.cnt = {}
        for e in self.eng:
            self.cur[e] = self._newsem("e_" + e)
            self.cnt[e] = 0
        self.known = {e: {} for e in self.eng}
        self.dma_sem = [self._newsem("dma%d" % i) for i in range(ndma)]
        self.dma_val = [0] * ndma
        self.n_hw = ndma
        self.dma_next = 0
        self.n_ins = 0
        self._uid = 0

    def _newsem(self, name):
        self._uid = getattr(self, "_uid", 0) + 1
        h = self.nc.alloc_semaphore("%s_%d" % (name, self._uid))
        self.sems.append(h)
        return len(self.sems) - 1

    def _wait(self, e, ev):
        si, val, src = ev
        if self.known[e].get(si, 0) >= val:
            return
        self.eng[e].wait_ge(self.sems[si], val)
        self.known[e][si] = val

    def _deps(self, e, reads, writes):
        for v in reads:
            for tr in v.trs:
                if tr.w is not None:
                    if tr.w[2] == e and e == "pe":
                        continue
                    self._wait(e, tr.w)
        for v in writes:
            for tr in v.trs:
                if tr.w is not None and not (tr.w[2] == e and e == "pe"):
                    self._wait(e, tr.w)
                for src, ev in tr.r.items():
                    if not (ev[2] == e and e == "pe"):
                        self._wait(e, ev)

    def _record(self, ev, reads, writes, key):
        for v in writes:
            for tr in v.trs:
                tr.w = ev
                tr.r = {}
        for v in reads:
            for tr in v.trs:
                tr.r[key] = ev

    def emit(self, e, fn, reads, writes):
        self._deps(e, reads, writes)
        ins = fn(self.eng[e])
        if self.cnt[e] >= self.LIMIT:
            self.cur[e] = self._newsem("e_" + e)
            self.cnt[e] = 0
        self.cnt[e] += 1
        ins.then_inc(self.sems[self.cur[e]], 1)
        ev = (self.cur[e], self.cnt[e], e)
        self._record(ev, reads, writes, e)
        self.n_ins += 1
        return ev

    def dma(self, out, in_, e="sp"):
        self._deps(e, [in_], [out])
        if e == "pool":
            self.dma_sem.append(self._newsem("swdma"))
            self.dma_val.append(0)
            slot = len(self.dma_sem) - 1
        else:
            slot = self.dma_next
            self.dma_next = (slot + 1) % self.n_hw
        si = self.dma_sem[slot]
        if self.dma_val[slot] > 0:
            self._wait(e, (si, self.dma_val[slot], "dma"))
        ins = self.eng[e].dma_start(out=out.ap, in_=in_.ap)
        self.dma_val[slot] += 16
        ins.then_inc(self.sems[si], 16)
        ev = (si, self.dma_val[slot], "dma%d" % slot)
        self._record(ev, [in_], [out], "dma%d" % slot)
        self.n_ins += 1
        return ev

    def barrier(self):
        evs = [(self.cur[f], self.cnt[f], f) for f in self.eng if self.cnt[f] > 0]
        evs += [(self.dma_sem[i], self.dma_val[i], "dma") for i in range(len(self.dma_sem)) if self.dma_val[i] > 0]
        for e in self.eng:
            for ev in evs:
                if not (ev[2] == e and e == "pe"):
                    self._wait(e, ev)

    def mm(self, out, lhsT, rhs, start=True, stop=True):
        return self.emit("pe", lambda E: E.matmul(out.ap, lhsT=lhsT.ap, rhs=rhs.ap, start=start, stop=stop),
                         [lhsT, rhs], [out])

    def transpose(self, out, in_, ident):
        return self.emit("pe", lambda E: E.transpose(out.ap, in_.ap, ident.ap), [in_, ident], [out])

    def act(self, out, in_, func, bias=None, scale=None, e="act"):
        kw = {}
        rd = [in_]
        if bias is not None:
            if isinstance(bias, V):
                kw["bias"] = bias.ap
                rd.append(bias)
            else:
                kw["bias"] = bias
        if scale is not None:
            if isinstance(scale, V):
                kw["scale"] = scale.ap
                rd.append(scale)
            else:
                kw["scale"] = scale
        return self.emit("act", lambda E: E.activation(out=out.ap, in_=in_.ap, func=func, **kw), rd, [out])

    def tt(self, out, in0, in1, op, e="dve"):
        return self.emit(e, lambda E: E.tensor_tensor(out=out.ap, in0=in0.ap, in1=in1.ap, op=op), [in0, in1], [out])

    def ts(self, out, in0, s1, s2, op0, op1=None, e="dve"):
        rd = [in0]
        a1 = s1
        a2 = s2
        if isinstance(s1, V):
            rd.append(s1)
            a1 = s1.ap
        if isinstance(s2, V):
            rd.append(s2)
            a2 = s2.ap
        if op1 is None:
            return self.emit(e, lambda E: E.tensor_scalar(out=out.ap, in0=in0.ap, scalar1=a1, scalar2=None, op0=op0),
                             rd, [out])
        return self.emit(e, lambda E: E.tensor_scalar(out=out.ap, in0=in0.ap, scalar1=a1, scalar2=a2, op0=op0, op1=op1),
                         rd, [out])

    def stt(self, out, in0, scalar, in1, op0, op1):
        rd = [in0, in1]
        a = scalar
        if isinstance(scalar, V):
            rd.append(scalar)
            a = scalar.ap
        return self.emit("dve", lambda E: E.scalar_tensor_tensor(out=out.ap, in0=in0.ap, scalar=a, in1=in1.ap,
                                                                 op0=op0, op1=op1), rd, [out])

    def copy(self, out, in_, e="dve"):
        if e == "act":
            return self.act(out, in_, AF.Copy)
        return self.emit(e, lambda E: E.tensor_copy(out=out.ap, in_=in_.ap), [in_], [out])

    def recip(self, out, in_):
        return self.emit("dve", lambda E: E.reciprocal(out=out.ap, in_=in_.ap), [in_], [out])

    def memset(self, out, val, e="dve"):
        return self.emit(e, lambda E: E.memset(out.ap, val), [], [out])


A_Q, A_K, A_V, A_G = 0, 256, 512, 768
B_R, B_K, B_V, B_GT, B_WL, B_AL = 1024, 1280, 1536, 1792, 2048, 2176
C_Q, C_K, C_V, C_G = 2304, 2560, 2816, 3072
D_BASE, D_G = 3328, 5632
MG = 5888
R_CQ, R_CK, R_D = 9984, 10240, 10496
NCH = WCOLS // 128


def _rot_perm():
    cols = []
    for base in (C_Q, C_K):
        for c in range(256):
            j = c % 32
            cols.append(base + c - j + (j + 16) % 32)
    for g in range(3):
        for part in (0, 256):
            base = D_BASE + g * 768 + part
            for c in range(256):
                j = c % 64
                cols.append(base + c - j + (j + 32) % 64)
    return np.asarray(cols, np.int64)


def _rope_tables():
    t = np.arange(T, dtype=np.float32)
    out = []
    for d in (32, 64):
        half = d // 2
        inv = np.power(np.float32(10000.0), -np.arange(half, dtype=np.float32) / np.float32(half)).astype(np.float32)
        ang = (t[:, None] * inv[None, :]).astype(np.float32)
        cos = np.cos(ang).astype(np.float32)
        sin = np.sin(ang).astype(np.float32)
        p = np.arange(128)
        j = p % d
        cosT = cos[:, j % half].T
        sgn = np.where(j < half, -1.0, 1.0).astype(np.float32)
        sinT = (sin[:, j % half].T * sgn[:, None]).astype(np.float32)
        out += [np.ascontiguousarray(cosT), np.ascontiguousarray(sinT)]
    return out


class Ctx:
    pass


def dram(nc, name, shape, dtype, kind):
    return nc.dram_tensor(name, list(shape), dtype, kind=kind).ap()


def build(nseq, parts=("A", "B", "C", "D"), dbg=False):
    nc = bass.Bass("TRN2", target_bir_lowering=False)
    fw = FW(nc)
    c = Ctx()
    c.nc, c.fw, c.parts, c.dbg = nc, fw, parts, dbg
    IN, OUT, INT = "ExternalInput", "ExternalOutput", "Internal"
    c.x = dram(nc, "x", [nseq, T, DM], F32, IN)
    c.y = dram(nc, "y", [nseq, T, DM], F32, OUT)
    c.w_in = dram(nc, "w_in", [2, DM, IN_COLS], F32, IN)
    c.w_rot = dram(nc, "w_rot", [2, DM, ROT_COLS], F32, IN)
    c.w_br = dram(nc, "w_branch", [2, 4, 256, DM], F32, IN)
    c.w_out = dram(nc, "w_out", [2, DM, DM], F32, IN)
    c.b_fm = dram(nc, "b_fm", [2, 128, NCH], F32, IN)
    c.b_cat = dram(nc, "b_cat", [2, WCOLS], F32, IN)
    c.vecs = dram(nc, "vecs", [8, DM], F32, IN)
    c.ident_b = dram(nc, "ident_b", [128, 128], BF16, IN)
    c.ident_f = dram(nc, "ident_f", [128, 128], F32, IN)
    c.rope = dram(nc, "rope", [4, 128, T], F32, IN)
    c.bandm = dram(nc, "bandm", [128, 3, 4, 128], BF16, IN)
    c.na_bias = dram(nc, "na_bias", [2, 128, 14 * 256], F32, IN)
    c.na_mask = dram(nc, "na_mask", [128, 14 * 256], F32, IN)
    c.rwp = dram(nc, "rwp", [2, 128, 64], F32, IN)
    c.rw_w2 = dram(nc, "rw_w2", [2, 128, 256], F32, IN)
    c.rw_a2 = dram(nc, "rw_a2", [2, 128, 256], F32, IN)
    c.df_lam = dram(nc, "df_lam", [2, 128], F32, IN)
    c.trimask = dram(nc, "trimask", [64, 2, 192], F32, IN)
    c.blk1 = dram(nc, "blk1", [128, 128], F32, IN)
    c.wbf = dram(nc, "wbf", [2, DM, WCOLS], BF16, INT)
    c.wbr_bf = dram(nc, "wbr_bf", [2, 4, 256, DM], BF16, INT)
    c.wout_bf = dram(nc, "wout_bf", [2, DM, DM], BF16, INT)
    c.xres = dram(nc, "xres", [T, DM], F32, INT)
    c.tr_wbf = [[Tr() for _ in range(NCH)] for _ in range(2)]
    c.tr_wbr = [Tr(), Tr()]
    c.tr_wout = [Tr(), Tr()]
    c.tr_xres = [Tr() for _ in range(16)]
    c.tr_in = Tr()
    c.tr_y = Tr()
    if dbg:
        c.dbg_out = dram(nc, "dbg", [128, 8, T], BF16, OUT)
        c.tr_dbg = Tr()

    def sb(name, shape, dtype, nslots=1):
        return Buf(nc.alloc_sbuf_tensor(name, list(shape), dtype), nslots)

    c.sb = sb
    c.xT = sb("xT", [128, 8, T], BF16)
    c.yT = sb("yT", [128, 8, T], BF16, nslots=4)
    c.wbuf = sb("wbuf", [128, 2, 8, 512], BF16, nslots=2)
    c.identb = sb("identb", [128, 128], BF16)
    c.identf = sb("identf", [128, 128], F32)
    c.bfm = sb("bfm", [128, 2, NCH], F32)
    c.vec_bc = sb("vec_bc", [128, 3, DM], F32)
    c.arena = nc.alloc_sbuf_tensor("arena", [128, ARENA_F32], F32)
    c.arena_off = 0
    c.ps = [Buf(nc.alloc_psum_tensor("ps%d" % i, [128, 512], F32)) for i in range(8)]
    c.ps_i = [0, 0]

    fw.dma(c.identb[:, :], V(c.ident_b[:, :], [c.tr_in]))
    fw.dma(c.identf[:, :], V(c.ident_f[:, :], [c.tr_in]))
    for l in range(2):
        fw.dma(c.bfm[:, l, :], V(c.b_fm[l], [c.tr_in]))

    convert_weights(c)
    for s in range(nseq):
        stage0(c, s)
        for l in range(2):
            layer(c, s, l)
    for i in range(len(fw.dma_sem)):
        if fw.dma_val[i] > 0:
            fw._wait("sp", (fw.dma_sem[i], fw.dma_val[i], "dma"))
    return nc, fw


def arena_reset(c):
    c.fw.barrier()
    c.arena_off = 0


def alloc(c, shape, dtype, nslots=1):
    n = int(np.prod(shape[1:]))
    n4 = (n + 1) // 2 if dtype == BF16 else n
    n4 = (n4 + 7) // 8 * 8
    assert c.arena_off + n4 <= ARENA_F32, ("arena overflow", c.arena_off, n4)
    ap = c.arena[0:shape[0], c.arena_off:c.arena_off + n4]
    c.arena_off += n4
    if dtype == BF16:
        ap = ap.bitcast(BF16)[:, 0:n]
    else:
        ap = ap[:, 0:n]
    if len(shape) > 2:
        names = " ".join("d%d" % i for i in range(len(shape) - 1))
        kw = {"d%d" % i: shape[i + 1] for i in range(len(shape) - 1)}
        ap = ap.rearrange("p (%s) -> p %s" % (names, names), **kw)
    return Buf(ap, nslots)


def load_vecs(c, rows):
    for i, r in enumerate(rows):
        c.fw.dma(c.vec_bc[:, i, :], V(c.vecs[r:r + 1, :].partition_broadcast(128), [c.tr_in]))


def psA(c):
    i = c.ps_i[0]
    c.ps_i[0] = (i + 1) % 4
    return c.ps[i]


def psB(c):
    i = c.ps_i[1]
    c.ps_i[1] = (i + 1) % 4
    return c.ps[4 + i]


def convert_weights(c):
    fw = c.fw
    rd = [c.tr_in]
    for l in range(2):
        blocks = [(0, 512 * i, 512) for i in range(11)] + [(0, 5632, 256)]
        for (_, c0, n) in blocks:
            trs = c.tr_wbf[l][c0 // 128:(c0 + n) // 128]
            fw.dma(V(c.wbf[l, :, c0:c0 + n], trs), V(c.w_in[l, :, c0:c0 + n], rd), e="pool")
        dst = c.wbf[l, :, MG:IN_COLS].rearrange("k (dc b j) -> k dc b j", dc=8, b=4)
        for b in range(4):
            src = c.w_in[l, :, MG + b * 1024:MG + (b + 1) * 1024].rearrange("k (dc j) -> k dc j", dc=8)
            fw.dma(V(dst[:, :, b, :], c.tr_wbf[l][MG // 128:IN_COLS // 128]), V(src, rd), e="pool")
        for i in range(4):
            c0 = IN_COLS + 512 * i
            fw.dma(V(c.wbf[l, :, c0:c0 + 512], c.tr_wbf[l][c0 // 128:c0 // 128 + 4]),
                   V(c.w_rot[l, :, 512 * i:512 * (i + 1)], rd), e="pool")
        for b in range(4):
            fw.dma(V(c.wbr_bf[l, b], [c.tr_wbr[l]]), V(c.w_br[l, b], rd), e="pool")
        for i in range(2):
            fw.dma(V(c.wout_bf[l, :, 512 * i:512 * (i + 1)], [c.tr_wout[l]]),
                   V(c.w_out[l, :, 512 * i:512 * (i + 1)], rd), e="pool")


def load_w(c, l, c0, n):
    fw = c.fw
    slot = getattr(c, "_wslot", 0)
    c._wslot = 1 - slot
    src = c.wbf[l, :, c0:c0 + n].rearrange("(kc p) n -> p kc n", p=128)
    trs = c.tr_wbf[l][c0 // 128:(c0 + n + 127) // 128]
    fw.dma(c.wbuf.s(slot, (slice(None), slot, slice(None), slice(0, n))), V(src, trs))
    return slot


def wv(c, slot, kc, j0, n):
    return c.wbuf.s(slot, (slice(None), slot, kc, slice(j0, j0 + n)))


def proj_fm(c, l, col_list, consume):
    fw = c.fw
    groups = []
    for i, c0 in enumerate(col_list):
        if groups and groups[-1][0] + groups[-1][1] == c0 and groups[-1][1] < 512:
            groups[-1][1] += 128
            groups[-1][2].append(i)
        else:
            groups.append([c0, 128, [i]])
    slots = [None] * len(groups)
    slots[0] = load_w(c, l, groups[0][0], groups[0][1])
    for gi, (g0, gn, idxs) in enumerate(groups):
        if gi + 1 < len(groups):
            slots[gi + 1] = load_w(c, l, groups[gi + 1][0], groups[gi + 1][1])
        for j, i in enumerate(idxs):
            for tb in range(NTB):
                ps = psA(c)
                for kc in range(8):
                    fw.mm(ps[:, :], wv(c, slots[gi], kc, j * 128, 128), c.xT[:, kc, tb * 512:(tb + 1) * 512],
                          start=(kc == 0), stop=(kc == 7))
                consume(i, tb, ps)


def ln_rows(c, z, gi, bi, out):
    fw = c.fw
    st = c.ln_st
    fw.emit("dve", lambda E: E.bn_stats(out=st.h[:, 0, :], in_=z.ap[:, 0:512]), [z], [st[:, 0, :]])
    fw.emit("dve", lambda E: E.bn_stats(out=st.h[:, 1, :], in_=z.ap[:, 512:1024]), [z], [st[:, 1, :]])
    mv = c.ln_mv
    fw.emit("dve", lambda E: E.bn_aggr(out=mv.h[:, 0:2], in_=st.h[:, :, :]), [st[:, :, :]], [mv[:, 0:2]])
    fw.ts(mv[:, 2:3], mv[:, 1:2], LN_EPS, None, ALU.add)
    fw.act(mv[:, 3:4], mv[:, 2:3], AF.Sqrt)
    fw.recip(mv[:, 4:5], mv[:, 3:4])
    fw.ts(out, z, mv[:, 0:1], mv[:, 4:5], ALU.subtract, ALU.mult)
    fw.tt(out, out, c.vec_bc[:, gi, :], ALU.mult, e="pool")
    fw.tt(out, out, c.vec_bc[:, bi, :], ALU.add, e="pool")


def to_xT(c, rows, tt):
    fw = c.fw
    xb = c.xb_t
    fw.copy(xb[:, :], rows, e="act")
    ps = psB(c)
    psb = V(ps.h[:, :].bitcast(BF16), ps.trs)
    for k in range(8):
        fw.transpose(V(psb.ap[:, k * 128:(k + 1) * 128], ps.trs), xb[:, k * 128:(k + 1) * 128], c.identb[:, :])
    fw.copy(c.xT[:, :, tt * 128:(tt + 1) * 128], V(psb.ap.rearrange("p (k t) -> p k t", k=8), ps.trs))


def ln_allocs(c):
    c.ln_st = alloc(c, [128, 2, 6], F32)
    c.ln_mv = alloc(c, [128, 8], F32)
    c.xb_t = alloc(c, [128, DM], BF16)
    c.zt = alloc(c, [128, 2, DM], F32, nslots=2)
    c.ot = alloc(c, [128, 2, DM], F32, nslots=2)


def stage0(c, s):
    fw = c.fw
    arena_reset(c)
    ln_allocs(c)
    load_vecs(c, [0, 1])
    for tt in range(16):
        sl = tt % 2
        z = c.zt.s(sl, (slice(None), sl, slice(None)))
        o = c.ot.s(sl, (slice(None), sl, slice(None)))
        fw.dma(z, V(c.x[s, tt * 128:(tt + 1) * 128, :], [c.tr_in]))
        ln_rows(c, z, 0, 1, o)
        fw.dma(V(c.xres[tt * 128:(tt + 1) * 128, :], [c.tr_xres[tt]]), o)
        to_xT(c, o, tt)


def gate_branch(c, l, gcol, bi):
    fw = c.fw

    def consume(i, tb, ps):
        ch = (gcol // 128) + i
        sg = c.g_sig.s(tb % 2, (slice(None), tb % 2, slice(None)))
        fw.act(sg, ps[:, :], AF.Sigmoid, bias=c.bfm[:, l, ch:ch + 1])
        fw.stt(sg, ps[:, :], c.bfm[:, l, ch:ch + 1], sg, ALU.add, ALU.mult)
        yv = c.yT.s(bi, (slice(None), bi * 2 + i, slice(tb * 512, (tb + 1) * 512)))
        fw.tt(yv, yv, sg, ALU.mult, e="pool")

    proj_fm(c, l, [gcol, gcol + 128], consume)


def final_stage(c, s, l):
    fw = c.fw
    arena_reset(c)
    ln_allocs(c)
    c.mergedT = alloc(c, [128, 8, T], BF16)
    c.wbr_sb = alloc(c, [128, 4, 2, DM], BF16)
    c.wout_sb = alloc(c, [128, 8, DM], BF16)
    c.f_sig = alloc(c, [128, 2, 512], F32, nslots=2)
    c.f_acc = alloc(c, [128, 2, 512], F32, nslots=2)
    c.f_tmp = alloc(c, [128, 2, 512], F32, nslots=2)
    load_vecs(c, [2 + 3 * l, 3 + 3 * l, 4 + 3 * l])
    for b in range(4):
        fw.dma(c.wbr_sb[:, b, :, :], V(c.wbr_bf[l, b].rearrange("(kc p) d -> p kc d", p=128), [c.tr_wbr[l]]))
    fw.dma(c.wout_sb[:, :, :], V(c.wout_bf[l].rearrange("(kc p) d -> p kc d", p=128), [c.tr_wout[l]]))
    slots = [None] * 8
    slots[0] = load_w(c, l, MG, 512)
    for dc in range(8):
        if dc + 1 < 8:
            slots[dc + 1] = load_w(c, l, MG + (dc + 1) * 512, 512)
        for tb in range(NTB):
            tok = slice(tb * 512, (tb + 1) * 512)
            ai = tb % 2
            acc = c.f_acc.s(ai, (slice(None), ai, slice(None)))
            for b in range(4):
                pg = psA(c)
                for kc in range(8):
                    fw.mm(pg[:, :], wv(c, slots[dc], kc, b * 128, 128), c.xT[:, kc, tok], start=(kc == 0), stop=(kc == 7))
                ch = MG // 128 + b * 8 + dc
                si = b % 2
                sg = c.f_sig.s(si, (slice(None), si, slice(None)))
                fw.act(sg, pg[:, :], AF.Sigmoid, bias=c.bfm[:, l, ch:ch + 1])
                pp = psB(c)
                for kc in range(2):
                    fw.mm(pp[:, :], c.wbr_sb[:, b, kc, dc * 128:(dc + 1) * 128],
                          c.yT.s(b, (slice(None), b * 2 + kc, tok)), start=(kc == 0), stop=(kc == 1))
                if b == 0:
                    fw.tt(acc, pp[:, :], sg, ALU.mult)
                else:
                    tm = c.f_tmp.s(si, (slice(None), si, slice(None)))
                    fw.tt(tm, pp[:, :], sg, ALU.mult)
                    if b < 3:
                        fw.tt(acc, acc, tm, ALU.add, e="pool")
                    else:
                        fw.tt(c.mergedT[:, dc, tok], acc, tm, ALU.add, e="pool")
    for tt in range(16):
        sl = tt % 2
        z = c.zt.s(sl, (slice(None), sl, slice(None)))
        o = c.ot.s(sl, (slice(None), sl, slice(None)))
        fw.dma(z, V(c.xres[tt * 128:(tt + 1) * 128, :], [c.tr_xres[tt]]))
        for hf in range(2):
            po = psA(c)
            for kc in range(8):
                fw.mm(po[:, :], c.mergedT[:, kc, tt * 128:(tt + 1) * 128], c.wout_sb[:, kc, hf * 512:(hf + 1) * 512],
                      start=(kc == 0), stop=(kc == 7))
            zh = V(z.ap[:, hf * 512:(hf + 1) * 512], z.trs)
            fw.stt(zh, zh, ALPHA, po[:, :], ALU.mult, ALU.add)
        fw.tt(z, z, c.vec_bc[:, 0, :], ALU.add, e="pool")
        ln_rows(c, z, 1, 2, o)
        if l == 0:
            fw.dma(V(c.xres[tt * 128:(tt + 1) * 128, :], [c.tr_xres[tt]]), o)
            to_xT(c, o, tt)
        else:
            fw.dma(V(c.y[s, tt * 128:(tt + 1) * 128, :], [c.tr_y]), o)


def layer(c, s, l):
    fw = c.fw
    if "B" in c.parts:
        branch_b(c, s, l)
    for bi, name in enumerate("ABCD"):
        if name not in c.parts:
            fw.memset(c.yT.s(bi, (slice(None), slice(bi * 2, bi * 2 + 2), slice(None))), 0.0, e="pool")
    if "A" in c.parts:
        branch_a(c, s, l)
    if "C" in c.parts:
        branch_c(c, s, l)
    if "D" in c.parts:
        branch_d(c, s, l)
    if c.dbg and l == 0 and s == 0:
        fw.dma(V(c.dbg_out[:, :, :], [c.tr_dbg]), c.yT[:, :, :])
    final_stage(c, s, l)


def rope_proj(c, l, qcol, rcol, cosT, sinT, dst, tmp, t2b):
    fw = c.fw

    def consume(i, tb, ps):
        ci = i // 2
        tok = slice(tb * 512, (tb + 1) * 512)
        if i % 2 == 0:
            ch = qcol // 128 + ci
            fw.stt(tmp[:, tb, :], ps[:, :], c.bfm[:, l, ch:ch + 1], cosT[:, tok], ALU.add, ALU.mult)
        else:
            ch = rcol // 128 + ci
            t2 = t2b.s(tb % 2, (slice(None), tb % 2, slice(None)))
            fw.stt(t2, ps[:, :], c.bfm[:, l, ch:ch + 1], sinT[:, tok], ALU.add, ALU.mult)
            fw.tt(dst[:, ci, tok], tmp[:, tb, :], t2, ALU.add, e="pool")

    proj_fm(c, l, [qcol, rcol, qcol + 128, rcol + 128], consume)


def plain_proj(c, l, col, dst):
    fw = c.fw

    def consume(i, tb, ps):
        ch = col // 128 + i
        fw.ts(dst[:, i, tb * 512:(tb + 1) * 512], ps[:, :], c.bfm[:, l, ch:ch + 1], None, ALU.add)

    proj_fm(c, l, [col, col + 128], consume)


def vaug_init(c, vaug):
    v5 = vaug.h.rearrange("p j (hp par) e -> p j hp par e", par=2)
    c.fw.memset(V(v5[:, :, :, 0, 64:128], vaug.trs), 1.0, e="pool")
    c.fw.memset(V(v5[:, :, :, 1, 0:64], vaug.trs), 1.0, e="pool")


def v_proj(c, l, vcol, vaug, vbias, tok_sel, ntiles=16):
    fw = c.fw
    slot = load_w(c, l, vcol, 256)
    fw.dma(vbias[:, :], V(c.b_cat[l:l + 1, vcol:vcol + 256].partition_broadcast(128), [c.tr_in]))
    v5 = vaug.h.rearrange("p j (hp par) e -> p j hp par e", par=2)
    b4 = vbias.h.rearrange("p (hp par e) -> p hp par e", hp=2, par=2)
    for j in range(ntiles):
        ps = psA(c)
        for kc in range(8):
            fw.mm(ps[:, 0:256], V(c.xT.h[:, kc, tok_sel(j)], c.xT.trs), wv(c, slot, kc, 0, 256),
                  start=(kc == 0), stop=(kc == 7))
        p4 = ps.h[:, 0:256].rearrange("p (hp par e) -> p hp par e", hp=2, par=2)
        fw.tt(V(v5[:, j, :, 0, 0:64], vaug.trs), V(p4[:, :, 0, :], ps.trs), V(b4[:, :, 0, :], vbias.trs), ALU.add)
        fw.tt(V(v5[:, j, :, 1, 64:128], vaug.trs), V(p4[:, :, 1, :], ps.trs), V(b4[:, :, 1, :], vbias.trs), ALU.add)


def branch_c(c, s, l):
    fw = c.fw
    arena_reset(c)
    lam_init = 0.8 - 0.6 * math.exp(-0.3 * l)
    cosT = alloc(c, [128, T], F32)
    sinT = alloc(c, [128, T], F32)
    fw.dma(cosT[:, :], V(c.rope[0], [c.tr_in]))
    fw.dma(sinT[:, :], V(c.rope[1], [c.tr_in]))
    qT = alloc(c, [128, 2, T], BF16)
    kT = alloc(c, [128, 2, T], BF16)
    tmp = alloc(c, [128, 4, 512], F32)
    t2b = alloc(c, [128, 2, 512], F32, nslots=2)
    vaug = alloc(c, [128, 16, 4, 128], BF16)
    vbias = alloc(c, [128, 256], F32)
    c.g_sig = alloc(c, [128, 2, 512], F32, nslots=2)
    pt = alloc(c, [128, 4, 512], BF16, nslots=4)
    sm = alloc(c, [128, 16], F32)
    lamt = alloc(c, [128, 128], F32)
    prm = alloc(c, [128, 64], F32)
    ofull = alloc(c, [128, 512], F32)
    osq = alloc(c, [128, 512], F32)
    w1 = alloc(c, [128, 2, 512], F32, nslots=2)
    w2 = alloc(c, [128, 2, 512], F32, nslots=2)
    blk = alloc(c, [128, 128], F32)
    fw.dma(blk[:, :], V(c.blk1[:, :], [c.tr_in]))
    fw.dma(prm[:, :], V(c.rwp[l], [c.tr_in]))
    fw.dma(lamt[:, :], V(c.df_lam[l:l + 1, :].partition_broadcast(128), [c.tr_in]))
    fw.tt(lamt[:, 0:32], lamt[:, 0:32], lamt[:, 32:64], ALU.mult)
    fw.tt(lamt[:, 64:96], lamt[:, 64:96], lamt[:, 96:128], ALU.mult)
    fw.emit("dve", lambda E: E.tensor_reduce(out=sm.h[:, 0:1], in_=lamt.h[:, 0:32], axis=mybir.AxisListType.X,
                                             op=ALU.add), [lamt[:, :]], [sm[:, :]])
    fw.emit("dve", lambda E: E.tensor_reduce(out=sm.h[:, 1:2], in_=lamt.h[:, 64:96], axis=mybir.AxisListType.X,
                                             op=ALU.add), [lamt[:, :]], [sm[:, :]])
    fw.act(sm[:, 2:4], sm[:, 0:2], AF.Exp)
    fw.tt(sm[:, 4:5], sm[:, 3:4], sm[:, 2:3], ALU.subtract)
    fw.ts(sm[:, 5:6], sm[:, 4:5], -lam_init, None, ALU.add)
    fw.ts(sm[:, 6:7], prm[:, 0:1], 1.0 - lam_init, None, ALU.mult)

    vaug_init(c, vaug)
    rope_proj(c, l, C_Q, R_CQ, cosT, sinT, qT, tmp, t2b)
    rope_proj(c, l, C_K, R_CK, cosT, sinT, kT, tmp, t2b)
    v_proj(c, l, C_V, vaug, vbias, lambda j: slice(j * 128, (j + 1) * 128))

    qz = [alloc(c, [128, 2, T], BF16), alloc(c, [128, 2, T], BF16)]
    for i in range(2):
        fw.ts(qz[i][:, :, :], qT[:, :, :], prm[:, 29 + i:30 + i], None, ALU.mult, e=("dve" if i == 0 else "pool"))
    scale = 32 ** -0.5
    for hp in range(2):
        for tb in range(NTB):
            tok = slice(tb * 512, (tb + 1) * 512)
            for par in range(2):
                h = hp * 2 + par
                olo, dlo = (0, 64) if par == 0 else (64, 0)
                accs = []
                for i in range(2):
                    pb = par * 64
                    acc = psB(c)
                    for kt in range(16):
                        sc = psA(c)
                        fw.mm(sc[:, :], kT[pb:pb + 64, hp, kt * 128:(kt + 1) * 128], qz[i][pb:pb + 64, hp, tok])
                        p = pt.s(kt % 4, (slice(None), kt % 4, slice(None)))
                        fw.act(p, sc[:, :], AF.Exp, scale=scale)
                        fw.mm(acc[:, :], vaug[:, kt, h, :], p, start=(kt == 0), stop=(kt == 15))
                    accs.append(acc)
                o = slice(olo, olo + 64)
                d = slice(dlo, dlo + 64)
                r1 = w1.s(0, (o, 0, slice(None)))
                r2 = w1.s(1, (o, 1, slice(None)))
                fw.recip(r1, accs[0][d, :])
                fw.recip(r2, accs[1][d, :])
                t1 = w2.s(0, (o, 0, slice(None)))
                t2 = w2.s(1, (o, 1, slice(None)))
                fw.tt(t1, accs[0][o, :], r1, ALU.mult)
                fw.tt(t2, accs[1][o, :], r2, ALU.mult)
                fw.stt(ofull[o, :], t2, sm[o, 5:6], t1, ALU.mult, ALU.add)
            fw.tt(osq[:, :], ofull[:, :], ofull[:, :], ALU.mult, e="pool")
            ss = psA(c)
            fw.mm(ss[:, :], blk[:, :], osq[:, :])
            fw.ts(osq[:, :], ss[:, :], 1.0 / 64.0, 1e-5, ALU.mult, ALU.add)
            fw.act(osq[:, :], osq[:, :], AF.Sqrt)
            fw.recip(osq[:, :], osq[:, :])
            fw.stt(c.yT.s(2, (slice(None), 4 + hp, tok)), ofull[:, :], sm[:, 6:7], osq[:, :], ALU.mult, ALU.mult)
    gate_branch(c, l, C_G, 2)


def _host_inputs(inp):
    f32 = np.float32
    g = lambda k: np.asarray(inp[k], f32)
    w_in = g("w_in")
    b_in = g("b_in")
    perm = _rot_perm()
    w_rot = np.ascontiguousarray(w_in[:, :, perm])
    b_cat = np.ascontiguousarray(np.concatenate([b_in, b_in[:, perm]], axis=1))
    b_fm = np.ascontiguousarray(b_cat.reshape(2, NCH, 128).transpose(0, 2, 1))
    vecs = np.zeros((8, DM), f32)
    vecs[0], vecs[1] = g("ln0_g"), g("ln0_b")
    for l in range(2):
        vecs[2 + 3 * l], vecs[3 + 3 * l], vecs[4 + 3 * l] = g("b_out")[l], g("ln_g")[l], g("ln_b")[l]
    rope = np.stack(_rope_tables(), 0)
    i = np.arange(128)[:, None]
    j = np.arange(128)[None, :]
    band = np.stack([(i - j >= 64), (np.abs(i - j) <= 64), (j - i >= 64)], 1).astype(f32)
    bandm = np.ascontiguousarray(np.broadcast_to(band[:, :, None, :], (128, 3, 4, 128))).astype(ml_dtypes.bfloat16)
    rpb = g("na_rpb")
    kap = np.arange(2)[:, None, None, None, None]
    kc = np.arange(64)[None, :, None, None, None]
    oi = np.arange(14)[None, None, :, None, None]
    hh = np.asarray([0, 2, 1, 3])[None, None, None, :, None]
    qc = np.arange(64)[None, None, None, None, :]
    dr = (oi - 7) + kap
    dc = np.clip(kc - qc + 15, 0, 30)
    shp = (2, 64, 14, 4, 64)
    na_bias = np.stack([rpb[l][np.broadcast_to(hh, shp), np.broadcast_to(dr + 7, shp), np.broadcast_to(dc, shp)]
                        for l in range(2)], 0).reshape(2, 128, 14 * 256).astype(f32)
    cst = np.clip(qc - 8, 0, 48)
    ok = (kc >= cst) & (kc < cst + 16)
    na_mask = np.ascontiguousarray(np.broadcast_to(ok, shp)).reshape(128, 14 * 256).astype(f32)
    rwp = np.zeros((2, 128, 64), f32)
    p = np.arange(128)
    mu = g("rw_mu")
    for l in range(2):
        rwp[l, :, 0] = g("df_subln_g")[l][p % 64]
        for hp in range(2):
            ch = hp * 128 + p
            for q in range(4):
                rwp[l, :, 1 + 2 * q + hp] = mu[l, q * 256 + ch]
            for d in range(2):
                rwp[l, :, 11 + d * 2 + hp] = g("rw_w0")[l, d, ch]
                rwp[l, :, 15 + d * 2 + hp] = g("rw_a0")[l, d, ch]
            rwp[l, :, 19 + hp] = g("rw_kk")[l, ch]
            rwp[l, :, 21 + hp] = g("rw_ka")[l, ch]
            rwp[l, :, 23 + hp] = g("rw_rk")[l].reshape(256)[ch]
            rwp[l, :, 25 + hp] = g("rw_lnx_g")[l, ch]
            rwp[l, :, 27 + hp] = g("rw_lnx_b")[l, ch]
        rwp[l, :, 29] = ((p % 64) < 32)
        rwp[l, :, 30] = ((p % 64) >= 32)
        rwp[l, :, 9] = mu[l, 1024 + p]
        rwp[l, :, 10] = mu[l, 1152 + p]
    s_ = np.arange(64)[:, None]
    t_ = np.arange(64)[None, :]
    tri = np.zeros((64, 2, 2, 64), f32)
    tri[:, 0, 0], tri[:, 0, 1] = (s_ < t_), (s_ <= t_)
    tri[:, 1, 0], tri[:, 1, 1] = (s_ > t_), (s_ >= t_)
    blk1 = np.zeros((128, 128), f32)
    blk1[:64, :64] = 1
    blk1[64:, 64:] = 1
    tri3 = np.zeros((64, 2, 192), f32)
    tri3[:, :, 0:128] = tri.reshape(64, 2, 128)
    tri3[:, 0, 128:192] = (s_ > t_)
    tri3[:, 1, 128:192] = (s_ < t_)
    return {
        "w_in": w_in, "w_rot": w_rot, "w_branch": g("w_branch"), "w_out": g("w_out"),
        "b_fm": b_fm, "b_cat": b_cat, "vecs": vecs,
        "ident_b": np.eye(128, dtype=f32).astype(ml_dtypes.bfloat16), "ident_f": np.eye(128, dtype=f32),
        "rope": np.ascontiguousarray(rope), "bandm": bandm, "na_bias": na_bias, "na_mask": na_mask,
        "rwp": rwp, "rw_w2": np.ascontiguousarray(g("rw_w2").reshape(2, 128, 256)),
        "rw_a2": np.ascontiguousarray(g("rw_a2").reshape(2, 128, 256)),
        "df_lam": np.ascontiguousarray(g("df_lam").reshape(2, 128)),
        "trimask": tri3, "blk1": blk1,
    }


_NC_CACHE = {}


def kernel(**inputs):
    xp = np.asarray(inputs["x_prompt"], np.float32)
    xs = np.asarray(inputs["x_sample"], np.float32)
    shared = _host_inputs(inputs)
    ncores = 8
    if "nc" not in _NC_CACHE:
        _NC_CACHE["nc"] = build(6)[0]
    nc = _NC_CACHE["nc"]
    in_maps = []
    for k in range(ncores):
        xk = np.concatenate([xp[4 * k:4 * k + 4], xs[2 * k:2 * k + 2]], axis=0)
        m = dict(shared)
        m["x"] = np.ascontiguousarray(xk)
        in_maps.append(m)
    res = run_bass_kernel_spmd(nc, in_maps, core_ids=list(range(ncores)))
    yp = np.empty_like(xp)
    ys = np.empty_like(xs)
    for k in range(ncores):
        yk = np.asarray(res.results[k]["y"], np.float32)
        yp[4 * k:4 * k + 4] = yk[0:4]
        ys[2 * k:2 * k + 2] = yk[4:6]
    return (yp, ys)


def branch_a(c, s, l):
    fw = c.fw
    arena_reset(c)
    qT = alloc(c, [128, 2, T], BF16)
    kT = alloc(c, [128, 2, T], BF16)
    vaug0 = alloc(c, [128, 16, 4, 128], BF16)
    vaug1 = alloc(c, [128, 15, 4, 128], BF16)
    vbias = alloc(c, [128, 256], F32)
    stg = alloc(c, [128, 14 * 256], F32)
    msk = alloc(c, [128, 14 * 256], F32)
    Mb = alloc(c, [128, 14, 256], BF16)
    pt = alloc(c, [128, 2, 4, 256], BF16, nslots=2)
    rd = alloc(c, [128, 2, 256], F32, nslots=2)
    c.g_sig = alloc(c, [128, 2, 512], F32, nslots=2)
    import os
    stop = int(os.environ.get("A_STOP", "99"))
    fw.dma(stg[:, :], V(c.na_bias[l], [c.tr_in]))
    fw.dma(msk[:, :], V(c.na_mask[:, :], [c.tr_in]))
    fw.act(stg[:, :], stg[:, :], AF.Exp)
    fw.tt(V(Mb.h.rearrange("p a b -> p (a b)"), Mb.trs), stg[:, :], msk[:, :], ALU.mult)
    if stop <= 1:
        return
    vaug_init(c, vaug0)
    vaug_init(c, vaug1)
    if stop <= 2:
        return
    plain_proj(c, l, A_Q, qT)
    plain_proj(c, l, A_K, kT)
    v_proj(c, l, A_V, vaug0, vbias, lambda j: slice(j * 128, (j + 1) * 128), 16)
    if stop <= 3:
        return
    v_proj(c, l, A_V, vaug1, vbias, lambda j: slice(64 + j * 128, 64 + (j + 1) * 128), 15)
    if stop <= 4:
        return
    for r in range(32 if stop > 8 else 1):
        rs = min(max(r - 4, 0), 24)
        sl = r % 2
        qs = slice(64 * r, 64 * r + 64)
        tiles = []
        for j in range(4):
            kr0 = rs + 2 * j
            oi = (rs - r + 2 * j) + 7
            p = pt.s(sl, (slice(None), sl, j, slice(None)))
            for par in range(2):
                sc = psA(c)
                pb = par * 64
                for hp in range(2):
                    fw.mm(sc[:, hp * 64:(hp + 1) * 64], kT[pb:pb + 64, hp, 64 * kr0:64 * kr0 + 128], qT[pb:pb + 64, hp, qs])
                fw.act(V(p.ap[:, par * 128:(par + 1) * 128], p.trs), sc[:, 0:128], AF.Exp, scale=0.125)
            fw.tt(p, p, Mb[:, oi, :], ALU.mult, e="pool")
            tiles.append((p, (vaug0, kr0 // 2) if kr0 % 2 == 0 else (vaug1, (kr0 - 1) // 2)))
        if stop == 5:
            continue
        acc = psB(c)
        for h in range(4):
            for j, (p, (va, ti)) in enumerate(tiles):
                pc = (h % 2) * 128 + (h // 2) * 64
                fw.mm(acc[:, h * 64:(h + 1) * 64], va[:, ti, h, :], V(p.ap[:, pc:pc + 64], p.trs),
                      start=(j == 0), stop=(j == 3))
        if stop == 6:
            continue
        a4 = acc.h[:, 0:256].rearrange("p (hp par q) -> p hp par q", hp=2, par=2)
        r4 = rd.h[:, sl, :].rearrange("p (hp par q) -> p hp par q", hp=2, par=2)
        rtr = [rd.trs[sl]]
        fw.recip(V(r4[0:64, :, 0, :], rtr), V(a4[64:128, :, 0, :], acc.trs))
        fw.recip(V(r4[64:128, :, 1, :], rtr), V(a4[0:64, :, 1, :], acc.trs))
        fw.tt(c.yT.s(0, (slice(0, 64), slice(0, 2), qs)), V(a4[0:64, :, 0, :], acc.trs), V(r4[0:64, :, 0, :], rtr), ALU.mult)
        fw.tt(c.yT.s(0, (slice(64, 128), slice(0, 2), qs)), V(a4[64:128, :, 1, :], acc.trs), V(r4[64:128, :, 1, :], rtr),
              ALU.mult)
    gate_branch(c, l, A_G, 0)


def branch_d(c, s, l):
    fw = c.fw
    arena_reset(c)
    cosT = alloc(c, [128, T], F32)
    sinT = alloc(c, [128, T], F32)
    fw.dma(cosT[:, :], V(c.rope[2], [c.tr_in]))
    fw.dma(sinT[:, :], V(c.rope[3], [c.tr_in]))
    acc = alloc(c, [128, 4, T], F32)
    qT = alloc(c, [128, 2, T], BF16)
    kT = alloc(c, [128, 2, T], BF16)
    tmp = alloc(c, [128, 4, 512], F32)
    t2b = alloc(c, [128, 2, 512], F32, nslots=2)
    c.g_sig = t2b
    vaug = alloc(c, [128, 16, 4, 128], BF16)
    vbias = alloc(c, [128, 256], F32)
    pt = alloc(c, [128, 2, 3, 512], BF16, nslots=2)
    bm = alloc(c, [128, 3, 512], BF16)
    fw.dma(bm[:, :, :], V(c.bandm.rearrange("p a h q -> p a (h q)"), [c.tr_in]))
    vaug_init(c, vaug)
    unit = 0
    for g, dil in enumerate((1, 4, 16)):
        nqb = T // dil // 128
        base = D_BASE + g * 768
        rope_proj(c, l, base, R_D + g * 512, cosT, sinT, qT, tmp, t2b)
        rope_proj(c, l, base + 256, R_D + g * 512 + 256, cosT, sinT, kT, tmp, t2b)

        def tsel(j, dil=dil, nqb=nqb):
            rho, jb = divmod(j, nqb)
            st = 128 * jb * dil + rho
            return slice(st, st + 127 * dil + 1, dil)

        v_proj(c, l, base + 512, vaug, vbias, tsel, 16)
        for rho in range(dil):
            for qb in range(nqb):
                qs = tsel(rho * nqb + qb)
                kts = [(jb, mt) for (jb, mt) in ((qb - 1, 0), (qb, 1), (qb + 1, 2)) if 0 <= jb < nqb]
                sl = unit % 2
                unit += 1
                ps_ = []
                for idx, (jb, mt) in enumerate(kts):
                    ks = tsel(rho * nqb + jb)
                    p = pt.s(sl, (slice(None), sl, idx, slice(None)))
                    for par in range(2):
                        sc = psA(c)
                        pb = par * 64
                        for hp in range(2):
                            fw.mm(sc[:, hp * 128:(hp + 1) * 128], V(kT.h[pb:pb + 64, hp, ks], kT.trs),
                                  V(qT.h[pb:pb + 64, hp, qs], qT.trs))
                        fw.act(V(p.ap[:, par * 256:(par + 1) * 256], p.trs), sc[:, 0:256], AF.Exp, scale=0.125)
                    fw.tt(p, p, bm[:, mt, :], ALU.mult, e="pool")
                    ps_.append(p)
                ob = psB(c)
                for h in range(4):
                    for idx, (jb, mt) in enumerate(kts):
                        pc = (h % 2) * 256 + (h // 2) * 128
                        fw.mm(ob[:, h * 128:(h + 1) * 128], vaug[:, rho * nqb + jb, h, :],
                              V(ps_[idx].ap[:, pc:pc + 128], ps_[idx].trs),
                              start=(idx == 0), stop=(idx == len(kts) - 1))
                dst = V(acc.h[:, :, qs], acc.trs)
                src = V(ob.h[:, :].rearrange("p (h q) -> p h q", h=4), ob.trs)
                if g == 0:
                    fw.copy(dst, src, e="act")
                else:
                    fw.tt(dst, src, dst, ALU.add)
    a5 = acc.h.rearrange("p (hp par) t -> p hp par t", par=2)
    t5 = tmp.h.rearrange("p (hp par) t -> p hp par t", par=2)
    for tb in range(NTB):
        tok = slice(tb * 512, (tb + 1) * 512)
        fw.recip(V(t5[0:64, :, 0, :], tmp.trs), V(a5[64:128, :, 0, tok], acc.trs))
        fw.recip(V(t5[64:128, :, 1, :], tmp.trs), V(a5[0:64, :, 1, tok], acc.trs))
        fw.tt(c.yT.s(3, (slice(0, 64), slice(6, 8), tok)), V(a5[0:64, :, 0, tok], acc.trs), V(t5[0:64, :, 0, :], tmp.trs),
              ALU.mult)
        fw.tt(c.yT.s(3, (slice(64, 128), slice(6, 8), tok)), V(a5[64:128, :, 1, tok], acc.trs),
              V(t5[64:128, :, 1, :], tmp.trs), ALU.mult)
    gate_branch(c, l, D_G, 3)


GC = 4


def yslot_f32(c, slot):
    ap = c.yT.h[:, 2 * slot:2 * slot + 2, :].rearrange("p a t -> p (a t)").bitcast(F32)
    return Buf(ap, 1), c.yT.trs[slot]


def branch_b(c, s, l):
    for hp in range(2):
        rwkv_hp(c, l, hp)


def rwkv_hp(c, l, hp):
    fw = c.fw
    arena_reset(c)
    EXPM05 = math.exp(-0.5)
    prm = alloc(c, [128, 64], F32)
    blk = alloc(c, [128, 128], F32)
    wst = alloc(c, [128, 2, 256], F32)
    w2sb = alloc(c, [128, 256], BF16)
    a2sb = alloc(c, [128, 256], BF16)
    der = alloc(c, [128, 16], F32)
    ones = alloc(c, [128, 64], F32)
    fw.dma(prm[:, :], V(c.rwp[l], [c.tr_in]))
    fw.dma(blk[:, :], V(c.blk1[:, :], [c.tr_in]))
    fw.dma(wst[:, 0, :], V(c.rw_w2[l], [c.tr_in]))
    fw.dma(wst[:, 1, :], V(c.rw_a2[l], [c.tr_in]))
    fw.copy(w2sb[:, :], wst[:, 0, :])
    fw.copy(a2sb[:, :], wst[:, 1, :])
    fw.memset(ones[:, :], 1.0)
    mucols = [1 + hp, 3 + hp, 5 + hp, 7 + hp, 9, 10]
    for i, mc in enumerate(mucols):
        fw.ts(der[:, i:i + 1], prm[:, mc:mc + 1], -1.0, 1.0, ALU.mult, ALU.add)
        fw.ts(der[:, 6 + i:7 + i], prm[:, mc:mc + 1], 0.5, None, ALU.mult)
    fw.ts(der[:, 12:13], prm[:, 21 + hp:22 + hp], -1.0, 1.0, ALU.mult, ALU.add)
    vT = alloc(c, [128, T], BF16)
    gs = alloc(c, [128, T], BF16)
    bv = alloc(c, [128, T], BF16)
    AR = [alloc(c, [128, 2, T], BF16) for _ in range(2)]
    BK = [alloc(c, [128, 2, T], BF16) for _ in range(2)]
    RLO = [alloc(c, [64, T], BF16) for _ in range(2)]
    eL = alloc(c, [128, 2, 32], F32)
    mark = c.arena_off

    uT = alloc(c, [128, 6, T + 2], BF16)
    fw.memset(uT[:, :, 0:1], 0.0)
    fw.memset(uT[:, :, T + 1:T + 2], 0.0)
    cols = [B_R + hp * 128, B_K + hp * 128, B_V + hp * 128, B_GT + hp * 128, B_WL, B_AL]

    def consume(i, tb, ps):
        ch = cols[i] // 128
        fw.act(uT[:, i, 1 + tb * 512:1 + (tb + 1) * 512], ps[:, :], AF.Identity, bias=c.bfm[:, l, ch:ch + 1])

    proj_fm(c, l, cols, consume)
    nt = 10
    tbuf = [alloc(c, [128, 512], F32) for _ in range(nt)]
    ex_ap = c.yT.h[:, 6:8, :].rearrange("p a t -> p (a t)").bitcast(F32)
    tbuf += [Buf(ex_ap[:, i * 512:(i + 1) * 512], 1) for i in range(4)]
    for b_ in tbuf[nt:]:
        b_.trs = [c.yT.trs[3]]
    xr, xk, kk, tA, tB, lw, L0, D_, eP, eM, eX, a_, kd, kdsum = [b_[:, :] for b_ in tbuf]
    twl = alloc(c, [128, 512], BF16)
    tal = alloc(c, [128, 512], BF16)
    for tb in range(NTB):
        tok = slice(tb * 512, (tb + 1) * 512)

        def shift(i, out):
            fw.tt(tA, uT[:, i, tb * 512:tb * 512 + 512], uT[:, i, tb * 512 + 2:tb * 512 + 514], ALU.add, e="pool")
            fw.ts(tA, tA, der[:, 6 + i:7 + i], None, ALU.mult, e="pool")
            fw.stt(out, uT[:, i, tb * 512 + 1:tb * 512 + 513], der[:, i:i + 1], tA, ALU.mult, ALU.add)

        shift(0, xr)
        shift(1, xk)
        shift(2, tB)
        fw.copy(vT[:, tok], tB, e="act")
        shift(3, tB)
        fw.act(D_, tB, AF.Sigmoid)
        fw.tt(gs[:, tok], tB, D_, ALU.mult, e="pool")
        shift(4, tB)
        fw.act(twl[:, :], tB, AF.Tanh)
        shift(5, tB)
        fw.copy(tal[:, :], tB, e="act")
        fw.ts(kk, xk, prm[:, 19 + hp:20 + hp], None, ALU.mult)
        fw.tt(tB, kk, kk, ALU.mult, e="pool")
        ps = psA(c)
        fw.mm(ps[:, :], blk[:, :], tB)
        fw.ts(tB, ps[:, :], 1e-24, None, ALU.max)
        fw.act(tB, tB, AF.Sqrt)
        fw.recip(tB, tB)
        fw.tt(kk, kk, tB, ALU.mult)
        for d in range(2):
            dsl = slice(d * 64, (d + 1) * 64)
            ps = psA(c)
            fw.mm(ps[:, :], w2sb[dsl, hp * 128:(hp + 1) * 128], twl[dsl, :])
            fw.act(lw, ps[:, :], AF.Sigmoid, bias=prm[:, 11 + d * 2 + hp:12 + d * 2 + hp])
            fw.ts(lw, lw, -EXPM05, None, ALU.mult, e="pool")
            for cc in range(8):
                cs = slice(cc * 64, (cc + 1) * 64)
                fw.emit("dve", lambda E, cs=cs: E.tensor_tensor_scan(out=L0.ap[:, cs], data0=ones.h[:, :], data1=lw.ap[:, cs],
                                                                    initial=0.0, op0=ALU.mult, op1=ALU.add),
                        [ones[:, :], lw], [L0])
            ltot = V(L0.ap[:, 63:512:64], L0.trs)
            fw.act(eL[:, d, tb * 8:(tb + 1) * 8], ltot, AF.Exp)
            if d == 0:
                fw.tt(D_, L0, lw, ALU.subtract, e="pool")
                fw.act(eP, L0, AF.Exp)
                fw.act(eM, L0, AF.Exp, scale=-1.0)
                fw.act(eX, D_, AF.Exp)
            else:
                l3 = V(L0.ap.rearrange("p (c t) -> p c t", t=64), L0.trs)
                lt3 = V(L0.ap[:, 63:512:64].unsqueeze(2).to_broadcast([128, 8, 64]), L0.trs)
                fw.tt(V(D_.ap.rearrange("p (c t) -> p c t", t=64), D_.trs), l3, lt3, ALU.subtract)
                fw.tt(lw, D_, lw, ALU.subtract, e="pool")
                fw.act(eP, lw, AF.Exp, scale=-1.0)
                fw.act(eM, lw, AF.Exp)
                fw.act(eX, D_, AF.Exp, scale=-1.0)
            ps = psA(c)
            fw.mm(ps[:, :], a2sb[dsl, hp * 128:(hp + 1) * 128], tal[dsl, :])
            fw.act(a_, ps[:, :], AF.Sigmoid, bias=prm[:, 15 + d * 2 + hp:16 + d * 2 + hp])
            fw.ts(tB, a_, prm[:, 21 + hp:22 + hp], der[:, 12:13], ALU.mult, ALU.add)
            fw.tt(kd, tB, xk, ALU.mult)
            if d == 0:
                fw.copy(kdsum, kd, e="pool")
            else:
                fw.tt(kdsum, kdsum, kd, ALU.add, e="pool")
            fw.tt(tB, kk, a_, ALU.mult, e="pool")
            fw.tt(BK[d][:, 0, tok], tB, eM, ALU.mult)
            fw.tt(BK[d][:, 1, tok], kd, eM, ALU.mult, e="pool")
            fw.tt(AR[d][:, 0, tok], kk, eX, ALU.mult)
            fw.tt(AR[d][:, 1, tok], xr, eP, ALU.mult, e="pool")
            fw.tt(RLO[d][0:64, tok], V(xr.ap[64:128, :], xr.trs), V(eP.ap[64:128, :], eP.trs), ALU.mult)
        fw.stt(tB, xr, prm[:, 23 + hp:24 + hp], kdsum, ALU.mult, ALU.mult)
        ps = psA(c)
        fw.mm(ps[:, :], blk[:, :], tB)
        fw.tt(bv[:, tok], ps[:, :], vT[:, tok], ALU.mult)

    fw.barrier()
    c.arena_off = mark
    ys = []
    for d in range(2):
        b_, tr_ = yslot_f32(c, (0, 2)[d])
        b_.trs = [tr_]
        ys.append(b_)
    eLs = alloc(c, [64, 2, 2, 32], F32)
    for d in range(2):
        for hl in range(2):
            fw.copy(eLs[:, d, hl, :], eL[hl * 64:(hl + 1) * 64, d, :], e="act")
    tri = alloc(c, [64, 2, 192], F32)
    fw.dma(tri[:, :, :], V(c.trimask[:, :, :], [c.tr_in]))
    NM = GC * 4
    tmT = alloc(c, [64, GC, 2, 4, 128], BF16)
    Am = alloc(c, [64, NM, 256], BF16)
    Nb = [alloc(c, [64, NM, 64], BF16) for _ in range(2)]
    NTb = [alloc(c, [64, NM, 64], BF16) for _ in range(3)]
    Rb = [alloc(c, [64, NM, 64], BF16) for _ in range(2)]
    PT = alloc(c, [64, NM, 64], BF16)
    Zb = alloc(c, [64, NM, 64], BF16)
    Sb = alloc(c, [64, 2, 4, 64], BF16, nslots=2)
    Ub = alloc(c, [64, 4, 64], BF16)
    fw.memset(Sb[:, :, :, :], 0.0)
    I64b = c.identb[0:64, 0:64]
    nsteps = T // 64
    slot = 0
    for g in range(nsteps // GC):
        def chunk(ci, d):
            it = g * GC + ci
            return it if d == 0 else nsteps - 1 - it

        for ci in range(GC):
            for d in range(2):
                cs = slice(chunk(ci, d) * 64, chunk(ci, d) * 64 + 64)
                pb_ = psB(c)
                pbv = pb_.h[:, :].bitcast(BF16)
                srcs = [AR[d][:, 0, cs], BK[d][:, 0, cs], BK[d][:, 1, cs], vT[:, cs]]
                for q, src in enumerate(srcs):
                    fw.transpose(V(pbv[0:64, q * 128:(q + 1) * 128], pb_.trs), src, c.identb[:, :])
                fw.copy(V(tmT.h[:, ci, d, :, :].rearrange("p q e -> p (q e)"), tmT.trs), V(pbv[0:64, 0:512], pb_.trs))
        bN = [psB(c), psB(c)]
        for ci in range(GC):
            bA = [psA(c), psA(c)]
            for hl in range(2):
                pb = hl * 64
                for d in range(2):
                    cs = slice(chunk(ci, d) * 64, chunk(ci, d) * 64 + 64)
                    ar = V(AR[d].h[pb:pb + 64, :, cs], AR[d].trs)
                    for q in range(2):
                        o0 = d * 256 + q * 128
                        fw.mm(bA[hl][0:64, o0:o0 + 128], BK[d][pb:pb + 64, q, cs], ar)
                    o1 = (ci * 2 + d) * 64
                    fw.mm(bN[hl][0:64, o1:o1 + 64], AR[d][pb:pb + 64, 0, cs], BK[d][pb:pb + 64, 0, cs])
                m0 = ci * 4 + hl
                dst = V(Am.h[:, m0:m0 + 3:2, :].rearrange("p d (q e) -> p d q e", q=2), Am.trs)
                src = V(bA[hl].h[0:64, :].rearrange("p (d q e) -> p d q e", d=2, q=2), bA[hl].trs)
                msk = V(tri.h[:, :, 0:128].unsqueeze(2).to_broadcast([64, 2, 2, 128]), tri.trs)
                fw.tt(dst, src, msk, ALU.mult)
        for hl in range(2):
            dst = V(NTb[0].h[:, hl:NM:2, :].rearrange("p (ci d) e -> p ci d e", d=2), NTb[0].trs)
            src = V(bN[hl].h[0:64, :].rearrange("p (ci d e) -> p ci d e", ci=GC, d=2), bN[hl].trs)
            msk = V(tri.h[:, :, 128:192].unsqueeze(1).to_broadcast([64, GC, 2, 64]), tri.trs)
            fw.tt(dst, src, msk, ALU.mult, e="pool" if False else "dve")
        idb = V(c.identf.h[0:64, 0:64].unsqueeze(1).to_broadcast([64, NM, 64]), c.identf.trs)
        fw.tt(Rb[0][:, :, :], idb, Am[:, :, 0:64], ALU.subtract)
        Ncur = V(Am.h[:, :, 0:64], Am.trs)
        NTcur = NTb[0][:, :, :]
        rcur = 0
        ni = 0
        nti = 0
        for lev in range(1, 6):
            halves = [(0, NM // 2), (NM // 2, NM)]
            ntn = (nti + 1) % 3
            for (m0, m1) in halves:
                pz = psA(c)
                for m in range(m0, m1):
                    fw.mm(pz[0:64, (m - m0) * 64:(m - m0 + 1) * 64], V(Ncur.ap[:, m, :], Ncur.trs), V(NTcur.ap[:, m, :], NTcur.trs))
                fw.copy(V(NTb[ntn].h[:, m0:m1, :].rearrange("p m e -> p (m e)"), NTb[ntn].trs), pz[0:64, 0:(m1 - m0) * 64], e="act")
            if lev < 5:
                nn = ni % 2
                for (m0, m1) in halves:
                    pz = psA(c)
                    for m in range(m0, m1):
                        fw.mm(pz[0:64, (m - m0) * 64:(m - m0 + 1) * 64], V(NTcur.ap[:, m, :], NTcur.trs), V(Ncur.ap[:, m, :], Ncur.trs))
                    fw.copy(V(Nb[nn].h[:, m0:m1, :].rearrange("p m e -> p (m e)"), Nb[nn].trs), pz[0:64, 0:(m1 - m0) * 64])
                ni += 1
            rn = 1 - rcur
            for (m0, m1) in halves:
                pz = psB(c)
                for m in range(m0, m1):
                    o = pz[0:64, (m - m0) * 64:(m - m0 + 1) * 64]
                    fw.mm(o, I64b, Rb[rcur][:, m, :], start=True, stop=False)
                    fw.mm(o, NTb[ntn][:, m, :], Rb[rcur][:, m, :], start=False, stop=True)
                fw.copy(V(Rb[rn].h[:, m0:m1, :].rearrange("p m e -> p (m e)"), Rb[rn].trs), pz[0:64, 0:(m1 - m0) * 64], e="act")
            rcur = rn
            if lev < 5:
                Ncur = Nb[nn][:, :, :]
            NTcur = NTb[ntn][:, :, :]
            nti = ntn
        TT = Rb[rcur]
        for (m0, m1) in [(0, NM // 2), (NM // 2, NM)]:
            pz = psA(c)
            pq = psB(c)
            for m in range(m0, m1):
                ci, d, hl = m // 4, (m // 2) % 2, m % 2
                fw.mm(pz[0:64, (m - m0) * 64:(m - m0 + 1) * 64], tmT[:, ci, d, 0, hl * 64:(hl + 1) * 64], TT[:, m, :])
                fw.mm(pq[0:64, (m - m0) * 64:(m - m0 + 1) * 64], Am[:, m, 128:192], tmT[:, ci, d, 3, hl * 64:(hl + 1) * 64])
            fw.copy(V(PT.h[:, m0:m1, :].rearrange("p m e -> p (m e)"), PT.trs), pz[0:64, 0:(m1 - m0) * 64], e="act")
            fw.copy(V(Zb.h[:, m0:m1, :].rearrange("p m e -> p (m e)"), Zb.trs), pq[0:64, 0:(m1 - m0) * 64])
        for ci in range(GC):
            it = g * GC + ci
            scur = Sb.s(slot, (slice(None), slot, slice(None), slice(None)))
            snew = Sb.s(1 - slot, (slice(None), 1 - slot, slice(None), slice(None)))
            pu = psB(c)
            for inst in range(4):
                m = ci * 4 + inst
                o = pu[0:64, inst * 64:(inst + 1) * 64]
                fw.mm(o, PT[:, m, :], V(scur.ap[:, inst, :], scur.trs), start=True, stop=False)
                fw.mm(o, TT[:, m, :], Zb[:, m, :], start=False, stop=True)
            fw.act(V(Ub.h.rearrange("p i e -> p (i e)"), Ub.trs), pu[0:64, 0:256], AF.Copy, scale=-1.0)
            pS = psA(c)
            pY = psB(c)
            for inst in range(4):
                m = ci * 4 + inst
                d, hl = inst // 2, inst % 2
                cs = slice(chunk(ci, d) * 64, chunk(ci, d) * 64 + 64)
                hs = slice(hl * 64, (hl + 1) * 64)
                o = pS[0:64, inst * 64:(inst + 1) * 64]
                fw.mm(o, I64b, V(scur.ap[:, inst, :], scur.trs), start=True, stop=False)
                fw.mm(o, tmT[:, ci, d, 2, hs], tmT[:, ci, d, 3, hs], start=False, stop=False)
                fw.mm(o, tmT[:, ci, d, 1, hs], Ub[:, inst, :], start=False, stop=True)
                oy = pY[0:64, inst * 64:(inst + 1) * 64]
                rt = AR[d][0:64, 1, cs] if hl == 0 else RLO[d][0:64, cs]
                fw.mm(oy, V(scur.ap[:, inst, :], scur.trs), rt, start=True, stop=False)
                fw.mm(oy, tmT[:, ci, d, 3, hs], Am[:, m, 192:256], start=False, stop=False)
                fw.mm(oy, Ub[:, inst, :], Am[:, m, 64:128], start=False, stop=True)
            for d in range(2):
                ch = chunk(ci, d)
                esc = V(eLs.h[:, d, :, ch:ch + 1].to_broadcast([64, 2, 64]), eLs.trs)
                fw.tt(V(snew.ap[:, 2 * d:2 * d + 2, :], snew.trs),
                      V(pS.h[0:64, d * 128:(d + 1) * 128].rearrange("p (i e) -> p i e", i=2), pS.trs), esc, ALU.mult)
                cs = slice(ch * 64, ch * 64 + 64)
                for hl in range(2):
                    inst = d * 2 + hl
                    fw.copy(ys[d][hl * 64:(hl + 1) * 64, cs], pY[0:64, inst * 64:(inst + 1) * 64], e="act")
            slot = 1 - slot

    fw.barrier()
    c.arena_off = mark
    pt_ = [alloc(c, [128, 512], F32)[:, :] for _ in range(6)]
    y_, sq, mean, var, t1, t2 = pt_
    for tb in range(NTB):
        tok = slice(tb * 512, (tb + 1) * 512)
        fw.tt(y_, ys[0][:, tok], ys[1][:, tok], ALU.add)
        fw.tt(sq, y_, y_, ALU.mult, e="pool")
        p1 = psA(c)
        fw.mm(p1[:, :], blk[:, :], y_)
        p2 = psA(c)
        fw.mm(p2[:, :], blk[:, :], sq)
        fw.ts(mean, p1[:, :], 1.0 / 64.0, None, ALU.mult)
        fw.tt(t1, mean, mean, ALU.mult, e="pool")
        fw.stt(var, p2[:, :], 1.0 / 64.0, t1, ALU.mult, ALU.subtract)
        fw.ts(var, var, 64e-5, None, ALU.add)
        fw.act(var, var, AF.Sqrt)
        fw.recip(var, var)
        fw.tt(t2, y_, mean, ALU.subtract, e="pool")
        fw.tt(t2, t2, var, ALU.mult)
        fw.ts(t2, t2, prm[:, 25 + hp:26 + hp], prm[:, 27 + hp:28 + hp], ALU.mult, ALU.add)
        fw.tt(t2, t2, bv[:, tok], ALU.add, e="pool")
        fw.tt(c.yT.s(1, (slice(None), 2 + hp, tok)), t2, gs[:, tok], ALU.mult)
```

```python
import math
import numpy as np
import ml_dtypes
import concourse.bass as bass
import concourse.mybir as mybir
from concourse.bass_utils import run_bass_kernel_spmd

F32 = mybir.dt.float32
BF16 = mybir.dt.bfloat16
AF = mybir.ActivationFunctionType
ALU = mybir.AluOpType

T = 2048
DM = 1024
NTB = 4
IN_COLS = 9984
ROT_COLS = 2048
WCOLS = IN_COLS + ROT_COLS
ALPHA = (2 * 2) ** 0.25
LN_EPS = 1e-5
ARENA_F32 = 27136


class Tr:
    __slots__ = ("w", "r")

    def __init__(self):
        self.w = None
        self.r = {}


class V:
    __slots__ = ("ap", "trs")

    def __init__(self, ap, trs):
        self.ap = ap
        self.trs = trs


class Buf:
    def __init__(self, h, nslots=1):
        self.h = h
        self.trs = [Tr() for _ in range(nslots)]

    def __getitem__(self, idx):
        return V(self.h[idx], self.trs)

    def s(self, slot, idx):
        return V(self.h[idx], [self.trs[slot]])


class FW:
    LIMIT = 30000

    def __init__(self, nc, ndma=24):
        self.nc = nc
        self.eng = {"pe": nc.tensor, "act": nc.scalar, "dve": nc.vector, "pool": nc.gpsimd, "sp": nc.sync}
        self.sems = []
        self.cur = {}
        self.cnt = {}
        for e in self.eng:
            self.cur[e] = self._newsem("e_" + e)
            self.cnt[e] = 0
        self.known = {e: {} for e in self.eng}
        self.dma_sem = [self._newsem("dma%d" % i) for i in range(ndma)]
        self.dma_val = [0] * ndma
        self.n_hw = ndma
        self.dma_next = 0
        self.n_ins = 0
        self._uid = 0

    def _newsem(self, name):
        self._uid = getattr(self, "_uid", 0) + 1
        h = self.nc.alloc_semaphore("%s_%d" % (name, self._uid))
        self.sems.append(h)
        return len(self.sems) - 1

    def _wait(self, e, ev):
        si, val, src = ev
        if self.known[e].get(si, 0) >= val:
            return
        self.eng[e].wait_ge(self.sems[si], val)
        self.known[e][si] = val

    def _deps(self, e, reads, writes):
        for v in reads:
            for tr in v.trs:
                if tr.w is not None:
                    if tr.w[2] == e and e == "pe":
                        continue
                    self._wait(e, tr.w)
        for v in writes:
            for tr in v.trs:
                if tr.w is not None and not (tr.w[2] == e and e == "pe"):
                    self._wait(e, tr.w)
                for src, ev in tr.r.items():
                    if not (ev[2] == e and e == "pe"):
                        self._wait(e, ev)

    def _record(self, ev, reads, writes, key):
        for v in writes:
            for tr in v.trs:
                tr.w = ev
                tr.r = {}
        for v in reads:
            for tr in v.trs:
                tr.r[key] = ev

    def emit(self, e, fn, reads, writes):
        self._deps(e, reads, writes)
        ins = fn(self.eng[e])
        if self.cnt[e] >= self.LIMIT:
            self.cur[e] = self._newsem("e_" + e)
            self.cnt[e] = 0
        self.cnt[e] += 1
        ins.then_inc(self.sems[self.cur[e]], 1)
        ev = (self.cur[e], self.cnt[e], e)
        self._record(ev, reads, writes, e)
        self.n_ins += 1
        return ev

    def dma(self, out, in_, e="sp"):
        self._deps(e, [in_], [out])
        if e == "pool":
            self.dma_sem.append(self._newsem("swdma"))
            self.dma_val.append(0)
            slot = len(self.dma_sem) - 1
        else:
            slot = self.dma_next
            self.dma_next = (slot + 1) % self.n_hw
        si = self.dma_sem[slot]
        if self.dma_val[slot] > 0:
            self._wait(e, (si, self.dma_val[slot], "dma"))
        ins = self.eng[e].dma_start(out=out.ap, in_=in_.ap)
        self.dma_val[slot] += 16
        ins.then_inc(self.sems[si], 16)
        ev = (si, self.dma_val[slot], "dma%d" % slot)
        self._record(ev, [in_], [out], "dma%d" % slot)
        self.n_ins += 1
        return ev

    def barrier(self):
        evs = [(self.cur[f], self.cnt[f], f) for f in self.eng if self.cnt[f] > 0]
        evs += [(self.dma_sem[i], self.dma_val[i], "dma") for i in range(len(self.dma_sem)) if self.dma_val[i] > 0]
        for e in self.eng:
            for ev in evs:
                if not (ev[2] == e and e == "pe"):
                    self._wait(e, ev)

    def mm(self, out, lhsT, rhs, start=True, stop=True):
        return self.emit("pe", lambda E: E.matmul(out.ap, lhsT=lhsT.ap, rhs=rhs.ap, start=start, stop=stop),
                         [lhsT, rhs], [out])

    def transpose(self, out, in_, ident):
        return self.emit("pe", lambda E: E.transpose(out.ap, in_.ap, ident.ap), [in_, ident], [out])

    def act(self, out, in_, func, bias=None, scale=None, e="act"):
        kw = {}
        rd = [in_]
        if bias is not None:
            if isinstance(bias, V):
                kw["bias"] = bias.ap
                rd.append(bias)
            else:
                kw["bias"] = bias
        if scale is not None:
            if isinstance(scale, V):
                kw["scale"] = scale.ap
                rd.append(scale)
            else:
                kw["scale"] = scale
        return self.emit("act", lambda E: E.activation(out=out.ap, in_=in_.ap, func=func, **kw), rd, [out])

    def tt(self, out, in0, in1, op, e="dve"):
        return self.emit(e, lambda E: E.tensor_tensor(out=out.ap, in0=in0.ap, in1=in1.ap, op=op), [in0, in1], [out])

    def ts(self, out, in0, s1, s2, op0, op1=None, e="dve"):
        rd = [in0]
        a1 = s1
        a2 = s2
        if isinstance(s1, V):
            rd.append(s1)
            a1 = s1.ap
        if isinstance(s2, V):
            rd.append(s2)
            a2 = s2.ap
        if op1 is None:
            return self.emit(e, lambda E: E.tensor_scalar(out=out.ap, in0=in0.ap, scalar1=a1, scalar2=None, op0=op0),
                             rd, [out])
        return self.emit(e, lambda E: E.tensor_scalar(out=out.ap, in0=in0.ap, scalar1=a1, scalar2=a2, op0=op0, op1=op1),
                         rd, [out])

    def stt(self, out, in0, scalar, in1, op0, op1):
        rd = [in0, in1]
        a = scalar
        if isinstance(scalar, V):
            rd.append(scalar)
            a = scalar.ap
        return self.emit("dve", lambda E: E.scalar_tensor_tensor(out=out.ap, in0=in0.ap, scalar=a, in1=in1.ap,
                                                                 op0=op0, op1=op1), rd, [out])

    def copy(self, out, in_, e="dve"):
        if e == "act":
            return self.act(out, in_, AF.Copy)
        return self.emit(e, lambda E: E.tensor_copy(out=out.ap, in_=in_.ap), [in_], [out])

    def recip(self, out, in_):
        return self.emit("dve", lambda E: E.reciprocal(out=out.ap, in_=in_.ap), [in_], [out])

    def memset(self, out, val, e="dve"):
        return self.emit(e, lambda E: E.memset(out.ap, val), [], [out])


A_Q, A_K, A_V, A_G = 0, 256, 512, 768
B_R, B_K, B_V, B_GT, B_WL, B_AL = 1024, 1280, 1536, 1792, 2048, 2176
C_Q, C_K, C_V, C_G = 2304, 2560, 2816, 3072
D_BASE, D_G = 3328, 5632
MG = 5888
R_CQ, R_CK, R_D = 9984, 10240, 10496
NCH = WCOLS // 128


def _rot_perm():
    cols = []
    for base in (C_Q, C_K):
        for c in range(256):
            j = c % 32
            cols.append(base + c - j + (j + 16) % 32)
    for g in range(3):
        for part in (0, 256):
            base = D_BASE + g * 768 + part
            for c in range(256):
                j = c % 64
                cols.append(base + c - j + (j + 32) % 64)
    return np.asarray(cols, np.int64)


def _rope_tables():
    t = np.arange(T, dtype=np.float32)
    out = []
    for d in (32, 64):
        half = d // 2
        inv = np.power(np.float32(10000.0), -np.arange(half, dtype=np.float32) / np.float32(half)).astype(np.float32)
        ang = (t[:, None] * inv[None, :]).astype(np.float32)
        cos = np.cos(ang).astype(np.float32)
        sin = np.sin(ang).astype(np.float32)
        p = np.arange(128)
        j = p % d
        cosT = cos[:, j % half].T
        sgn = np.where(j < half, -1.0, 1.0).astype(np.float32)
        sinT = (sin[:, j % half].T * sgn[:, None]).astype(np.float32)
        out += [np.ascontiguousarray(cosT), np.ascontiguousarray(sinT)]
    return out


class Ctx:
    pass


def dram(nc, name, shape, dtype, kind):
    return nc.dram_tensor(name, list(shape), dtype, kind=kind).ap()


def build(nseq, parts=("A", "B", "C", "D"), dbg=False):
    nc = bass.Bass("TRN2", target_bir_lowering=False)
    fw = FW(nc)
    c = Ctx()
    c.nc, c.fw, c.parts, c.dbg = nc, fw, parts, dbg
    IN, OUT, INT = "ExternalInput", "ExternalOutput", "Internal"
    c.x = dram(nc, "x", [nseq, T, DM], F32, IN)
    c.y = dram(nc, "y", [nseq, T, DM], F32, OUT)
    c.w_in = dram(nc, "w_in", [2, DM, IN_COLS], F32, IN)
    c.w_rot = dram(nc, "w_rot", [2, DM, ROT_COLS], F32, IN)
    c.w_br = dram(nc, "w_branch", [2, 4, 256, DM], F32, IN)
    c.w_out = dram(nc, "w_out", [2, DM, DM], F32, IN)
    c.b_fm = dram(nc, "b_fm", [2, 128, NCH], F32, IN)
    c.b_cat = dram(nc, "b_cat", [2, WCOLS], F32, IN)
    c.vecs = dram(nc, "vecs", [8, DM], F32, IN)
    c.ident_b = dram(nc, "ident_b", [128, 128], BF16, IN)
    c.ident_f = dram(nc, "ident_f", [128, 128], F32, IN)
    c.rope = dram(nc, "rope", [4, 128, T], F32, IN)
    c.bandm = dram(nc, "bandm", [128, 3, 4, 128], BF16, IN)
    c.na_bias = dram(nc, "na_bias", [2, 128, 14 * 256], F32, IN)
    c.na_mask = dram(nc, "na_mask", [128, 14 * 256], F32, IN)
    c.rwp = dram(nc, "rwp", [2, 128, 64], F32, IN)
    c.rw_w2 = dram(nc, "rw_w2", [2, 128, 256], F32, IN)
    c.rw_a2 = dram(nc, "rw_a2", [2, 128, 256], F32, IN)
    c.df_lam = dram(nc, "df_lam", [2, 128], F32, IN)
    c.trimask = dram(nc, "trimask", [64, 2, 192], F32, IN)
    c.blk1 = dram(nc, "blk1", [128, 128], F32, IN)
    c.wbf = dram(nc, "wbf", [2, DM, WCOLS], BF16, INT)
    c.wbr_bf = dram(nc, "wbr_bf", [2, 4, 256, DM], BF16, INT)
    c.wout_bf = dram(nc, "wout_bf", [2, DM, DM], BF16, INT)
    c.xres = dram(nc, "xres", [T, DM], F32, INT)
    c.tr_wbf = [[Tr() for _ in range(NCH)] for _ in range(2)]
    c.tr_wbr = [Tr(), Tr()]
    c.tr_wout = [Tr(), Tr()]
    c.tr_xres = [Tr() for _ in range(16)]
    c.tr_in = Tr()
    c.tr_y = Tr()
    if dbg:
        c.dbg_out = dram(nc, "dbg", [128, 8, T], BF16, OUT)
        c.tr_dbg = Tr()

    def sb(name, shape, dtype, nslots=1):
        return Buf(nc.alloc_sbuf_tensor(name, list(shape), dtype), nslots)

    c.sb = sb
    c.xT = sb("xT", [128, 8, T], BF16)
    c.yT = sb("yT", [128, 8, T], BF16, nslots=4)
    c.wbuf = sb("wbuf", [128, 2, 8, 512], BF16, nslots=2)
    c.identb = sb("identb", [128, 128], BF16)
    c.identf = sb("identf", [128, 128], F32)
    c.bfm = sb("bfm", [128, 2, NCH], F32)
    c.vec_bc = sb("vec_bc", [128, 3, DM], F32)
    c.arena = nc.alloc_sbuf_tensor("arena", [128, ARENA_F32], F32)
    c.arena_off = 0
    c.ps = [Buf(nc.alloc_psum_tensor("ps%d" % i, [128, 512], F32)) for i in range(8)]
    c.ps_i = [0, 0]

    fw.dma(c.identb[:, :], V(c.ident_b[:, :], [c.tr_in]))
    fw.dma(c.identf[:, :], V(c.ident_f[:, :], [c.tr_in]))
    for l in range(2):
        fw.dma(c.bfm[:, l, :], V(c.b_fm[l], [c.tr_in]))

    convert_weights(c)
    for s in range(nseq):
        stage0(c, s)
        for l in range(2):
            layer(c, s, l)
    for i in range(len(fw.dma_sem)):
        if fw.dma_val[i] > 0:
            fw._wait("sp", (fw.dma_sem[i], fw.dma_val[i], "dma"))
    return nc, fw


def arena_reset(c):
    c.fw.barrier()
    c.arena_off = 0


def alloc(c, shape, dtype, nslots=1):
    n = int(np.prod(shape[1:]))
    n4 = (n + 1) // 2 if dtype == BF16 else n
    n4 = (n4 + 7) // 8 * 8
    assert c.arena_off + n4 <= ARENA_F32, ("arena overflow", c.arena_off, n4)
    ap = c.arena[0:shape[0], c.arena_off:c.arena_off + n4]
    c.arena_off += n4
    if dtype == BF16:
        ap = ap.bitcast(BF16)[:, 0:n]
    else:
        ap = ap[:, 0:n]
    if len(shape) > 2:
        names = " ".join("d%d" % i for i in range(len(shape) - 1))
        kw = {"d%d" % i: shape[i + 1] for i in range(len(shape) - 1)}
        ap = ap.rearrange("p (%s) -> p %s" % (names, names), **kw)
    return Buf(ap, nslots)


def load_vecs(c, rows):
    for i, r in enumerate(rows):
        c.fw.dma(c.vec_bc[:, i, :], V(c.vecs[r:r + 1, :].partition_broadcast(128), [c.tr_in]))


def psA(c):
    i = c.ps_i[0]
    c.ps_i[0] = (i + 1) % 4
    return c.ps[i]


def psB(c):
    i = c.ps_i[1]
    c.ps_i[1] = (i + 1) % 4
    return c.ps[4 + i]


def convert_weights(c):
    fw = c.fw
    rd = [c.tr_in]
    for l in range(2):
        blocks = [(0, 512 * i, 512) for i in range(11)] + [(0, 5632, 256)]
        for (_, c0, n) in blocks:
            trs = c.tr_wbf[l][c0 // 128:(c0 + n) // 128]
            fw.dma(V(c.wbf[l, :, c0:c0 + n], trs), V(c.w_in[l, :, c0:c0 + n], rd), e="pool")
        dst = c.wbf[l, :, MG:IN_COLS].rearrange("k (dc b j) -> k dc b j", dc=8, b=4)
        for b in range(4):
            src = c.w_in[l, :, MG + b * 1024:MG + (b + 1) * 1024].rearrange("k (dc j) -> k dc j", dc=8)
            fw.dma(V(dst[:, :, b, :], c.tr_wbf[l][MG // 128:IN_COLS // 128]), V(src, rd), e="pool")
        for i in range(4):
            c0 = IN_COLS + 512 * i
            fw.dma(V(c.wbf[l, :, c0:c0 + 512], c.tr_wbf[l][c0 // 128:c0 // 128 + 4]),
                   V(c.w_rot[l, :, 512 * i:512 * (i + 1)], rd), e="pool")
        for b in range(4):
            fw.dma(V(c.wbr_bf[l, b], [c.tr_wbr[l]]), V(c.w_br[l, b], rd), e="pool")
        for i in range(2):
            fw.dma(V(c.wout_bf[l, :, 512 * i:512 * (i + 1)], [c.tr_wout[l]]),
                   V(c.w_out[l, :, 512 * i:512 * (i + 1)], rd), e="pool")


def load_w(c, l, c0, n):
    fw = c.fw
    slot = getattr(c, "_wslot", 0)
    c._wslot = 1 - slot
    src = c.wbf[l, :, c0:c0 + n].rearrange("(kc p) n -> p kc n", p=128)
    trs = c.tr_wbf[l][c0 // 128:(c0 + n + 127) // 128]
    fw.dma(c.wbuf.s(slot, (slice(None), slot, slice(None), slice(0, n))), V(src, trs))
    return slot


def wv(c, slot, kc, j0, n):
    return c.wbuf.s(slot, (slice(None), slot, kc, slice(j0, j0 + n)))


def proj_fm(c, l, col_list, consume):
    fw = c.fw
    groups = []
    for i, c0 in enumerate(col_list):
        if groups and groups[-1][0] + groups[-1][1] == c0 and groups[-1][1] < 512:
            groups[-1][1] += 128
            groups[-1][2].append(i)
        else:
            groups.append([c0, 128, [i]])
    slots = [None] * len(groups)
    slots[0] = load_w(c, l, groups[0][0], groups[0][1])
    for gi, (g0, gn, idxs) in enumerate(groups):
        if gi + 1 < len(groups):
            slots[gi + 1] = load_w(c, l, groups[gi + 1][0], groups[gi + 1][1])
        for j, i in enumerate(idxs):
            for tb in range(NTB):
                ps = psA(c)
                for kc in range(8):
                    fw.mm(ps[:, :], wv(c, slots[gi], kc, j * 128, 128), c.xT[:, kc, tb * 512:(tb + 1) * 512],
                          start=(kc == 0), stop=(kc == 7))
                consume(i, tb, ps)


def ln_rows(c, z, gi, bi, out):
    fw = c.fw
    st = c.ln_st
    fw.emit("dve", lambda E: E.bn_stats(out=st.h[:, 0, :], in_=z.ap[:, 0:512]), [z], [st[:, 0, :]])
    fw.emit("dve", lambda E: E.bn_stats(out=st.h[:, 1, :], in_=z.ap[:, 512:1024]), [z], [st[:, 1, :]])
    mv = c.ln_mv
    fw.emit("dve", lambda E: E.bn_aggr(out=mv.h[:, 0:2], in_=st.h[:, :, :]), [st[:, :, :]], [mv[:, 0:2]])
    fw.ts(mv[:, 2:3], mv[:, 1:2], LN_EPS, None, ALU.add)
    fw.act(mv[:, 3:4], mv[:, 2:3], AF.Sqrt)
    fw.recip(mv[:, 4:5], mv[:, 3:4])
    fw.ts(out, z, mv[:, 0:1], mv[:, 4:5], ALU.subtract, ALU.mult)
    fw.tt(out, out, c.vec_bc[:, gi, :], ALU.mult, e="pool")
    fw.tt(out, out, c.vec_bc[:, bi, :], ALU.add, e="pool")


def to_xT(c, rows, tt):
    fw = c.fw
    xb = c.xb_t
    fw.copy(xb[:, :], rows, e="act")
    ps = psB(c)
    psb = V(ps.h[:, :].bitcast(BF16), ps.trs)
    for k in range(8):
        fw.transpose(V(psb.ap[:, k * 128:(k + 1) * 128], ps.trs), xb[:, k * 128:(k + 1) * 128], c.identb[:, :])
    fw.copy(c.xT[:, :, tt * 128:(tt + 1) * 128], V(psb.ap.rearrange("p (k t) -> p k t", k=8), ps.trs))


def ln_allocs(c):
    c.ln_st = alloc(c, [128, 2, 6], F32)
    c.ln_mv = alloc(c, [128, 8], F32)
    c.xb_t = alloc(c, [128, DM], BF16)
    c.zt = alloc(c, [128, 2, DM], F32, nslots=2)
    c.ot = alloc(c, [128, 2, DM], F32, nslots=2)


def stage0(c, s):
    fw = c.fw
    arena_reset(c)
    ln_allocs(c)
    load_vecs(c, [0, 1])
    for tt in range(16):
        sl = tt % 2
        z = c.zt.s(sl, (slice(None), sl, slice(None)))
        o = c.ot.s(sl, (slice(None), sl, slice(None)))
        fw.dma(z, V(c.x[s, tt * 128:(tt + 1) * 128, :], [c.tr_in]))
        ln_rows(c, z, 0, 1, o)
        fw.dma(V(c.xres[tt * 128:(tt + 1) * 128, :], [c.tr_xres[tt]]), o)
        to_xT(c, o, tt)


def gate_branch(c, l, gcol, bi):
    fw = c.fw

    def consume(i, tb, ps):
        ch = (gcol // 128) + i
        sg = c.g_sig.s(tb % 2, (slice(None), tb % 2, slice(None)))
        fw.act(sg, ps[:, :], AF.Sigmoid, bias=c.bfm[:, l, ch:ch + 1])
        fw.stt(sg, ps[:, :], c.bfm[:, l, ch:ch + 1], sg, ALU.add, ALU.mult)
        yv = c.yT.s(bi, (slice(None), bi * 2 + i, slice(tb * 512, (tb + 1) * 512)))
        fw.tt(yv, yv, sg, ALU.mult, e="pool")

    proj_fm(c, l, [gcol, gcol + 128], consume)


def final_stage(c, s, l):
    fw = c.fw
    arena_reset(c)
    ln_allocs(c)
    c.mergedT = alloc(c, [128, 8, T], BF16)
    c.wbr_sb = alloc(c, [128, 4, 2, DM], BF16)
    c.wout_sb = alloc(c, [128, 8, DM], BF16)
    c.f_sig = alloc(c, [128, 2, 512], F32, nslots=2)
    c.f_acc = alloc(c, [128, 2, 512], F32, nslots=2)
    c.f_tmp = alloc(c, [128, 2, 512], F32, nslots=2)
    load_vecs(c, [2 + 3 * l, 3 + 3 * l, 4 + 3 * l])
    for b in range(4):
        fw.dma(c.wbr_sb[:, b, :, :], V(c.wbr_bf[l, b].rearrange("(kc p) d -> p kc d", p=128), [c.tr_wbr[l]]))
    fw.dma(c.wout_sb[:, :, :], V(c.wout_bf[l].rearrange("(kc p) d -> p kc d", p=128), [c.tr_wout[l]]))
    slots = [None] * 8
    slots[0] = load_w(c, l, MG, 512)
    for dc in range(8):
        if dc + 1 < 8:
            slots[dc + 1] = load_w(c, l, MG + (dc + 1) * 512, 512)
        for tb in range(NTB):
            tok = slice(tb * 512, (tb + 1) * 512)
            ai = tb % 2
            acc = c.f_acc.s(ai, (slice(None), ai, slice(None)))
            for b in range(4):
                pg = psA(c)
                for kc in range(8):
                    fw.mm(pg[:, :], wv(c, slots[dc], kc, b * 128, 128), c.xT[:, kc, tok], start=(kc == 0), stop=(kc == 7))
                ch = MG // 128 + b * 8 + dc
                si = b % 2
                sg = c.f_sig.s(si, (slice(None), si, slice(None)))
                fw.act(sg, pg[:, :], AF.Sigmoid, bias=c.bfm[:, l, ch:ch + 1])
                pp = psB(c)
                for kc in range(2):
                    fw.mm(pp[:, :], c.wbr_sb[:, b, kc, dc * 128:(dc + 1) * 128],
                          c.yT.s(b, (slice(None), b * 2 + kc, tok)), start=(kc == 0), stop=(kc == 1))
                if b == 0:
                    fw.tt(acc, pp[:, :], sg, ALU.mult)
                else:
                    tm = c.f_tmp.s(si, (slice(None), si, slice(None)))
                    fw.tt(tm, pp[:, :], sg, ALU.mult)
                    if b < 3:
                        fw.tt(acc, acc, tm, ALU.add, e="pool")
                    else:
                        fw.tt(c.mergedT[:, dc, tok], acc, tm, ALU.add, e="pool")
    for tt in range(16):
        sl = tt % 2
        z = c.zt.s(sl, (slice(None), sl, slice(None)))
        o = c.ot.s(sl, (slice(None), sl, slice(None)))
        fw.dma(z, V(c.xres[tt * 128:(tt + 1) * 128, :], [c.tr_xres[tt]]))
        for hf in range(2):
            po = psA(c)
            for kc in range(8):
                fw.mm(po[:, :], c.mergedT[:, kc, tt * 128:(tt + 1) * 128], c.wout_sb[:, kc, hf * 512:(hf + 1) * 512],
                      start=(kc == 0), stop=(kc == 7))
            zh = V(z.ap[:, hf * 512:(hf + 1) * 512], z.trs)
            fw.stt(zh, zh, ALPHA, po[:, :], ALU.mult, ALU.add)
        fw.tt(z, z, c.vec_bc[:, 0, :], ALU.add, e="pool")
        ln_rows(c, z, 1, 2, o)
        if l == 0:
            fw.dma(V(c.xres[tt * 128:(tt + 1) * 128, :], [c.tr_xres[tt]]), o)
            to_xT(c, o, tt)
        else:
            fw.dma(V(c.y[s, tt * 128:(tt + 1) * 128, :], [c.tr_y]), o)


def layer(c, s, l):
    fw = c.fw
    if "B" in c.parts:
        branch_b(c, s, l)
    for bi, name in enumerate("ABCD"):
        if name not in c.parts:
            fw.memset(c.yT.s(bi, (slice(None), slice(bi * 2, bi * 2 + 2), slice(None))), 0.0, e="pool")
    if "A" in c.parts:
        branch_a(c, s, l)
    if "C" in c.parts:
        branch_c(c, s, l)
    if "D" in c.parts:
        branch_d(c, s, l)
    if c.dbg and l == 0 and s == 0:
        fw.dma(V(c.dbg_out[:, :, :], [c.tr_dbg]), c.yT[:, :, :])
    final_stage(c, s, l)


def rope_proj(c, l, qcol, rcol, cosT, sinT, dst, tmp, t2b):
    fw = c.fw

    def consume(i, tb, ps):
        ci = i // 2
        tok = slice(tb * 512, (tb + 1) * 512)
        if i % 2 == 0:
            ch = qcol // 128 + ci
            fw.stt(tmp[:, tb, :], ps[:, :], c.bfm[:, l, ch:ch + 1], cosT[:, tok], ALU.add, ALU.mult)
        else:
            ch = rcol // 128 + ci
            t2 = t2b.s(tb % 2, (slice(None), tb % 2, slice(None)))
            fw.stt(t2, ps[:, :], c.bfm[:, l, ch:ch + 1], sinT[:, tok], ALU.add, ALU.mult)
            fw.tt(dst[:, ci, tok], tmp[:, tb, :], t2, ALU.add, e="pool")

    proj_fm(c, l, [qcol, rcol, qcol + 128, rcol + 128], consume)


def plain_proj(c, l, col, dst):
    fw = c.fw

    def consume(i, tb, ps):
        ch = col // 128 + i
        fw.ts(dst[:, i, tb * 512:(tb + 1) * 512], ps[:, :], c.bfm[:, l, ch:ch + 1], None, ALU.add)

    proj_fm(c, l, [col, col + 128], consume)


def vaug_init(c, vaug):
    v5 = vaug.h.rearrange("p j (hp par) e -> p j hp par e", par=2)
    c.fw.memset(V(v5[:, :, :, 0, 64:128], vaug.trs), 1.0, e="pool")
    c.fw.memset(V(v5[:, :, :, 1, 0:64], vaug.trs), 1.0, e="pool")


def v_proj(c, l, vcol, vaug, vbias, tok_sel, ntiles=16):
    fw = c.fw
    slot = load_w(c, l, vcol, 256)
    fw.dma(vbias[:, :], V(c.b_cat[l:l + 1, vcol:vcol + 256].partition_broadcast(128), [c.tr_in]))
    v5 = vaug.h.rearrange("p j (hp par) e -> p j hp par e", par=2)
    b4 = vbias.h.rearrange("p (hp par e) -> p hp par e", hp=2, par=2)
    for j in range(ntiles):
        ps = psA(c)
        for kc in range(8):
            fw.mm(ps[:, 0:256], V(c.xT.h[:, kc, tok_sel(j)], c.xT.trs), wv(c, slot, kc, 0, 256),
                  start=(kc == 0), stop=(kc == 7))
        p4 = ps.h[:, 0:256].rearrange("p (hp par e) -> p hp par e", hp=2, par=2)
        fw.tt(V(v5[:, j, :, 0, 0:64], vaug.trs), V(p4[:, :, 0, :], ps.trs), V(b4[:, :, 0, :], vbias.trs), ALU.add)
        fw.tt(V(v5[:, j, :, 1, 64:128], vaug.trs), V(p4[:, :, 1, :], ps.trs), V(b4[:, :, 1, :], vbias.trs), ALU.add)


def branch_c(c, s, l):
    fw = c.fw
    arena_reset(c)
    lam_init = 0.8 - 0.6 * math.exp(-0.3 * l)
    cosT = alloc(c, [128, T], F32)
    sinT = alloc(c, [128, T], F32)
    fw.dma(cosT[:, :], V(c.rope[0], [c.tr_in]))
    fw.dma(sinT[:, :], V(c.rope[1], [c.tr_in]))
    qT = alloc(c, [128, 2, T], BF16)
    kT = alloc(c, [128, 2, T], BF16)
    tmp = alloc(c, [128, 4, 512], F32)
    t2b = alloc(c, [128, 2, 512], F32, nslots=2)
    vaug = alloc(c, [128, 16, 4, 128], BF16)
    vbias = alloc(c, [128, 256], F32)
    c.g_sig = alloc(c, [128, 2, 512], F32, nslots=2)
    pt = alloc(c, [128, 4, 512], BF16, nslots=4)
    sm = alloc(c, [128, 16], F32)
    lamt = alloc(c, [128, 128], F32)
    prm = alloc(c, [128, 64], F32)
    ofull = alloc(c, [128, 512], F32)
    osq = alloc(c, [128, 512], F32)
    w1 = alloc(c, [128, 2, 512], F32, nslots=2)
    w2 = alloc(c, [128, 2, 512], F32, nslots=2)
    blk = alloc(c, [128, 128], F32)
    fw.dma(blk[:, :], V(c.blk1[:, :], [c.tr_in]))
    fw.dma(prm[:, :], V(c.rwp[l], [c.tr_in]))
    fw.dma(lamt[:, :], V(c.df_lam[l:l + 1, :].partition_broadcast(128), [c.tr_in]))
    fw.tt(lamt[:, 0:32], lamt[:, 0:32], lamt[:, 32:64], ALU.mult)
    fw.tt(lamt[:, 64:96], lamt[:, 64:96], lamt[:, 96:128], ALU.mult)
    fw.emit("dve", lambda E: E.tensor_reduce(out=sm.h[:, 0:1], in_=lamt.h[:, 0:32], axis=mybir.AxisListType.X,
                                             op=ALU.add), [lamt[:, :]], [sm[:, :]])
    fw.emit("dve", lambda E: E.tensor_reduce(out=sm.h[:, 1:2], in_=lamt.h[:, 64:96], axis=mybir.AxisListType.X,
                                             op=ALU.add), [lamt[:, :]], [sm[:, :]])
    fw.act(sm[:, 2:4], sm[:, 0:2], AF.Exp)
    fw.tt(sm[:, 4:5], sm[:, 3:4], sm[:, 2:3], ALU.subtract)
    fw.ts(sm[:, 5:6], sm[:, 4:5], -lam_init, None, ALU.add)
    fw.ts(sm[:, 6:7], prm[:, 0:1], 1.0 - lam_init, None, ALU.mult)

    vaug_init(c, vaug)
    rope_proj(c, l, C_Q, R_CQ, cosT, sinT, qT, tmp, t2b)
    rope_proj(c, l, C_K, R_CK, cosT, sinT, kT, tmp, t2b)
    v_proj(c, l, C_V, vaug, vbias, lambda j: slice(j * 128, (j + 1) * 128))

    qz = [alloc(c, [128, 2, T], BF16), alloc(c, [128, 2, T], BF16)]
    for i in range(2):
        fw.ts(qz[i][:, :, :], qT[:, :, :], prm[:, 29 + i:30 + i], None, ALU.mult, e=("dve" if i == 0 else "pool"))
    scale = 32 ** -0.5
    LA = 2
    items = [(hp, tb, par, i, kt) for hp in range(2) for tb in range(NTB) for par in range(2) for i in range(2)
             for kt in range(16)]
    pbuf = {}
    state = {}

    def stage1(n):
        hp, tb, par, i, kt = items[n]
        tok = slice(tb * 512, (tb + 1) * 512)
        pb = par * 64
        sc = psA(c)
        fw.mm(sc[:, :], kT[pb:pb + 64, hp, kt * 128:(kt + 1) * 128], qz[i][pb:pb + 64, hp, tok])
        p = pt.s(n % 4, (slice(None), n % 4, slice(None)))
        fw.act(p, sc[:, :], AF.Exp, scale=scale)
        pbuf[n] = p

    def stage2(n):
        hp, tb, par, i, kt = items[n]
        tok = slice(tb * 512, (tb + 1) * 512)
        h = hp * 2 + par
        if kt == 0:
            state[("acc", i)] = psB(c)
        acc = state[("acc", i)]
        fw.mm(acc[:, :], vaug[:, kt, h, :], pbuf.pop(n), start=(kt == 0), stop=(kt == 15))
        if kt == 15 and i == 1:
            accs = [state[("acc", 0)], state[("acc", 1)]]
            olo, dlo = (0, 64) if par == 0 else (64, 0)
            o = slice(olo, olo + 64)
            d = slice(dlo, dlo + 64)
            r1 = w1.s(0, (o, 0, slice(None)))
            r2 = w1.s(1, (o, 1, slice(None)))
            fw.recip(r1, accs[0][d, :])
            fw.recip(r2, accs[1][d, :])
            t1 = w2.s(0, (o, 0, slice(None)))
            t2 = w2.s(1, (o, 1, slice(None)))
            fw.tt(t1, accs[0][o, :], r1, ALU.mult)
            fw.tt(t2, accs[1][o, :], r2, ALU.mult)
            fw.stt(ofull[o, :], t2, sm[o, 5:6], t1, ALU.mult, ALU.add)
            if par == 1:
                fw.tt(osq[:, :], ofull[:, :], ofull[:, :], ALU.mult)
                ss = psA(c)
                fw.mm(ss[:, :], blk[:, :], osq[:, :])
                fw.ts(osq[:, :], ss[:, :], 1.0 / 64.0, 1e-5, ALU.mult, ALU.add)
                fw.act(osq[:, :], osq[:, :], AF.Sqrt)
                fw.recip(osq[:, :], osq[:, :])
                fw.stt(c.yT.s(2, (slice(None), 4 + hp, tok)), ofull[:, :], sm[:, 6:7], osq[:, :], ALU.mult, ALU.mult)

    for n in range(len(items) + LA):
        if n < len(items):
            stage1(n)
        if n >= LA:
            stage2(n - LA)
    gate_branch(c, l, C_G, 2)


def _host_inputs(inp):
    f32 = np.float32
    g = lambda k: np.asarray(inp[k], f32)
    w_in = g("w_in")
    b_in = g("b_in")
    perm = _rot_perm()
    w_rot = np.ascontiguousarray(w_in[:, :, perm])
    b_cat = np.ascontiguousarray(np.concatenate([b_in, b_in[:, perm]], axis=1))
    b_fm = np.ascontiguousarray(b_cat.reshape(2, NCH, 128).transpose(0, 2, 1))
    vecs = np.zeros((8, DM), f32)
    vecs[0], vecs[1] = g("ln0_g"), g("ln0_b")
    for l in range(2):
        vecs[2 + 3 * l], vecs[3 + 3 * l], vecs[4 + 3 * l] = g("b_out")[l], g("ln_g")[l], g("ln_b")[l]
    rope = np.stack(_rope_tables(), 0)
    i = np.arange(128)[:, None]
    j = np.arange(128)[None, :]
    band = np.stack([(i - j >= 64), (np.abs(i - j) <= 64), (j - i >= 64)], 1).astype(f32)
    bandm = np.ascontiguousarray(np.broadcast_to(band[:, :, None, :], (128, 3, 4, 128))).astype(ml_dtypes.bfloat16)
    rpb = g("na_rpb")
    kap = np.arange(2)[:, None, None, None, None]
    kc = np.arange(64)[None, :, None, None, None]
    oi = np.arange(14)[None, None, :, None, None]
    hh = np.asarray([0, 2, 1, 3])[None, None, None, :, None]
    qc = np.arange(64)[None, None, None, None, :]
    dr = (oi - 7) + kap
    dc = np.clip(kc - qc + 15, 0, 30)
    shp = (2, 64, 14, 4, 64)
    na_bias = np.stack([rpb[l][np.broadcast_to(hh, shp), np.broadcast_to(dr + 7, shp), np.broadcast_to(dc, shp)]
                        for l in range(2)], 0).reshape(2, 128, 14 * 256).astype(f32)
    cst = np.clip(qc - 8, 0, 48)
    ok = (kc >= cst) & (kc < cst + 16)
    na_mask = np.ascontiguousarray(np.broadcast_to(ok, shp)).reshape(128, 14 * 256).astype(f32)
    rwp = np.zeros((2, 128, 64), f32)
    p = np.arange(128)
    mu = g("rw_mu")
    for l in range(2):
        rwp[l, :, 0] = g("df_subln_g")[l][p % 64]
        for hp in range(2):
            ch = hp * 128 + p
            for q in range(4):
                rwp[l, :, 1 + 2 * q + hp] = mu[l, q * 256 + ch]
            for d in range(2):
                rwp[l, :, 11 + d * 2 + hp] = g("rw_w0")[l, d, ch]
                rwp[l, :, 15 + d * 2 + hp] = g("rw_a0")[l, d, ch]
            rwp[l, :, 19 + hp] = g("rw_kk")[l, ch]
            rwp[l, :, 21 + hp] = g("rw_ka")[l, ch]
            rwp[l, :, 23 + hp] = g("rw_rk")[l].reshape(256)[ch]
            rwp[l, :, 25 + hp] = g("rw_lnx_g")[l, ch]
            rwp[l, :, 27 + hp] = g("rw_lnx_b")[l, ch]
        rwp[l, :, 29] = ((p % 64) < 32)
        rwp[l, :, 30] = ((p % 64) >= 32)
        rwp[l, :, 9] = mu[l, 1024 + p]
        rwp[l, :, 10] = mu[l, 1152 + p]
    s_ = np.arange(64)[:, None]
    t_ = np.arange(64)[None, :]
    tri = np.zeros((64, 2, 2, 64), f32)
    tri[:, 0, 0], tri[:, 0, 1] = (s_ < t_), (s_ <= t_)
    tri[:, 1, 0], tri[:, 1, 1] = (s_ > t_), (s_ >= t_)
    blk1 = np.zeros((128, 128), f32)
    blk1[:64, :64] = 1
    blk1[64:, 64:] = 1
    tri3 = np.zeros((64, 2, 192), f32)
    tri3[:, :, 0:128] = tri.reshape(64, 2, 128)
    tri3[:, 0, 128:192] = (s_ > t_)
    tri3[:, 1, 128:192] = (s_ < t_)
    return {
        "w_in": w_in, "w_rot": w_rot, "w_branch": g("w_branch"), "w_out": g("w_out"),
        "b_fm": b_fm, "b_cat": b_cat, "vecs": vecs,
        "ident_b": np.eye(128, dtype=f32).astype(ml_dtypes.bfloat16), "ident_f": np.eye(128, dtype=f32),
        "rope": np.ascontiguousarray(rope), "bandm": bandm, "na_bias": na_bias, "na_mask": na_mask,
        "rwp": rwp, "rw_w2": np.ascontiguousarray(g("rw_w2").reshape(2, 128, 256)),
        "rw_a2": np.ascontiguousarray(g("rw_a2").reshape(2, 128, 256)),
        "df_lam": np.ascontiguousarray(g("df_lam").reshape(2, 128)),
        "trimask": tri3, "blk1": blk1,
    }


_NC_CACHE = {}


def kernel(**inputs):
    xp = np.asarray(inputs["x_prompt"], np.float32)
    xs = np.asarray(inputs["x_sample"], np.float32)
    shared = _host_inputs(inputs)
    ncores = 8
    if "nc" not in _NC_CACHE:
        _NC_CACHE["nc"] = build(6)[0]
    nc = _NC_CACHE["nc"]
    in_maps = []
    for k in range(ncores):
        xk = np.concatenate([xp[4 * k:4 * k + 4], xs[2 * k:2 * k + 2]], axis=0)
        m = dict(shared)
        m["x"] = np.ascontiguousarray(xk)
        in_maps.append(m)
    res = run_bass_kernel_spmd(nc, in_maps, core_ids=list(range(ncores)))
    yp = np.empty_like(xp)
    ys = np.empty_like(xs)
    for k in range(ncores):
        yk = np.asarray(res.results[k]["y"], np.float32)
        yp[4 * k:4 * k + 4] = yk[0:4]
        ys[2 * k:2 * k + 2] = yk[4:6]
    return (yp, ys)


def branch_a(c, s, l):
    fw = c.fw
    arena_reset(c)
    qT = alloc(c, [128, 2, T], BF16)
    kT = alloc(c, [128, 2, T], BF16)
    vaug0 = alloc(c, [128, 16, 4, 128], BF16)
    vaug1 = alloc(c, [128, 15, 4, 128], BF16)
    vbias = alloc(c, [128, 256], F32)
    stg = alloc(c, [128, 14 * 256], F32)
    msk = alloc(c, [128, 14 * 256], F32)
    Mb = alloc(c, [128, 14, 256], BF16)
    pt = alloc(c, [128, 2, 4, 256], BF16, nslots=2)
    rd = alloc(c, [128, 2, 256], F32, nslots=2)
    c.g_sig = alloc(c, [128, 2, 512], F32, nslots=2)
    fw.dma(stg[:, :], V(c.na_bias[l], [c.tr_in]))
    fw.dma(msk[:, :], V(c.na_mask[:, :], [c.tr_in]))
    fw.act(stg[:, :], stg[:, :], AF.Exp)
    fw.tt(V(Mb.h.rearrange("p a b -> p (a b)"), Mb.trs), stg[:, :], msk[:, :], ALU.mult)
    vaug_init(c, vaug0)
    vaug_init(c, vaug1)
    plain_proj(c, l, A_Q, qT)
    plain_proj(c, l, A_K, kT)
    v_proj(c, l, A_V, vaug0, vbias, lambda j: slice(j * 128, (j + 1) * 128), 16)
    v_proj(c, l, A_V, vaug1, vbias, lambda j: slice(64 + j * 128, 64 + (j + 1) * 128), 15)
    rows = {}

    def stage1(r):
        rs = min(max(r - 4, 0), 24)
        sl = r % 2
        qs = slice(64 * r, 64 * r + 64)
        tiles = []
        for j in range(4):
            kr0 = rs + 2 * j
            oi = (rs - r + 2 * j) + 7
            p = pt.s(sl, (slice(None), sl, j, slice(None)))
            for par in range(2):
                sc = psA(c)
                pb = par * 64
                for hp in range(2):
                    fw.mm(sc[:, hp * 64:(hp + 1) * 64], kT[pb:pb + 64, hp, 64 * kr0:64 * kr0 + 128], qT[pb:pb + 64, hp, qs])
                fw.act(V(p.ap[:, par * 128:(par + 1) * 128], p.trs), sc[:, 0:128], AF.Exp, scale=0.125)
            fw.tt(p, p, Mb[:, oi, :], ALU.mult, e=("pool" if j % 2 == 0 else "dve"))
            tiles.append((p, (vaug0, kr0 // 2) if kr0 % 2 == 0 else (vaug1, (kr0 - 1) // 2)))
        rows[r] = tiles

    def stage2(r):
        tiles = rows.pop(r)
        sl = r % 2
        qs = slice(64 * r, 64 * r + 64)
        acc = psB(c)
        for h in range(4):
            for j, (p, (va, ti)) in enumerate(tiles):
                pc = (h % 2) * 128 + (h // 2) * 64
                fw.mm(acc[:, h * 64:(h + 1) * 64], va[:, ti, h, :], V(p.ap[:, pc:pc + 64], p.trs),
                      start=(j == 0), stop=(j == 3))
        a4 = acc.h[:, 0:256].rearrange("p (hp par q) -> p hp par q", hp=2, par=2)
        r4 = rd.h[:, sl, :].rearrange("p (hp par q) -> p hp par q", hp=2, par=2)
        rtr = [rd.trs[sl]]
        fw.recip(V(r4[0:64, :, 0, :], rtr), V(a4[64:128, :, 0, :], acc.trs))
        fw.recip(V(r4[64:128, :, 1, :], rtr), V(a4[0:64, :, 1, :], acc.trs))
        fw.tt(c.yT.s(0, (slice(0, 64), slice(0, 2), qs)), V(a4[0:64, :, 0, :], acc.trs), V(r4[0:64, :, 0, :], rtr), ALU.mult)
        fw.tt(c.yT.s(0, (slice(64, 128), slice(0, 2), qs)), V(a4[64:128, :, 1, :], acc.trs), V(r4[64:128, :, 1, :], rtr),
              ALU.mult)

    stage1(0)
    for r in range(32):
        if r + 1 < 32:
            stage1(r + 1)
        stage2(r)
    gate_branch(c, l, A_G, 0)


def branch_d(c, s, l):
    fw = c.fw
    arena_reset(c)
    cosT = alloc(c, [128, T], F32)
    sinT = alloc(c, [128, T], F32)
    fw.dma(cosT[:, :], V(c.rope[2], [c.tr_in]))
    fw.dma(sinT[:, :], V(c.rope[3], [c.tr_in]))
    acc = alloc(c, [128, 4, T], F32)
    qT = alloc(c, [128, 2, T], BF16)
    kT = alloc(c, [128, 2, T], BF16)
    tmp = alloc(c, [128, 4, 512], F32)
    t2b = alloc(c, [128, 2, 512], F32, nslots=2)
    c.g_sig = t2b
    vaug = alloc(c, [128, 16, 4, 128], BF16)
    vbias = alloc(c, [128, 256], F32)
    pt = alloc(c, [128, 2, 3, 512], BF16, nslots=2)
    bm = alloc(c, [128, 3, 512], BF16)
    fw.dma(bm[:, :, :], V(c.bandm.rearrange("p a h q -> p a (h q)"), [c.tr_in]))
    vaug_init(c, vaug)
    unit = 0
    for g, dil in enumerate((1, 4, 16)):
        nqb = T // dil // 128
        base = D_BASE + g * 768
        rope_proj(c, l, base, R_D + g * 512, cosT, sinT, qT, tmp, t2b)
        rope_proj(c, l, base + 256, R_D + g * 512 + 256, cosT, sinT, kT, tmp, t2b)

        def tsel(j, dil=dil, nqb=nqb):
            rho, jb = divmod(j, nqb)
            st = 128 * jb * dil + rho
            return slice(st, st + 127 * dil + 1, dil)

        v_proj(c, l, base + 512, vaug, vbias, tsel, 16)
        units = [(rho, qb) for rho in range(dil) for qb in range(nqb)]
        ust = {}

        def stage1(ui, units=units, tsel=tsel, nqb=nqb, ust=ust):
            rho, qb = units[ui]
            qs = tsel(rho * nqb + qb)
            kts = [(jb, mt) for (jb, mt) in ((qb - 1, 0), (qb, 1), (qb + 1, 2)) if 0 <= jb < nqb]
            sl = (unit0 + ui) % 2
            ps_ = []
            for idx, (jb, mt) in enumerate(kts):
                ks = tsel(rho * nqb + jb)
                p = pt.s(sl, (slice(None), sl, idx, slice(None)))
                for par in range(2):
                    sc = psA(c)
                    pb = par * 64
                    for hp in range(2):
                        fw.mm(sc[:, hp * 128:(hp + 1) * 128], V(kT.h[pb:pb + 64, hp, ks], kT.trs),
                              V(qT.h[pb:pb + 64, hp, qs], qT.trs))
                    fw.act(V(p.ap[:, par * 256:(par + 1) * 256], p.trs), sc[:, 0:256], AF.Exp, scale=0.125)
                fw.tt(p, p, bm[:, mt, :], ALU.mult, e=("pool" if idx % 2 == 0 else "dve"))
                ps_.append(p)
            ust[ui] = (qs, kts, ps_)

        def stage2(ui, units=units, nqb=nqb, ust=ust, g=g):
            rho, qb = units[ui]
            qs, kts, ps_ = ust.pop(ui)
            ob = psB(c)
            for h in range(4):
                for idx, (jb, mt) in enumerate(kts):
                    pc = (h % 2) * 256 + (h // 2) * 128
                    fw.mm(ob[:, h * 128:(h + 1) * 128], vaug[:, rho * nqb + jb, h, :],
                          V(ps_[idx].ap[:, pc:pc + 128], ps_[idx].trs),
                          start=(idx == 0), stop=(idx == len(kts) - 1))
            dst = V(acc.h[:, :, qs], acc.trs)
            src = V(ob.h[:, :].rearrange("p (h q) -> p h q", h=4), ob.trs)
            if g == 0:
                fw.copy(dst, src, e="act")
            else:
                fw.tt(dst, src, dst, ALU.add)

        unit0 = unit
        stage1(0)
        for ui in range(len(units)):
            if ui + 1 < len(units):
                stage1(ui + 1)
            stage2(ui)
        unit += len(units)
    a5 = acc.h.rearrange("p (hp par) t -> p hp par t", par=2)
    t5 = tmp.h.rearrange("p (hp par) t -> p hp par t", par=2)
    for tb in range(NTB):
        tok = slice(tb * 512, (tb + 1) * 512)
        fw.recip(V(t5[0:64, :, 0, :], tmp.trs), V(a5[64:128, :, 0, tok], acc.trs))
        fw.recip(V(t5[64:128, :, 1, :], tmp.trs), V(a5[0:64, :, 1, tok], acc.trs))
        fw.tt(c.yT.s(3, (slice(0, 64), slice(6, 8), tok)), V(a5[0:64, :, 0, tok], acc.trs), V(t5[0:64, :, 0, :], tmp.trs),
              ALU.mult)
        fw.tt(c.yT.s(3, (slice(64, 128), slice(6, 8), tok)), V(a5[64:128, :, 1, tok], acc.trs),
              V(t5[64:128, :, 1, :], tmp.trs), ALU.mult)
    gate_branch(c, l, D_G, 3)


GC = 4


def yslot_f32(c, slot):
    ap = c.yT.h[:, 2 * slot:2 * slot + 2, :].rearrange("p a t -> p (a t)").bitcast(F32)
    return Buf(ap, 1), c.yT.trs[slot]


def branch_b(c, s, l):
    for hp in range(2):
        rwkv_hp(c, l, hp)


def rwkv_hp(c, l, hp):
    fw = c.fw
    arena_reset(c)
    CD = -math.exp(-0.5)
    prm = alloc(c, [128, 64], F32)
    blk = alloc(c, [128, 128], F32)
    wst = alloc(c, [128, 2, 256], F32)
    w2sb = alloc(c, [128, 256], BF16)
    a2sb = alloc(c, [128, 256], BF16)
    der = alloc(c, [128, 16], F32)
    ones = alloc(c, [128, 64], F32)
    fw.dma(prm[:, :], V(c.rwp[l], [c.tr_in]))
    fw.dma(blk[:, :], V(c.blk1[:, :], [c.tr_in]))
    fw.dma(wst[:, 0, :], V(c.rw_w2[l], [c.tr_in]))
    fw.dma(wst[:, 1, :], V(c.rw_a2[l], [c.tr_in]))
    fw.copy(w2sb[:, :], wst[:, 0, :])
    fw.copy(a2sb[:, :], wst[:, 1, :])
    fw.memset(ones[:, :], 1.0)
    mucols = [1 + hp, 3 + hp, 5 + hp, 7 + hp, 9, 10]
    for i, mc in enumerate(mucols):
        fw.ts(der[:, i:i + 1], prm[:, mc:mc + 1], -1.0, 1.0, ALU.mult, ALU.add)
        fw.ts(der[:, 6 + i:7 + i], prm[:, mc:mc + 1], 0.5, None, ALU.mult)
    fw.ts(der[:, 12:13], prm[:, 21 + hp:22 + hp], -1.0, 1.0, ALU.mult, ALU.add)
    vT = alloc(c, [128, T], BF16)
    gs = alloc(c, [128, T], BF16)
    bv = alloc(c, [128, T], BF16)
    AR = [alloc(c, [128, 2, T], BF16) for _ in range(2)]
    BK = [alloc(c, [128, 2, T], BF16) for _ in range(2)]
    RLO = [alloc(c, [64, T], BF16) for _ in range(2)]
    eL = alloc(c, [128, 2, 32], F32)
    mark = c.arena_off

    uT = alloc(c, [128, 6, T + 2], BF16)
    fw.memset(uT[:, :, 0:1], 0.0)
    fw.memset(uT[:, :, T + 1:T + 2], 0.0)
    cols = [B_R + hp * 128, B_K + hp * 128, B_V + hp * 128, B_GT + hp * 128, B_WL, B_AL]

    def consume(i, tb, ps):
        ch = cols[i] // 128
        fw.act(uT[:, i, 1 + tb * 512:1 + (tb + 1) * 512], ps[:, :], AF.Identity, bias=c.bfm[:, l, ch:ch + 1])

    proj_fm(c, l, cols, consume)
    nt = 10
    tbuf = [alloc(c, [128, 512], F32) for _ in range(nt)]
    ex_ap = c.yT.h[:, 6:8, :].rearrange("p a t -> p (a t)").bitcast(F32)
    tbuf += [Buf(ex_ap[:, i * 512:(i + 1) * 512], 1) for i in range(4)]
    for b_ in tbuf[nt:]:
        b_.trs = [c.yT.trs[3]]
    xr, xk, kk, tA, tB, lw, L0, D_, eP, eM, eX, a_, kd, kdsum = [b_[:, :] for b_ in tbuf]
    twl = alloc(c, [128, 512], BF16)
    tal = alloc(c, [128, 512], BF16)
    for tb in range(NTB):
        tok = slice(tb * 512, (tb + 1) * 512)

        def shift(i, out):
            fw.tt(tA, uT[:, i, tb * 512:tb * 512 + 512], uT[:, i, tb * 512 + 2:tb * 512 + 514], ALU.add, e="pool")
            fw.ts(tA, tA, der[:, 6 + i:7 + i], None, ALU.mult)
            fw.stt(out, uT[:, i, tb * 512 + 1:tb * 512 + 513], der[:, i:i + 1], tA, ALU.mult, ALU.add)

        shift(0, xr)
        shift(1, xk)
        shift(2, tB)
        fw.copy(vT[:, tok], tB, e="act")
        shift(3, tB)
        fw.act(D_, tB, AF.Sigmoid)
        fw.tt(gs[:, tok], tB, D_, ALU.mult, e="pool")
        shift(4, tB)
        fw.act(twl[:, :], tB, AF.Tanh)
        shift(5, tB)
        fw.copy(tal[:, :], tB, e="act")
        fw.ts(kk, xk, prm[:, 19 + hp:20 + hp], None, ALU.mult)
        fw.tt(tB, kk, kk, ALU.mult, e="pool")
        ps = psA(c)
        fw.mm(ps[:, :], blk[:, :], tB)
        fw.ts(tB, ps[:, :], 1e-24, None, ALU.max)
        fw.act(tB, tB, AF.Sqrt)
        fw.recip(tB, tB)
        fw.tt(kk, kk, tB, ALU.mult)
        for d in range(2):
            dsl = slice(d * 64, (d + 1) * 64)
            ps = psA(c)
            fw.mm(ps[:, :], w2sb[dsl, hp * 128:(hp + 1) * 128], twl[dsl, :])
            fw.act(lw, ps[:, :], AF.Sigmoid, bias=prm[:, 11 + d * 2 + hp:12 + d * 2 + hp])
            for cc in range(8):
                cs = slice(cc * 64, (cc + 1) * 64)
                fw.emit("dve", lambda E, cs=cs: E.tensor_tensor_scan(out=L0.ap[:, cs], data0=ones.h[:, :], data1=lw.ap[:, cs],
                                                                    initial=0.0, op0=ALU.mult, op1=ALU.add),
                        [ones[:, :], lw], [L0])
            ltot = V(L0.ap[:, 63:512:64], L0.trs)
            fw.act(eL[:, d, tb * 8:(tb + 1) * 8], ltot, AF.Exp, scale=CD)
            if d == 0:
                fw.tt(D_, L0, lw, ALU.subtract, e="pool")
                fw.act(eP, L0, AF.Exp, scale=CD)
                fw.act(eM, L0, AF.Exp, scale=-CD)
                fw.act(eX, D_, AF.Exp, scale=CD)
            else:
                l3 = V(L0.ap.rearrange("p (c t) -> p c t", t=64), L0.trs)
                lt3 = V(L0.ap[:, 63:512:64].unsqueeze(2).to_broadcast([128, 8, 64]), L0.trs)
                fw.tt(V(D_.ap.rearrange("p (c t) -> p c t", t=64), D_.trs), l3, lt3, ALU.subtract)
                fw.tt(lw, D_, lw, ALU.subtract, e="pool")
                fw.act(eP, lw, AF.Exp, scale=-CD)
                fw.act(eM, lw, AF.Exp, scale=CD)
                fw.act(eX, D_, AF.Exp, scale=-CD)
            ps = psA(c)
            fw.mm(ps[:, :], a2sb[dsl, hp * 128:(hp + 1) * 128], tal[dsl, :])
            fw.act(a_, ps[:, :], AF.Sigmoid, bias=prm[:, 15 + d * 2 + hp:16 + d * 2 + hp])
            fw.ts(tB, a_, prm[:, 21 + hp:22 + hp], der[:, 12:13], ALU.mult, ALU.add)
            fw.tt(kd, tB, xk, ALU.mult)
            if d == 0:
                fw.copy(kdsum, kd, e="pool")
            else:
                fw.tt(kdsum, kdsum, kd, ALU.add, e="pool")
            fw.tt(tB, kk, a_, ALU.mult, e="pool")
            fw.tt(BK[d][:, 0, tok], tB, eM, ALU.mult)
            fw.tt(BK[d][:, 1, tok], kd, eM, ALU.mult, e="pool")
            fw.tt(AR[d][:, 0, tok], kk, eX, ALU.mult)
            fw.tt(AR[d][:, 1, tok], xr, eP, ALU.mult, e="pool")
            fw.tt(RLO[d][0:64, tok], V(xr.ap[64:128, :], xr.trs), V(eP.ap[64:128, :], eP.trs), ALU.mult)
        fw.stt(tB, xr, prm[:, 23 + hp:24 + hp], kdsum, ALU.mult, ALU.mult)
        ps = psA(c)
        fw.mm(ps[:, :], blk[:, :], tB)
        fw.tt(bv[:, tok], ps[:, :], vT[:, tok], ALU.mult)

    fw.barrier()
    c.arena_off = mark
    ys = []
    for d in range(2):
        b_, tr_ = yslot_f32(c, (0, 2)[d])
        b_.trs = [tr_]
        ys.append(b_)
    eLs = alloc(c, [64, 2, 2, 32], F32)
    for d in range(2):
        for hl in range(2):
            fw.copy(eLs[:, d, hl, :], eL[hl * 64:(hl + 1) * 64, d, :], e="act")
    tri = alloc(c, [64, 2, 192], F32)
    fw.dma(tri[:, :, :], V(c.trimask[:, :, :], [c.tr_in]))
    NM = GC * 4
    tmT = alloc(c, [64, GC, 2, 4, 128], BF16)
    Am = alloc(c, [64, NM, 256], BF16)
    Nb = [alloc(c, [64, NM, 64], BF16) for _ in range(2)]
    NTb = [alloc(c, [64, NM, 64], BF16) for _ in range(3)]
    Rb = [alloc(c, [64, NM, 64], BF16) for _ in range(2)]
    PT = alloc(c, [64, NM, 64], BF16)
    Zb = alloc(c, [64, NM, 64], BF16)
    Sb = alloc(c, [64, 2, 4, 64], BF16, nslots=2)
    Ub = alloc(c, [64, 4, 64], BF16)
    fw.memset(Sb[:, :, :, :], 0.0)
    I64b = c.identb[0:64, 0:64]
    nsteps = T // 64
    slot = 0
    for g in range(nsteps // GC):
        def chunk(ci, d):
            it = g * GC + ci
            return it if d == 0 else nsteps - 1 - it

        for ci in range(GC):
            for d in range(2):
                cs = slice(chunk(ci, d) * 64, chunk(ci, d) * 64 + 64)
                pb_ = psB(c)
                pbv = pb_.h[:, :].bitcast(BF16)
                srcs = [AR[d][:, 0, cs], BK[d][:, 0, cs], BK[d][:, 1, cs], vT[:, cs]]
                for q, src in enumerate(srcs):
                    fw.transpose(V(pbv[0:64, q * 128:(q + 1) * 128], pb_.trs), src, c.identb[:, :])
                fw.copy(V(tmT.h[:, ci, d, :, :].rearrange("p q e -> p (q e)"), tmT.trs), V(pbv[0:64, 0:512], pb_.trs))
        bN = [psB(c), psB(c)]
        for ci in range(GC):
            bA = [psA(c), psA(c)]
            for hl in range(2):
                pb = hl * 64
                for d in range(2):
                    cs = slice(chunk(ci, d) * 64, chunk(ci, d) * 64 + 64)
                    ar = V(AR[d].h[pb:pb + 64, :, cs], AR[d].trs)
                    for q in range(2):
                        o0 = d * 256 + q * 128
                        fw.mm(bA[hl][0:64, o0:o0 + 128], BK[d][pb:pb + 64, q, cs], ar)
                    o1 = (ci * 2 + d) * 64
                    fw.mm(bN[hl][0:64, o1:o1 + 64], AR[d][pb:pb + 64, 0, cs], BK[d][pb:pb + 64, 0, cs])
                m0 = ci * 4 + hl
                dst = V(Am.h[:, m0:m0 + 3:2, :].rearrange("p d (q e) -> p d q e", q=2), Am.trs)
                src = V(bA[hl].h[0:64, :].rearrange("p (d q e) -> p d q e", d=2, q=2), bA[hl].trs)
                msk = V(tri.h[:, :, 0:128].unsqueeze(2).to_broadcast([64, 2, 2, 128]), tri.trs)
                fw.tt(dst, src, msk, ALU.mult)
        for hl in range(2):
            dst = V(NTb[0].h[:, hl:NM:2, :].rearrange("p (ci d) e -> p ci d e", d=2), NTb[0].trs)
            src = V(bN[hl].h[0:64, :].rearrange("p (ci d e) -> p ci d e", ci=GC, d=2), bN[hl].trs)
            msk = V(tri.h[:, :, 128:192].unsqueeze(1).to_broadcast([64, GC, 2, 64]), tri.trs)
            fw.tt(dst, src, msk, ALU.mult, e="pool" if False else "dve")
        idb = V(c.identf.h[0:64, 0:64].unsqueeze(1).to_broadcast([64, NM, 64]), c.identf.trs)
        fw.tt(Rb[0][:, :, :], idb, Am[:, :, 0:64], ALU.subtract)
        Ncur = V(Am.h[:, :, 0:64], Am.trs)
        NTcur = NTb[0][:, :, :]
        rcur = 0
        ni = 0
        nti = 0
        for lev in range(1, 6):
            halves = [(0, NM // 2), (NM // 2, NM)]
            ntn = (nti + 1) % 3
            for (m0, m1) in halves:
                pz = psA(c)
                for m in range(m0, m1):
                    fw.mm(pz[0:64, (m - m0) * 64:(m - m0 + 1) * 64], V(Ncur.ap[:, m, :], Ncur.trs), V(NTcur.ap[:, m, :], NTcur.trs))
                fw.copy(V(NTb[ntn].h[:, m0:m1, :].rearrange("p m e -> p (m e)"), NTb[ntn].trs), pz[0:64, 0:(m1 - m0) * 64], e="act")
            if lev < 5:
                nn = ni % 2
                for (m0, m1) in halves:
                    pz = psA(c)
                    for m in range(m0, m1):
                        fw.mm(pz[0:64, (m - m0) * 64:(m - m0 + 1) * 64], V(NTcur.ap[:, m, :], NTcur.trs), V(Ncur.ap[:, m, :], Ncur.trs))
                    fw.copy(V(Nb[nn].h[:, m0:m1, :].rearrange("p m e -> p (m e)"), Nb[nn].trs), pz[0:64, 0:(m1 - m0) * 64])
                ni += 1
            rn = 1 - rcur
            for (m0, m1) in halves:
                pz = psB(c)
                for m in range(m0, m1):
                    o = pz[0:64, (m - m0) * 64:(m - m0 + 1) * 64]
                    fw.mm(o, I64b, Rb[rcur][:, m, :], start=True, stop=False)
                    fw.mm(o, NTb[ntn][:, m, :], Rb[rcur][:, m, :], start=False, stop=True)
                fw.copy(V(Rb[rn].h[:, m0:m1, :].rearrange("p m e -> p (m e)"), Rb[rn].trs), pz[0:64, 0:(m1 - m0) * 64], e="act")
            rcur = rn
            if lev < 5:
                Ncur = Nb[nn][:, :, :]
            NTcur = NTb[ntn][:, :, :]
            nti = ntn
        TT = Rb[rcur]
        for (m0, m1) in [(0, NM // 2), (NM // 2, NM)]:
            pz = psA(c)
            pq = psB(c)
            for m in range(m0, m1):
                ci, d, hl = m // 4, (m // 2) % 2, m % 2
                fw.mm(pz[0:64, (m - m0) * 64:(m - m0 + 1) * 64], tmT[:, ci, d, 0, hl * 64:(hl + 1) * 64], TT[:, m, :])
                fw.mm(pq[0:64, (m - m0) * 64:(m - m0 + 1) * 64], Am[:, m, 128:192], tmT[:, ci, d, 3, hl * 64:(hl + 1) * 64])
            fw.copy(V(PT.h[:, m0:m1, :].rearrange("p m e -> p (m e)"), PT.trs), pz[0:64, 0:(m1 - m0) * 64], e="act")
            fw.copy(V(Zb.h[:, m0:m1, :].rearrange("p m e -> p (m e)"), Zb.trs), pq[0:64, 0:(m1 - m0) * 64])
        for ci in range(GC):
            it = g * GC + ci
            scur = Sb.s(slot, (slice(None), slot, slice(None), slice(None)))
            snew = Sb.s(1 - slot, (slice(None), 1 - slot, slice(None), slice(None)))
            pu = psB(c)
            for inst in range(4):
                m = ci * 4 + inst
                o = pu[0:64, inst * 64:(inst + 1) * 64]
                fw.mm(o, PT[:, m, :], V(scur.ap[:, inst, :], scur.trs), start=True, stop=False)
                fw.mm(o, TT[:, m, :], Zb[:, m, :], start=False, stop=True)
            fw.act(V(Ub.h.rearrange("p i e -> p (i e)"), Ub.trs), pu[0:64, 0:256], AF.Copy, scale=-1.0)
            pS = psA(c)
            pY = psB(c)
            for inst in range(4):
                m = ci * 4 + inst
                d, hl = inst // 2, inst % 2
                cs = slice(chunk(ci, d) * 64, chunk(ci, d) * 64 + 64)
                hs = slice(hl * 64, (hl + 1) * 64)
                o = pS[0:64, inst * 64:(inst + 1) * 64]
                fw.mm(o, I64b, V(scur.ap[:, inst, :], scur.trs), start=True, stop=False)
                fw.mm(o, tmT[:, ci, d, 2, hs], tmT[:, ci, d, 3, hs], start=False, stop=False)
                fw.mm(o, tmT[:, ci, d, 1, hs], Ub[:, inst, :], start=False, stop=True)
                oy = pY[0:64, inst * 64:(inst + 1) * 64]
                rt = AR[d][0:64, 1, cs] if hl == 0 else RLO[d][0:64, cs]
                fw.mm(oy, V(scur.ap[:, inst, :], scur.trs), rt, start=True, stop=False)
                fw.mm(oy, tmT[:, ci, d, 3, hs], Am[:, m, 192:256], start=False, stop=False)
                fw.mm(oy, Ub[:, inst, :], Am[:, m, 64:128], start=False, stop=True)
            for d in range(2):
                ch = chunk(ci, d)
                esc = V(eLs.h[:, d, :, ch:ch + 1].to_broadcast([64, 2, 64]), eLs.trs)
                fw.tt(V(snew.ap[:, 2 * d:2 * d + 2, :], snew.trs),
                      V(pS.h[0:64, d * 128:(d + 1) * 128].rearrange("p (i e) -> p i e", i=2), pS.trs), esc, ALU.mult)
                cs = slice(ch * 64, ch * 64 + 64)
                for hl in range(2):
                    inst = d * 2 + hl
                    fw.copy(ys[d][hl * 64:(hl + 1) * 64, cs], pY[0:64, inst * 64:(inst + 1) * 64], e="act")
            slot = 1 - slot

    fw.barrier()
    c.arena_off = mark
    pt_ = [alloc(c, [128, 512], F32)[:, :] for _ in range(6)]
    y_, sq, mean, var, t1, t2 = pt_
    for tb in range(NTB):
        tok = slice(tb * 512, (tb + 1) * 512)
        fw.tt(y_, ys[0][:, tok], ys[1][:, tok], ALU.add)
        fw.tt(sq, y_, y_, ALU.mult, e="pool")
        p1 = psA(c)
        fw.mm(p1[:, :], blk[:, :], y_)
        p2 = psA(c)
        fw.mm(p2[:, :], blk[:, :], sq)
        fw.ts(mean, p1[:, :], 1.0 / 64.0, None, ALU.mult)
        fw.tt(t1, mean, mean, ALU.mult, e="pool")
        fw.stt(var, p2[:, :], 1.0 / 64.0, t1, ALU.mult, ALU.subtract)
        fw.ts(var, var, 64e-5, None, ALU.add)
        fw.act(var, var, AF.Sqrt)
        fw.recip(var, var)
        fw.tt(t2, y_, mean, ALU.subtract, e="pool")
        fw.tt(t2, t2, var, ALU.mult)
        fw.ts(t2, t2, prm[:, 25 + hp:26 + hp], prm[:, 27 + hp:28 + hp], ALU.mult, ALU.add)
        fw.tt(t2, t2, bv[:, tok], ALU.add, e="pool")
        fw.tt(c.yT.s(1, (slice(None), 2 + hp, tok)), t2, gs[:, tok], ALU.mult)
```

```python
import math
import os
import numpy as np
import ml_dtypes
import concourse.bass as bass
import concourse.mybir as mybir
from concourse.bass_utils import run_bass_kernel_spmd

F32 = mybir.dt.float32
BF16 = mybir.dt.bfloat16
AF = mybir.ActivationFunctionType
ALU = mybir.AluOpType

T = 2048
DM = 1024
NTB = 4
IN_COLS = 9984
ROT_COLS = 2048
WCOLS = IN_COLS + ROT_COLS
ALPHA = (2 * 2) ** 0.25
LN_EPS = 1e-5
ARENA_F32 = 30208


class Tr:
    __slots__ = ("w", "r", "excl")

    def __init__(self, excl=False):
        self.w = None
        self.r = {}
        self.excl = excl


class V:
    __slots__ = ("ap", "trs")

    def __init__(self, ap, trs):
        self.ap = ap
        self.trs = trs


class Buf:
    def __init__(self, h, nslots=1):
        self.h = h
        self.trs = [Tr() for _ in range(nslots)]

    def __getitem__(self, idx):
        return V(self.h[idx], self.trs)

    def s(self, slot, idx):
        return V(self.h[idx], [self.trs[slot]])


class FW:
    LIMIT = 30000

    def __init__(self, nc, ndma=24):
        self.nc = nc
        self.eng = {"pe": nc.tensor, "act": nc.scalar, "dve": nc.vector, "pool": nc.gpsimd, "sp": nc.sync}
        self.sems = []
        self.cur = {}
        self.cnt = {}
        for e in self.eng:
            self.cur[e] = self._newsem("e_" + e)
            self.cnt[e] = 0
        self.known = {e: {} for e in self.eng}
        self.dma_sem = [self._newsem("dma%d" % i) for i in range(ndma)]
        self.dma_val = [0] * ndma
        self.n_hw = ndma
        self.dma_next = 0
        self.n_ins = 0
        self._uid = 0

    def _newsem(self, name):
        self._uid = getattr(self, "_uid", 0) + 1
        h = self.nc.alloc_semaphore("%s_%d" % (name, self._uid))
        self.sems.append(h)
        return len(self.sems) - 1

    def _wait(self, e, ev):
        si, val, src = ev
        if self.known[e].get(si, 0) >= val:
            return
        self.eng[e].wait_ge(self.sems[si], val)
        self.known[e][si] = val

    def _deps(self, e, reads, writes):
        for v in reads:
            for tr in v.trs:
                if tr.w is not None:
                    if tr.w[2] == e and e == "pe":
                        continue
                    self._wait(e, tr.w)
                if tr.excl:
                    for src, ev in tr.r.items():
                        if ev[2] != e:
                            self._wait(e, ev)
        for v in writes:
            for tr in v.trs:
                if tr.w is not None and not (tr.w[2] == e and e == "pe"):
                    self._wait(e, tr.w)
                for src, ev in tr.r.items():
                    if not (ev[2] == e and e == "pe"):
                        self._wait(e, ev)

    def _record(self, ev, reads, writes, key):
        for v in writes:
            for tr in v.trs:
                tr.w = ev
                tr.r = {}
        for v in reads:
            for tr in v.trs:
                tr.r[key] = ev

    def emit(self, e, fn, reads, writes):
        self._deps(e, reads, writes)
        ins = fn(self.eng[e])
        if self.cnt[e] >= self.LIMIT:
            self.cur[e] = self._newsem("e_" + e)
            self.cnt[e] = 0
        self.cnt[e] += 1
        ins.then_inc(self.sems[self.cur[e]], 1)
        ev = (self.cur[e], self.cnt[e], e)
        self._record(ev, reads, writes, e)
        self.n_ins += 1
        return ev

    def dma(self, out, in_, e="sp"):
        self._deps(e, [in_], [out])
        if e == "pool":
            self.dma_sem.append(self._newsem("swdma"))
            self.dma_val.append(0)
            slot = len(self.dma_sem) - 1
        else:
            slot = self.dma_next
            self.dma_next = (slot + 1) % self.n_hw
        si = self.dma_sem[slot]
        if self.dma_val[slot] > 0:
            self._wait(e, (si, self.dma_val[slot], "dma"))
        ins = self.eng[e].dma_start(out=out.ap, in_=in_.ap)
        self.dma_val[slot] += 16
        ins.then_inc(self.sems[si], 16)
        ev = (si, self.dma_val[slot], "dma%d" % slot)
        self._record(ev, [in_], [out], "dma%d" % slot)
        self.n_ins += 1
        return ev

    def barrier(self):
        evs = [(self.cur[f], self.cnt[f], f) for f in self.eng if self.cnt[f] > 0]
        evs += [(self.dma_sem[i], self.dma_val[i], "dma") for i in range(len(self.dma_sem)) if self.dma_val[i] > 0]
        for e in self.eng:
            for ev in evs:
                if not (ev[2] == e and e == "pe"):
                    self._wait(e, ev)

    def mm(self, out, lhsT, rhs, start=True, stop=True):
        return self.emit("pe", lambda E: E.matmul(out.ap, lhsT=lhsT.ap, rhs=rhs.ap, start=start, stop=stop),
                         [lhsT, rhs], [out])

    def transpose(self, out, in_, ident):
        return self.emit("pe", lambda E: E.transpose(out.ap, in_.ap, ident.ap), [in_, ident], [out])

    def act(self, out, in_, func, bias=None, scale=None, e="act"):
        kw = {}
        rd = [in_]
        if bias is not None:
            if isinstance(bias, V):
                kw["bias"] = bias.ap
                rd.append(bias)
            else:
                kw["bias"] = bias
        if scale is not None:
            if isinstance(scale, V):
                kw["scale"] = scale.ap
                rd.append(scale)
            else:
                kw["scale"] = scale
        return self.emit("act", lambda E: E.activation(out=out.ap, in_=in_.ap, func=func, **kw), rd, [out])

    def tt(self, out, in0, in1, op, e="dve"):
        return self.emit(e, lambda E: E.tensor_tensor(out=out.ap, in0=in0.ap, in1=in1.ap, op=op), [in0, in1], [out])

    def ts(self, out, in0, s1, s2, op0, op1=None, e="dve"):
        rd = [in0]
        a1 = s1
        a2 = s2
        if isinstance(s1, V):
            rd.append(s1)
            a1 = s1.ap
        if isinstance(s2, V):
            rd.append(s2)
            a2 = s2.ap
        if op1 is None:
            return self.emit(e, lambda E: E.tensor_scalar(out=out.ap, in0=in0.ap, scalar1=a1, scalar2=None, op0=op0),
                             rd, [out])
        return self.emit(e, lambda E: E.tensor_scalar(out=out.ap, in0=in0.ap, scalar1=a1, scalar2=a2, op0=op0, op1=op1),
                         rd, [out])

    def stt(self, out, in0, scalar, in1, op0, op1):
        rd = [in0, in1]
        a = scalar
        if isinstance(scalar, V):
            rd.append(scalar)
            a = scalar.ap
        return self.emit("dve", lambda E: E.scalar_tensor_tensor(out=out.ap, in0=in0.ap, scalar=a, in1=in1.ap,
                                                                 op0=op0, op1=op1), rd, [out])

    def copy(self, out, in_, e="dve"):
        if e == "act":
            return self.act(out, in_, AF.Copy)
        return self.emit(e, lambda E: E.tensor_copy(out=out.ap, in_=in_.ap), [in_], [out])

    def recip(self, out, in_):
        return self.emit("dve", lambda E: E.reciprocal(out=out.ap, in_=in_.ap), [in_], [out])

    def memset(self, out, val, e="dve"):
        return self.emit(e, lambda E: E.memset(out.ap, val), [], [out])


A_Q, A_K, A_V, A_G = 0, 256, 512, 768
B_R, B_K, B_V, B_GT, B_WL, B_AL = 1024, 1280, 1536, 1792, 2048, 2176
C_Q, C_K, C_V, C_G = 2304, 2560, 2816, 3072
D_BASE, D_G = 3328, 5632
MG = 5888
R_CQ, R_CK, R_D = 9984, 10240, 10496
NCH = WCOLS // 128


def _rot_perm():
    cols = []
    for base in (C_Q, C_K):
        for c in range(256):
            j = c % 32
            cols.append(base + c - j + (j + 16) % 32)
    for g in range(3):
        for part in (0, 256):
            base = D_BASE + g * 768 + part
            for c in range(256):
                j = c % 64
                cols.append(base + c - j + (j + 32) % 64)
    return np.asarray(cols, np.int64)


def _rope_tables():
    t = np.arange(T, dtype=np.float32)
    out = []
    for d in (32, 64):
        half = d // 2
        inv = np.power(np.float32(10000.0), -np.arange(half, dtype=np.float32) / np.float32(half)).astype(np.float32)
        ang = (t[:, None] * inv[None, :]).astype(np.float32)
        cos = np.cos(ang).astype(np.float32)
        sin = np.sin(ang).astype(np.float32)
        p = np.arange(128)
        j = p % d
        cosT = cos[:, j % half].T
        sgn = np.where(j < half, -1.0, 1.0).astype(np.float32)
        sinT = (sin[:, j % half].T * sgn[:, None]).astype(np.float32)
        out += [np.ascontiguousarray(cosT), np.ascontiguousarray(sinT)]
    return out


class Ctx:
    pass


def dram(nc, name, shape, dtype, kind):
    return nc.dram_tensor(name, list(shape), dtype, kind=kind).ap()


def build(nseq, parts=("A", "B", "C", "D"), dbg=False):
    nc = bass.Bass("TRN2", target_bir_lowering=False)
    fw = FW(nc)
    c = Ctx()
    c.nc, c.fw, c.parts, c.dbg = nc, fw, parts, dbg
    IN, OUT, INT = "ExternalInput", "ExternalOutput", "Internal"
    c.x = dram(nc, "x", [nseq, T, DM], F32, IN)
    c.y = dram(nc, "y", [nseq, T, DM], F32, OUT)
    c.w_in = dram(nc, "w_in", [2, DM, IN_COLS], F32, IN)
    c.w_rot = dram(nc, "w_rot", [2, DM, ROT_COLS], F32, IN)
    c.w_br = dram(nc, "w_branch", [2, 4, 256, DM], F32, IN)
    c.w_out = dram(nc, "w_out", [2, DM, DM], F32, IN)
    c.b_fm = dram(nc, "b_fm", [2, 128, NCH], F32, IN)
    c.b_cat = dram(nc, "b_cat", [2, WCOLS], F32, IN)
    c.vecs = dram(nc, "vecs", [8, DM], F32, IN)
    c.ident_b = dram(nc, "ident_b", [128, 128], BF16, IN)
    c.ident_f = dram(nc, "ident_f", [128, 128], F32, IN)
    c.rope = dram(nc, "rope", [4, 128, T], F32, IN)
    c.bandm = dram(nc, "bandm", [128, 3, 4, 128], BF16, IN)
    c.na_bias = dram(nc, "na_bias", [2, 128, 14 * 256], F32, IN)
    c.na_mask = dram(nc, "na_mask", [128, 14 * 256], F32, IN)
    c.rwp = dram(nc, "rwp", [2, 128, 64], F32, IN)
    c.rw_w2 = dram(nc, "rw_w2", [2, 128, 256], F32, IN)
    c.rw_a2 = dram(nc, "rw_a2", [2, 128, 256], F32, IN)
    c.df_lam = dram(nc, "df_lam", [2, 128], F32, IN)
    c.trimask = dram(nc, "trimask", [64, 2, 192], F32, IN)
    c.blk1 = dram(nc, "blk1", [128, 128], F32, IN)
    c.wbf = dram(nc, "wbf", [2, DM, WCOLS], BF16, INT)
    c.wbr_bf = dram(nc, "wbr_bf", [2, 4, 256, DM], BF16, INT)
    c.wout_bf = dram(nc, "wout_bf", [2, DM, DM], BF16, INT)
    c.xres = dram(nc, "xres", [T, DM], F32, INT)
    c.tr_wbf = [[Tr() for _ in range(NCH)] for _ in range(2)]
    c.tr_wbr = [Tr(), Tr()]
    c.tr_wout = [Tr(), Tr()]
    c.tr_xres = [Tr() for _ in range(16)]
    c.tr_in = Tr()
    c.tr_y = Tr()
    if dbg:
        c.dbg_out = dram(nc, "dbg", [128, 8, T], BF16, OUT)
        c.tr_dbg = Tr()

    def sb(name, shape, dtype, nslots=1):
        return Buf(nc.alloc_sbuf_tensor(name, list(shape), dtype), nslots)

    c.sb = sb
    c.xT = sb("xT", [128, 8, T], BF16)
    c.yT = sb("yT", [128, 8, T], BF16, nslots=4)
    c.wbuf = sb("wbuf", [128, 2, 8, 512], BF16, nslots=2)
    c.identb = sb("identb", [128, 128], BF16)
    c.identf = sb("identf", [128, 128], F32)
    c.bfm = sb("bfm", [128, 2, NCH], F32)
    c.arena = nc.alloc_sbuf_tensor("arena", [128, ARENA_F32], F32)
    c.arena_off = 0
    c.ps = [Buf(nc.alloc_psum_tensor("ps%d" % i, [128, 512], F32)) for i in range(8)]
    for b_ in c.ps:
        b_.trs = [Tr(excl=True)]
    c.ps_i = [0, 0]

    fw.dma(c.identb[:, :], V(c.ident_b[:, :], [c.tr_in]))
    fw.dma(c.identf[:, :], V(c.ident_f[:, :], [c.tr_in]))
    for l in range(2):
        fw.dma(c.bfm[:, l, :], V(c.b_fm[l], [c.tr_in]))

    convert_weights(c)
    for s in range(nseq):
        stage0(c, s)
        for l in range(2):
            layer(c, s, l)
    for i in range(len(fw.dma_sem)):
        if fw.dma_val[i] > 0:
            fw._wait("sp", (fw.dma_sem[i], fw.dma_val[i], "dma"))
    return nc, fw


def arena_reset(c):
    c.fw.barrier()
    c.arena_off = 0


def alloc(c, shape, dtype, nslots=1):
    n = int(np.prod(shape[1:]))
    n4 = (n + 1) // 2 if dtype == BF16 else n
    n4 = (n4 + 7) // 8 * 8
    assert c.arena_off + n4 <= ARENA_F32, ("arena overflow", c.arena_off, n4)
    ap = c.arena[0:shape[0], c.arena_off:c.arena_off + n4]
    c.arena_off += n4
    if dtype == BF16:
        ap = ap.bitcast(BF16)[:, 0:n]
    else:
        ap = ap[:, 0:n]
    if len(shape) > 2:
        names = " ".join("d%d" % i for i in range(len(shape) - 1))
        kw = {"d%d" % i: shape[i + 1] for i in range(len(shape) - 1)}
        ap = ap.rearrange("p (%s) -> p %s" % (names, names), **kw)
    return Buf(ap, nslots)


def load_vecs(c, rows):
    for i, r in enumerate(rows):
        c.fw.dma(c.vec_bc[:, i, :], V(c.vecs[r:r + 1, :].partition_broadcast(128), [c.tr_in]))


def psA(c):
    i = c.ps_i[0]
    c.ps_i[0] = (i + 1) % 4
    return c.ps[i]


def psB(c):
    i = c.ps_i[1]
    c.ps_i[1] = (i + 1) % 4
    return c.ps[4 + i]


def convert_weights(c):
    fw = c.fw
    rd = [c.tr_in]
    for l in range(2):
        blocks = [(0, 512 * i, 512) for i in range(11)] + [(0, 5632, 256)]
        for (_, c0, n) in blocks:
            trs = c.tr_wbf[l][c0 // 128:(c0 + n) // 128]
            fw.dma(V(c.wbf[l, :, c0:c0 + n], trs), V(c.w_in[l, :, c0:c0 + n], rd), e="pool")
        dst = c.wbf[l, :, MG:IN_COLS].rearrange("k (dc b j) -> k dc b j", dc=8, b=4)
        for b in range(4):
            src = c.w_in[l, :, MG + b * 1024:MG + (b + 1) * 1024].rearrange("k (dc j) -> k dc j", dc=8)
            fw.dma(V(dst[:, :, b, :], c.tr_wbf[l][MG // 128:IN_COLS // 128]), V(src, rd), e="pool")
        for i in range(4):
            c0 = IN_COLS + 512 * i
            fw.dma(V(c.wbf[l, :, c0:c0 + 512], c.tr_wbf[l][c0 // 128:c0 // 128 + 4]),
                   V(c.w_rot[l, :, 512 * i:512 * (i + 1)], rd), e="pool")
        for b in range(4):
            fw.dma(V(c.wbr_bf[l, b], [c.tr_wbr[l]]), V(c.w_br[l, b], rd), e="pool")
        for i in range(2):
            fw.dma(V(c.wout_bf[l, :, 512 * i:512 * (i + 1)], [c.tr_wout[l]]),
                   V(c.w_out[l, :, 512 * i:512 * (i + 1)], rd), e="pool")


def load_w(c, l, c0, n):
    fw = c.fw
    slot = getattr(c, "_wslot", 0)
    c._wslot = 1 - slot
    src = c.wbf[l, :, c0:c0 + n].rearrange("(kc p) n -> p kc n", p=128)
    trs = c.tr_wbf[l][c0 // 128:(c0 + n + 127) // 128]
    fw.dma(c.wbuf.s(slot, (slice(None), slot, slice(None), slice(0, n))), V(src, trs))
    return slot


def wv(c, slot, kc, j0, n):
    return c.wbuf.s(slot, (slice(None), slot, kc, slice(j0, j0 + n)))


def proj_fm(c, l, col_list, consume):
    fw = c.fw
    groups = []
    for i, c0 in enumerate(col_list):
        if groups and groups[-1][0] + groups[-1][1] == c0 and groups[-1][1] < 512:
            groups[-1][1] += 128
            groups[-1][2].append(i)
        else:
            groups.append([c0, 128, [i]])
    slots = [None] * len(groups)
    slots[0] = load_w(c, l, groups[0][0], groups[0][1])
    for gi, (g0, gn, idxs) in enumerate(groups):
        if gi + 1 < len(groups):
            slots[gi + 1] = load_w(c, l, groups[gi + 1][0], groups[gi + 1][1])
        for j, i in enumerate(idxs):
            for tb in range(NTB):
                ps = psA(c)
                for kc in range(8):
                    fw.mm(ps[:, :], wv(c, slots[gi], kc, j * 128, 128), c.xT[:, kc, tb * 512:(tb + 1) * 512],
                          start=(kc == 0), stop=(kc == 7))
                consume(i, tb, ps)


def ln_rows(c, z, gi, bi, out):
    fw = c.fw
    st = c.ln_st
    fw.emit("dve", lambda E: E.bn_stats(out=st.h[:, 0, :], in_=z.ap[:, 0:512]), [z], [st[:, 0, :]])
    fw.emit("dve", lambda E: E.bn_stats(out=st.h[:, 1, :], in_=z.ap[:, 512:1024]), [z], [st[:, 1, :]])
    mv = c.ln_mv
    fw.emit("dve", lambda E: E.bn_aggr(out=mv.h[:, 0:2], in_=st.h[:, :, :]), [st[:, :, :]], [mv[:, 0:2]])
    fw.ts(mv[:, 2:3], mv[:, 1:2], LN_EPS, None, ALU.add)
    fw.act(mv[:, 3:4], mv[:, 2:3], AF.Sqrt)
    fw.recip(mv[:, 4:5], mv[:, 3:4])
    fw.ts(out, z, mv[:, 0:1], mv[:, 4:5], ALU.subtract, ALU.mult)
    fw.tt(out, out, c.vec_bc[:, gi, :], ALU.mult, e="pool")
    fw.tt(out, out, c.vec_bc[:, bi, :], ALU.add, e="pool")


def to_xT(c, rows, tt):
    fw = c.fw
    xb = c.xb_t
    fw.copy(xb[:, :], rows, e="act")
    ps = psB(c)
    psb = V(ps.h[:, :].bitcast(BF16), ps.trs)
    for k in range(8):
        fw.transpose(V(psb.ap[:, k * 128:(k + 1) * 128], ps.trs), xb[:, k * 128:(k + 1) * 128], c.identb[:, :])
    fw.copy(c.xT[:, :, tt * 128:(tt + 1) * 128], V(psb.ap.rearrange("p (k t) -> p k t", k=8), ps.trs))


def ln_allocs(c):
    c.ln_st = alloc(c, [128, 2, 6], F32)
    c.ln_mv = alloc(c, [128, 8], F32)
    c.xb_t = alloc(c, [128, DM], BF16)
    c.zt = alloc(c, [128, 2, DM], F32, nslots=2)
    c.ot = alloc(c, [128, 2, DM], F32, nslots=2)
    c.vec_bc = alloc(c, [128, 3, DM], F32)


def stage0(c, s):
    fw = c.fw
    arena_reset(c)
    ln_allocs(c)
    load_vecs(c, [0, 1])
    for tt in range(16):
        sl = tt % 2
        z = c.zt.s(sl, (slice(None), sl, slice(None)))
        o = c.ot.s(sl, (slice(None), sl, slice(None)))
        fw.dma(z, V(c.x[s, tt * 128:(tt + 1) * 128, :], [c.tr_in]))
        ln_rows(c, z, 0, 1, o)
        fw.dma(V(c.xres[tt * 128:(tt + 1) * 128, :], [c.tr_xres[tt]]), o)
        to_xT(c, o, tt)


def gate_branch(c, l, gcol, bi):
    fw = c.fw

    def consume(i, tb, ps):
        ch = (gcol // 128) + i
        sg = c.g_sig.s(tb % 2, (slice(None), tb % 2, slice(None)))
        fw.act(sg, ps[:, :], AF.Sigmoid, bias=c.bfm[:, l, ch:ch + 1])
        fw.stt(sg, ps[:, :], c.bfm[:, l, ch:ch + 1], sg, ALU.add, ALU.mult)
        yv = c.yT.s(bi, (slice(None), bi * 2 + i, slice(tb * 512, (tb + 1) * 512)))
        fw.tt(yv, yv, sg, ALU.mult, e="pool")

    proj_fm(c, l, [gcol, gcol + 128], consume)


def final_stage(c, s, l):
    fw = c.fw
    arena_reset(c)
    ln_allocs(c)
    c.mergedT = alloc(c, [128, 8, T], BF16)
    c.wbr_sb = alloc(c, [128, 4, 2, DM], BF16)
    c.wout_sb = alloc(c, [128, 8, DM], BF16)
    c.f_sig = alloc(c, [128, 2, 512], F32, nslots=2)
    c.f_acc = alloc(c, [128, 2, 512], F32, nslots=2)
    c.f_tmp = alloc(c, [128, 2, 512], F32, nslots=2)
    load_vecs(c, [2 + 3 * l, 3 + 3 * l, 4 + 3 * l])
    for b in range(4):
        fw.dma(c.wbr_sb[:, b, :, :], V(c.wbr_bf[l, b].rearrange("(kc p) d -> p kc d", p=128), [c.tr_wbr[l]]))
    fw.dma(c.wout_sb[:, :, :], V(c.wout_bf[l].rearrange("(kc p) d -> p kc d", p=128), [c.tr_wout[l]]))
    slots = [None] * 8
    slots[0] = load_w(c, l, MG, 512)
    for dc in range(8):
        if dc + 1 < 8:
            slots[dc + 1] = load_w(c, l, MG + (dc + 1) * 512, 512)
        for tb in range(NTB):
            tok = slice(tb * 512, (tb + 1) * 512)
            ai = tb % 2
            acc = c.f_acc.s(ai, (slice(None), ai, slice(None)))
            for b in range(4):
                pg = psA(c)
                for kc in range(8):
                    fw.mm(pg[:, :], wv(c, slots[dc], kc, b * 128, 128), c.xT[:, kc, tok], start=(kc == 0), stop=(kc == 7))
                ch = MG // 128 + b * 8 + dc
                si = b % 2
                sg = c.f_sig.s(si, (slice(None), si, slice(None)))
                fw.act(sg, pg[:, :], AF.Sigmoid, bias=c.bfm[:, l, ch:ch + 1])
                pp = psB(c)
                for kc in range(2):
                    fw.mm(pp[:, :], c.wbr_sb[:, b, kc, dc * 128:(dc + 1) * 128],
                          c.yT.s(b, (slice(None), b * 2 + kc, tok)), start=(kc == 0), stop=(kc == 1))
                if b == 0:
                    fw.tt(acc, pp[:, :], sg, ALU.mult)
                else:
                    tm = c.f_tmp.s(si, (slice(None), si, slice(None)))
                    fw.tt(tm, pp[:, :], sg, ALU.mult)
                    if b < 3:
                        fw.tt(acc, acc, tm, ALU.add, e="pool")
                    else:
                        fw.tt(c.mergedT[:, dc, tok], acc, tm, ALU.add, e="pool")
    for tt in range(16):
        sl = tt % 2
        z = c.zt.s(sl, (slice(None), sl, slice(None)))
        o = c.ot.s(sl, (slice(None), sl, slice(None)))
        fw.dma(z, V(c.xres[tt * 128:(tt + 1) * 128, :], [c.tr_xres[tt]]))
        for hf in range(2):
            po = psA(c)
            for kc in range(8):
                fw.mm(po[:, :], c.mergedT[:, kc, tt * 128:(tt + 1) * 128], c.wout_sb[:, kc, hf * 512:(hf + 1) * 512],
                      start=(kc == 0), stop=(kc == 7))
            zh = V(z.ap[:, hf * 512:(hf + 1) * 512], z.trs)
            fw.stt(zh, zh, ALPHA, po[:, :], ALU.mult, ALU.add)
        fw.tt(z, z, c.vec_bc[:, 0, :], ALU.add, e="pool")
        ln_rows(c, z, 1, 2, o)
        if l == 0:
            fw.dma(V(c.xres[tt * 128:(tt + 1) * 128, :], [c.tr_xres[tt]]), o)
            to_xT(c, o, tt)
        else:
            fw.dma(V(c.y[s, tt * 128:(tt + 1) * 128, :], [c.tr_y]), o)


def layer(c, s, l):
    fw = c.fw
    if "B" in c.parts:
        branch_b(c, s, l)
    for bi, name in enumerate("ABCD"):
        if name not in c.parts:
            fw.memset(c.yT.s(bi, (slice(None), slice(bi * 2, bi * 2 + 2), slice(None))), 0.0, e="pool")
    if "A" in c.parts:
        branch_a(c, s, l)
    if "C" in c.parts:
        branch_c(c, s, l)
    if "D" in c.parts:
        branch_d(c, s, l)
    if c.dbg and l == 0 and s == 0:
        fw.dma(V(c.dbg_out[:, :, :], [c.tr_dbg]), c.yT[:, :, :])
    final_stage(c, s, l)


def rope_proj(c, l, qcol, rcol, cosT, sinT, dst, tmp, t2b):
    fw = c.fw

    def consume(i, tb, ps):
        ci = i // 2
        tok = slice(tb * 512, (tb + 1) * 512)
        if i % 2 == 0:
            ch = qcol // 128 + ci
            fw.stt(tmp[:, tb, :], ps[:, :], c.bfm[:, l, ch:ch + 1], cosT[:, tok], ALU.add, ALU.mult)
        else:
            ch = rcol // 128 + ci
            t2 = t2b.s(tb % 2, (slice(None), tb % 2, slice(None)))
            fw.stt(t2, ps[:, :], c.bfm[:, l, ch:ch + 1], sinT[:, tok], ALU.add, ALU.mult)
            fw.tt(dst[:, ci, tok], tmp[:, tb, :], t2, ALU.add, e="pool")

    proj_fm(c, l, [qcol, rcol, qcol + 128, rcol + 128], consume)


def plain_proj(c, l, col, dst):
    fw = c.fw

    def consume(i, tb, ps):
        ch = col // 128 + i
        fw.ts(dst[:, i, tb * 512:(tb + 1) * 512], ps[:, :], c.bfm[:, l, ch:ch + 1], None, ALU.add)

    proj_fm(c, l, [col, col + 128], consume)


def vaug_init(c, vaug):
    v5 = vaug.h.rearrange("p j (hp par) e -> p j hp par e", par=2)
    c.fw.memset(V(v5[:, :, :, 0, 64:128], vaug.trs), 1.0, e="pool")
    c.fw.memset(V(v5[:, :, :, 1, 0:64], vaug.trs), 1.0, e="pool")


def v_proj(c, l, vcol, vaug, vbias, tok_sel, ntiles=16):
    fw = c.fw
    slot = load_w(c, l, vcol, 256)
    fw.dma(vbias[:, :], V(c.b_cat[l:l + 1, vcol:vcol + 256].partition_broadcast(128), [c.tr_in]))
    v5 = vaug.h.rearrange("p j (hp par) e -> p j hp par e", par=2)
    b4 = vbias.h.rearrange("p (hp par e) -> p hp par e", hp=2, par=2)
    for j in range(ntiles):
        ps = psA(c)
        for kc in range(8):
            fw.mm(ps[:, 0:256], V(c.xT.h[:, kc, tok_sel(j)], c.xT.trs), wv(c, slot, kc, 0, 256),
                  start=(kc == 0), stop=(kc == 7))
        p4 = ps.h[:, 0:256].rearrange("p (hp par e) -> p hp par e", hp=2, par=2)
        fw.tt(V(v5[:, j, :, 0, 0:64], vaug.trs), V(p4[:, :, 0, :], ps.trs), V(b4[:, :, 0, :], vbias.trs), ALU.add)
        fw.tt(V(v5[:, j, :, 1, 64:128], vaug.trs), V(p4[:, :, 1, :], ps.trs), V(b4[:, :, 1, :], vbias.trs), ALU.add)


def branch_c(c, s, l):
    fw = c.fw
    arena_reset(c)
    lam_init = 0.8 - 0.6 * math.exp(-0.3 * l)
    cosT = alloc(c, [128, T], F32)
    sinT = alloc(c, [128, T], F32)
    fw.dma(cosT[:, :], V(c.rope[0], [c.tr_in]))
    fw.dma(sinT[:, :], V(c.rope[1], [c.tr_in]))
    qT = alloc(c, [128, 2, T], BF16)
    kT = alloc(c, [128, 2, T], BF16)
    tmp = alloc(c, [128, 4, 512], F32)
    t2b = alloc(c, [128, 2, 512], F32, nslots=2)
    vaug = alloc(c, [128, 16, 4, 128], BF16)
    vbias = alloc(c, [128, 256], F32)
    c.g_sig = alloc(c, [128, 2, 512], F32, nslots=2)
    pt = alloc(c, [128, 4, 512], BF16, nslots=4)
    sm = alloc(c, [128, 16], F32)
    lamt = alloc(c, [128, 128], F32)
    prm = alloc(c, [128, 64], F32)
    ofull = alloc(c, [128, 512], F32)
    osq = alloc(c, [128, 512], F32)
    w1 = alloc(c, [128, 2, 512], F32, nslots=2)
    w2 = alloc(c, [128, 2, 512], F32, nslots=2)
    blk = alloc(c, [128, 128], F32)
    fw.dma(blk[:, :], V(c.blk1[:, :], [c.tr_in]))
    fw.dma(prm[:, :], V(c.rwp[l], [c.tr_in]))
    fw.dma(lamt[:, :], V(c.df_lam[l:l + 1, :].partition_broadcast(128), [c.tr_in]))
    fw.tt(lamt[:, 0:32], lamt[:, 0:32], lamt[:, 32:64], ALU.mult)
    fw.tt(lamt[:, 64:96], lamt[:, 64:96], lamt[:, 96:128], ALU.mult)
    fw.emit("dve", lambda E: E.tensor_reduce(out=sm.h[:, 0:1], in_=lamt.h[:, 0:32], axis=mybir.AxisListType.X,
                                             op=ALU.add), [lamt[:, :]], [sm[:, :]])
    fw.emit("dve", lambda E: E.tensor_reduce(out=sm.h[:, 1:2], in_=lamt.h[:, 64:96], axis=mybir.AxisListType.X,
                                             op=ALU.add), [lamt[:, :]], [sm[:, :]])
    fw.act(sm[:, 2:4], sm[:, 0:2], AF.Exp)
    fw.tt(sm[:, 4:5], sm[:, 3:4], sm[:, 2:3], ALU.subtract)
    fw.ts(sm[:, 5:6], sm[:, 4:5], -lam_init, None, ALU.add)
    fw.ts(sm[:, 6:7], prm[:, 0:1], 1.0 - lam_init, None, ALU.mult)

    vaug_init(c, vaug)
    rope_proj(c, l, C_Q, R_CQ, cosT, sinT, qT, tmp, t2b)
    rope_proj(c, l, C_K, R_CK, cosT, sinT, kT, tmp, t2b)
    v_proj(c, l, C_V, vaug, vbias, lambda j: slice(j * 128, (j + 1) * 128))

    qz = [alloc(c, [128, 2, T], BF16), alloc(c, [128, 2, T], BF16)]
    for i in range(2):
        fw.ts(qz[i][:, :, :], qT[:, :, :], prm[:, 29 + i:30 + i], None, ALU.mult, e=("dve" if i == 0 else "pool"))
    scale = 32 ** -0.5
    items = [(hp, tb, i, kt, par) for hp in range(2) for tb in range(NTB) for i in range(2) for kt in range(16)
             for par in range(2)]
    pbuf = {}
    state = {}

    def stage1(n):
        hp, tb, i, kt, par = items[n]
        tok = slice(tb * 512, (tb + 1) * 512)
        pb = par * 64
        sc = psA(c)
        fw.mm(sc[:, :], kT[pb:pb + 64, hp, kt * 128:(kt + 1) * 128], qz[i][pb:pb + 64, hp, tok])
        p = pt.s(n % 4, (slice(None), n % 4, slice(None)))
        fw.act(p, sc[:, :], AF.Exp, scale=scale)
        pbuf[n] = p

    def stage2(n):
        hp, tb, i, kt, par = items[n]
        tok = slice(tb * 512, (tb + 1) * 512)
        h = hp * 2 + par
        if kt == 0:
            state[("acc", par)] = psB(c)
        acc = state[("acc", par)]
        fw.mm(acc[:, :], vaug[:, kt, h, :], pbuf.pop(n), start=(kt == 0), stop=(kt == 15))
        if kt == 15:
            olo, dlo = (0, 64) if par == 0 else (64, 0)
            o = slice(olo, olo + 64)
            d = slice(dlo, dlo + 64)
            r1 = w1.s(i, (o, i, slice(None)))
            fw.recip(r1, acc[d, :])
            t1 = w2.s(i, (o, i, slice(None)))
            fw.tt(t1, acc[o, :], r1, ALU.mult)
            if i == 1:
                t0 = w2.s(0, (o, 0, slice(None)))
                fw.stt(ofull[o, :], t1, sm[o, 5:6], t0, ALU.mult, ALU.add)
                if par == 1:
                    fw.tt(osq[:, :], ofull[:, :], ofull[:, :], ALU.mult)
                    ss = psB(c)
                    fw.mm(ss[:, :], blk[:, :], osq[:, :])
                    fw.ts(osq[:, :], ss[:, :], 1.0 / 64.0, 1e-5, ALU.mult, ALU.add)
                    fw.act(osq[:, :], osq[:, :], AF.Sqrt)
                    fw.recip(osq[:, :], osq[:, :])
                    fw.stt(c.yT.s(2, (slice(None), 4 + hp, tok)), ofull[:, :], sm[:, 6:7], osq[:, :], ALU.mult, ALU.mult)

    npair = len(items) // 2
    for m in range(npair + 1):
        if m < npair:
            stage1(2 * m)
            stage1(2 * m + 1)
        if m >= 1:
            stage2(2 * m - 2)
            stage2(2 * m - 1)
    gate_branch(c, l, C_G, 2)


def _host_inputs(inp):
    f32 = np.float32
    g = lambda k: np.asarray(inp[k], f32)
    w_in = g("w_in")
    b_in = g("b_in")
    perm = _rot_perm()
    w_rot = np.ascontiguousarray(w_in[:, :, perm])
    b_cat = np.ascontiguousarray(np.concatenate([b_in, b_in[:, perm]], axis=1))
    b_fm = np.ascontiguousarray(b_cat.reshape(2, NCH, 128).transpose(0, 2, 1))
    vecs = np.zeros((8, DM), f32)
    vecs[0], vecs[1] = g("ln0_g"), g("ln0_b")
    for l in range(2):
        vecs[2 + 3 * l], vecs[3 + 3 * l], vecs[4 + 3 * l] = g("b_out")[l], g("ln_g")[l], g("ln_b")[l]
    rope = np.stack(_rope_tables(), 0)
    i = np.arange(128)[:, None]
    j = np.arange(128)[None, :]
    band = np.stack([(i - j >= 64), (np.abs(i - j) <= 64), (j - i >= 64)], 1).astype(f32)
    bandm = np.ascontiguousarray(np.broadcast_to(band[:, :, None, :], (128, 3, 4, 128))).astype(ml_dtypes.bfloat16)
    rpb = g("na_rpb")
    kap = np.arange(2)[:, None, None, None, None]
    kc = np.arange(64)[None, :, None, None, None]
    oi = np.arange(14)[None, None, :, None, None]
    hh = np.asarray([0, 2, 1, 3])[None, None, None, :, None]
    qc = np.arange(64)[None, None, None, None, :]
    dr = (oi - 7) + kap
    dc = np.clip(kc - qc + 15, 0, 30)
    shp = (2, 64, 14, 4, 64)
    na_bias = np.stack([rpb[l][np.broadcast_to(hh, shp), np.broadcast_to(dr + 7, shp), np.broadcast_to(dc, shp)]
                        for l in range(2)], 0).reshape(2, 128, 14 * 256).astype(f32)
    cst = np.clip(qc - 8, 0, 48)
    ok = (kc >= cst) & (kc < cst + 16)
    na_mask = np.ascontiguousarray(np.broadcast_to(ok, shp)).reshape(128, 14 * 256).astype(f32)
    rwp = np.zeros((2, 128, 64), f32)
    p = np.arange(128)
    mu = g("rw_mu")
    for l in range(2):
        rwp[l, :, 0] = g("df_subln_g")[l][p % 64]
        for hp in range(2):
            ch = hp * 128 + p
            for q in range(4):
                rwp[l, :, 1 + 2 * q + hp] = mu[l, q * 256 + ch]
            for d in range(2):
                rwp[l, :, 11 + d * 2 + hp] = g("rw_w0")[l, d, ch]
                rwp[l, :, 15 + d * 2 + hp] = g("rw_a0")[l, d, ch]
            rwp[l, :, 19 + hp] = g("rw_kk")[l, ch]
            rwp[l, :, 21 + hp] = g("rw_ka")[l, ch]
            rwp[l, :, 23 + hp] = g("rw_rk")[l].reshape(256)[ch]
            rwp[l, :, 25 + hp] = g("rw_lnx_g")[l, ch]
            rwp[l, :, 27 + hp] = g("rw_lnx_b")[l, ch]
        rwp[l, :, 29] = ((p % 64) < 32)
        rwp[l, :, 30] = ((p % 64) >= 32)
        rwp[l, :, 9] = mu[l, 1024 + p]
        rwp[l, :, 10] = mu[l, 1152 + p]
    s_ = np.arange(64)[:, None]
    t_ = np.arange(64)[None, :]
    tri = np.zeros((64, 2, 2, 64), f32)
    tri[:, 0, 0], tri[:, 0, 1] = (s_ < t_), (s_ <= t_)
    tri[:, 1, 0], tri[:, 1, 1] = (s_ > t_), (s_ >= t_)
    blk1 = np.zeros((128, 128), f32)
    blk1[:64, :64] = 1
    blk1[64:, 64:] = 1
    tri3 = np.zeros((64, 2, 192), f32)
    tri3[:, :, 0:128] = tri.reshape(64, 2, 128)
    tri3[:, 0, 128:192] = (s_ > t_)
    tri3[:, 1, 128:192] = (s_ < t_)
    return {
        "w_in": w_in, "w_rot": w_rot, "w_branch": g("w_branch"), "w_out": g("w_out"),
        "b_fm": b_fm, "b_cat": b_cat, "vecs": vecs,
        "ident_b": np.eye(128, dtype=f32).astype(ml_dtypes.bfloat16), "ident_f": np.eye(128, dtype=f32),
        "rope": np.ascontiguousarray(rope), "bandm": bandm, "na_bias": na_bias, "na_mask": na_mask,
        "rwp": rwp, "rw_w2": np.ascontiguousarray(g("rw_w2").reshape(2, 128, 256)),
        "rw_a2": np.ascontiguousarray(g("rw_a2").reshape(2, 128, 256)),
        "df_lam": np.ascontiguousarray(g("df_lam").reshape(2, 128)),
        "trimask": tri3, "blk1": blk1,
    }


_NC_CACHE = {}


def kernel(**inputs):
    xp = np.asarray(inputs["x_prompt"], np.float32)
    xs = np.asarray(inputs["x_sample"], np.float32)
    shared = _host_inputs(inputs)
    ncores = 8
    if "nc" not in _NC_CACHE:
        _NC_CACHE["nc"] = build(6)[0]
    nc = _NC_CACHE["nc"]
    in_maps = []
    for k in range(ncores):
        xk = np.concatenate([xp[4 * k:4 * k + 4], xs[2 * k:2 * k + 2]], axis=0)
        m = dict(shared)
        m["x"] = np.ascontiguousarray(xk)
        in_maps.append(m)
    res = run_bass_kernel_spmd(nc, in_maps, core_ids=list(range(ncores)))
    yp = np.empty_like(xp)
    ys = np.empty_like(xs)
    for k in range(ncores):
        yk = np.asarray(res.results[k]["y"], np.float32)
        yp[4 * k:4 * k + 4] = yk[0:4]
        ys[2 * k:2 * k + 2] = yk[4:6]
    return (yp, ys)


def branch_a(c, s, l):
    fw = c.fw
    arena_reset(c)
    qT = alloc(c, [128, 2, T], BF16)
    kT = alloc(c, [128, 2, T], BF16)
    vaug0 = alloc(c, [128, 16, 4, 128], BF16)
    vaug1 = alloc(c, [128, 15, 4, 128], BF16)
    vbias = alloc(c, [128, 256], F32)
    stg = alloc(c, [128, 14 * 256], F32)
    msk = alloc(c, [128, 14 * 256], F32)
    Mb = alloc(c, [128, 14, 256], BF16)
    pt = alloc(c, [128, 2, 4, 256], BF16, nslots=2)
    rd = alloc(c, [128, 2, 256], F32, nslots=2)
    c.g_sig = alloc(c, [128, 2, 512], F32, nslots=2)
    fw.dma(stg[:, :], V(c.na_bias[l], [c.tr_in]))
    fw.dma(msk[:, :], V(c.na_mask[:, :], [c.tr_in]))
    fw.act(stg[:, :], stg[:, :], AF.Exp)
    fw.tt(V(Mb.h.rearrange("p a b -> p (a b)"), Mb.trs), stg[:, :], msk[:, :], ALU.mult)
    vaug_init(c, vaug0)
    vaug_init(c, vaug1)
    plain_proj(c, l, A_Q, qT)
    plain_proj(c, l, A_K, kT)
    v_proj(c, l, A_V, vaug0, vbias, lambda j: slice(j * 128, (j + 1) * 128), 16)
    v_proj(c, l, A_V, vaug1, vbias, lambda j: slice(64 + j * 128, 64 + (j + 1) * 128), 15)
    rows = {}

    def stage1(r):
        rs = min(max(r - 4, 0), 24)
        sl = r % 2
        qs = slice(64 * r, 64 * r + 64)
        tiles = []
        for j in range(4):
            kr0 = rs + 2 * j
            oi = (rs - r + 2 * j) + 7
            p = pt.s(sl, (slice(None), sl, j, slice(None)))
            scs = [psA(c), psA(c)]
            for hp in range(2):
                for par in range(2):
                    pb = par * 64
                    fw.mm(scs[par][:, hp * 64:(hp + 1) * 64], kT[pb:pb + 64, hp, 64 * kr0:64 * kr0 + 128],
                          qT[pb:pb + 64, hp, qs])
            for par in range(2):
                fw.act(V(p.ap[:, par * 128:(par + 1) * 128], p.trs), scs[par][:, 0:128], AF.Exp, scale=0.125)
            fw.tt(p, p, Mb[:, oi, :], ALU.mult, e=("pool" if j % 2 == 0 else "dve"))
            tiles.append((p, (vaug0, kr0 // 2) if kr0 % 2 == 0 else (vaug1, (kr0 - 1) // 2)))
        rows[r] = tiles

    def stage2(r):
        tiles = rows.pop(r)
        sl = r % 2
        qs = slice(64 * r, 64 * r + 64)
        acc = psB(c)
        for h in range(4):
            for j, (p, (va, ti)) in enumerate(tiles):
                pc = (h % 2) * 128 + (h // 2) * 64
                fw.mm(acc[:, h * 64:(h + 1) * 64], va[:, ti, h, :], V(p.ap[:, pc:pc + 64], p.trs),
                      start=(j == 0), stop=(j == 3))
        a4 = acc.h[:, 0:256].rearrange("p (hp par q) -> p hp par q", hp=2, par=2)
        r4 = rd.h[:, sl, :].rearrange("p (hp par q) -> p hp par q", hp=2, par=2)
        rtr = [rd.trs[sl]]
        fw.recip(V(r4[0:64, :, 0, :], rtr), V(a4[64:128, :, 0, :], acc.trs))
        fw.recip(V(r4[64:128, :, 1, :], rtr), V(a4[0:64, :, 1, :], acc.trs))
        fw.tt(c.yT.s(0, (slice(0, 64), slice(0, 2), qs)), V(a4[0:64, :, 0, :], acc.trs), V(r4[0:64, :, 0, :], rtr), ALU.mult)
        fw.tt(c.yT.s(0, (slice(64, 128), slice(0, 2), qs)), V(a4[64:128, :, 1, :], acc.trs), V(r4[64:128, :, 1, :], rtr),
              ALU.mult)

    stage1(0)
    for r in range(32):
        if r + 1 < 32:
            stage1(r + 1)
        stage2(r)
    gate_branch(c, l, A_G, 0)


def branch_d(c, s, l):
    fw = c.fw
    arena_reset(c)
    cosT = alloc(c, [128, T], F32)
    sinT = alloc(c, [128, T], F32)
    fw.dma(cosT[:, :], V(c.rope[2], [c.tr_in]))
    fw.dma(sinT[:, :], V(c.rope[3], [c.tr_in]))
    acc = alloc(c, [128, 4, T], F32)
    qT = alloc(c, [128, 2, T], BF16)
    kT = alloc(c, [128, 2, T], BF16)
    tmp = alloc(c, [128, 4, 512], F32)
    t2b = alloc(c, [128, 2, 512], F32, nslots=2)
    c.g_sig = t2b
    vaug = alloc(c, [128, 16, 4, 128], BF16)
    vbias = alloc(c, [128, 256], F32)
    pt = alloc(c, [128, 2, 3, 512], BF16, nslots=2)
    bm = alloc(c, [128, 3, 512], BF16)
    fw.dma(bm[:, :, :], V(c.bandm.rearrange("p a h q -> p a (h q)"), [c.tr_in]))
    vaug_init(c, vaug)
    unit = 0
    for g, dil in enumerate((1, 4, 16)):
        nqb = T // dil // 128
        base = D_BASE + g * 768
        rope_proj(c, l, base, R_D + g * 512, cosT, sinT, qT, tmp, t2b)
        rope_proj(c, l, base + 256, R_D + g * 512 + 256, cosT, sinT, kT, tmp, t2b)

        def tsel(j, dil=dil, nqb=nqb):
            rho, jb = divmod(j, nqb)
            st = 128 * jb * dil + rho
            return slice(st, st + 127 * dil + 1, dil)

        v_proj(c, l, base + 512, vaug, vbias, tsel, 16)
        units = [(rho, qb) for rho in range(dil) for qb in range(nqb)]
        ust = {}

        def stage1(ui, units=units, tsel=tsel, nqb=nqb, ust=ust):
            rho, qb = units[ui]
            qs = tsel(rho * nqb + qb)
            kts = [(jb, mt) for (jb, mt) in ((qb - 1, 0), (qb, 1), (qb + 1, 2)) if 0 <= jb < nqb]
            sl = (unit0 + ui) % 2
            ps_ = []
            for idx, (jb, mt) in enumerate(kts):
                ks = tsel(rho * nqb + jb)
                p = pt.s(sl, (slice(None), sl, idx, slice(None)))
                scs = [psA(c), psA(c)]
                for hp in range(2):
                    for par in range(2):
                        pb = par * 64
                        fw.mm(scs[par][:, hp * 128:(hp + 1) * 128], V(kT.h[pb:pb + 64, hp, ks], kT.trs),
                              V(qT.h[pb:pb + 64, hp, qs], qT.trs))
                for par in range(2):
                    fw.act(V(p.ap[:, par * 256:(par + 1) * 256], p.trs), scs[par][:, 0:256], AF.Exp, scale=0.125)
                fw.tt(p, p, bm[:, mt, :], ALU.mult, e=("pool" if idx % 2 == 0 else "dve"))
                ps_.append(p)
            ust[ui] = (qs, kts, ps_)

        def stage2(ui, units=units, nqb=nqb, ust=ust, g=g):
            rho, qb = units[ui]
            qs, kts, ps_ = ust.pop(ui)
            ob = psB(c)
            for h in range(4):
                for idx, (jb, mt) in enumerate(kts):
                    pc = (h % 2) * 256 + (h // 2) * 128
                    fw.mm(ob[:, h * 128:(h + 1) * 128], vaug[:, rho * nqb + jb, h, :],
                          V(ps_[idx].ap[:, pc:pc + 128], ps_[idx].trs),
                          start=(idx == 0), stop=(idx == len(kts) - 1))
            dst = V(acc.h[:, :, qs], acc.trs)
            src = V(ob.h[:, :].rearrange("p (h q) -> p h q", h=4), ob.trs)
            if g == 0:
                fw.copy(dst, src, e="act")
            else:
                fw.tt(dst, src, dst, ALU.add)

        unit0 = unit
        stage1(0)
        for ui in range(len(units)):
            if ui + 1 < len(units):
                stage1(ui + 1)
            stage2(ui)
        unit += len(units)
    a5 = acc.h.rearrange("p (hp par) t -> p hp par t", par=2)
    t5 = tmp.h.rearrange("p (hp par) t -> p hp par t", par=2)
    for tb in range(NTB):
        tok = slice(tb * 512, (tb + 1) * 512)
        fw.recip(V(t5[0:64, :, 0, :], tmp.trs), V(a5[64:128, :, 0, tok], acc.trs))
        fw.recip(V(t5[64:128, :, 1, :], tmp.trs), V(a5[0:64, :, 1, tok], acc.trs))
        fw.tt(c.yT.s(3, (slice(0, 64), slice(6, 8), tok)), V(a5[0:64, :, 0, tok], acc.trs), V(t5[0:64, :, 0, :], tmp.trs),
              ALU.mult)
        fw.tt(c.yT.s(3, (slice(64, 128), slice(6, 8), tok)), V(a5[64:128, :, 1, tok], acc.trs),
              V(t5[64:128, :, 1, :], tmp.trs), ALU.mult)
    gate_branch(c, l, D_G, 3)


USE_NTI = os.environ.get("B_NTI", "1") == "1"
GC = int(os.environ.get("B_GC", "4"))


def yslot_f32(c, slot):
    ap = c.yT.h[:, 2 * slot:2 * slot + 2, :].rearrange("p a t -> p (a t)").bitcast(F32)
    return Buf(ap, 1), c.yT.trs[slot]


def branch_b(c, s, l):
    for hp in range(2):
        rwkv_hp(c, l, hp)


def rwkv_hp(c, l, hp):
    fw = c.fw
    arena_reset(c)
    CD = -math.exp(-0.5)
    prm = alloc(c, [128, 64], F32)
    blk = alloc(c, [128, 128], F32)
    wst = alloc(c, [128, 2, 256], F32)
    w2sb = alloc(c, [128, 256], BF16)
    a2sb = alloc(c, [128, 256], BF16)
    der = alloc(c, [128, 16], F32)
    ones = alloc(c, [128, 64], F32)
    fw.dma(prm[:, :], V(c.rwp[l], [c.tr_in]))
    fw.dma(blk[:, :], V(c.blk1[:, :], [c.tr_in]))
    fw.dma(wst[:, 0, :], V(c.rw_w2[l], [c.tr_in]))
    fw.dma(wst[:, 1, :], V(c.rw_a2[l], [c.tr_in]))
    fw.copy(w2sb[:, :], wst[:, 0, :])
    fw.copy(a2sb[:, :], wst[:, 1, :])
    fw.memset(ones[:, :], 1.0)
    mucols = [1 + hp, 3 + hp, 5 + hp, 7 + hp, 9, 10]
    for i, mc in enumerate(mucols):
        fw.ts(der[:, i:i + 1], prm[:, mc:mc + 1], -1.0, 1.0, ALU.mult, ALU.add)
        fw.ts(der[:, 6 + i:7 + i], prm[:, mc:mc + 1], 0.5, None, ALU.mult)
    fw.ts(der[:, 12:13], prm[:, 21 + hp:22 + hp], -1.0, 1.0, ALU.mult, ALU.add)
    vT = alloc(c, [128, T], BF16)
    gs = alloc(c, [128, T], BF16)
    bv = alloc(c, [128, T], BF16)
    AR = [alloc(c, [128, 2, T], BF16) for _ in range(2)]
    BK = [alloc(c, [128, 2, T], BF16) for _ in range(2)]
    RLO = [alloc(c, [64, T], BF16) for _ in range(2)]
    eL = alloc(c, [128, 2, 32], F32)
    mark = c.arena_off

    uT = alloc(c, [128, 6, T + 2], BF16)
    fw.memset(uT[:, :, 0:1], 0.0)
    fw.memset(uT[:, :, T + 1:T + 2], 0.0)
    cols = [B_R + hp * 128, B_K + hp * 128, B_V + hp * 128, B_GT + hp * 128, B_WL, B_AL]

    def consume(i, tb, ps):
        ch = cols[i] // 128
        fw.act(uT[:, i, 1 + tb * 512:1 + (tb + 1) * 512], ps[:, :], AF.Identity, bias=c.bfm[:, l, ch:ch + 1])

    proj_fm(c, l, cols, consume)
    nt = 10
    tbuf = [alloc(c, [128, 512], F32) for _ in range(nt)]
    ex_ap = c.yT.h[:, 6:8, :].rearrange("p a t -> p (a t)").bitcast(F32)
    tbuf += [Buf(ex_ap[:, i * 512:(i + 1) * 512], 1) for i in range(4)]
    for b_ in tbuf[nt:]:
        b_.trs = [c.yT.trs[3]]
    xr, xk, kk, tA, tB, lw, L0, D_, eP, eM, eX, a_, kd, kdsum = [b_[:, :] for b_ in tbuf]
    twl = alloc(c, [128, 512], BF16)
    tal = alloc(c, [128, 512], BF16)
    for tb in range(NTB):
        tok = slice(tb * 512, (tb + 1) * 512)

        def shift(i, out):
            fw.tt(tA, uT[:, i, tb * 512:tb * 512 + 512], uT[:, i, tb * 512 + 2:tb * 512 + 514], ALU.add, e="pool")
            fw.ts(tA, tA, der[:, 6 + i:7 + i], None, ALU.mult)
            fw.stt(out, uT[:, i, tb * 512 + 1:tb * 512 + 513], der[:, i:i + 1], tA, ALU.mult, ALU.add)

        shift(0, xr)
        shift(1, xk)
        shift(2, tB)
        fw.copy(vT[:, tok], tB, e="act")
        shift(3, tB)
        fw.act(D_, tB, AF.Sigmoid)
        fw.tt(gs[:, tok], tB, D_, ALU.mult, e="pool")
        shift(4, tB)
        fw.act(twl[:, :], tB, AF.Tanh)
        shift(5, tB)
        fw.copy(tal[:, :], tB, e="act")
        fw.ts(kk, xk, prm[:, 19 + hp:20 + hp], None, ALU.mult)
        fw.tt(tB, kk, kk, ALU.mult, e="pool")
        ps = psA(c)
        fw.mm(ps[:, :], blk[:, :], tB)
        fw.ts(tB, ps[:, :], 1e-24, None, ALU.max)
        fw.act(tB, tB, AF.Sqrt)
        fw.recip(tB, tB)
        fw.tt(kk, kk, tB, ALU.mult)
        for d in range(2):
            dsl = slice(d * 64, (d + 1) * 64)
            ps = psA(c)
            fw.mm(ps[:, :], w2sb[dsl, hp * 128:(hp + 1) * 128], twl[dsl, :])
            fw.act(lw, ps[:, :], AF.Sigmoid, bias=prm[:, 11 + d * 2 + hp:12 + d * 2 + hp])
            for cc in range(8):
                cs = slice(cc * 64, (cc + 1) * 64)
                fw.emit("dve", lambda E, cs=cs: E.tensor_tensor_scan(out=L0.ap[:, cs], data0=ones.h[:, :], data1=lw.ap[:, cs],
                                                                    initial=0.0, op0=ALU.mult, op1=ALU.add),
                        [ones[:, :], lw], [L0])
            ltot = V(L0.ap[:, 63:512:64], L0.trs)
            fw.act(eL[:, d, tb * 8:(tb + 1) * 8], ltot, AF.Exp, scale=CD)
            if d == 0:
                fw.tt(D_, L0, lw, ALU.subtract, e="pool")
                fw.act(eP, L0, AF.Exp, scale=CD)
                fw.act(eM, L0, AF.Exp, scale=-CD)
                fw.act(eX, D_, AF.Exp, scale=CD)
            else:
                l3 = V(L0.ap.rearrange("p (c t) -> p c t", t=64), L0.trs)
                lt3 = V(L0.ap[:, 63:512:64].unsqueeze(2).to_broadcast([128, 8, 64]), L0.trs)
                fw.tt(V(D_.ap.rearrange("p (c t) -> p c t", t=64), D_.trs), l3, lt3, ALU.subtract)
                fw.tt(lw, D_, lw, ALU.subtract, e="pool")
                fw.act(eP, lw, AF.Exp, scale=-CD)
                fw.act(eM, lw, AF.Exp, scale=CD)
                fw.act(eX, D_, AF.Exp, scale=-CD)
            ps = psA(c)
            fw.mm(ps[:, :], a2sb[dsl, hp * 128:(hp + 1) * 128], tal[dsl, :])
            fw.act(a_, ps[:, :], AF.Sigmoid, bias=prm[:, 15 + d * 2 + hp:16 + d * 2 + hp])
            fw.ts(tB, a_, prm[:, 21 + hp:22 + hp], der[:, 12:13], ALU.mult, ALU.add)
            fw.tt(kd, tB, xk, ALU.mult)
            if d == 0:
                fw.copy(kdsum, kd, e="pool")
            else:
                fw.tt(kdsum, kdsum, kd, ALU.add, e="pool")
            fw.tt(tB, kk, a_, ALU.mult, e="pool")
            fw.tt(BK[d][:, 0, tok], tB, eM, ALU.mult)
            fw.tt(BK[d][:, 1, tok], kd, eM, ALU.mult, e="pool")
            fw.tt(AR[d][:, 0, tok], kk, eX, ALU.mult)
            fw.tt(AR[d][:, 1, tok], xr, eP, ALU.mult, e="pool")
            fw.tt(RLO[d][0:64, tok], V(xr.ap[64:128, :], xr.trs), V(eP.ap[64:128, :], eP.trs), ALU.mult)
        fw.stt(tB, xr, prm[:, 23 + hp:24 + hp], kdsum, ALU.mult, ALU.mult)
        ps = psA(c)
        fw.mm(ps[:, :], blk[:, :], tB)
        fw.tt(bv[:, tok], ps[:, :], vT[:, tok], ALU.mult)

    fw.barrier()
    c.arena_off = mark
    ys = []
    for d in range(2):
        b_, tr_ = yslot_f32(c, (0, 2)[d])
        b_.trs = [tr_]
        ys.append(b_)
    eLs = alloc(c, [64, 2, 2, 32], F32)
    for d in range(2):
        for hl in range(2):
            fw.copy(eLs[:, d, hl, :], eL[hl * 64:(hl + 1) * 64, d, :], e="act")
    tri = alloc(c, [64, 2, 192], F32)
    fw.dma(tri[:, :, :], V(c.trimask[:, :, :], [c.tr_in]))
    NM = GC * 4
    tmT2 = [alloc(c, [64, GC, 2, 4, 128], BF16) for _ in range(2)]
    Am2 = [alloc(c, [64, NM, 256], BF16) for _ in range(2)]
    TT2 = [alloc(c, [64, NM, 64], BF16) for _ in range(2)]
    PT2 = [alloc(c, [64, NM, 64], BF16) for _ in range(2)]
    Zb2 = [alloc(c, [64, NM, 64], BF16) for _ in range(2)]
    Nb = [alloc(c, [64, NM, 64], BF16) for _ in range(2)]
    NTb = [alloc(c, [64, NM, 64], BF16) for _ in range(2)]
    NTI = alloc(c, [64, NM, 64], BF16)
    Rb = [alloc(c, [64, NM, 64], BF16) for _ in range(2)]
    Sb = alloc(c, [64, 2, 4, 64], BF16, nslots=2)
    Ub = alloc(c, [64, 4, 64], BF16)
    fw.memset(Sb[:, :, :, :], 0.0)
    I64b = c.identb[0:64, 0:64]
    nsteps = T // 64
    ngroups = nsteps // GC
    halves = [(0, NM // 2), (NM // 2, NM)]
    idb = V(c.identf.h[0:64, 0:64].unsqueeze(1).to_broadcast([64, NM // 2, 64]), c.identf.trs)

    def chunk_of(g, ci, d):
        it = g * GC + ci
        return it if d == 0 else nsteps - 1 - it

    def flat(buf, m0, m1):
        return V(buf.h[:, m0:m1, :].rearrange("p m e -> p (m e)"), buf.trs)

    def phase1(g):
        tmT, Am, TT, PT, Zb = tmT2[g % 2], Am2[g % 2], TT2[g % 2], PT2[g % 2], Zb2[g % 2]
        for ci in range(GC):
            for d in range(2):
                ch = chunk_of(g, ci, d)
                cs = slice(ch * 64, ch * 64 + 64)
                pb_ = psB(c)
                pbv = pb_.h[:, :].bitcast(BF16)
                srcs = [AR[d][:, 0, cs], BK[d][:, 0, cs], BK[d][:, 1, cs], vT[:, cs]]
                for q, src in enumerate(srcs):
                    fw.transpose(V(pbv[0:64, q * 128:(q + 1) * 128], pb_.trs), src, c.identb[:, :])
                fw.copy(V(tmT.h[:, ci, d, :, :].rearrange("p q e -> p (q e)"), tmT.trs), V(pbv[0:64, 0:512], pb_.trs),
                        e=("act" if (d == 0 or os.environ.get("B_TMT_ACT", "0") == "1") else "dve"))
            yield
        bN = [psB(c), psB(c)]
        for ci in range(GC):
            bA = [psA(c), psA(c)]
            hl_outer = os.environ.get("B_HLOUTER", "1") == "1"
            order = [(d, hl) for hl in range(2) for d in range(2)] if hl_outer else [(d, hl) for d in range(2) for hl in range(2)]
            for (d, hl) in order:
                ch = chunk_of(g, ci, d)
                cs = slice(ch * 64, ch * 64 + 64)
                pb = hl * 64
                for q in range(2):
                    o0 = d * 256 + q * 128
                    fw.mm(bA[hl][0:64, o0:o0 + 128], BK[d][pb:pb + 64, q, cs], V(AR[d].h[pb:pb + 64, :, cs], AR[d].trs))
                o1 = (ci * 2 + d) * 64
                fw.mm(bN[hl][0:64, o1:o1 + 64], AR[d][pb:pb + 64, 0, cs], BK[d][pb:pb + 64, 0, cs])
            for hl in range(2):
                m0 = ci * 4 + hl
                dst = V(Am.h[:, m0:m0 + 3:2, :].rearrange("p d (q e) -> p d q e", q=2), Am.trs)
                src = V(bA[hl].h[0:64, :].rearrange("p (d q e) -> p d q e", d=2, q=2), bA[hl].trs)
                msk = V(tri.h[:, :, 0:128].unsqueeze(2).to_broadcast([64, 2, 2, 128]), tri.trs)
                fw.tt(dst, src, msk, ALU.mult)
            yield
        for hl in range(2):
            dst = V(NTb[0].h[:, hl:NM:2, :].rearrange("p (ci d) e -> p ci d e", d=2), NTb[0].trs)
            src = V(bN[hl].h[0:64, :].rearrange("p (ci d e) -> p ci d e", ci=GC, d=2), bN[hl].trs)
            msk = V(tri.h[:, :, 128:192].unsqueeze(1).to_broadcast([64, GC, 2, 64]), tri.trs)
            fw.tt(dst, src, msk, ALU.mult)
        for (m0, m1) in halves:
            fw.tt(Rb[0][:, m0:m1, :], idb, Am[:, m0:m1, 0:64], ALU.subtract)
        yield
        Ncur = V(Am.h[:, :, 0:64], Am.trs)
        NTcur = NTb[0][:, :, :]
        rcur = 0
        nti = 0
        for lev in range(1, 6):
            ntn = 1 - nti
            nn = lev % 2
            pzs = []
            for (m0, m1) in halves:
                pz = psA(c)
                for m in range(m0, m1):
                    fw.mm(pz[0:64, (m - m0) * 64:(m - m0 + 1) * 64], V(Ncur.ap[:, m, :], Ncur.trs), V(NTcur.ap[:, m, :], NTcur.trs))
                pzs.append(pz)
            for hi, (m0, m1) in enumerate(halves):
                pz = pzs[hi]
                src3 = V(pz.h[0:64, 0:(m1 - m0) * 64].rearrange("p (m e) -> p m e", e=64), pz.trs)
                if USE_NTI:
                    fw.tt(NTI[:, m0:m1, :], src3, idb, ALU.add)
                if lev < 5 or not USE_NTI:
                    fw.copy(flat(NTb[ntn], m0, m1), pz[0:64, 0:(m1 - m0) * 64], e="act")
            yield
            if lev < 5:
                for (m0, m1) in halves:
                    pz = psA(c)
                    for m in range(m0, m1):
                        fw.mm(pz[0:64, (m - m0) * 64:(m - m0 + 1) * 64], V(NTcur.ap[:, m, :], NTcur.trs), V(Ncur.ap[:, m, :], Ncur.trs))
                    fw.copy(flat(Nb[nn], m0, m1), pz[0:64, 0:(m1 - m0) * 64], e="act")
                yield
            rn = 1 - rcur
            rdst = TT if lev == 5 else Rb[rn]
            for (m0, m1) in halves:
                pz = psB(c)
                for m in range(m0, m1):
                    o = pz[0:64, (m - m0) * 64:(m - m0 + 1) * 64]
                    if USE_NTI:
                        fw.mm(o, NTI[:, m, :], Rb[rcur][:, m, :])
                    else:
                        fw.mm(o, I64b, Rb[rcur][:, m, :], start=True, stop=False)
                        fw.mm(o, NTb[ntn][:, m, :], Rb[rcur][:, m, :], start=False, stop=True)
                fw.copy(flat(rdst, m0, m1), pz[0:64, 0:(m1 - m0) * 64])
            yield
            rcur = rn
            if lev < 5:
                Ncur = Nb[nn][:, :, :]
                NTcur = NTb[ntn][:, :, :]
                nti = ntn
        for (m0, m1) in halves:
            pz = psA(c)
            pq = psB(c)
            for m in range(m0, m1):
                ci, d, hl = m // 4, (m // 2) % 2, m % 2
                fw.mm(pz[0:64, (m - m0) * 64:(m - m0 + 1) * 64], tmT[:, ci, d, 0, hl * 64:(hl + 1) * 64], TT[:, m, :])
                fw.mm(pq[0:64, (m - m0) * 64:(m - m0 + 1) * 64], Am[:, m, 128:192], tmT[:, ci, d, 3, hl * 64:(hl + 1) * 64])
            fw.copy(flat(PT, m0, m1), pz[0:64, 0:(m1 - m0) * 64], e="act")
            fw.copy(flat(Zb, m0, m1), pq[0:64, 0:(m1 - m0) * 64])
            yield

    def drain(gen, n):
        if gen is None:
            return None
        for _ in range(n):
            try:
                next(gen)
            except StopIteration:
                return None
        return gen

    gen = phase1(0)
    drain(gen, 1000)
    slot = 0
    NY = 2 * GC + 2 + 14 + 2
    per_pt = -(-NY // (2 * GC))
    for g in range(ngroups):
        tmT, Am, TT, PT, Zb = tmT2[g % 2], Am2[g % 2], TT2[g % 2], PT2[g % 2], Zb2[g % 2]
        gen = phase1(g + 1) if g + 1 < ngroups else None
        if os.environ.get("B_NOPIPE", "0") == "1":
            gen = drain(gen, 1000)
        for ci in range(GC):
            scur = Sb.s(slot, (slice(None), slot, slice(None), slice(None)))
            snew = Sb.s(1 - slot, (slice(None), 1 - slot, slice(None), slice(None)))
            pu = psB(c)
            for inst in range(4):
                m = ci * 4 + inst
                o = pu[0:64, inst * 64:(inst + 1) * 64]
                fw.mm(o, PT[:, m, :], V(scur.ap[:, inst, :], scur.trs), start=True, stop=False)
                fw.mm(o, TT[:, m, :], Zb[:, m, :], start=False, stop=True)
            fw.act(V(Ub.h.rearrange("p i e -> p (i e)"), Ub.trs), pu[0:64, 0:256], AF.Copy, scale=-1.0)
            gen = drain(gen, per_pt)
            pS = psA(c)
            pY = psB(c)
            for inst in range(4):
                m = ci * 4 + inst
                d, hl = inst // 2, inst % 2
                ch = chunk_of(g, ci, d)
                cs = slice(ch * 64, ch * 64 + 64)
                hs = slice(hl * 64, (hl + 1) * 64)
                o = pS[0:64, inst * 64:(inst + 1) * 64]
                fw.mm(o, I64b, V(scur.ap[:, inst, :], scur.trs), start=True, stop=False)
                fw.mm(o, tmT[:, ci, d, 2, hs], tmT[:, ci, d, 3, hs], start=False, stop=False)
                fw.mm(o, tmT[:, ci, d, 1, hs], Ub[:, inst, :], start=False, stop=True)
            for d in range(2):
                ch = chunk_of(g, ci, d)
                esc = V(eLs.h[:, d, :, ch:ch + 1].to_broadcast([64, 2, 64]), eLs.trs)
                fw.tt(V(snew.ap[:, 2 * d:2 * d + 2, :], snew.trs),
                      V(pS.h[0:64, d * 128:(d + 1) * 128].rearrange("p (i e) -> p i e", i=2), pS.trs), esc, ALU.mult)
            for inst in range(4):
                m = ci * 4 + inst
                d, hl = inst // 2, inst % 2
                ch = chunk_of(g, ci, d)
                cs = slice(ch * 64, ch * 64 + 64)
                hs = slice(hl * 64, (hl + 1) * 64)
                oy = pY[0:64, inst * 64:(inst + 1) * 64]
                rt = AR[d][0:64, 1, cs] if hl == 0 else RLO[d][0:64, cs]
                fw.mm(oy, V(scur.ap[:, inst, :], scur.trs), rt, start=True, stop=False)
                fw.mm(oy, tmT[:, ci, d, 3, hs], Am[:, m, 192:256], start=False, stop=False)
                fw.mm(oy, Ub[:, inst, :], Am[:, m, 64:128], start=False, stop=True)
            for d in range(2):
                ch = chunk_of(g, ci, d)
                cs = slice(ch * 64, ch * 64 + 64)
                for hl in range(2):
                    inst = d * 2 + hl
                    fw.copy(ys[d][hl * 64:(hl + 1) * 64, cs], pY[0:64, inst * 64:(inst + 1) * 64], e="act")
            gen = drain(gen, per_pt)
            slot = 1 - slot
        drain(gen, 1000)

    fw.barrier()
    c.arena_off = mark
    pt_ = [alloc(c, [128, 512], F32)[:, :] for _ in range(6)]
    y_, sq, mean, var, t1, t2 = pt_
    for tb in range(NTB):
        tok = slice(tb * 512, (tb + 1) * 512)
        fw.tt(y_, ys[0][:, tok], ys[1][:, tok], ALU.add)
        fw.tt(sq, y_, y_, ALU.mult, e="pool")
        p1 = psA(c)
        fw.mm(p1[:, :], blk[:, :], y_)
        p2 = psA(c)
        fw.mm(p2[:, :], blk[:, :], sq)
        fw.ts(mean, p1[:, :], 1.0 / 64.0, None, ALU.mult)
        fw.tt(t1, mean, mean, ALU.mult, e="pool")
        fw.stt(var, p2[:, :], 1.0 / 64.0, t1, ALU.mult, ALU.subtract)
        fw.ts(var, var, 64e-5, None, ALU.add)
        fw.act(var, var, AF.Sqrt)
        fw.recip(var, var)
        fw.tt(t2, y_, mean, ALU.subtract, e="pool")
        fw.tt(t2, t2, var, ALU.mult)
        fw.ts(t2, t2, prm[:, 25 + hp:26 + hp], prm[:, 27 + hp:28 + hp], ALU.mult, ALU.add)
        fw.tt(t2, t2, bv[:, tok], ALU.add, e="pool")
        fw.tt(c.yT.s(1, (slice(None), 2 + hp, tok)), t2, gs[:, tok], ALU.mult)
```

```python
import math
import os
import numpy as np
import ml_dtypes
import concourse.bass as bass
import concourse.mybir as mybir
from concourse.bass_utils import run_bass_kernel_spmd

F32 = mybir.dt.float32
BF16 = mybir.dt.bfloat16
AF = mybir.ActivationFunctionType
ALU = mybir.AluOpType

T = 2048
DM = 1024
NTB = 4
IN_COLS = 9984
ROT_COLS = 2048
WCOLS = IN_COLS + ROT_COLS
ALPHA = (2 * 2) ** 0.25
LN_EPS = 1e-5
ARENA_F32 = 30208


class Tr:
    __slots__ = ("w", "r", "excl")

    def __init__(self, excl=False):
        self.w = None
        self.r = {}
        self.excl = excl


class V:
    __slots__ = ("ap", "trs")

    def __init__(self, ap, trs):
        self.ap = ap
        self.trs = trs


class Buf:
    def __init__(self, h, nslots=1):
        self.h = h
        self.trs = [Tr() for _ in range(nslots)]

    def __getitem__(self, idx):
        return V(self.h[idx], self.trs)

    def s(self, slot, idx):
        return V(self.h[idx], [self.trs[slot]])


class FW:
    LIMIT = 30000

    def __init__(self, nc, ndma=24):
        self.nc = nc
        self.eng = {"pe": nc.tensor, "act": nc.scalar, "dve": nc.vector, "pool": nc.gpsimd, "sp": nc.sync}
        self.sems = []
        self.cur = {}
        self.cnt = {}
        for e in self.eng:
            self.cur[e] = self._newsem("e_" + e)
            self.cnt[e] = 0
        self.known = {e: {} for e in self.eng}
        self.dma_sem = [self._newsem("dma%d" % i) for i in range(ndma)]
        self.dma_val = [0] * ndma
        self.n_hw = ndma
        self.dma_next = 0
        self.n_ins = 0
        self._uid = 0

    def _newsem(self, name):
        self._uid = getattr(self, "_uid", 0) + 1
        h = self.nc.alloc_semaphore("%s_%d" % (name, self._uid))
        self.sems.append(h)
        return len(self.sems) - 1

    def _wait(self, e, ev):
        si, val, src = ev
        if self.known[e].get(si, 0) >= val:
            return
        self.eng[e].wait_ge(self.sems[si], val)
        self.known[e][si] = val

    def _deps(self, e, reads, writes):
        for v in reads:
            for tr in v.trs:
                if tr.w is not None:
                    if tr.w[2] == e and e == "pe":
                        continue
                    self._wait(e, tr.w)
                if tr.excl:
                    for src, ev in tr.r.items():
                        if ev[2] != e:
                            self._wait(e, ev)
        for v in writes:
            for tr in v.trs:
                if tr.w is not None and not (tr.w[2] == e and e == "pe"):
                    self._wait(e, tr.w)
                for src, ev in tr.r.items():
                    if not (ev[2] == e and e == "pe"):
                        self._wait(e, ev)

    def _record(self, ev, reads, writes, key):
        for v in writes:
            for tr in v.trs:
                tr.w = ev
                tr.r = {}
        for v in reads:
            for tr in v.trs:
                tr.r[key] = ev

    def emit(self, e, fn, reads, writes):
        self._deps(e, reads, writes)
        ins = fn(self.eng[e])
        if self.cnt[e] >= self.LIMIT:
            self.cur[e] = self._newsem("e_" + e)
            self.cnt[e] = 0
        self.cnt[e] += 1
        ins.then_inc(self.sems[self.cur[e]], 1)
        ev = (self.cur[e], self.cnt[e], e)
        self._record(ev, reads, writes, e)
        self.n_ins += 1
        return ev

    def dma(self, out, in_, e="sp"):
        self._deps(e, [in_], [out])
        if e == "pool":
            self.dma_sem.append(self._newsem("swdma"))
            self.dma_val.append(0)
            slot = len(self.dma_sem) - 1
        else:
            slot = self.dma_next
            self.dma_next = (slot + 1) % self.n_hw
        si = self.dma_sem[slot]
        if self.dma_val[slot] > 0:
            self._wait(e, (si, self.dma_val[slot], "dma"))
        ins = self.eng[e].dma_start(out=out.ap, in_=in_.ap)
        self.dma_val[slot] += 16
        ins.then_inc(self.sems[si], 16)
        ev = (si, self.dma_val[slot], "dma%d" % slot)
        self._record(ev, [in_], [out], "dma%d" % slot)
        self.n_ins += 1
        return ev

    def barrier(self):
        evs = [(self.cur[f], self.cnt[f], f) for f in self.eng if self.cnt[f] > 0]
        evs += [(self.dma_sem[i], self.dma_val[i], "dma") for i in range(len(self.dma_sem)) if self.dma_val[i] > 0]
        for e in self.eng:
            for ev in evs:
                if not (ev[2] == e and e == "pe"):
                    self._wait(e, ev)

    def mm(self, out, lhsT, rhs, start=True, stop=True):
        return self.emit("pe", lambda E: E.matmul(out.ap, lhsT=lhsT.ap, rhs=rhs.ap, start=start, stop=stop),
                         [lhsT, rhs], [out])

    def transpose(self, out, in_, ident):
        return self.emit("pe", lambda E: E.transpose(out.ap, in_.ap, ident.ap), [in_, ident], [out])

    def act(self, out, in_, func, bias=None, scale=None, e="act"):
        kw = {}
        rd = [in_]
        if bias is not None:
            if isinstance(bias, V):
                kw["bias"] = bias.ap
                rd.append(bias)
            else:
                kw["bias"] = bias
        if scale is not None:
            if isinstance(scale, V):
                kw["scale"] = scale.ap
                rd.append(scale)
            else:
                kw["scale"] = scale
        return self.emit("act", lambda E: E.activation(out=out.ap, in_=in_.ap, func=func, **kw), rd, [out])

    def tt(self, out, in0, in1, op, e="dve"):
        return self.emit(e, lambda E: E.tensor_tensor(out=out.ap, in0=in0.ap, in1=in1.ap, op=op), [in0, in1], [out])

    def ts(self, out, in0, s1, s2, op0, op1=None, e="dve"):
        rd = [in0]
        a1 = s1
        a2 = s2
        if isinstance(s1, V):
            rd.append(s1)
            a1 = s1.ap
        if isinstance(s2, V):
            rd.append(s2)
            a2 = s2.ap
        if op1 is None:
            return self.emit(e, lambda E: E.tensor_scalar(out=out.ap, in0=in0.ap, scalar1=a1, scalar2=None, op0=op0),
                             rd, [out])
        return self.emit(e, lambda E: E.tensor_scalar(out=out.ap, in0=in0.ap, scalar1=a1, scalar2=a2, op0=op0, op1=op1),
                         rd, [out])

    def stt(self, out, in0, scalar, in1, op0, op1):
        rd = [in0, in1]
        a = scalar
        if isinstance(scalar, V):
            rd.append(scalar)
            a = scalar.ap
        return self.emit("dve", lambda E: E.scalar_tensor_tensor(out=out.ap, in0=in0.ap, scalar=a, in1=in1.ap,
                                                                 op0=op0, op1=op1), rd, [out])

    def copy(self, out, in_, e="dve"):
        if e == "act":
            return self.act(out, in_, AF.Copy)
        return self.emit(e, lambda E: E.tensor_copy(out=out.ap, in_=in_.ap), [in_], [out])

    def recip(self, out, in_):
        return self.emit("dve", lambda E: E.reciprocal(out=out.ap, in_=in_.ap), [in_], [out])

    def memset(self, out, val, e="dve"):
        return self.emit(e, lambda E: E.memset(out.ap, val), [], [out])


A_Q, A_K, A_V, A_G = 0, 256, 512, 768
B_R, B_K, B_V, B_GT, B_WL, B_AL = 1024, 1280, 1536, 1792, 2048, 2176
C_Q, C_K, C_V, C_G = 2304, 2560, 2816, 3072
D_BASE, D_G = 3328, 5632
MG = 5888
R_CQ, R_CK, R_D = 9984, 10240, 10496
NCH = WCOLS // 128


def _rot_perm():
    cols = []
    for base in (C_Q, C_K):
        for c in range(256):
            j = c % 32
            cols.append(base + c - j + (j + 16) % 32)
    for g in range(3):
        for part in (0, 256):
            base = D_BASE + g * 768 + part
            for c in range(256):
                j = c % 64
                cols.append(base + c - j + (j + 32) % 64)
    return np.asarray(cols, np.int64)


def _rope_tables():
    t = np.arange(T, dtype=np.float32)
    out = []
    for d in (32, 64):
        half = d // 2
        inv = np.power(np.float32(10000.0), -np.arange(half, dtype=np.float32) / np.float32(half)).astype(np.float32)
        ang = (t[:, None] * inv[None, :]).astype(np.float32)
        cos = np.cos(ang).astype(np.float32)
        sin = np.sin(ang).astype(np.float32)
        p = np.arange(128)
        j = p % d
        cosT = cos[:, j % half].T
        sgn = np.where(j < half, -1.0, 1.0).astype(np.float32)
        sinT = (sin[:, j % half].T * sgn[:, None]).astype(np.float32)
        out += [np.ascontiguousarray(cosT), np.ascontiguousarray(sinT)]
    return out


class Ctx:
    pass


def dram(nc, name, shape, dtype, kind):
    return nc.dram_tensor(name, list(shape), dtype, kind=kind).ap()


def build(nseq, parts=("A", "B", "C", "D"), dbg=False):
    nc = bass.Bass("TRN2", target_bir_lowering=False)
    fw = FW(nc)
    c = Ctx()
    c.nc, c.fw, c.parts, c.dbg = nc, fw, parts, dbg
    IN, OUT, INT = "ExternalInput", "ExternalOutput", "Internal"
    c.x = dram(nc, "x", [nseq, T, DM], F32, IN)
    c.y = dram(nc, "y", [nseq, T, DM], F32, OUT)
    c.w_in = dram(nc, "w_in", [2, DM, IN_COLS], F32, IN)
    c.w_rot = dram(nc, "w_rot", [2, DM, ROT_COLS], F32, IN)
    c.w_br = dram(nc, "w_branch", [2, 4, 256, DM], F32, IN)
    c.w_out = dram(nc, "w_out", [2, DM, DM], F32, IN)
    c.b_fm = dram(nc, "b_fm", [2, 128, NCH], F32, IN)
    c.b_cat = dram(nc, "b_cat", [2, WCOLS], F32, IN)
    c.vecs = dram(nc, "vecs", [8, DM], F32, IN)
    c.ident_b = dram(nc, "ident_b", [128, 128], BF16, IN)
    c.ident_f = dram(nc, "ident_f", [128, 128], F32, IN)
    c.rope = dram(nc, "rope", [4, 128, T], F32, IN)
    c.bandm = dram(nc, "bandm", [128, 3, 4, 128], BF16, IN)
    c.na_bias = dram(nc, "na_bias", [2, 128, 14 * 256], F32, IN)
    c.na_mask = dram(nc, "na_mask", [128, 14 * 256], F32, IN)
    c.rwp = dram(nc, "rwp", [2, 128, 64], F32, IN)
    c.rw_w2 = dram(nc, "rw_w2", [2, 128, 256], F32, IN)
    c.rw_a2 = dram(nc, "rw_a2", [2, 128, 256], F32, IN)
    c.df_lam = dram(nc, "df_lam", [2, 128], F32, IN)
    c.trimask = dram(nc, "trimask", [64, 2, 192], F32, IN)
    c.blk1 = dram(nc, "blk1", [128, 128], F32, IN)
    c.perms = dram(nc, "perms", [2, 128, 128], BF16, IN)
    c.wbf = dram(nc, "wbf", [2, DM, WCOLS], BF16, INT)
    c.wbr_bf = dram(nc, "wbr_bf", [2, 4, 256, DM], BF16, INT)
    c.wout_bf = dram(nc, "wout_bf", [2, DM, DM], BF16, INT)
    c.xres = dram(nc, "xres", [T, DM], F32, INT)
    c.tr_wbf = [[Tr() for _ in range(NCH)] for _ in range(2)]
    c.tr_wbr = [Tr(), Tr()]
    c.tr_wout = [Tr(), Tr()]
    c.tr_xres = [Tr() for _ in range(16)]
    c.tr_in = Tr()
    c.tr_y = Tr()
    if dbg:
        c.dbg_out = dram(nc, "dbg", [128, 8, T], BF16, OUT)
        c.tr_dbg = Tr()

    def sb(name, shape, dtype, nslots=1):
        return Buf(nc.alloc_sbuf_tensor(name, list(shape), dtype), nslots)

    c.sb = sb
    c.xT = sb("xT", [128, 8, T], BF16)
    c.yT = sb("yT", [128, 8, T], BF16, nslots=4)
    c.wbuf = sb("wbuf", [128, 2, 8, 512], BF16, nslots=2)
    c.identb = sb("identb", [128, 128], BF16)
    c.identf = sb("identf", [128, 128], F32)
    c.bfm = sb("bfm", [128, 2, NCH], F32)
    c.arena = nc.alloc_sbuf_tensor("arena", [128, ARENA_F32], F32)
    c.arena_off = 0
    c.ps = [Buf(nc.alloc_psum_tensor("ps%d" % i, [128, 512], F32)) for i in range(8)]
    for b_ in c.ps:
        b_.trs = [Tr(excl=True)]
    c.ps_i = [0, 0]

    fw.dma(c.identb[:, :], V(c.ident_b[:, :], [c.tr_in]))
    fw.dma(c.identf[:, :], V(c.ident_f[:, :], [c.tr_in]))
    for l in range(2):
        fw.dma(c.bfm[:, l, :], V(c.b_fm[l], [c.tr_in]))

    convert_weights(c)
    for s in range(nseq):
        stage0(c, s)
        for l in range(2):
            layer(c, s, l)
    for i in range(len(fw.dma_sem)):
        if fw.dma_val[i] > 0:
            fw._wait("sp", (fw.dma_sem[i], fw.dma_val[i], "dma"))
    return nc, fw


def arena_reset(c):
    c.fw.barrier()
    c.arena_off = 0


def alloc(c, shape, dtype, nslots=1):
    n = int(np.prod(shape[1:]))
    n4 = (n + 1) // 2 if dtype == BF16 else n
    n4 = (n4 + 7) // 8 * 8
    assert c.arena_off + n4 <= ARENA_F32, ("arena overflow", c.arena_off, n4)
    ap = c.arena[0:shape[0], c.arena_off:c.arena_off + n4]
    c.arena_off += n4
    if dtype == BF16:
        ap = ap.bitcast(BF16)[:, 0:n]
    else:
        ap = ap[:, 0:n]
    if len(shape) > 2:
        names = " ".join("d%d" % i for i in range(len(shape) - 1))
        kw = {"d%d" % i: shape[i + 1] for i in range(len(shape) - 1)}
        ap = ap.rearrange("p (%s) -> p %s" % (names, names), **kw)
    return Buf(ap, nslots)


def load_vecs(c, rows):
    for i, r in enumerate(rows):
        c.fw.dma(c.vec_bc[:, i, :], V(c.vecs[r:r + 1, :].partition_broadcast(128), [c.tr_in]))


def psA(c):
    i = c.ps_i[0]
    c.ps_i[0] = (i + 1) % 4
    return c.ps[i]


def psB(c):
    i = c.ps_i[1]
    c.ps_i[1] = (i + 1) % 4
    return c.ps[4 + i]


def convert_weights(c):
    fw = c.fw
    rd = [c.tr_in]
    for l in range(2):
        blocks = [(0, 512 * i, 512) for i in range(11)] + [(0, 5632, 256)]
        for (_, c0, n) in blocks:
            trs = c.tr_wbf[l][c0 // 128:(c0 + n) // 128]
            fw.dma(V(c.wbf[l, :, c0:c0 + n], trs), V(c.w_in[l, :, c0:c0 + n], rd), e="pool")
        dst = c.wbf[l, :, MG:IN_COLS].rearrange("k (dc b j) -> k dc b j", dc=8, b=4)
        for b in range(4):
            src = c.w_in[l, :, MG + b * 1024:MG + (b + 1) * 1024].rearrange("k (dc j) -> k dc j", dc=8)
            fw.dma(V(dst[:, :, b, :], c.tr_wbf[l][MG // 128:IN_COLS // 128]), V(src, rd), e="pool")
        for b in range(4):
            fw.dma(V(c.wbr_bf[l, b], [c.tr_wbr[l]]), V(c.w_br[l, b], rd), e="pool")
        for i in range(2):
            fw.dma(V(c.wout_bf[l, :, 512 * i:512 * (i + 1)], [c.tr_wout[l]]),
                   V(c.w_out[l, :, 512 * i:512 * (i + 1)], rd), e="pool")


def load_w(c, l, c0, n):
    fw = c.fw
    slot = getattr(c, "_wslot", 0)
    c._wslot = 1 - slot
    src = c.wbf[l, :, c0:c0 + n].rearrange("(kc p) n -> p kc n", p=128)
    trs = c.tr_wbf[l][c0 // 128:(c0 + n + 127) // 128]
    fw.dma(c.wbuf.s(slot, (slice(None), slot, slice(None), slice(0, n))), V(src, trs))
    return slot


def wv(c, slot, kc, j0, n):
    return c.wbuf.s(slot, (slice(None), slot, kc, slice(j0, j0 + n)))


def proj_fm(c, l, col_list, consume):
    fw = c.fw
    groups = []
    for i, c0 in enumerate(col_list):
        if groups and groups[-1][0] + groups[-1][1] == c0 and groups[-1][1] < 512:
            groups[-1][1] += 128
            groups[-1][2].append(i)
        else:
            groups.append([c0, 128, [i]])
    slots = [None] * len(groups)
    slots[0] = load_w(c, l, groups[0][0], groups[0][1])
    for gi, (g0, gn, idxs) in enumerate(groups):
        if gi + 1 < len(groups):
            slots[gi + 1] = load_w(c, l, groups[gi + 1][0], groups[gi + 1][1])
        for j, i in enumerate(idxs):
            for tb in range(NTB):
                ps = psA(c)
                for kc in range(8):
                    fw.mm(ps[:, :], wv(c, slots[gi], kc, j * 128, 128), c.xT[:, kc, tb * 512:(tb + 1) * 512],
                          start=(kc == 0), stop=(kc == 7))
                consume(i, tb, ps)


def ln_rows(c, z, gi, bi, out):
    fw = c.fw
    st = c.ln_st
    fw.emit("dve", lambda E: E.bn_stats(out=st.h[:, 0, :], in_=z.ap[:, 0:512]), [z], [st[:, 0, :]])
    fw.emit("dve", lambda E: E.bn_stats(out=st.h[:, 1, :], in_=z.ap[:, 512:1024]), [z], [st[:, 1, :]])
    mv = c.ln_mv
    fw.emit("dve", lambda E: E.bn_aggr(out=mv.h[:, 0:2], in_=st.h[:, :, :]), [st[:, :, :]], [mv[:, 0:2]])
    fw.ts(mv[:, 2:3], mv[:, 1:2], LN_EPS, None, ALU.add)
    fw.act(mv[:, 3:4], mv[:, 2:3], AF.Sqrt)
    fw.recip(mv[:, 4:5], mv[:, 3:4])
    fw.stt(mv[:, 5:6], mv[:, 0:1], -1.0, mv[:, 4:5], ALU.mult, ALU.mult)
    fw.act(out, z, AF.Identity, bias=mv[:, 5:6], scale=mv[:, 4:5])
    fw.tt(out, out, c.vec_bc[:, gi, :], ALU.mult)
    fw.tt(out, out, c.vec_bc[:, bi, :], ALU.add, e="pool")


def to_xT(c, rows, tt):
    fw = c.fw
    xb = c.xb_t
    fw.copy(xb[:, :], rows, e="act")
    ps = psB(c)
    psb = V(ps.h[:, :].bitcast(BF16), ps.trs)
    for k in range(8):
        fw.transpose(V(psb.ap[:, k * 128:(k + 1) * 128], ps.trs), xb[:, k * 128:(k + 1) * 128], c.identb[:, :])
    fw.copy(c.xT[:, :, tt * 128:(tt + 1) * 128], V(psb.ap.rearrange("p (k t) -> p k t", k=8), ps.trs))


def ln_allocs(c):
    c.ln_st = alloc(c, [128, 2, 6], F32)
    c.ln_mv = alloc(c, [128, 8], F32)
    c.xb_t = alloc(c, [128, DM], BF16)
    c.zt = alloc(c, [128, 2, DM], F32, nslots=2)
    c.ot = alloc(c, [128, 2, DM], F32, nslots=2)
    c.vec_bc = alloc(c, [128, 3, DM], F32)


def stage0(c, s):
    fw = c.fw
    arena_reset(c)
    ln_allocs(c)
    load_vecs(c, [0, 1])
    for tt in range(16):
        sl = tt % 2
        z = c.zt.s(sl, (slice(None), sl, slice(None)))
        o = c.ot.s(sl, (slice(None), sl, slice(None)))
        fw.dma(z, V(c.x[s, tt * 128:(tt + 1) * 128, :], [c.tr_in]))
        ln_rows(c, z, 0, 1, o)
        fw.dma(V(c.xres[tt * 128:(tt + 1) * 128, :], [c.tr_xres[tt]]), o)
        to_xT(c, o, tt)


def gate_branch(c, l, gcol, bi):
    fw = c.fw

    def consume(i, tb, ps):
        ch = (gcol // 128) + i
        sg = c.g_sig.s(tb % 2, (slice(None), tb % 2, slice(None)))
        fw.act(sg, ps[:, :], AF.Sigmoid, bias=c.bfm[:, l, ch:ch + 1])
        fw.stt(sg, ps[:, :], c.bfm[:, l, ch:ch + 1], sg, ALU.add, ALU.mult)
        yv = c.yT.s(bi, (slice(None), bi * 2 + i, slice(tb * 512, (tb + 1) * 512)))
        fw.tt(yv, yv, sg, ALU.mult, e="pool")

    proj_fm(c, l, [gcol, gcol + 128], consume)


def final_stage(c, s, l):
    fw = c.fw
    arena_reset(c)
    ln_allocs(c)
    c.mergedT = alloc(c, [128, 8, T], BF16)
    c.wbr_sb = alloc(c, [128, 4, 2, DM], BF16)
    c.wout_sb = alloc(c, [128, 8, DM], BF16)
    c.f_sig = alloc(c, [128, 2, 512], F32, nslots=2)
    c.f_acc = alloc(c, [128, 2, 512], F32, nslots=2)
    c.f_tmp = alloc(c, [128, 2, 512], F32, nslots=2)
    load_vecs(c, [2 + 3 * l, 3 + 3 * l, 4 + 3 * l])
    c.ones_row = alloc(c, [1, 128], BF16)
    c.bout_row = alloc(c, [1, DM], BF16)
    bstage = alloc(c, [1, DM], F32)
    fw.memset(c.ones_row[0:1, :], 1.0)
    fw.dma(bstage[0:1, :], V(c.vecs[2 + 3 * l:3 + 3 * l, :], [c.tr_in]))
    fw.copy(c.bout_row[0:1, :], bstage[0:1, :])
    for b in range(4):
        fw.dma(c.wbr_sb[:, b, :, :], V(c.wbr_bf[l, b].rearrange("(kc p) d -> p kc d", p=128), [c.tr_wbr[l]]))
    fw.dma(c.wout_sb[:, :, :], V(c.wout_bf[l].rearrange("(kc p) d -> p kc d", p=128), [c.tr_wout[l]]))
    slots = [None] * 8
    slots[0] = load_w(c, l, MG, 512)
    for dc in range(8):
        if dc + 1 < 8:
            slots[dc + 1] = load_w(c, l, MG + (dc + 1) * 512, 512)
        for tb in range(NTB):
            tok = slice(tb * 512, (tb + 1) * 512)
            ai = tb % 2
            acc = c.f_acc.s(ai, (slice(None), ai, slice(None)))
            for b in range(4):
                pg = psA(c)
                for kc in range(8):
                    fw.mm(pg[:, :], wv(c, slots[dc], kc, b * 128, 128), c.xT[:, kc, tok], start=(kc == 0), stop=(kc == 7))
                ch = MG // 128 + b * 8 + dc
                si = b % 2
                sg = c.f_sig.s(si, (slice(None), si, slice(None)))
                fw.act(sg, pg[:, :], AF.Sigmoid, bias=c.bfm[:, l, ch:ch + 1])
                pp = psB(c)
                for kc in range(2):
                    fw.mm(pp[:, :], c.wbr_sb[:, b, kc, dc * 128:(dc + 1) * 128],
                          c.yT.s(b, (slice(None), b * 2 + kc, tok)), start=(kc == 0), stop=(kc == 1))
                if b == 0:
                    fw.tt(acc, pp[:, :], sg, ALU.mult)
                else:
                    tm = c.f_tmp.s(si, (slice(None), si, slice(None)))
                    fw.tt(tm, pp[:, :], sg, ALU.mult)
                    if b < 3:
                        fw.tt(acc, acc, tm, ALU.add, e="pool")
                    else:
                        fw.tt(c.mergedT[:, dc, tok], acc, tm, ALU.add, e="pool")
    for tt in range(16):
        sl = tt % 2
        z = c.zt.s(sl, (slice(None), sl, slice(None)))
        o = c.ot.s(sl, (slice(None), sl, slice(None)))
        fw.dma(z, V(c.xres[tt * 128:(tt + 1) * 128, :], [c.tr_xres[tt]]))
        for hf in range(2):
            po = psA(c)
            for kc in range(8):
                fw.mm(po[:, :], c.mergedT[:, kc, tt * 128:(tt + 1) * 128], c.wout_sb[:, kc, hf * 512:(hf + 1) * 512],
                      start=(kc == 0), stop=False)
            fw.mm(po[:, :], c.ones_row[0:1, :], c.bout_row[0:1, hf * 512:(hf + 1) * 512], start=False, stop=True)
            zh = V(z.ap[:, hf * 512:(hf + 1) * 512], z.trs)
            fw.stt(zh, zh, ALPHA, po[:, :], ALU.mult, ALU.add)
        ln_rows(c, z, 1, 2, o)
        if l == 0:
            fw.dma(V(c.xres[tt * 128:(tt + 1) * 128, :], [c.tr_xres[tt]]), o)
            to_xT(c, o, tt)
        else:
            fw.dma(V(c.y[s, tt * 128:(tt + 1) * 128, :], [c.tr_y]), o)


def layer(c, s, l):
    fw = c.fw
    if "B" in c.parts:
        branch_b(c, s, l)
    for bi, name in enumerate("ABCD"):
        if name not in c.parts:
            fw.memset(c.yT.s(bi, (slice(None), slice(bi * 2, bi * 2 + 2), slice(None))), 0.0, e="pool")
    if "A" in c.parts:
        branch_a(c, s, l)
    if "C" in c.parts:
        branch_c(c, s, l)
    if "D" in c.parts:
        branch_d(c, s, l)
    if c.dbg and l == 0 and s == 0:
        fw.dma(V(c.dbg_out[:, :, :], [c.tr_dbg]), c.yT[:, :, :])
    final_stage(c, s, l)


def rope_proj(c, l, qcol, perm, cosT, sinT, dst, tmp, t2b, qbb):
    fw = c.fw

    def consume(i, tb, ps):
        tok = slice(tb * 512, (tb + 1) * 512)
        ch = qcol // 128 + i
        sl = (i * NTB + tb) % 2
        qb = qbb.s(sl, (slice(None), sl, slice(None)))
        fw.act(qb, ps[:, :], AF.Identity, bias=c.bfm[:, l, ch:ch + 1])
        t1 = t2b.s(sl, (slice(None), sl, slice(None)))
        fw.stt(t1, ps[:, :], c.bfm[:, l, ch:ch + 1], cosT[:, tok], ALU.add, ALU.mult)
        pr = psB(c)
        fw.mm(pr[:, :], perm, qb)
        t2 = V(tmp.h[:, sl, :], [tmp.trs[0]])
        fw.tt(t2, pr[:, :], sinT[:, tok], ALU.mult)
        fw.tt(dst[:, i, tok], t1, t2, ALU.add, e="pool")

    proj_fm(c, l, [qcol, qcol + 128], consume)


def plain_proj(c, l, col, dst):
    fw = c.fw

    def consume(i, tb, ps):
        ch = col // 128 + i
        fw.ts(dst[:, i, tb * 512:(tb + 1) * 512], ps[:, :], c.bfm[:, l, ch:ch + 1], None, ALU.add)

    proj_fm(c, l, [col, col + 128], consume)


def vaug_init(c, vaug):
    v5 = vaug.h.rearrange("p j (hp par) e -> p j hp par e", par=2)
    c.fw.memset(V(v5[:, :, :, 0, 64:128], vaug.trs), 1.0, e="pool")
    c.fw.memset(V(v5[:, :, :, 1, 0:64], vaug.trs), 1.0, e="pool")


def v_proj(c, l, vcol, vaug, vbias, tok_sel, ntiles=16):
    fw = c.fw
    slot = load_w(c, l, vcol, 256)
    fw.dma(vbias[:, :], V(c.b_cat[l:l + 1, vcol:vcol + 256].partition_broadcast(128), [c.tr_in]))
    v5 = vaug.h.rearrange("p j (hp par) e -> p j hp par e", par=2)
    b4 = vbias.h.rearrange("p (hp par e) -> p hp par e", hp=2, par=2)
    for j in range(ntiles):
        ps = psA(c)
        for kc in range(8):
            fw.mm(ps[:, 0:256], V(c.xT.h[:, kc, tok_sel(j)], c.xT.trs), wv(c, slot, kc, 0, 256),
                  start=(kc == 0), stop=(kc == 7))
        p4 = ps.h[:, 0:256].rearrange("p (hp par e) -> p hp par e", hp=2, par=2)
        fw.tt(V(v5[:, j, :, 0, 0:64], vaug.trs), V(p4[:, :, 0, :], ps.trs), V(b4[:, :, 0, :], vbias.trs), ALU.add)
        fw.tt(V(v5[:, j, :, 1, 64:128], vaug.trs), V(p4[:, :, 1, :], ps.trs), V(b4[:, :, 1, :], vbias.trs), ALU.add)


def branch_c(c, s, l):
    fw = c.fw
    arena_reset(c)
    lam_init = 0.8 - 0.6 * math.exp(-0.3 * l)
    cosT = alloc(c, [128, T], F32)
    sinT = alloc(c, [128, T], F32)
    fw.dma(cosT[:, :], V(c.rope[0], [c.tr_in]))
    fw.dma(sinT[:, :], V(c.rope[1], [c.tr_in]))
    qT = alloc(c, [128, 2, T], BF16)
    kT = alloc(c, [128, 2, T], BF16)
    tmp = alloc(c, [128, 4, 512], F32)
    t2b = alloc(c, [128, 2, 512], F32, nslots=2)
    vaug = alloc(c, [128, 16, 4, 128], BF16)
    vbias = alloc(c, [128, 256], F32)
    c.g_sig = alloc(c, [128, 2, 512], F32, nslots=2)
    pt = alloc(c, [128, 4, 512], BF16, nslots=4)
    sm = alloc(c, [128, 16], F32)
    lamt = alloc(c, [128, 128], F32)
    prm = alloc(c, [128, 64], F32)
    ofull = alloc(c, [128, 512], F32)
    osq = alloc(c, [128, 512], F32)
    w1 = alloc(c, [128, 2, 512], F32, nslots=2)
    w2 = alloc(c, [128, 2, 512], F32, nslots=2)
    blk = alloc(c, [128, 128], F32)
    permC = alloc(c, [128, 128], BF16)
    qbb = alloc(c, [128, 2, 512], BF16, nslots=2)
    fw.dma(permC[:, :], V(c.perms[0], [c.tr_in]))
    fw.dma(blk[:, :], V(c.blk1[:, :], [c.tr_in]))
    fw.dma(prm[:, :], V(c.rwp[l], [c.tr_in]))
    fw.dma(lamt[:, :], V(c.df_lam[l:l + 1, :].partition_broadcast(128), [c.tr_in]))
    fw.tt(lamt[:, 0:32], lamt[:, 0:32], lamt[:, 32:64], ALU.mult)
    fw.tt(lamt[:, 64:96], lamt[:, 64:96], lamt[:, 96:128], ALU.mult)
    fw.emit("dve", lambda E: E.tensor_reduce(out=sm.h[:, 0:1], in_=lamt.h[:, 0:32], axis=mybir.AxisListType.X,
                                             op=ALU.add), [lamt[:, :]], [sm[:, :]])
    fw.emit("dve", lambda E: E.tensor_reduce(out=sm.h[:, 1:2], in_=lamt.h[:, 64:96], axis=mybir.AxisListType.X,
                                             op=ALU.add), [lamt[:, :]], [sm[:, :]])
    fw.act(sm[:, 2:4], sm[:, 0:2], AF.Exp)
    fw.tt(sm[:, 4:5], sm[:, 3:4], sm[:, 2:3], ALU.subtract)
    fw.ts(sm[:, 5:6], sm[:, 4:5], -lam_init, None, ALU.add)
    fw.ts(sm[:, 6:7], prm[:, 0:1], 1.0 - lam_init, None, ALU.mult)

    vaug_init(c, vaug)
    rope_proj(c, l, C_Q, permC[:, :], cosT, sinT, qT, tmp, t2b, qbb)
    rope_proj(c, l, C_K, permC[:, :], cosT, sinT, kT, tmp, t2b, qbb)
    v_proj(c, l, C_V, vaug, vbias, lambda j: slice(j * 128, (j + 1) * 128))

    qz = [alloc(c, [128, 2, T], BF16), alloc(c, [128, 2, T], BF16)]
    for i in range(2):
        fw.ts(qz[i][:, :, :], qT[:, :, :], prm[:, 29 + i:30 + i], None, ALU.mult, e=("dve" if i == 0 else "pool"))
    scale = 32 ** -0.5
    items = [(hp, tb, i, kt, par) for hp in range(2) for tb in range(NTB) for i in range(2) for kt in range(16)
             for par in range(2)]
    pbuf = {}
    state = {}

    def stage1(n):
        hp, tb, i, kt, par = items[n]
        tok = slice(tb * 512, (tb + 1) * 512)
        pb = par * 64
        sc = psA(c)
        fw.mm(sc[:, :], kT[pb:pb + 64, hp, kt * 128:(kt + 1) * 128], qz[i][pb:pb + 64, hp, tok])
        p = pt.s(n % 4, (slice(None), n % 4, slice(None)))
        fw.act(p, sc[:, :], AF.Exp, scale=scale)
        pbuf[n] = p

    def stage2(n):
        hp, tb, i, kt, par = items[n]
        tok = slice(tb * 512, (tb + 1) * 512)
        h = hp * 2 + par
        if kt == 0:
            state[("acc", par)] = psB(c)
        acc = state[("acc", par)]
        fw.mm(acc[:, :], vaug[:, kt, h, :], pbuf.pop(n), start=(kt == 0), stop=(kt == 15))
        if kt == 15:
            olo, dlo = (0, 64) if par == 0 else (64, 0)
            o = slice(olo, olo + 64)
            d = slice(dlo, dlo + 64)
            r1 = w1.s(i, (o, i, slice(None)))
            fw.recip(r1, acc[d, :])
            t1 = w2.s(i, (o, i, slice(None)))
            fw.tt(t1, acc[o, :], r1, ALU.mult)
            if i == 1:
                t0 = w2.s(0, (o, 0, slice(None)))
                fw.stt(ofull[o, :], t1, sm[o, 5:6], t0, ALU.mult, ALU.add)
                if par == 1:
                    fw.tt(osq[:, :], ofull[:, :], ofull[:, :], ALU.mult)
                    ss = psB(c)
                    fw.mm(ss[:, :], blk[:, :], osq[:, :])
                    fw.ts(osq[:, :], ss[:, :], 1.0 / 64.0, 1e-5, ALU.mult, ALU.add)
                    fw.act(osq[:, :], osq[:, :], AF.Sqrt)
                    fw.recip(osq[:, :], osq[:, :])
                    fw.stt(c.yT.s(2, (slice(None), 4 + hp, tok)), ofull[:, :], sm[:, 6:7], osq[:, :], ALU.mult, ALU.mult)

    npair = len(items) // 2
    for m in range(npair + 1):
        if m < npair:
            stage1(2 * m)
            stage1(2 * m + 1)
        if m >= 1:
            stage2(2 * m - 2)
            stage2(2 * m - 1)
    gate_branch(c, l, C_G, 2)


def _host_inputs(inp):
    f32 = np.float32
    g = lambda k: np.asarray(inp[k], f32)
    w_in = g("w_in")
    b_in = g("b_in")
    perm = _rot_perm()
    w_rot = np.ascontiguousarray(w_in[:, :, perm])
    b_cat = np.ascontiguousarray(np.concatenate([b_in, b_in[:, perm]], axis=1))
    b_fm = np.ascontiguousarray(b_cat.reshape(2, NCH, 128).transpose(0, 2, 1))
    vecs = np.zeros((8, DM), f32)
    vecs[0], vecs[1] = g("ln0_g"), g("ln0_b")
    for l in range(2):
        vecs[2 + 3 * l], vecs[3 + 3 * l], vecs[4 + 3 * l] = g("b_out")[l], g("ln_g")[l], g("ln_b")[l]
    rope = np.stack(_rope_tables(), 0)
    i = np.arange(128)[:, None]
    j = np.arange(128)[None, :]
    band = np.stack([(i - j >= 64), (np.abs(i - j) <= 64), (j - i >= 64)], 1).astype(f32)
    bandm = np.ascontiguousarray(np.broadcast_to(band[:, :, None, :], (128, 3, 4, 128))).astype(ml_dtypes.bfloat16)
    rpb = g("na_rpb")
    kap = np.arange(2)[:, None, None, None, None]
    kc = np.arange(64)[None, :, None, None, None]
    oi = np.arange(14)[None, None, :, None, None]
    hh = np.asarray([0, 2, 1, 3])[None, None, None, :, None]
    qc = np.arange(64)[None, None, None, None, :]
    dr = (oi - 7) + kap
    dc = np.clip(kc - qc + 15, 0, 30)
    shp = (2, 64, 14, 4, 64)
    na_bias = np.stack([rpb[l][np.broadcast_to(hh, shp), np.broadcast_to(dr + 7, shp), np.broadcast_to(dc, shp)]
                        for l in range(2)], 0).reshape(2, 128, 14 * 256).astype(f32)
    cst = np.clip(qc - 8, 0, 48)
    ok = (kc >= cst) & (kc < cst + 16)
    na_mask = np.ascontiguousarray(np.broadcast_to(ok, shp)).reshape(128, 14 * 256).astype(f32)
    rwp = np.zeros((2, 128, 64), f32)
    p = np.arange(128)
    mu = g("rw_mu")
    for l in range(2):
        rwp[l, :, 0] = g("df_subln_g")[l][p % 64]
        for hp in range(2):
            ch = hp * 128 + p
            for q in range(4):
                rwp[l, :, 1 + 2 * q + hp] = mu[l, q * 256 + ch]
            for d in range(2):
                rwp[l, :, 11 + d * 2 + hp] = g("rw_w0")[l, d, ch]
                rwp[l, :, 15 + d * 2 + hp] = g("rw_a0")[l, d, ch]
            rwp[l, :, 19 + hp] = g("rw_kk")[l, ch]
            rwp[l, :, 21 + hp] = g("rw_ka")[l, ch]
            rwp[l, :, 23 + hp] = g("rw_rk")[l].reshape(256)[ch]
            rwp[l, :, 25 + hp] = g("rw_lnx_g")[l, ch]
            rwp[l, :, 27 + hp] = g("rw_lnx_b")[l, ch]
        rwp[l, :, 29] = ((p % 64) < 32)
        rwp[l, :, 30] = ((p % 64) >= 32)
        rwp[l, :, 9] = mu[l, 1024 + p]
        rwp[l, :, 10] = mu[l, 1152 + p]
    s_ = np.arange(64)[:, None]
    t_ = np.arange(64)[None, :]
    tri = np.zeros((64, 2, 2, 64), f32)
    tri[:, 0, 0], tri[:, 0, 1] = (s_ < t_), (s_ <= t_)
    tri[:, 1, 0], tri[:, 1, 1] = (s_ > t_), (s_ >= t_)
    blk1 = np.zeros((128, 128), f32)
    blk1[:64, :64] = 1
    blk1[64:, 64:] = 1
    perms = np.zeros((2, 128, 128), f32)
    pp = np.arange(128)
    for qi, dd in enumerate((32, 64)):
        jj = pp % dd
        partner = pp - jj + (jj + dd // 2) % dd
        perms[qi, partner, pp] = 1.0
    tri3 = np.zeros((64, 2, 192), f32)
    tri3[:, :, 0:128] = tri.reshape(64, 2, 128)
    tri3[:, 0, 128:192] = (s_ > t_)
    tri3[:, 1, 128:192] = (s_ < t_)
    return {
        "w_in": w_in, "w_rot": w_rot, "w_branch": g("w_branch"), "w_out": g("w_out"),
        "b_fm": b_fm, "b_cat": b_cat, "vecs": vecs,
        "ident_b": np.eye(128, dtype=f32).astype(ml_dtypes.bfloat16), "ident_f": np.eye(128, dtype=f32),
        "rope": np.ascontiguousarray(rope), "bandm": bandm, "na_bias": na_bias, "na_mask": na_mask,
        "rwp": rwp, "rw_w2": np.ascontiguousarray(g("rw_w2").reshape(2, 128, 256)),
        "rw_a2": np.ascontiguousarray(g("rw_a2").reshape(2, 128, 256)),
        "df_lam": np.ascontiguousarray(g("df_lam").reshape(2, 128)),
        "trimask": tri3, "blk1": blk1, "perms": perms.astype(ml_dtypes.bfloat16),
    }


_NC_CACHE = {}


def kernel(**inputs):
    xp = np.asarray(inputs["x_prompt"], np.float32)
    xs = np.asarray(inputs["x_sample"], np.float32)
    shared = _host_inputs(inputs)
    ncores = 8
    if "nc" not in _NC_CACHE:
        _NC_CACHE["nc"] = build(6)[0]
    nc = _NC_CACHE["nc"]
    in_maps = []
    for k in range(ncores):
        xk = np.concatenate([xp[4 * k:4 * k + 4], xs[2 * k:2 * k + 2]], axis=0)
        m = dict(shared)
        m["x"] = np.ascontiguousarray(xk)
        in_maps.append(m)
    res = run_bass_kernel_spmd(nc, in_maps, core_ids=list(range(ncores)))
    yp = np.empty_like(xp)
    ys = np.empty_like(xs)
    for k in range(ncores):
        yk = np.asarray(res.results[k]["y"], np.float32)
        yp[4 * k:4 * k + 4] = yk[0:4]
        ys[2 * k:2 * k + 2] = yk[4:6]
    return (yp, ys)


def branch_a(c, s, l):
    fw = c.fw
    arena_reset(c)
    qT = alloc(c, [128, 2, T], BF16)
    kT = alloc(c, [128, 2, T], BF16)
    vaug0 = alloc(c, [128, 16, 4, 128], BF16)
    vaug1 = alloc(c, [128, 15, 4, 128], BF16)
    vbias = alloc(c, [128, 256], F32)
    stg = alloc(c, [128, 14 * 256], F32)
    msk = alloc(c, [128, 14 * 256], F32)
    Mb = alloc(c, [128, 14, 256], BF16)
    pt = alloc(c, [128, 2, 4, 256], BF16, nslots=2)
    rd = alloc(c, [128, 2, 256], F32, nslots=2)
    c.g_sig = alloc(c, [128, 2, 512], F32, nslots=2)
    fw.dma(stg[:, :], V(c.na_bias[l], [c.tr_in]))
    fw.dma(msk[:, :], V(c.na_mask[:, :], [c.tr_in]))
    fw.act(stg[:, :], stg[:, :], AF.Exp)
    fw.tt(V(Mb.h.rearrange("p a b -> p (a b)"), Mb.trs), stg[:, :], msk[:, :], ALU.mult)
    vaug_init(c, vaug0)
    vaug_init(c, vaug1)
    plain_proj(c, l, A_Q, qT)
    plain_proj(c, l, A_K, kT)
    v_proj(c, l, A_V, vaug0, vbias, lambda j: slice(j * 128, (j + 1) * 128), 16)
    v_proj(c, l, A_V, vaug1, vbias, lambda j: slice(64 + j * 128, 64 + (j + 1) * 128), 15)
    rows = {}

    def stage1(r):
        rs = min(max(r - 4, 0), 24)
        sl = r % 2
        qs = slice(64 * r, 64 * r + 64)
        tiles = []
        for j in range(4):
            kr0 = rs + 2 * j
            oi = (rs - r + 2 * j) + 7
            p = pt.s(sl, (slice(None), sl, j, slice(None)))
            scs = [psA(c), psA(c)]
            for hp in range(2):
                for par in range(2):
                    pb = par * 64
                    fw.mm(scs[par][:, hp * 64:(hp + 1) * 64], kT[pb:pb + 64, hp, 64 * kr0:64 * kr0 + 128],
                          qT[pb:pb + 64, hp, qs])
            for par in range(2):
                fw.act(V(p.ap[:, par * 128:(par + 1) * 128], p.trs), scs[par][:, 0:128], AF.Exp, scale=0.125)
            fw.tt(p, p, Mb[:, oi, :], ALU.mult, e=("pool" if j % 2 == 0 else "dve"))
            tiles.append((p, (vaug0, kr0 // 2) if kr0 % 2 == 0 else (vaug1, (kr0 - 1) // 2)))
        rows[r] = tiles

    def stage2(r):
        tiles = rows.pop(r)
        sl = r % 2
        qs = slice(64 * r, 64 * r + 64)
        acc = psB(c)
        for h in range(4):
            for j, (p, (va, ti)) in enumerate(tiles):
                pc = (h % 2) * 128 + (h // 2) * 64
                fw.mm(acc[:, h * 64:(h + 1) * 64], va[:, ti, h, :], V(p.ap[:, pc:pc + 64], p.trs),
                      start=(j == 0), stop=(j == 3))
        a4 = acc.h[:, 0:256].rearrange("p (hp par q) -> p hp par q", hp=2, par=2)
        r4 = rd.h[:, sl, :].rearrange("p (hp par q) -> p hp par q", hp=2, par=2)
        rtr = [rd.trs[sl]]
        fw.recip(V(r4[0:64, :, 0, :], rtr), V(a4[64:128, :, 0, :], acc.trs))
        fw.recip(V(r4[64:128, :, 1, :], rtr), V(a4[0:64, :, 1, :], acc.trs))
        fw.tt(c.yT.s(0, (slice(0, 64), slice(0, 2), qs)), V(a4[0:64, :, 0, :], acc.trs), V(r4[0:64, :, 0, :], rtr), ALU.mult)
        fw.tt(c.yT.s(0, (slice(64, 128), slice(0, 2), qs)), V(a4[64:128, :, 1, :], acc.trs), V(r4[64:128, :, 1, :], rtr),
              ALU.mult)

    stage1(0)
    for r in range(32):
        if r + 1 < 32:
            stage1(r + 1)
        stage2(r)
    gate_branch(c, l, A_G, 0)


def branch_d(c, s, l):
    fw = c.fw
    arena_reset(c)
    cosT = alloc(c, [128, T], F32)
    sinT = alloc(c, [128, T], F32)
    fw.dma(cosT[:, :], V(c.rope[2], [c.tr_in]))
    fw.dma(sinT[:, :], V(c.rope[3], [c.tr_in]))
    acc = alloc(c, [128, 4, T], F32)
    qT = alloc(c, [128, 2, T], BF16)
    kT = alloc(c, [128, 2, T], BF16)
    tmp = alloc(c, [128, 4, 512], F32)
    t2b = alloc(c, [128, 2, 512], F32, nslots=2)
    c.g_sig = t2b
    vaug = alloc(c, [128, 16, 4, 128], BF16)
    vbias = alloc(c, [128, 256], F32)
    pt = alloc(c, [128, 2, 3, 512], BF16, nslots=2)
    bm = alloc(c, [128, 3, 512], BF16)
    permD = alloc(c, [128, 128], BF16)
    qbb = alloc(c, [128, 2, 512], BF16, nslots=2)
    fw.dma(permD[:, :], V(c.perms[1], [c.tr_in]))
    fw.dma(bm[:, :, :], V(c.bandm.rearrange("p a h q -> p a (h q)"), [c.tr_in]))
    vaug_init(c, vaug)
    unit = 0
    for g, dil in enumerate((1, 4, 16)):
        nqb = T // dil // 128
        base = D_BASE + g * 768
        rope_proj(c, l, base, permD[:, :], cosT, sinT, qT, tmp, t2b, qbb)
        rope_proj(c, l, base + 256, permD[:, :], cosT, sinT, kT, tmp, t2b, qbb)

        def tsel(j, dil=dil, nqb=nqb):
            rho, jb = divmod(j, nqb)
            st = 128 * jb * dil + rho
            return slice(st, st + 127 * dil + 1, dil)

        v_proj(c, l, base + 512, vaug, vbias, tsel, 16)
        units = [(rho, qb) for rho in range(dil) for qb in range(nqb)]
        ust = {}

        def stage1(ui, units=units, tsel=tsel, nqb=nqb, ust=ust):
            rho, qb = units[ui]
            qs = tsel(rho * nqb + qb)
            kts = [(jb, mt) for (jb, mt) in ((qb - 1, 0), (qb, 1), (qb + 1, 2)) if 0 <= jb < nqb]
            sl = (unit0 + ui) % 2
            ps_ = []
            for idx, (jb, mt) in enumerate(kts):
                ks = tsel(rho * nqb + jb)
                p = pt.s(sl, (slice(None), sl, idx, slice(None)))
                scs = [psA(c), psA(c)]
                for hp in range(2):
                    for par in range(2):
                        pb = par * 64
                        fw.mm(scs[par][:, hp * 128:(hp + 1) * 128], V(kT.h[pb:pb + 64, hp, ks], kT.trs),
                              V(qT.h[pb:pb + 64, hp, qs], qT.trs))
                for par in range(2):
                    fw.act(V(p.ap[:, par * 256:(par + 1) * 256], p.trs), scs[par][:, 0:256], AF.Exp, scale=0.125)
                fw.tt(p, p, bm[:, mt, :], ALU.mult, e=("pool" if idx % 2 == 0 else "dve"))
                ps_.append(p)
            ust[ui] = (qs, kts, ps_)

        def stage2(ui, units=units, nqb=nqb, ust=ust, g=g):
            rho, qb = units[ui]
            qs, kts, ps_ = ust.pop(ui)
            ob = psB(c)
            for h in range(4):
                for idx, (jb, mt) in enumerate(kts):
                    pc = (h % 2) * 256 + (h // 2) * 128
                    fw.mm(ob[:, h * 128:(h + 1) * 128], vaug[:, rho * nqb + jb, h, :],
                          V(ps_[idx].ap[:, pc:pc + 128], ps_[idx].trs),
                          start=(idx == 0), stop=(idx == len(kts) - 1))
            dst = V(acc.h[:, :, qs], acc.trs)
            src = V(ob.h[:, :].rearrange("p (h q) -> p h q", h=4), ob.trs)
            if g == 0:
                fw.copy(dst, src, e="act")
            else:
                fw.tt(dst, src, dst, ALU.add)

        unit0 = unit
        stage1(0)
        for ui in range(len(units)):
            if ui + 1 < len(units):
                stage1(ui + 1)
            stage2(ui)
        unit += len(units)
    a5 = acc.h.rearrange("p (hp par) t -> p hp par t", par=2)
    t5 = tmp.h.rearrange("p (hp par) t -> p hp par t", par=2)
    for tb in range(NTB):
        tok = slice(tb * 512, (tb + 1) * 512)
        fw.recip(V(t5[0:64, :, 0, :], tmp.trs), V(a5[64:128, :, 0, tok], acc.trs))
        fw.recip(V(t5[64:128, :, 1, :], tmp.trs), V(a5[0:64, :, 1, tok], acc.trs))
        fw.tt(c.yT.s(3, (slice(0, 64), slice(6, 8), tok)), V(a5[0:64, :, 0, tok], acc.trs), V(t5[0:64, :, 0, :], tmp.trs),
              ALU.mult)
        fw.tt(c.yT.s(3, (slice(64, 128), slice(6, 8), tok)), V(a5[64:128, :, 1, tok], acc.trs),
              V(t5[64:128, :, 1, :], tmp.trs), ALU.mult)
    gate_branch(c, l, D_G, 3)


USE_NTI = os.environ.get("B_NTI", "1") == "1"
GC = int(os.environ.get("B_GC", "4"))


def yslot_f32(c, slot):
    ap = c.yT.h[:, 2 * slot:2 * slot + 2, :].rearrange("p a t -> p (a t)").bitcast(F32)
    return Buf(ap, 1), c.yT.trs[slot]


def branch_b(c, s, l):
    for hp in range(2):
        rwkv_hp(c, l, hp)


def rwkv_hp(c, l, hp):
    fw = c.fw
    arena_reset(c)
    CD = -math.exp(-0.5)
    prm = alloc(c, [128, 64], F32)
    blk = alloc(c, [128, 128], F32)
    wst = alloc(c, [128, 2, 256], F32)
    w2sb = alloc(c, [128, 256], BF16)
    a2sb = alloc(c, [128, 256], BF16)
    der = alloc(c, [128, 16], F32)
    ones = alloc(c, [128, 64], F32)
    fw.dma(prm[:, :], V(c.rwp[l], [c.tr_in]))
    fw.dma(blk[:, :], V(c.blk1[:, :], [c.tr_in]))
    fw.dma(wst[:, 0, :], V(c.rw_w2[l], [c.tr_in]))
    fw.dma(wst[:, 1, :], V(c.rw_a2[l], [c.tr_in]))
    fw.copy(w2sb[:, :], wst[:, 0, :])
    fw.copy(a2sb[:, :], wst[:, 1, :])
    fw.memset(ones[:, :], 1.0)
    mucols = [1 + hp, 3 + hp, 5 + hp, 7 + hp, 9, 10]
    for i, mc in enumerate(mucols):
        fw.ts(der[:, i:i + 1], prm[:, mc:mc + 1], -1.0, 1.0, ALU.mult, ALU.add)
        fw.ts(der[:, 6 + i:7 + i], prm[:, mc:mc + 1], 0.5, None, ALU.mult)
    fw.ts(der[:, 12:13], prm[:, 21 + hp:22 + hp], -1.0, 1.0, ALU.mult, ALU.add)
    vT = alloc(c, [128, T], BF16)
    gs = alloc(c, [128, T], BF16)
    bv = alloc(c, [128, T], BF16)
    AR = [alloc(c, [128, 2, T], BF16) for _ in range(2)]
    BK = [alloc(c, [128, 2, T], BF16) for _ in range(2)]
    RLO = [alloc(c, [64, T], BF16) for _ in range(2)]
    eL = alloc(c, [128, 2, 32], F32)
    mark = c.arena_off

    uT = alloc(c, [128, 6, T + 2], BF16)
    fw.memset(uT[:, :, 0:1], 0.0)
    fw.memset(uT[:, :, T + 1:T + 2], 0.0)
    cols = [B_R + hp * 128, B_K + hp * 128, B_V + hp * 128, B_GT + hp * 128, B_WL, B_AL]

    def consume(i, tb, ps):
        ch = cols[i] // 128
        fw.act(uT[:, i, 1 + tb * 512:1 + (tb + 1) * 512], ps[:, :], AF.Identity, bias=c.bfm[:, l, ch:ch + 1])

    proj_fm(c, l, cols, consume)
    tbuf = [alloc(c, [128, 512], F32)[:, :] for _ in range(11)]
    for slot_ in (0, 2, 3):
        ex_ap = c.yT.h[:, 2 * slot_:2 * slot_ + 2, :].rearrange("p a t -> p (a t)").bitcast(F32)
        for i in range(4):
            tbuf.append(V(ex_ap[:, i * 512:(i + 1) * 512], [Tr()]))
    xr, xk, kk, tA, tB = tbuf[0:5]
    dtmp = [tbuf[5:14], tbuf[14:23]]
    rmask = alloc(c, [128, 512], F32)
    fw.memset(rmask[:, :], 1.0)
    fw.memset(V(rmask.h[:, 0:512:64], rmask.trs), 0.0)
    twl = alloc(c, [128, 512], BF16)
    tal = alloc(c, [128, 512], BF16)
    for tb in range(NTB):
        tok = slice(tb * 512, (tb + 1) * 512)

        def shift(i, out):
            fw.tt(tA, uT[:, i, tb * 512:tb * 512 + 512], uT[:, i, tb * 512 + 2:tb * 512 + 514], ALU.add, e="pool")
            fw.ts(tA, tA, der[:, 6 + i:7 + i], None, ALU.mult)
            fw.stt(out, uT[:, i, tb * 512 + 1:tb * 512 + 513], der[:, i:i + 1], tA, ALU.mult, ALU.add)

        shift(0, xr)
        shift(1, xk)
        shift(2, tB)
        fw.copy(vT[:, tok], tB, e="act")
        shift(3, tB)
        fw.act(kk, tB, AF.Sigmoid)
        fw.tt(gs[:, tok], tB, kk, ALU.mult, e="pool")
        shift(4, tB)
        fw.act(twl[:, :], tB, AF.Tanh)
        shift(5, tB)
        fw.copy(tal[:, :], tB, e="act")
        fw.ts(kk, xk, prm[:, 19 + hp:20 + hp], None, ALU.mult)
        fw.tt(tB, kk, kk, ALU.mult, e="pool")
        ps = psA(c)
        fw.mm(ps[:, :], blk[:, :], tB)
        fw.ts(tB, ps[:, :], 1e-24, None, ALU.max)
        fw.act(tB, tB, AF.Sqrt)
        fw.recip(tB, tB)
        fw.tt(kk, kk, tB, ALU.mult)

        def dir_chain(d):
            lw, L0, D_, eP, eM, eX, a_, kd, tD = dtmp[d]
            dsl = slice(d * 64, (d + 1) * 64)
            ps = psA(c)
            fw.mm(ps[:, :], w2sb[dsl, hp * 128:(hp + 1) * 128], twl[dsl, :])
            yield
            fw.act(lw, ps[:, :], AF.Sigmoid, bias=prm[:, 11 + d * 2 + hp:12 + d * 2 + hp])
            yield
            ps2 = psA(c)
            fw.mm(ps2[:, :], a2sb[dsl, hp * 128:(hp + 1) * 128], tal[dsl, :])
            fw.emit("dve", lambda E: E.tensor_tensor_scan(out=L0.ap, data0=rmask.h[:, :], data1=lw.ap,
                                                        initial=0.0, op0=ALU.mult, op1=ALU.add),
                    [rmask[:, :], lw], [L0])
            yield
            fw.act(a_, ps2[:, :], AF.Sigmoid, bias=prm[:, 15 + d * 2 + hp:16 + d * 2 + hp])
            yield
            ltot = V(L0.ap[:, 63:512:64], L0.trs)
            fw.act(eL[:, d, tb * 8:(tb + 1) * 8], ltot, AF.Exp, scale=CD)
            if d == 0:
                fw.tt(D_, L0, lw, ALU.subtract, e="pool")
                yield
                fw.act(eP, L0, AF.Exp, scale=CD)
                yield
                fw.act(eM, L0, AF.Exp, scale=-CD)
                yield
                fw.act(eX, D_, AF.Exp, scale=CD)
                yield
            else:
                l3 = V(L0.ap.rearrange("p (c t) -> p c t", t=64), L0.trs)
                lt3 = V(L0.ap[:, 63:512:64].unsqueeze(2).to_broadcast([128, 8, 64]), L0.trs)
                fw.tt(V(D_.ap.rearrange("p (c t) -> p c t", t=64), D_.trs), l3, lt3, ALU.subtract)
                yield
                fw.tt(lw, D_, lw, ALU.subtract, e="pool")
                yield
                fw.act(eX, D_, AF.Exp, scale=-CD)
                yield
                fw.act(eP, lw, AF.Exp, scale=-CD)
                yield
                fw.act(eM, lw, AF.Exp, scale=CD)
                yield
            fw.ts(tD, a_, prm[:, 21 + hp:22 + hp], der[:, 12:13], ALU.mult, ALU.add)
            yield
            fw.tt(kd, tD, xk, ALU.mult)
            yield
            fw.tt(tD, kk, a_, ALU.mult, e="pool")
            yield
            fw.tt(AR[d][:, 0, tok], kk, eX, ALU.mult)
            yield
            fw.tt(BK[d][:, 0, tok], tD, eM, ALU.mult)
            yield
            fw.tt(BK[d][:, 1, tok], kd, eM, ALU.mult, e="pool")
            yield
            fw.tt(AR[d][:, 1, tok], xr, eP, ALU.mult)
            yield
            fw.tt(RLO[d][0:64, tok], V(xr.ap[64:128, :], xr.trs), V(eP.ap[64:128, :], eP.trs), ALU.mult)
            yield

        gens = [dir_chain(0), dir_chain(1)]
        while gens:
            for g_ in list(gens):
                try:
                    next(g_)
                except StopIteration:
                    gens.remove(g_)
        fw.tt(tA, dtmp[0][7], dtmp[1][7], ALU.add, e="pool")
        fw.stt(tB, xr, prm[:, 23 + hp:24 + hp], tA, ALU.mult, ALU.mult)
        ps = psA(c)
        fw.mm(ps[:, :], blk[:, :], tB)
        fw.tt(bv[:, tok], ps[:, :], vT[:, tok], ALU.mult)

    fw.barrier()
    c.arena_off = mark
    ys = []
    for d in range(2):
        b_, tr_ = yslot_f32(c, (0, 2)[d])
        b_.trs = [tr_]
        ys.append(b_)
    eLs = alloc(c, [64, 2, 2, 32], F32)
    for d in range(2):
        for hl in range(2):
            fw.copy(eLs[:, d, hl, :], eL[hl * 64:(hl + 1) * 64, d, :], e="act")
    tri = alloc(c, [64, 2, 192], F32)
    fw.dma(tri[:, :, :], V(c.trimask[:, :, :], [c.tr_in]))
    NM = GC * 4
    tmT2 = [alloc(c, [64, GC, 2, 4, 128], BF16) for _ in range(2)]
    Am2 = [alloc(c, [64, NM, 256], BF16) for _ in range(2)]
    TT2 = [alloc(c, [64, NM, 64], BF16) for _ in range(2)]
    PT2 = [alloc(c, [64, NM, 64], BF16) for _ in range(2)]
    Zb2 = [alloc(c, [64, NM, 64], BF16) for _ in range(2)]
    Nb = [alloc(c, [64, NM, 64], BF16) for _ in range(2)]
    NTb = [alloc(c, [64, NM, 64], BF16) for _ in range(2)]
    NTI = alloc(c, [64, NM, 64], BF16)
    Rb = [alloc(c, [64, NM, 64], BF16) for _ in range(2)]
    Sb = alloc(c, [64, 2, 4, 64], BF16, nslots=2)
    Ub = alloc(c, [64, 4, 64], BF16)
    fw.memset(Sb[:, :, :, :], 0.0)
    I64b = c.identb[0:64, 0:64]
    nsteps = T // 64
    ngroups = nsteps // GC
    halves = [(0, NM // 2), (NM // 2, NM)]
    idb = V(c.identf.h[0:64, 0:64].unsqueeze(1).to_broadcast([64, NM // 2, 64]), c.identf.trs)

    def chunk_of(g, ci, d):
        it = g * GC + ci
        return it if d == 0 else nsteps - 1 - it

    def flat(buf, m0, m1):
        return V(buf.h[:, m0:m1, :].rearrange("p m e -> p (m e)"), buf.trs)

    def phase1(g):
        tmT, Am, TT, PT, Zb = tmT2[g % 2], Am2[g % 2], TT2[g % 2], PT2[g % 2], Zb2[g % 2]
        for ci in range(GC):
            for d in range(2):
                ch = chunk_of(g, ci, d)
                cs = slice(ch * 64, ch * 64 + 64)
                pb_ = psB(c)
                pbv = pb_.h[:, :].bitcast(BF16)
                srcs = [AR[d][:, 0, cs], BK[d][:, 0, cs], BK[d][:, 1, cs], vT[:, cs]]
                for q, src in enumerate(srcs):
                    fw.transpose(V(pbv[0:64, q * 128:(q + 1) * 128], pb_.trs), src, c.identb[:, :])
                fw.copy(V(tmT.h[:, ci, d, :, :].rearrange("p q e -> p (q e)"), tmT.trs), V(pbv[0:64, 0:512], pb_.trs),
                        e=("act" if (d == 0 or os.environ.get("B_TMT_ACT", "0") == "1") else "dve"))
            yield
        bN = [psB(c), psB(c)]
        for ci in range(GC):
            bA = [psA(c), psA(c)]
            hl_outer = os.environ.get("B_HLOUTER", "1") == "1"
            order = [(d, hl) for hl in range(2) for d in range(2)] if hl_outer else [(d, hl) for d in range(2) for hl in range(2)]
            for (d, hl) in order:
                ch = chunk_of(g, ci, d)
                cs = slice(ch * 64, ch * 64 + 64)
                pb = hl * 64
                for q in range(2):
                    o0 = d * 256 + q * 128
                    fw.mm(bA[hl][0:64, o0:o0 + 128], BK[d][pb:pb + 64, q, cs], V(AR[d].h[pb:pb + 64, :, cs], AR[d].trs))
                o1 = (ci * 2 + d) * 64
                fw.mm(bN[hl][0:64, o1:o1 + 64], AR[d][pb:pb + 64, 0, cs], BK[d][pb:pb + 64, 0, cs])
            for hl in range(2):
                m0 = ci * 4 + hl
                dst = V(Am.h[:, m0:m0 + 3:2, :].rearrange("p d (q e) -> p d q e", q=2), Am.trs)
                src = V(bA[hl].h[0:64, :].rearrange("p (d q e) -> p d q e", d=2, q=2), bA[hl].trs)
                msk = V(tri.h[:, :, 0:128].unsqueeze(2).to_broadcast([64, 2, 2, 128]), tri.trs)
                fw.tt(dst, src, msk, ALU.mult)
            yield
        for hl in range(2):
            dst = V(NTb[0].h[:, hl:NM:2, :].rearrange("p (ci d) e -> p ci d e", d=2), NTb[0].trs)
            src = V(bN[hl].h[0:64, :].rearrange("p (ci d e) -> p ci d e", ci=GC, d=2), bN[hl].trs)
            msk = V(tri.h[:, :, 128:192].unsqueeze(1).to_broadcast([64, GC, 2, 64]), tri.trs)
            fw.tt(dst, src, msk, ALU.mult)
        for (m0, m1) in halves:
            fw.tt(Rb[0][:, m0:m1, :], idb, Am[:, m0:m1, 0:64], ALU.subtract)
        yield
        Ncur = V(Am.h[:, :, 0:64], Am.trs)
        NTcur = NTb[0][:, :, :]
        rcur = 0
        nti = 0
        for lev in range(1, 6):
            ntn = 1 - nti
            nn = lev % 2
            pzs = []
            for (m0, m1) in halves:
                pz = psA(c)
                for m in range(m0, m1):
                    fw.mm(pz[0:64, (m - m0) * 64:(m - m0 + 1) * 64], V(Ncur.ap[:, m, :], Ncur.trs), V(NTcur.ap[:, m, :], NTcur.trs))
                pzs.append(pz)
            for hi, (m0, m1) in enumerate(halves):
                pz = pzs[hi]
                src3 = V(pz.h[0:64, 0:(m1 - m0) * 64].rearrange("p (m e) -> p m e", e=64), pz.trs)
                if USE_NTI:
                    fw.tt(NTI[:, m0:m1, :], src3, idb, ALU.add)
                if lev < 5 or not USE_NTI:
                    fw.copy(flat(NTb[ntn], m0, m1), pz[0:64, 0:(m1 - m0) * 64], e="act")
            yield
            if lev < 5:
                for (m0, m1) in halves:
                    pz = psA(c)
                    for m in range(m0, m1):
                        fw.mm(pz[0:64, (m - m0) * 64:(m - m0 + 1) * 64], V(NTcur.ap[:, m, :], NTcur.trs), V(Ncur.ap[:, m, :], Ncur.trs))
                    fw.copy(flat(Nb[nn], m0, m1), pz[0:64, 0:(m1 - m0) * 64], e="act")
                yield
            rn = 1 - rcur
            rdst = TT if lev == 5 else Rb[rn]
            for (m0, m1) in halves:
                pz = psB(c)
                for m in range(m0, m1):
                    o = pz[0:64, (m - m0) * 64:(m - m0 + 1) * 64]
                    if USE_NTI:
                        fw.mm(o, NTI[:, m, :], Rb[rcur][:, m, :])
                    else:
                        fw.mm(o, I64b, Rb[rcur][:, m, :], start=True, stop=False)
                        fw.mm(o, NTb[ntn][:, m, :], Rb[rcur][:, m, :], start=False, stop=True)
                fw.copy(flat(rdst, m0, m1), pz[0:64, 0:(m1 - m0) * 64])
            yield
            rcur = rn
            if lev < 5:
                Ncur = Nb[nn][:, :, :]
                NTcur = NTb[ntn][:, :, :]
                nti = ntn
        for (m0, m1) in halves:
            pz = psA(c)
            pq = psB(c)
            for m in range(m0, m1):
                ci, d, hl = m // 4, (m // 2) % 2, m % 2
                fw.mm(pz[0:64, (m - m0) * 64:(m - m0 + 1) * 64], tmT[:, ci, d, 0, hl * 64:(hl + 1) * 64], TT[:, m, :])
                fw.mm(pq[0:64, (m - m0) * 64:(m - m0 + 1) * 64], Am[:, m, 128:192], tmT[:, ci, d, 3, hl * 64:(hl + 1) * 64])
            fw.copy(flat(PT, m0, m1), pz[0:64, 0:(m1 - m0) * 64], e="act")
            fw.copy(flat(Zb, m0, m1), pq[0:64, 0:(m1 - m0) * 64])
            yield

    def drain(gen, n):
        if gen is None:
            return None
        for _ in range(n):
            try:
                next(gen)
            except StopIteration:
                return None
        return gen

    gen = phase1(0)
    drain(gen, 1000)
    slot = 0
    NY = 2 * GC + 2 + 14 + 2
    per_pt = -(-NY // (2 * GC))
    for g in range(ngroups):
        tmT, Am, TT, PT, Zb = tmT2[g % 2], Am2[g % 2], TT2[g % 2], PT2[g % 2], Zb2[g % 2]
        gen = phase1(g + 1) if g + 1 < ngroups else None
        if os.environ.get("B_NOPIPE", "0") == "1":
            gen = drain(gen, 1000)
        for ci in range(GC):
            scur = Sb.s(slot, (slice(None), slot, slice(None), slice(None)))
            snew = Sb.s(1 - slot, (slice(None), 1 - slot, slice(None), slice(None)))
            pu = psB(c)
            for inst in range(4):
                m = ci * 4 + inst
                o = pu[0:64, inst * 64:(inst + 1) * 64]
                fw.mm(o, PT[:, m, :], V(scur.ap[:, inst, :], scur.trs), start=True, stop=False)
                fw.mm(o, TT[:, m, :], Zb[:, m, :], start=False, stop=True)
            fw.act(V(Ub.h.rearrange("p i e -> p (i e)"), Ub.trs), pu[0:64, 0:256], AF.Copy, scale=-1.0)
            gen = drain(gen, per_pt)
            pS = psA(c)
            pY = psB(c)
            for inst in range(4):
                m = ci * 4 + inst
                d, hl = inst // 2, inst % 2
                ch = chunk_of(g, ci, d)
                cs = slice(ch * 64, ch * 64 + 64)
                hs = slice(hl * 64, (hl + 1) * 64)
                o = pS[0:64, inst * 64:(inst + 1) * 64]
                fw.mm(o, I64b, V(scur.ap[:, inst, :], scur.trs), start=True, stop=False)
                fw.mm(o, tmT[:, ci, d, 2, hs], tmT[:, ci, d, 3, hs], start=False, stop=False)
                fw.mm(o, tmT[:, ci, d, 1, hs], Ub[:, inst, :], start=False, stop=True)
            for d in range(2):
                ch = chunk_of(g, ci, d)
                esc = V(eLs.h[:, d, :, ch:ch + 1].to_broadcast([64, 2, 64]), eLs.trs)
                fw.tt(V(snew.ap[:, 2 * d:2 * d + 2, :], snew.trs),
                      V(pS.h[0:64, d * 128:(d + 1) * 128].rearrange("p (i e) -> p i e", i=2), pS.trs), esc, ALU.mult)
            for inst in range(4):
                m = ci * 4 + inst
                d, hl = inst // 2, inst % 2
                ch = chunk_of(g, ci, d)
                cs = slice(ch * 64, ch * 64 + 64)
                hs = slice(hl * 64, (hl + 1) * 64)
                oy = pY[0:64, inst * 64:(inst + 1) * 64]
                rt = AR[d][0:64, 1, cs] if hl == 0 else RLO[d][0:64, cs]
                fw.mm(oy, V(scur.ap[:, inst, :], scur.trs), rt, start=True, stop=False)
                fw.mm(oy, tmT[:, ci, d, 3, hs], Am[:, m, 192:256], start=False, stop=False)
                fw.mm(oy, Ub[:, inst, :], Am[:, m, 64:128], start=False, stop=True)
            for d in range(2):
                ch = chunk_of(g, ci, d)
                cs = slice(ch * 64, ch * 64 + 64)
                for hl in range(2):
                    inst = d * 2 + hl
                    fw.copy(ys[d][hl * 64:(hl + 1) * 64, cs], pY[0:64, inst * 64:(inst + 1) * 64], e="act")
            gen = drain(gen, per_pt)
            slot = 1 - slot
        drain(gen, 1000)

    fw.barrier()
    c.arena_off = mark
    pt_ = [alloc(c, [128, 512], F32)[:, :] for _ in range(6)]
    y_, sq, mean, var, t1, t2 = pt_
    for tb in range(NTB):
        tok = slice(tb * 512, (tb + 1) * 512)
        fw.tt(y_, ys[0][:, tok], ys[1][:, tok], ALU.add)
        fw.tt(sq, y_, y_, ALU.mult, e="pool")
        p1 = psA(c)
        fw.mm(p1[:, :], blk[:, :], y_)
        p2 = psA(c)
        fw.mm(p2[:, :], blk[:, :], sq)
        fw.ts(mean, p1[:, :], 1.0 / 64.0, None, ALU.mult)
        fw.tt(t1, mean, mean, ALU.mult, e="pool")
        fw.stt(var, p2[:, :], 1.0 / 64.0, t1, ALU.mult, ALU.subtract)
        fw.ts(var, var, 64e-5, None, ALU.add)
        fw.act(var, var, AF.Sqrt)
        fw.recip(var, var)
        fw.tt(t2, y_, mean, ALU.subtract, e="pool")
        fw.tt(t2, t2, var, ALU.mult)
        fw.ts(t2, t2, prm[:, 25 + hp:26 + hp], prm[:, 27 + hp:28 + hp], ALU.mult, ALU.add)
        fw.tt(t2, t2, bv[:, tok], ALU.add, e="pool")
        fw.tt(c.yT.s(1, (slice(None), 2 + hp, tok)), t2, gs[:, tok], ALU.mult)
```

```python
import math
import os
import numpy as np
import ml_dtypes
import concourse.bass as bass
import concourse.mybir as mybir
from concourse.bass_utils import run_bass_kernel_spmd

F32 = mybir.dt.float32
BF16 = mybir.dt.bfloat16
AF = mybir.ActivationFunctionType
ALU = mybir.AluOpType

T = 2048
DM = 1024
NTB = 4
IN_COLS = 9984
ROT_COLS = 2048
WCOLS = IN_COLS + ROT_COLS
ALPHA = (2 * 2) ** 0.25
LN_EPS = 1e-5
ARENA_F32 = 30208


class Tr:
    __slots__ = ("w", "r", "excl")

    def __init__(self, excl=False):
        self.w = None
        self.r = {}
        self.excl = excl


class V:
    __slots__ = ("ap", "trs")

    def __init__(self, ap, trs):
        self.ap = ap
        self.trs = trs


class Buf:
    def __init__(self, h, nslots=1):
        self.h = h
        self.trs = [Tr() for _ in range(nslots)]

    def __getitem__(self, idx):
        return V(self.h[idx], self.trs)

    def s(self, slot, idx):
        return V(self.h[idx], [self.trs[slot]])


class FW:
    LIMIT = 30000

    def __init__(self, nc, ndma=24):
        self.nc = nc
        self.eng = {"pe": nc.tensor, "act": nc.scalar, "dve": nc.vector, "pool": nc.gpsimd, "sp": nc.sync}
        self.sems = []
        self.cur = {}
        self.cnt = {}
        for e in self.eng:
            self.cur[e] = self._newsem("e_" + e)
            self.cnt[e] = 0
        self.known = {e: {} for e in self.eng}
        self.dma_sem = [self._newsem("dma%d" % i) for i in range(ndma)]
        self.dma_val = [0] * ndma
        self.n_hw = ndma
        self.dma_next = 0
        self.n_ins = 0
        self._uid = 0

    def _newsem(self, name):
        self._uid = getattr(self, "_uid", 0) + 1
        h = self.nc.alloc_semaphore("%s_%d" % (name, self._uid))
        self.sems.append(h)
        return len(self.sems) - 1

    def _wait(self, e, ev):
        si, val, src = ev
        if self.known[e].get(si, 0) >= val:
            return
        self.eng[e].wait_ge(self.sems[si], val)
        self.known[e][si] = val

    def _deps(self, e, reads, writes):
        for v in reads:
            for tr in v.trs:
                if tr.w is not None:
                    if tr.w[2] == e and e == "pe":
                        continue
                    self._wait(e, tr.w)
                if tr.excl:
                    for src, ev in tr.r.items():
                        if ev[2] != e:
                            self._wait(e, ev)
        for v in writes:
            for tr in v.trs:
                if tr.w is not None and not (tr.w[2] == e and e == "pe"):
                    self._wait(e, tr.w)
                for src, ev in tr.r.items():
                    if not (ev[2] == e and e == "pe"):
                        self._wait(e, ev)

    def _record(self, ev, reads, writes, key):
        for v in writes:
            for tr in v.trs:
                tr.w = ev
                tr.r = {}
        for v in reads:
            for tr in v.trs:
                tr.r[key] = ev

    def emit(self, e, fn, reads, writes):
        self._deps(e, reads, writes)
        ins = fn(self.eng[e])
        if self.cnt[e] >= self.LIMIT:
            self.cur[e] = self._newsem("e_" + e)
            self.cnt[e] = 0
        self.cnt[e] += 1
        ins.then_inc(self.sems[self.cur[e]], 1)
        ev = (self.cur[e], self.cnt[e], e)
        self._record(ev, reads, writes, e)
        self.n_ins += 1
        return ev

    def dma(self, out, in_, e="sp"):
        self._deps(e, [in_], [out])
        if e == "pool":
            self.dma_sem.append(self._newsem("swdma"))
            self.dma_val.append(0)
            slot = len(self.dma_sem) - 1
        else:
            slot = self.dma_next
            self.dma_next = (slot + 1) % self.n_hw
        si = self.dma_sem[slot]
        if self.dma_val[slot] > 0:
            self._wait(e, (si, self.dma_val[slot], "dma"))
        ins = self.eng[e].dma_start(out=out.ap, in_=in_.ap)
        self.dma_val[slot] += 16
        ins.then_inc(self.sems[si], 16)
        ev = (si, self.dma_val[slot], "dma%d" % slot)
        self._record(ev, [in_], [out], "dma%d" % slot)
        self.n_ins += 1
        return ev

    def barrier(self):
        evs = [(self.cur[f], self.cnt[f], f) for f in self.eng if self.cnt[f] > 0]
        evs += [(self.dma_sem[i], self.dma_val[i], "dma") for i in range(len(self.dma_sem)) if self.dma_val[i] > 0]
        for e in self.eng:
            for ev in evs:
                if not (ev[2] == e and e == "pe"):
                    self._wait(e, ev)

    def mm(self, out, lhsT, rhs, start=True, stop=True):
        return self.emit("pe", lambda E: E.matmul(out.ap, lhsT=lhsT.ap, rhs=rhs.ap, start=start, stop=stop),
                         [lhsT, rhs], [out])

    def transpose(self, out, in_, ident):
        return self.emit("pe", lambda E: E.transpose(out.ap, in_.ap, ident.ap), [in_, ident], [out])

    def act(self, out, in_, func, bias=None, scale=None, e="act"):
        kw = {}
        rd = [in_]
        if bias is not None:
            if isinstance(bias, V):
                kw["bias"] = bias.ap
                rd.append(bias)
            else:
                kw["bias"] = bias
        if scale is not None:
            if isinstance(scale, V):
                kw["scale"] = scale.ap
                rd.append(scale)
            else:
                kw["scale"] = scale
        return self.emit("act", lambda E: E.activation(out=out.ap, in_=in_.ap, func=func, **kw), rd, [out])

    def tt(self, out, in0, in1, op, e="dve"):
        return self.emit(e, lambda E: E.tensor_tensor(out=out.ap, in0=in0.ap, in1=in1.ap, op=op), [in0, in1], [out])

    def ts(self, out, in0, s1, s2, op0, op1=None, e="dve"):
        rd = [in0]
        a1 = s1
        a2 = s2
        if isinstance(s1, V):
            rd.append(s1)
            a1 = s1.ap
        if isinstance(s2, V):
            rd.append(s2)
            a2 = s2.ap
        if op1 is None:
            return self.emit(e, lambda E: E.tensor_scalar(out=out.ap, in0=in0.ap, scalar1=a1, scalar2=None, op0=op0),
                             rd, [out])
        return self.emit(e, lambda E: E.tensor_scalar(out=out.ap, in0=in0.ap, scalar1=a1, scalar2=a2, op0=op0, op1=op1),
                         rd, [out])

    def stt(self, out, in0, scalar, in1, op0, op1):
        rd = [in0, in1]
        a = scalar
        if isinstance(scalar, V):
            rd.append(scalar)
            a = scalar.ap
        return self.emit("dve", lambda E: E.scalar_tensor_tensor(out=out.ap, in0=in0.ap, scalar=a, in1=in1.ap,
                                                                 op0=op0, op1=op1), rd, [out])

    def copy(self, out, in_, e="dve"):
        if e == "act":
            return self.act(out, in_, AF.Copy)
        return self.emit(e, lambda E: E.tensor_copy(out=out.ap, in_=in_.ap), [in_], [out])

    def recip(self, out, in_):
        return self.emit("dve", lambda E: E.reciprocal(out=out.ap, in_=in_.ap), [in_], [out])

    def memset(self, out, val, e="dve"):
        return self.emit(e, lambda E: E.memset(out.ap, val), [], [out])


A_Q, A_K, A_V, A_G = 0, 256, 512, 768
B_R, B_K, B_V, B_GT, B_WL, B_AL = 1024, 1280, 1536, 1792, 2048, 2176
C_Q, C_K, C_V, C_G = 2304, 2560, 2816, 3072
D_BASE, D_G = 3328, 5632
MG = 5888
R_CQ, R_CK, R_D = 9984, 10240, 10496
NCH = WCOLS // 128


def _rot_perm():
    cols = []
    for base in (C_Q, C_K):
        for c in range(256):
            j = c % 32
            cols.append(base + c - j + (j + 16) % 32)
    for g in range(3):
        for part in (0, 256):
            base = D_BASE + g * 768 + part
            for c in range(256):
                j = c % 64
                cols.append(base + c - j + (j + 32) % 64)
    return np.asarray(cols, np.int64)


def _rope_tables():
    t = np.arange(T, dtype=np.float32)
    out = []
    for d in (32, 64):
        half = d // 2
        inv = np.power(np.float32(10000.0), -np.arange(half, dtype=np.float32) / np.float32(half)).astype(np.float32)
        ang = (t[:, None] * inv[None, :]).astype(np.float32)
        cos = np.cos(ang).astype(np.float32)
        sin = np.sin(ang).astype(np.float32)
        p = np.arange(128)
        j = p % d
        cosT = cos[:, j % half].T
        sgn = np.where(j < half, -1.0, 1.0).astype(np.float32)
        sinT = (sin[:, j % half].T * sgn[:, None]).astype(np.float32)
        out += [np.ascontiguousarray(cosT), np.ascontiguousarray(sinT)]
    return out


class Ctx:
    pass


def dram(nc, name, shape, dtype, kind):
    return nc.dram_tensor(name, list(shape), dtype, kind=kind).ap()


def build(nseq, parts=("A", "B", "C", "D"), dbg=False):
    nc = bass.Bass("TRN2", target_bir_lowering=False)
    fw = FW(nc)
    c = Ctx()
    c.nc, c.fw, c.parts, c.dbg = nc, fw, parts, dbg
    IN, OUT, INT = "ExternalInput", "ExternalOutput", "Internal"
    c.x = dram(nc, "x", [nseq, T, DM], F32, IN)
    c.y = dram(nc, "y", [nseq, T, DM], F32, OUT)
    c.w_in = dram(nc, "w_in", [2, DM, IN_COLS], F32, IN)
    c.w_rot = dram(nc, "w_rot", [2, DM, ROT_COLS], F32, IN)
    c.w_br = dram(nc, "w_branch", [2, 4, 256, DM], F32, IN)
    c.w_out = dram(nc, "w_out", [2, DM, DM], F32, IN)
    c.b_fm = dram(nc, "b_fm", [2, 128, NCH], F32, IN)
    c.b_cat = dram(nc, "b_cat", [2, WCOLS], F32, IN)
    c.vecs = dram(nc, "vecs", [8, DM], F32, IN)
    c.ident_b = dram(nc, "ident_b", [128, 128], BF16, IN)
    c.ident_f = dram(nc, "ident_f", [128, 128], F32, IN)
    c.rope = dram(nc, "rope", [4, 128, T], F32, IN)
    c.bandm = dram(nc, "bandm", [128, 3, 4, 128], BF16, IN)
    c.na_bias = dram(nc, "na_bias", [2, 128, 14 * 256], F32, IN)
    c.na_mask = dram(nc, "na_mask", [128, 14 * 256], F32, IN)
    c.rwp = dram(nc, "rwp", [2, 128, 64], F32, IN)
    c.rw_w2 = dram(nc, "rw_w2", [2, 128, 256], F32, IN)
    c.rw_a2 = dram(nc, "rw_a2", [2, 128, 256], F32, IN)
    c.df_lam = dram(nc, "df_lam", [2, 128], F32, IN)
    c.trimask = dram(nc, "trimask", [64, 2, 192], F32, IN)
    c.blk1 = dram(nc, "blk1", [128, 128], F32, IN)
    c.perms = dram(nc, "perms", [2, 128, 128], BF16, IN)
    c.wbf = dram(nc, "wbf", [2, DM, WCOLS], BF16, INT)
    c.wbr_bf = dram(nc, "wbr_bf", [2, 4, 256, DM], BF16, INT)
    c.wout_bf = dram(nc, "wout_bf", [2, DM, DM], BF16, INT)
    c.xres = dram(nc, "xres", [T, DM], F32, INT)
    c.tr_wbf = [[Tr() for _ in range(NCH)] for _ in range(2)]
    c.tr_wbr = [Tr(), Tr()]
    c.tr_wout = [Tr(), Tr()]
    c.tr_xres = [Tr() for _ in range(16)]
    c.tr_in = Tr()
    c.tr_y = Tr()
    if dbg:
        c.dbg_out = dram(nc, "dbg", [128, 8, T], BF16, OUT)
        c.tr_dbg = Tr()

    def sb(name, shape, dtype, nslots=1):
        return Buf(nc.alloc_sbuf_tensor(name, list(shape), dtype), nslots)

    c.sb = sb
    c.xT = sb("xT", [128, 8, T], BF16)
    c.yT = sb("yT", [128, 8, T], BF16, nslots=4)
    c.wbuf = sb("wbuf", [128, 2, 8, 512], BF16, nslots=2)
    c.identb = sb("identb", [128, 128], BF16)
    c.identf = sb("identf", [128, 128], F32)
    c.bfm = sb("bfm", [128, 2, NCH], F32)
    c.arena = nc.alloc_sbuf_tensor("arena", [128, ARENA_F32], F32)
    c.arena_off = 0
    c.arena_summ = {}
    c.arena_live = []
    c.ps = [Buf(nc.alloc_psum_tensor("ps%d" % i, [128, 512], F32)) for i in range(8)]
    for b_ in c.ps:
        b_.trs = [Tr(excl=True)]
    c.ps_i = [0, 0]

    fw.dma(c.identb[:, :], V(c.ident_b[:, :], [c.tr_in]))
    fw.dma(c.identf[:, :], V(c.ident_f[:, :], [c.tr_in]))
    for l in range(2):
        fw.dma(c.bfm[:, l, :], V(c.b_fm[l], [c.tr_in]))

    convert_weights(c)
    for s in range(nseq):
        stage0(c, s)
        for l in range(2):
            layer(c, s, l)
    for i in range(len(fw.dma_sem)):
        if fw.dma_val[i] > 0:
            fw._wait("sp", (fw.dma_sem[i], fw.dma_val[i], "dma"))
    return nc, fw


def _arena_retire(c):
    summ = c.arena_summ
    for buf in c.arena_live:
        for tr in buf.trs:
            evs = list(tr.r.values())
            if tr.w is not None:
                evs.append(tr.w)
            for ev in evs:
                key = ev[2]
                old = summ.get(key)
                if old is None or (ev[0], ev[1]) > (old[0], old[1]):
                    summ[key] = ev
    c.arena_live = []


def arena_reset(c):
    _arena_retire(c)
    c.arena_off = 0


def arena_release(c, mark):
    _arena_retire(c)
    c.arena_off = mark


def alloc(c, shape, dtype, nslots=1):
    n = int(np.prod(shape[1:]))
    n4 = (n + 1) // 2 if dtype == BF16 else n
    n4 = (n4 + 7) // 8 * 8
    assert c.arena_off + n4 <= ARENA_F32, ("arena overflow", c.arena_off, n4)
    ap = c.arena[0:shape[0], c.arena_off:c.arena_off + n4]
    c.arena_off += n4
    if dtype == BF16:
        ap = ap.bitcast(BF16)[:, 0:n]
    else:
        ap = ap[:, 0:n]
    if len(shape) > 2:
        names = " ".join("d%d" % i for i in range(len(shape) - 1))
        kw = {"d%d" % i: shape[i + 1] for i in range(len(shape) - 1)}
        ap = ap.rearrange("p (%s) -> p %s" % (names, names), **kw)
    b = Buf(ap, nslots)
    for tr in b.trs:
        tr.r = dict(c.arena_summ)
    c.arena_live.append(b)
    return b


def load_vecs(c, rows):
    for i, r in enumerate(rows):
        c.fw.dma(c.vec_bc[:, i, :], V(c.vecs[r:r + 1, :].partition_broadcast(128), [c.tr_in]))


def psA(c):
    i = c.ps_i[0]
    c.ps_i[0] = (i + 1) % 4
    return c.ps[i]


def psB(c):
    i = c.ps_i[1]
    c.ps_i[1] = (i + 1) % 4
    return c.ps[4 + i]


def convert_weights(c):
    fw = c.fw
    rd = [c.tr_in]
    for l in range(2):
        blocks = [(0, 512 * i, 512) for i in range(11)] + [(0, 5632, 256)]
        for (_, c0, n) in blocks:
            trs = c.tr_wbf[l][c0 // 128:(c0 + n) // 128]
            fw.dma(V(c.wbf[l, :, c0:c0 + n], trs), V(c.w_in[l, :, c0:c0 + n], rd), e="pool")
        dst = c.wbf[l, :, MG:IN_COLS].rearrange("k (dc b j) -> k dc b j", dc=8, b=4)
        for b in range(4):
            src = c.w_in[l, :, MG + b * 1024:MG + (b + 1) * 1024].rearrange("k (dc j) -> k dc j", dc=8)
            fw.dma(V(dst[:, :, b, :], c.tr_wbf[l][MG // 128:IN_COLS // 128]), V(src, rd), e="pool")
        for b in range(4):
            fw.dma(V(c.wbr_bf[l, b], [c.tr_wbr[l]]), V(c.w_br[l, b], rd), e="pool")
        for i in range(2):
            fw.dma(V(c.wout_bf[l, :, 512 * i:512 * (i + 1)], [c.tr_wout[l]]),
                   V(c.w_out[l, :, 512 * i:512 * (i + 1)], rd), e="pool")


def load_w(c, l, c0, n):
    fw = c.fw
    slot = getattr(c, "_wslot", 0)
    c._wslot = 1 - slot
    src = c.wbf[l, :, c0:c0 + n].rearrange("(kc p) n -> p kc n", p=128)
    trs = c.tr_wbf[l][c0 // 128:(c0 + n + 127) // 128]
    fw.dma(c.wbuf.s(slot, (slice(None), slot, slice(None), slice(0, n))), V(src, trs))
    return slot


def wv(c, slot, kc, j0, n):
    return c.wbuf.s(slot, (slice(None), slot, kc, slice(j0, j0 + n)))


def proj_fm(c, l, col_list, consume):
    fw = c.fw
    groups = []
    for i, c0 in enumerate(col_list):
        if groups and groups[-1][0] + groups[-1][1] == c0 and groups[-1][1] < 512:
            groups[-1][1] += 128
            groups[-1][2].append(i)
        else:
            groups.append([c0, 128, [i]])
    slots = [None] * len(groups)
    slots[0] = load_w(c, l, groups[0][0], groups[0][1])
    for gi, (g0, gn, idxs) in enumerate(groups):
        if gi + 1 < len(groups):
            slots[gi + 1] = load_w(c, l, groups[gi + 1][0], groups[gi + 1][1])
        for j, i in enumerate(idxs):
            for tb in range(NTB):
                ps = psA(c)
                for kc in range(8):
                    fw.mm(ps[:, :], wv(c, slots[gi], kc, j * 128, 128), c.xT[:, kc, tb * 512:(tb + 1) * 512],
                          start=(kc == 0), stop=(kc == 7))
                consume(i, tb, ps)


def ln_stats(c, z, sl):
    fw = c.fw
    st = c.ln_st.s(sl, (slice(None), sl, slice(None), slice(None)))
    mv = c.ln_mv
    tr = [mv.trs[sl]]
    fw.emit("dve", lambda E: E.bn_stats(out=st.ap[:, 0, :], in_=z.ap[:, 0:512]), [z], [st])
    fw.emit("dve", lambda E: E.bn_stats(out=st.ap[:, 1, :], in_=z.ap[:, 512:1024]), [z], [st])
    m = lambda a, b: V(mv.h[:, sl, a:b], tr)
    fw.emit("dve", lambda E: E.bn_aggr(out=mv.h[:, sl, 0:2], in_=st.ap), [st], [m(0, 2)])
    fw.ts(m(2, 3), m(1, 2), LN_EPS, None, ALU.add)
    fw.act(m(3, 4), m(2, 3), AF.Sqrt)
    fw.recip(m(4, 5), m(3, 4))
    fw.stt(m(5, 6), m(0, 1), -1.0, m(4, 5), ALU.mult, ALU.mult)


def ln_apply(c, z, gi, bi, out, sl):
    fw = c.fw
    tr = [c.ln_mv.trs[sl]]
    fw.act(out, z, AF.Identity, bias=V(c.ln_mv.h[:, sl, 5:6], tr), scale=V(c.ln_mv.h[:, sl, 4:5], tr))
    fw.tt(out, out, c.vec_bc[:, gi, :], ALU.mult)
    fw.tt(out, out, c.vec_bc[:, bi, :], ALU.add, e="pool")


def to_xT(c, rows, tt):
    fw = c.fw
    xb = c.xb_t
    fw.copy(xb[:, :], rows, e="act")
    ps = psB(c)
    psb = V(ps.h[:, :].bitcast(BF16), ps.trs)
    for k in range(8):
        fw.transpose(V(psb.ap[:, k * 128:(k + 1) * 128], ps.trs), xb[:, k * 128:(k + 1) * 128], c.identb[:, :])
    fw.copy(c.xT[:, :, tt * 128:(tt + 1) * 128], V(psb.ap.rearrange("p (k t) -> p k t", k=8), ps.trs))


def run_pipeline3(sa, sb, sc, n):
    for t in range(-2, n):
        if 0 <= t + 2 < n:
            sa(t + 2)
        if 0 <= t + 1 < n:
            sb(t + 1)
        if 0 <= t < n:
            sc(t)


def ln_allocs(c):
    c.ln_st = alloc(c, [128, 2, 2, 6], F32, nslots=2)
    c.ln_mv = alloc(c, [128, 2, 8], F32, nslots=2)
    c.xb_t = alloc(c, [128, DM], BF16)
    c.zt = alloc(c, [128, 3, DM], F32, nslots=3)
    c.ot = alloc(c, [128, 2, DM], F32, nslots=2)
    c.vec_bc = alloc(c, [128, 3, DM], F32)


def stage0(c, s):
    fw = c.fw
    arena_reset(c)
    ln_allocs(c)
    load_vecs(c, [0, 1])
    def zslot(tt):
        return c.zt.s(tt % 3, (slice(None), tt % 3, slice(None)))

    fw.dma(zslot(0), V(c.x[s, 0:128, :], [c.tr_in]))

    def stage_a(tt):
        if tt + 1 < 16:
            fw.dma(zslot(tt + 1), V(c.x[s, (tt + 1) * 128:(tt + 2) * 128, :], [c.tr_in]))
        ln_stats(c, zslot(tt), tt % 2)

    def stage_b1(tt):
        sl = tt % 2
        ln_apply(c, zslot(tt), 0, 1, c.ot.s(sl, (slice(None), sl, slice(None))), sl)

    def stage_b2(tt):
        sl = tt % 2
        o = c.ot.s(sl, (slice(None), sl, slice(None)))
        fw.dma(V(c.xres[tt * 128:(tt + 1) * 128, :], [c.tr_xres[tt]]), o)
        to_xT(c, o, tt)

    run_pipeline3(stage_a, stage_b1, stage_b2, 16)


def gate_branch(c, l, gcol, bi):
    fw = c.fw

    def consume(i, tb, ps):
        ch = (gcol // 128) + i
        sg = c.g_sig.s(tb % 2, (slice(None), tb % 2, slice(None)))
        fw.act(sg, ps[:, :], AF.Sigmoid, bias=c.bfm[:, l, ch:ch + 1])
        fw.stt(sg, ps[:, :], c.bfm[:, l, ch:ch + 1], sg, ALU.add, ALU.mult)
        yv = c.yT.s(bi, (slice(None), bi * 2 + i, slice(tb * 512, (tb + 1) * 512)))
        fw.tt(yv, yv, sg, ALU.mult, e="pool")

    proj_fm(c, l, [gcol, gcol + 128], consume)


def final_stage(c, s, l):
    fw = c.fw
    arena_reset(c)
    ln_allocs(c)
    c.mergedT = alloc(c, [128, 8, T], BF16)
    c.wbr_sb = alloc(c, [128, 4, 2, DM], BF16)
    c.wout_sb = alloc(c, [128, 8, DM], BF16)
    c.f_sig = alloc(c, [128, 2, 512], F32, nslots=2)
    c.f_acc = alloc(c, [128, 2, 512], F32, nslots=2)
    c.f_tmp = alloc(c, [128, 2, 512], F32, nslots=2)
    load_vecs(c, [2 + 3 * l, 3 + 3 * l, 4 + 3 * l])
    c.ones_row = alloc(c, [1, 128], BF16)
    c.bout_row = alloc(c, [1, DM], BF16)
    bstage = alloc(c, [1, DM], F32)
    fw.memset(c.ones_row[0:1, :], 1.0)
    fw.dma(bstage[0:1, :], V(c.vecs[2 + 3 * l:3 + 3 * l, :], [c.tr_in]))
    fw.copy(c.bout_row[0:1, :], bstage[0:1, :])
    for b in range(4):
        fw.dma(c.wbr_sb[:, b, :, :], V(c.wbr_bf[l, b].rearrange("(kc p) d -> p kc d", p=128), [c.tr_wbr[l]]))
    fw.dma(c.wout_sb[:, :, :], V(c.wout_bf[l].rearrange("(kc p) d -> p kc d", p=128), [c.tr_wout[l]]))
    slots = [None] * 8
    slots[0] = load_w(c, l, MG, 512)
    for dc in range(8):
        if dc + 1 < 8:
            slots[dc + 1] = load_w(c, l, MG + (dc + 1) * 512, 512)
        for tb in range(NTB):
            tok = slice(tb * 512, (tb + 1) * 512)
            ai = tb % 2
            acc = c.f_acc.s(ai, (slice(None), ai, slice(None)))
            for b in range(4):
                pg = psA(c)
                for kc in range(8):
                    fw.mm(pg[:, :], wv(c, slots[dc], kc, b * 128, 128), c.xT[:, kc, tok], start=(kc == 0), stop=(kc == 7))
                ch = MG // 128 + b * 8 + dc
                si = b % 2
                sg = c.f_sig.s(si, (slice(None), si, slice(None)))
                fw.act(sg, pg[:, :], AF.Sigmoid, bias=c.bfm[:, l, ch:ch + 1])
                pp = psB(c)
                for kc in range(2):
                    fw.mm(pp[:, :], c.wbr_sb[:, b, kc, dc * 128:(dc + 1) * 128],
                          c.yT.s(b, (slice(None), b * 2 + kc, tok)), start=(kc == 0), stop=(kc == 1))
                if b == 0:
                    fw.tt(acc, pp[:, :], sg, ALU.mult)
                else:
                    tm = c.f_tmp.s(si, (slice(None), si, slice(None)))
                    fw.tt(tm, pp[:, :], sg, ALU.mult)
                    if b < 3:
                        fw.tt(acc, acc, tm, ALU.add, e="pool")
                    else:
                        fw.tt(c.mergedT[:, dc, tok], acc, tm, ALU.add, e="pool")
    def zslot(tt):
        return c.zt.s(tt % 3, (slice(None), tt % 3, slice(None)))

    fw.dma(zslot(0), V(c.xres[0:128, :], [c.tr_xres[0]]))

    def stage_a(tt):
        z = zslot(tt)
        if tt + 1 < 16:
            fw.dma(zslot(tt + 1), V(c.xres[(tt + 1) * 128:(tt + 2) * 128, :], [c.tr_xres[tt + 1]]))
        for hf in range(2):
            po = psA(c)
            for kc in range(8):
                fw.mm(po[:, :], c.mergedT[:, kc, tt * 128:(tt + 1) * 128], c.wout_sb[:, kc, hf * 512:(hf + 1) * 512],
                      start=(kc == 0), stop=False)
            fw.mm(po[:, :], c.ones_row[0:1, :], c.bout_row[0:1, hf * 512:(hf + 1) * 512], start=False, stop=True)
            zh = V(z.ap[:, hf * 512:(hf + 1) * 512], z.trs)
            fw.stt(zh, zh, ALPHA, po[:, :], ALU.mult, ALU.add)
        ln_stats(c, z, tt % 2)

    def stage_b1(tt):
        sl = tt % 2
        ln_apply(c, zslot(tt), 1, 2, c.ot.s(sl, (slice(None), sl, slice(None))), sl)

    def stage_b2(tt):
        sl = tt % 2
        o = c.ot.s(sl, (slice(None), sl, slice(None)))
        if l == 0:
            fw.dma(V(c.xres[tt * 128:(tt + 1) * 128, :], [c.tr_xres[tt]]), o)
            to_xT(c, o, tt)
        else:
            fw.dma(V(c.y[s, tt * 128:(tt + 1) * 128, :], [c.tr_y]), o)

    run_pipeline3(stage_a, stage_b1, stage_b2, 16)


def layer(c, s, l):
    fw = c.fw
    if "B" in c.parts:
        branch_b(c, s, l)
    for bi, name in enumerate("ABCD"):
        if name not in c.parts:
            fw.memset(c.yT.s(bi, (slice(None), slice(bi * 2, bi * 2 + 2), slice(None))), 0.0, e="pool")
    if "A" in c.parts:
        branch_a(c, s, l)
    if "C" in c.parts:
        branch_c(c, s, l)
    if "D" in c.parts:
        branch_d(c, s, l)
    if c.dbg and l == 0 and s == 0:
        fw.dma(V(c.dbg_out[:, :, :], [c.tr_dbg]), c.yT[:, :, :])
    final_stage(c, s, l)


def rope_proj(c, l, qcol, perm, cosT, sinT, dst, tmp, t2b, qbb):
    fw = c.fw

    def consume(i, tb, ps):
        tok = slice(tb * 512, (tb + 1) * 512)
        ch = qcol // 128 + i
        sl = (i * NTB + tb) % 2
        qb = qbb.s(sl, (slice(None), sl, slice(None)))
        fw.act(qb, ps[:, :], AF.Identity, bias=c.bfm[:, l, ch:ch + 1])
        t1 = t2b.s(sl, (slice(None), sl, slice(None)))
        fw.stt(t1, ps[:, :], c.bfm[:, l, ch:ch + 1], cosT[:, tok], ALU.add, ALU.mult)
        pr = psB(c)
        fw.mm(pr[:, :], perm, qb)
        t2 = V(tmp.h[:, sl, :], [tmp.trs[0]])
        fw.tt(t2, pr[:, :], sinT[:, tok], ALU.mult)
        fw.tt(dst[:, i, tok], t1, t2, ALU.add, e="pool")

    proj_fm(c, l, [qcol, qcol + 128], consume)


def plain_proj(c, l, col, dst):
    fw = c.fw

    def consume(i, tb, ps):
        ch = col // 128 + i
        fw.ts(dst[:, i, tb * 512:(tb + 1) * 512], ps[:, :], c.bfm[:, l, ch:ch + 1], None, ALU.add)

    proj_fm(c, l, [col, col + 128], consume)


def vaug_init(c, vaug):
    v5 = vaug.h.rearrange("p j (hp par) e -> p j hp par e", par=2)
    c.fw.memset(V(v5[:, :, :, 0, 64:128], vaug.trs), 1.0, e="pool")
    c.fw.memset(V(v5[:, :, :, 1, 0:64], vaug.trs), 1.0, e="pool")


def v_proj(c, l, vcol, vaug, vbias, tok_sel, ntiles=16):
    fw = c.fw
    slot = load_w(c, l, vcol, 256)
    fw.dma(vbias[:, :], V(c.b_cat[l:l + 1, vcol:vcol + 256].partition_broadcast(128), [c.tr_in]))
    v5 = vaug.h.rearrange("p j (hp par) e -> p j hp par e", par=2)
    b4 = vbias.h.rearrange("p (hp par e) -> p hp par e", hp=2, par=2)
    for j in range(ntiles):
        ps = psA(c)
        for kc in range(8):
            fw.mm(ps[:, 0:256], V(c.xT.h[:, kc, tok_sel(j)], c.xT.trs), wv(c, slot, kc, 0, 256),
                  start=(kc == 0), stop=(kc == 7))
        p4 = ps.h[:, 0:256].rearrange("p (hp par e) -> p hp par e", hp=2, par=2)
        fw.tt(V(v5[:, j, :, 0, 0:64], vaug.trs), V(p4[:, :, 0, :], ps.trs), V(b4[:, :, 0, :], vbias.trs), ALU.add)
        fw.tt(V(v5[:, j, :, 1, 64:128], vaug.trs), V(p4[:, :, 1, :], ps.trs), V(b4[:, :, 1, :], vbias.trs), ALU.add)


def branch_c(c, s, l):
    fw = c.fw
    arena_reset(c)
    lam_init = 0.8 - 0.6 * math.exp(-0.3 * l)
    cosT = alloc(c, [128, T], F32)
    sinT = alloc(c, [128, T], F32)
    fw.dma(cosT[:, :], V(c.rope[0], [c.tr_in]))
    fw.dma(sinT[:, :], V(c.rope[1], [c.tr_in]))
    qT = alloc(c, [128, 2, T], BF16)
    kT = alloc(c, [128, 2, T], BF16)
    tmp = alloc(c, [128, 4, 512], F32)
    t2b = alloc(c, [128, 2, 512], F32, nslots=2)
    vaug = alloc(c, [128, 16, 4, 128], BF16)
    vbias = alloc(c, [128, 256], F32)
    c.g_sig = alloc(c, [128, 2, 512], F32, nslots=2)
    pt = alloc(c, [128, 4, 512], BF16, nslots=4)
    sm = alloc(c, [128, 16], F32)
    lamt = alloc(c, [128, 128], F32)
    prm = alloc(c, [128, 64], F32)
    ofull = alloc(c, [128, 512], F32)
    osq = alloc(c, [128, 512], F32)
    w1 = alloc(c, [128, 2, 512], F32, nslots=2)
    w2 = alloc(c, [128, 2, 512], F32, nslots=2)
    blk = alloc(c, [128, 128], F32)
    permC = alloc(c, [128, 128], BF16)
    qbb = alloc(c, [128, 2, 512], BF16, nslots=2)
    fw.dma(permC[:, :], V(c.perms[0], [c.tr_in]))
    fw.dma(blk[:, :], V(c.blk1[:, :], [c.tr_in]))
    fw.dma(prm[:, :], V(c.rwp[l], [c.tr_in]))
    fw.dma(lamt[:, :], V(c.df_lam[l:l + 1, :].partition_broadcast(128), [c.tr_in]))
    fw.tt(lamt[:, 0:32], lamt[:, 0:32], lamt[:, 32:64], ALU.mult)
    fw.tt(lamt[:, 64:96], lamt[:, 64:96], lamt[:, 96:128], ALU.mult)
    fw.emit("dve", lambda E: E.tensor_reduce(out=sm.h[:, 0:1], in_=lamt.h[:, 0:32], axis=mybir.AxisListType.X,
                                             op=ALU.add), [lamt[:, :]], [sm[:, :]])
    fw.emit("dve", lambda E: E.tensor_reduce(out=sm.h[:, 1:2], in_=lamt.h[:, 64:96], axis=mybir.AxisListType.X,
                                             op=ALU.add), [lamt[:, :]], [sm[:, :]])
    fw.act(sm[:, 2:4], sm[:, 0:2], AF.Exp)
    fw.tt(sm[:, 4:5], sm[:, 3:4], sm[:, 2:3], ALU.subtract)
    fw.ts(sm[:, 5:6], sm[:, 4:5], -lam_init, None, ALU.add)
    fw.ts(sm[:, 6:7], prm[:, 0:1], 1.0 - lam_init, None, ALU.mult)

    vaug_init(c, vaug)
    rope_proj(c, l, C_Q, permC[:, :], cosT, sinT, qT, tmp, t2b, qbb)
    rope_proj(c, l, C_K, permC[:, :], cosT, sinT, kT, tmp, t2b, qbb)
    v_proj(c, l, C_V, vaug, vbias, lambda j: slice(j * 128, (j + 1) * 128))

    qz = [alloc(c, [128, 2, T], BF16), alloc(c, [128, 2, T], BF16)]
    for i in range(2):
        fw.ts(qz[i][:, :, :], qT[:, :, :], prm[:, 29 + i:30 + i], None, ALU.mult, e=("dve" if i == 0 else "pool"))
    scale = 32 ** -0.5
    items = [(hp, tb, i, kt, par) for hp in range(2) for tb in range(NTB) for i in range(2) for kt in range(16)
             for par in range(2)]
    pbuf = {}
    state = {}

    def stage1(n):
        hp, tb, i, kt, par = items[n]
        tok = slice(tb * 512, (tb + 1) * 512)
        pb = par * 64
        sc = psA(c)
        fw.mm(sc[:, :], kT[pb:pb + 64, hp, kt * 128:(kt + 1) * 128], qz[i][pb:pb + 64, hp, tok])
        p = pt.s(n % 4, (slice(None), n % 4, slice(None)))
        fw.act(p, sc[:, :], AF.Exp, scale=scale)
        pbuf[n] = p

    def stage2(n):
        hp, tb, i, kt, par = items[n]
        tok = slice(tb * 512, (tb + 1) * 512)
        h = hp * 2 + par
        if kt == 0:
            state[("acc", par)] = psB(c)
        acc = state[("acc", par)]
        fw.mm(acc[:, :], vaug[:, kt, h, :], pbuf.pop(n), start=(kt == 0), stop=(kt == 15))
        if kt == 15:
            olo, dlo = (0, 64) if par == 0 else (64, 0)
            o = slice(olo, olo + 64)
            d = slice(dlo, dlo + 64)
            r1 = w1.s(i, (o, i, slice(None)))
            fw.recip(r1, acc[d, :])
            t1 = w2.s(i, (o, i, slice(None)))
            fw.tt(t1, acc[o, :], r1, ALU.mult)
            if i == 1:
                t0 = w2.s(0, (o, 0, slice(None)))
                fw.stt(ofull[o, :], t1, sm[o, 5:6], t0, ALU.mult, ALU.add)
                if par == 1:
                    fw.tt(osq[:, :], ofull[:, :], ofull[:, :], ALU.mult)
                    ss = psB(c)
                    fw.mm(ss[:, :], blk[:, :], osq[:, :])
                    fw.ts(osq[:, :], ss[:, :], 1.0 / 64.0, 1e-5, ALU.mult, ALU.add)
                    fw.act(osq[:, :], osq[:, :], AF.Sqrt)
                    fw.recip(osq[:, :], osq[:, :])
                    fw.stt(c.yT.s(2, (slice(None), 4 + hp, tok)), ofull[:, :], sm[:, 6:7], osq[:, :], ALU.mult, ALU.mult)

    npair = len(items) // 2
    for m in range(npair + 1):
        if m < npair:
            stage1(2 * m)
            stage1(2 * m + 1)
        if m >= 1:
            stage2(2 * m - 2)
            stage2(2 * m - 1)
    gate_branch(c, l, C_G, 2)


def _host_inputs(inp):
    f32 = np.float32
    g = lambda k: np.asarray(inp[k], f32)
    w_in = g("w_in")
    b_in = g("b_in")
    perm = _rot_perm()
    w_rot = np.ascontiguousarray(w_in[:, :, perm])
    b_cat = np.ascontiguousarray(np.concatenate([b_in, b_in[:, perm]], axis=1))
    b_fm = np.ascontiguousarray(b_cat.reshape(2, NCH, 128).transpose(0, 2, 1))
    vecs = np.zeros((8, DM), f32)
    vecs[0], vecs[1] = g("ln0_g"), g("ln0_b")
    for l in range(2):
        vecs[2 + 3 * l], vecs[3 + 3 * l], vecs[4 + 3 * l] = g("b_out")[l], g("ln_g")[l], g("ln_b")[l]
    rope = np.stack(_rope_tables(), 0)
    i = np.arange(128)[:, None]
    j = np.arange(128)[None, :]
    band = np.stack([(i - j >= 64), (np.abs(i - j) <= 64), (j - i >= 64)], 1).astype(f32)
    bandm = np.ascontiguousarray(np.broadcast_to(band[:, :, None, :], (128, 3, 4, 128))).astype(ml_dtypes.bfloat16)
    rpb = g("na_rpb")
    kap = np.arange(2)[:, None, None, None, None]
    kc = np.arange(64)[None, :, None, None, None]
    oi = np.arange(14)[None, None, :, None, None]
    hh = np.asarray([0, 2, 1, 3])[None, None, None, :, None]
    qc = np.arange(64)[None, None, None, None, :]
    dr = (oi - 7) + kap
    dc = np.clip(kc - qc + 15, 0, 30)
    shp = (2, 64, 14, 4, 64)
    na_bias = np.stack([rpb[l][np.broadcast_to(hh, shp), np.broadcast_to(dr + 7, shp), np.broadcast_to(dc, shp)]
                        for l in range(2)], 0).reshape(2, 128, 14 * 256).astype(f32)
    cst = np.clip(qc - 8, 0, 48)
    ok = (kc >= cst) & (kc < cst + 16)
    na_mask = np.ascontiguousarray(np.broadcast_to(ok, shp)).reshape(128, 14 * 256).astype(f32)
    rwp = np.zeros((2, 128, 64), f32)
    p = np.arange(128)
    mu = g("rw_mu")
    for l in range(2):
        rwp[l, :, 0] = g("df_subln_g")[l][p % 64]
        for hp in range(2):
            ch = hp * 128 + p
            for q in range(4):
                rwp[l, :, 1 + 2 * q + hp] = mu[l, q * 256 + ch]
            for d in range(2):
                rwp[l, :, 11 + d * 2 + hp] = g("rw_w0")[l, d, ch]
                rwp[l, :, 15 + d * 2 + hp] = g("rw_a0")[l, d, ch]
            rwp[l, :, 19 + hp] = g("rw_kk")[l, ch]
            rwp[l, :, 21 + hp] = g("rw_ka")[l, ch]
            rwp[l, :, 23 + hp] = g("rw_rk")[l].reshape(256)[ch]
            rwp[l, :, 25 + hp] = g("rw_lnx_g")[l, ch]
            rwp[l, :, 27 + hp] = g("rw_lnx_b")[l, ch]
        rwp[l, :, 29] = ((p % 64) < 32)
        rwp[l, :, 30] = ((p % 64) >= 32)
        rwp[l, :, 9] = mu[l, 1024 + p]
        rwp[l, :, 10] = mu[l, 1152 + p]
    s_ = np.arange(64)[:, None]
    t_ = np.arange(64)[None, :]
    tri = np.zeros((64, 2, 2, 64), f32)
    tri[:, 0, 0], tri[:, 0, 1] = (s_ < t_), (s_ <= t_)
    tri[:, 1, 0], tri[:, 1, 1] = (s_ > t_), (s_ >= t_)
    blk1 = np.zeros((128, 128), f32)
    blk1[:64, :64] = 1
    blk1[64:, 64:] = 1
    perms = np.zeros((2, 128, 128), f32)
    pp = np.arange(128)
    for qi, dd in enumerate((32, 64)):
        jj = pp % dd
        partner = pp - jj + (jj + dd // 2) % dd
        perms[qi, partner, pp] = 1.0
    tri3 = np.zeros((64, 2, 192), f32)
    tri3[:, :, 0:128] = tri.reshape(64, 2, 128)
    tri3[:, 0, 128:192] = (s_ > t_)
    tri3[:, 1, 128:192] = (s_ < t_)
    return {
        "w_in": w_in, "w_rot": w_rot, "w_branch": g("w_branch"), "w_out": g("w_out"),
        "b_fm": b_fm, "b_cat": b_cat, "vecs": vecs,
        "ident_b": np.eye(128, dtype=f32).astype(ml_dtypes.bfloat16), "ident_f": np.eye(128, dtype=f32),
        "rope": np.ascontiguousarray(rope), "bandm": bandm, "na_bias": na_bias, "na_mask": na_mask,
        "rwp": rwp, "rw_w2": np.ascontiguousarray(g("rw_w2").reshape(2, 128, 256)),
        "rw_a2": np.ascontiguousarray(g("rw_a2").reshape(2, 128, 256)),
        "df_lam": np.ascontiguousarray(g("df_lam").reshape(2, 128)),
        "trimask": tri3, "blk1": blk1, "perms": perms.astype(ml_dtypes.bfloat16),
    }


_NC_CACHE = {}


def kernel(**inputs):
    xp = np.asarray(inputs["x_prompt"], np.float32)
    xs = np.asarray(inputs["x_sample"], np.float32)
    shared = _host_inputs(inputs)
    ncores = 8
    if "nc" not in _NC_CACHE:
        _NC_CACHE["nc"] = build(6)[0]
    nc = _NC_CACHE["nc"]
    in_maps = []
    for k in range(ncores):
        xk = np.concatenate([xp[4 * k:4 * k + 4], xs[2 * k:2 * k + 2]], axis=0)
        m = dict(shared)
        m["x"] = np.ascontiguousarray(xk)
        in_maps.append(m)
    res = run_bass_kernel_spmd(nc, in_maps, core_ids=list(range(ncores)))
    yp = np.empty_like(xp)
    ys = np.empty_like(xs)
    for k in range(ncores):
        yk = np.asarray(res.results[k]["y"], np.float32)
        yp[4 * k:4 * k + 4] = yk[0:4]
        ys[2 * k:2 * k + 2] = yk[4:6]
    return (yp, ys)


def branch_a(c, s, l):
    fw = c.fw
    arena_reset(c)
    qT = alloc(c, [128, 2, T], BF16)
    kT = alloc(c, [128, 2, T], BF16)
    vaug0 = alloc(c, [128, 16, 4, 128], BF16)
    vaug1 = alloc(c, [128, 15, 4, 128], BF16)
    vbias = alloc(c, [128, 256], F32)
    stg = alloc(c, [128, 14 * 256], F32)
    msk = alloc(c, [128, 14 * 256], F32)
    Mb = alloc(c, [128, 14, 256], BF16)
    pt = alloc(c, [128, 2, 4, 256], BF16, nslots=2)
    rd = alloc(c, [128, 2, 256], F32, nslots=2)
    c.g_sig = alloc(c, [128, 2, 512], F32, nslots=2)
    fw.dma(stg[:, :], V(c.na_bias[l], [c.tr_in]))
    fw.dma(msk[:, :], V(c.na_mask[:, :], [c.tr_in]))
    fw.act(stg[:, :], stg[:, :], AF.Exp)
    fw.tt(V(Mb.h.rearrange("p a b -> p (a b)"), Mb.trs), stg[:, :], msk[:, :], ALU.mult)
    vaug_init(c, vaug0)
    vaug_init(c, vaug1)
    plain_proj(c, l, A_Q, qT)
    plain_proj(c, l, A_K, kT)
    v_proj(c, l, A_V, vaug0, vbias, lambda j: slice(j * 128, (j + 1) * 128), 16)
    v_proj(c, l, A_V, vaug1, vbias, lambda j: slice(64 + j * 128, 64 + (j + 1) * 128), 15)
    rows = {}

    def stage1(r):
        rs = min(max(r - 4, 0), 24)
        sl = r % 2
        qs = slice(64 * r, 64 * r + 64)
        tiles = []
        for j in range(4):
            kr0 = rs + 2 * j
            oi = (rs - r + 2 * j) + 7
            p = pt.s(sl, (slice(None), sl, j, slice(None)))
            scs = [psA(c), psA(c)]
            for hp in range(2):
                for par in range(2):
                    pb = par * 64
                    fw.mm(scs[par][:, hp * 64:(hp + 1) * 64], kT[pb:pb + 64, hp, 64 * kr0:64 * kr0 + 128],
                          qT[pb:pb + 64, hp, qs])
            for par in range(2):
                fw.act(V(p.ap[:, par * 128:(par + 1) * 128], p.trs), scs[par][:, 0:128], AF.Exp, scale=0.125)
            fw.tt(p, p, Mb[:, oi, :], ALU.mult, e=("pool" if j % 2 == 0 else "dve"))
            tiles.append((p, (vaug0, kr0 // 2) if kr0 % 2 == 0 else (vaug1, (kr0 - 1) // 2)))
        rows[r] = tiles

    def stage2(r):
        tiles = rows.pop(r)
        sl = r % 2
        qs = slice(64 * r, 64 * r + 64)
        acc = psB(c)
        for h in range(4):
            for j, (p, (va, ti)) in enumerate(tiles):
                pc = (h % 2) * 128 + (h // 2) * 64
                fw.mm(acc[:, h * 64:(h + 1) * 64], va[:, ti, h, :], V(p.ap[:, pc:pc + 64], p.trs),
                      start=(j == 0), stop=(j == 3))
        a4 = acc.h[:, 0:256].rearrange("p (hp par q) -> p hp par q", hp=2, par=2)
        r4 = rd.h[:, sl, :].rearrange("p (hp par q) -> p hp par q", hp=2, par=2)
        rtr = [rd.trs[sl]]
        fw.recip(V(r4[0:64, :, 0, :], rtr), V(a4[64:128, :, 0, :], acc.trs))
        fw.recip(V(r4[64:128, :, 1, :], rtr), V(a4[0:64, :, 1, :], acc.trs))
        fw.tt(c.yT.s(0, (slice(0, 64), slice(0, 2), qs)), V(a4[0:64, :, 0, :], acc.trs), V(r4[0:64, :, 0, :], rtr), ALU.mult)
        fw.tt(c.yT.s(0, (slice(64, 128), slice(0, 2), qs)), V(a4[64:128, :, 1, :], acc.trs), V(r4[64:128, :, 1, :], rtr),
              ALU.mult)

    stage1(0)
    for r in range(32):
        if r + 1 < 32:
            stage1(r + 1)
        stage2(r)
    gate_branch(c, l, A_G, 0)


def branch_d(c, s, l):
    fw = c.fw
    arena_reset(c)
    cosT = alloc(c, [128, T], F32)
    sinT = alloc(c, [128, T], F32)
    fw.dma(cosT[:, :], V(c.rope[2], [c.tr_in]))
    fw.dma(sinT[:, :], V(c.rope[3], [c.tr_in]))
    acc = alloc(c, [128, 4, T], F32)
    qT = alloc(c, [128, 2, T], BF16)
    kT = alloc(c, [128, 2, T], BF16)
    tmp = alloc(c, [128, 4, 512], F32)
    t2b = alloc(c, [128, 2, 512], F32, nslots=2)
    c.g_sig = t2b
    vaug = alloc(c, [128, 16, 4, 128], BF16)
    vbias = alloc(c, [128, 256], F32)
    pt = alloc(c, [128, 2, 3, 512], BF16, nslots=2)
    bm = alloc(c, [128, 3, 512], BF16)
    permD = alloc(c, [128, 128], BF16)
    qbb = alloc(c, [128, 2, 512], BF16, nslots=2)
    fw.dma(permD[:, :], V(c.perms[1], [c.tr_in]))
    fw.dma(bm[:, :, :], V(c.bandm.rearrange("p a h q -> p a (h q)"), [c.tr_in]))
    vaug_init(c, vaug)
    unit = 0
    for g, dil in enumerate((1, 4, 16)):
        nqb = T // dil // 128
        base = D_BASE + g * 768
        rope_proj(c, l, base, permD[:, :], cosT, sinT, qT, tmp, t2b, qbb)
        rope_proj(c, l, base + 256, permD[:, :], cosT, sinT, kT, tmp, t2b, qbb)

        def tsel(j, dil=dil, nqb=nqb):
            rho, jb = divmod(j, nqb)
            st = 128 * jb * dil + rho
            return slice(st, st + 127 * dil + 1, dil)

        v_proj(c, l, base + 512, vaug, vbias, tsel, 16)
        units = [(rho, qb) for rho in range(dil) for qb in range(nqb)]
        ust = {}

        def stage1(ui, units=units, tsel=tsel, nqb=nqb, ust=ust):
            rho, qb = units[ui]
            qs = tsel(rho * nqb + qb)
            kts = [(jb, mt) for (jb, mt) in ((qb - 1, 0), (qb, 1), (qb + 1, 2)) if 0 <= jb < nqb]
            sl = (unit0 + ui) % 2
            ps_ = []
            for idx, (jb, mt) in enumerate(kts):
                ks = tsel(rho * nqb + jb)
                p = pt.s(sl, (slice(None), sl, idx, slice(None)))
                scs = [psA(c), psA(c)]
                for hp in range(2):
                    for par in range(2):
                        pb = par * 64
                        fw.mm(scs[par][:, hp * 128:(hp + 1) * 128], V(kT.h[pb:pb + 64, hp, ks], kT.trs),
                              V(qT.h[pb:pb + 64, hp, qs], qT.trs))
                for par in range(2):
                    fw.act(V(p.ap[:, par * 256:(par + 1) * 256], p.trs), scs[par][:, 0:256], AF.Exp, scale=0.125)
                fw.tt(p, p, bm[:, mt, :], ALU.mult, e=("pool" if idx % 2 == 0 else "dve"))
                ps_.append(p)
            ust[ui] = (qs, kts, ps_)

        def stage2(ui, units=units, nqb=nqb, ust=ust, g=g):
            rho, qb = units[ui]
            qs, kts, ps_ = ust.pop(ui)
            ob = psB(c)
            for h in range(4):
                for idx, (jb, mt) in enumerate(kts):
                    pc = (h % 2) * 256 + (h // 2) * 128
                    fw.mm(ob[:, h * 128:(h + 1) * 128], vaug[:, rho * nqb + jb, h, :],
                          V(ps_[idx].ap[:, pc:pc + 128], ps_[idx].trs),
                          start=(idx == 0), stop=(idx == len(kts) - 1))
            dst = V(acc.h[:, :, qs], acc.trs)
            src = V(ob.h[:, :].rearrange("p (h q) -> p h q", h=4), ob.trs)
            if g == 0:
                fw.copy(dst, src, e="act")
            else:
                fw.tt(dst, src, dst, ALU.add)

        unit0 = unit
        stage1(0)
        for ui in range(len(units)):
            if ui + 1 < len(units):
                stage1(ui + 1)
            stage2(ui)
        unit += len(units)
    a5 = acc.h.rearrange("p (hp par) t -> p hp par t", par=2)
    t5 = tmp.h.rearrange("p (hp par) t -> p hp par t", par=2)
    for tb in range(NTB):
        tok = slice(tb * 512, (tb + 1) * 512)
        fw.recip(V(t5[0:64, :, 0, :], tmp.trs), V(a5[64:128, :, 0, tok], acc.trs))
        fw.recip(V(t5[64:128, :, 1, :], tmp.trs), V(a5[0:64, :, 1, tok], acc.trs))
        fw.tt(c.yT.s(3, (slice(0, 64), slice(6, 8), tok)), V(a5[0:64, :, 0, tok], acc.trs), V(t5[0:64, :, 0, :], tmp.trs),
              ALU.mult)
        fw.tt(c.yT.s(3, (slice(64, 128), slice(6, 8), tok)), V(a5[64:128, :, 1, tok], acc.trs),
              V(t5[64:128, :, 1, :], tmp.trs), ALU.mult)
    gate_branch(c, l, D_G, 3)


USE_NTI = os.environ.get("B_NTI", "1") == "1"
GC = int(os.environ.get("B_GC", "4"))


def yslot_f32(c, slot):
    ap = c.yT.h[:, 2 * slot:2 * slot + 2, :].rearrange("p a t -> p (a t)").bitcast(F32)
    return Buf(ap, 1), c.yT.trs[slot]


def branch_b(c, s, l):
    for hp in range(2):
        rwkv_hp(c, l, hp)


def rwkv_hp(c, l, hp):
    fw = c.fw
    arena_reset(c)
    CD = -math.exp(-0.5)
    prm = alloc(c, [128, 64], F32)
    blk = alloc(c, [128, 128], F32)
    wst = alloc(c, [128, 2, 256], F32)
    w2sb = alloc(c, [128, 256], BF16)
    a2sb = alloc(c, [128, 256], BF16)
    der = alloc(c, [128, 16], F32)
    ones = alloc(c, [128, 64], F32)
    fw.dma(prm[:, :], V(c.rwp[l], [c.tr_in]))
    fw.dma(blk[:, :], V(c.blk1[:, :], [c.tr_in]))
    fw.dma(wst[:, 0, :], V(c.rw_w2[l], [c.tr_in]))
    fw.dma(wst[:, 1, :], V(c.rw_a2[l], [c.tr_in]))
    fw.copy(w2sb[:, :], wst[:, 0, :])
    fw.copy(a2sb[:, :], wst[:, 1, :])
    fw.memset(ones[:, :], 1.0)
    mucols = [1 + hp, 3 + hp, 5 + hp, 7 + hp, 9, 10]
    for i, mc in enumerate(mucols):
        fw.ts(der[:, i:i + 1], prm[:, mc:mc + 1], -1.0, 1.0, ALU.mult, ALU.add)
        fw.ts(der[:, 6 + i:7 + i], prm[:, mc:mc + 1], 0.5, None, ALU.mult)
    fw.ts(der[:, 12:13], prm[:, 21 + hp:22 + hp], -1.0, 1.0, ALU.mult, ALU.add)
    vT = alloc(c, [128, T], BF16)
    gs = alloc(c, [128, T], BF16)
    bv = alloc(c, [128, T], BF16)
    AR = [alloc(c, [128, 2, T], BF16) for _ in range(2)]
    BK = [alloc(c, [128, 2, T], BF16) for _ in range(2)]
    RLO = [alloc(c, [64, T], BF16) for _ in range(2)]
    eL = alloc(c, [128, 2, 32], F32)
    mark = c.arena_off

    uT = alloc(c, [128, 6, T + 2], BF16)
    fw.memset(uT[:, :, 0:1], 0.0)
    fw.memset(uT[:, :, T + 1:T + 2], 0.0)
    cols = [B_R + hp * 128, B_K + hp * 128, B_V + hp * 128, B_GT + hp * 128, B_WL, B_AL]

    def consume(i, tb, ps):
        ch = cols[i] // 128
        fw.act(uT[:, i, 1 + tb * 512:1 + (tb + 1) * 512], ps[:, :], AF.Identity, bias=c.bfm[:, l, ch:ch + 1])

    proj_fm(c, l, cols, consume)
    tbuf = [alloc(c, [128, 512], F32)[:, :] for _ in range(11)]
    slot_tmps = []
    for slot_ in (0, 2, 3):
        ex_ap = c.yT.h[:, 2 * slot_:2 * slot_ + 2, :].rearrange("p a t -> p (a t)").bitcast(F32)
        for i in range(4):
            t_ = Tr()
            t_.r = dict(c.yT.trs[slot_].r)
            t_.w = c.yT.trs[slot_].w
            slot_tmps.append((slot_, t_))
            tbuf.append(V(ex_ap[:, i * 512:(i + 1) * 512], [t_]))
    xr, xk, kk, tA, tB = tbuf[0:5]
    dtmp = [tbuf[5:14], tbuf[14:23]]
    rmask = alloc(c, [128, 512], F32)
    fw.memset(rmask[:, :], 1.0)
    fw.memset(V(rmask.h[:, 0:512:64], rmask.trs), 0.0)
    twl = alloc(c, [128, 512], BF16)
    tal = alloc(c, [128, 512], BF16)
    for tb in range(NTB):
        tok = slice(tb * 512, (tb + 1) * 512)

        def shift(i, out):
            fw.tt(tA, uT[:, i, tb * 512:tb * 512 + 512], uT[:, i, tb * 512 + 2:tb * 512 + 514], ALU.add, e="pool")
            fw.ts(tA, tA, der[:, 6 + i:7 + i], None, ALU.mult)
            fw.stt(out, uT[:, i, tb * 512 + 1:tb * 512 + 513], der[:, i:i + 1], tA, ALU.mult, ALU.add)

        shift(0, xr)
        shift(1, xk)
        shift(2, tB)
        fw.copy(vT[:, tok], tB, e="act")
        shift(3, tB)
        fw.act(kk, tB, AF.Sigmoid)
        fw.tt(gs[:, tok], tB, kk, ALU.mult, e="pool")
        shift(4, tB)
        fw.act(twl[:, :], tB, AF.Tanh)
        shift(5, tB)
        fw.copy(tal[:, :], tB, e="act")
        fw.ts(kk, xk, prm[:, 19 + hp:20 + hp], None, ALU.mult)
        fw.tt(tB, kk, kk, ALU.mult, e="pool")
        ps = psA(c)
        fw.mm(ps[:, :], blk[:, :], tB)
        fw.ts(tB, ps[:, :], 1e-24, None, ALU.max)
        fw.act(tB, tB, AF.Sqrt)
        fw.recip(tB, tB)
        fw.tt(kk, kk, tB, ALU.mult)

        def dir_chain(d):
            lw, L0, D_, eP, eM, eX, a_, kd, tD = dtmp[d]
            dsl = slice(d * 64, (d + 1) * 64)
            ps = psA(c)
            fw.mm(ps[:, :], w2sb[dsl, hp * 128:(hp + 1) * 128], twl[dsl, :])
            yield
            fw.act(lw, ps[:, :], AF.Sigmoid, bias=prm[:, 11 + d * 2 + hp:12 + d * 2 + hp])
            yield
            ps2 = psA(c)
            fw.mm(ps2[:, :], a2sb[dsl, hp * 128:(hp + 1) * 128], tal[dsl, :])
            fw.emit("dve", lambda E: E.tensor_tensor_scan(out=L0.ap, data0=rmask.h[:, :], data1=lw.ap,
                                                        initial=0.0, op0=ALU.mult, op1=ALU.add),
                    [rmask[:, :], lw], [L0])
            yield
            fw.act(a_, ps2[:, :], AF.Sigmoid, bias=prm[:, 15 + d * 2 + hp:16 + d * 2 + hp])
            yield
            ltot = V(L0.ap[:, 63:512:64], L0.trs)
            fw.act(eL[:, d, tb * 8:(tb + 1) * 8], ltot, AF.Exp, scale=CD)
            if d == 0:
                fw.tt(D_, L0, lw, ALU.subtract, e="pool")
                yield
                fw.act(eP, L0, AF.Exp, scale=CD)
                yield
                fw.act(eM, L0, AF.Exp, scale=-CD)
                yield
                fw.act(eX, D_, AF.Exp, scale=CD)
                yield
            else:
                l3 = V(L0.ap.rearrange("p (c t) -> p c t", t=64), L0.trs)
                lt3 = V(L0.ap[:, 63:512:64].unsqueeze(2).to_broadcast([128, 8, 64]), L0.trs)
                fw.tt(V(D_.ap.rearrange("p (c t) -> p c t", t=64), D_.trs), l3, lt3, ALU.subtract)
                yield
                fw.tt(lw, D_, lw, ALU.subtract, e="pool")
                yield
                fw.act(eX, D_, AF.Exp, scale=-CD)
                yield
                fw.act(eP, lw, AF.Exp, scale=-CD)
                yield
                fw.act(eM, lw, AF.Exp, scale=CD)
                yield
            fw.ts(tD, a_, prm[:, 21 + hp:22 + hp], der[:, 12:13], ALU.mult, ALU.add)
            yield
            fw.tt(kd, tD, xk, ALU.mult)
            yield
            fw.tt(tD, kk, a_, ALU.mult, e="pool")
            yield
            fw.tt(AR[d][:, 0, tok], kk, eX, ALU.mult)
            yield
            fw.tt(BK[d][:, 0, tok], tD, eM, ALU.mult)
            yield
            fw.tt(BK[d][:, 1, tok], kd, eM, ALU.mult, e="pool")
            yield
            fw.tt(AR[d][:, 1, tok], xr, eP, ALU.mult)
            yield
            fw.tt(RLO[d][0:64, tok], V(xr.ap[64:128, :], xr.trs), V(eP.ap[64:128, :], eP.trs), ALU.mult)
            yield

        gens = [dir_chain(0), dir_chain(1)]
        while gens:
            for g_ in list(gens):
                try:
                    next(g_)
                except StopIteration:
                    gens.remove(g_)
        fw.tt(tA, dtmp[0][7], dtmp[1][7], ALU.add, e="pool")
        fw.stt(tB, xr, prm[:, 23 + hp:24 + hp], tA, ALU.mult, ALU.mult)
        ps = psA(c)
        fw.mm(ps[:, :], blk[:, :], tB)
        fw.tt(bv[:, tok], ps[:, :], vT[:, tok], ALU.mult)

    arena_release(c, mark)
    for slot_, t_ in slot_tmps:
        dst = c.yT.trs[slot_]
        evs = list(t_.r.values()) + ([t_.w] if t_.w is not None else [])
        for ev in evs:
            old_ = dst.r.get(ev[2])
            if old_ is None or (ev[0], ev[1]) > (old_[0], old_[1]):
                dst.r[ev[2]] = ev
    ys = []
    for d in range(2):
        b_, tr_ = yslot_f32(c, (0, 2)[d])
        b_.trs = [tr_]
        ys.append(b_)
    eLs = alloc(c, [64, 2, 2, 32], F32)
    for d in range(2):
        for hl in range(2):
            fw.copy(eLs[:, d, hl, :], eL[hl * 64:(hl + 1) * 64, d, :], e="act")
    tri = alloc(c, [64, 2, 192], F32)
    fw.dma(tri[:, :, :], V(c.trimask[:, :, :], [c.tr_in]))
    NM = GC * 4
    tmT2 = [alloc(c, [64, GC, 2, 4, 128], BF16) for _ in range(2)]
    Am2 = [alloc(c, [64, NM, 256], BF16) for _ in range(2)]
    TT2 = [alloc(c, [64, NM, 64], BF16) for _ in range(2)]
    PT2 = [alloc(c, [64, NM, 64], BF16) for _ in range(2)]
    Zb2 = [alloc(c, [64, NM, 64], BF16) for _ in range(2)]
    Nb = [alloc(c, [64, NM, 64], BF16) for _ in range(2)]
    NTb = [alloc(c, [64, NM, 64], BF16) for _ in range(2)]
    NTI = alloc(c, [64, NM, 64], BF16)
    Rb = [alloc(c, [64, NM, 64], BF16) for _ in range(2)]
    Sb = alloc(c, [64, 2, 4, 64], BF16, nslots=2)
    Ub = alloc(c, [64, 4, 64], BF16)
    fw.memset(Sb[:, :, :, :], 0.0)
    I64b = c.identb[0:64, 0:64]
    nsteps = T // 64
    ngroups = nsteps // GC
    halves = [(0, NM // 2), (NM // 2, NM)]
    idb = V(c.identf.h[0:64, 0:64].unsqueeze(1).to_broadcast([64, NM // 2, 64]), c.identf.trs)

    def chunk_of(g, ci, d):
        it = g * GC + ci
        return it if d == 0 else nsteps - 1 - it

    def flat(buf, m0, m1):
        return V(buf.h[:, m0:m1, :].rearrange("p m e -> p (m e)"), buf.trs)

    def phase1(g):
        tmT, Am, TT, PT, Zb = tmT2[g % 2], Am2[g % 2], TT2[g % 2], PT2[g % 2], Zb2[g % 2]
        for ci in range(GC):
            for d in range(2):
                ch = chunk_of(g, ci, d)
                cs = slice(ch * 64, ch * 64 + 64)
                pb_ = psB(c)
                pbv = pb_.h[:, :].bitcast(BF16)
                srcs = [AR[d][:, 0, cs], BK[d][:, 0, cs], BK[d][:, 1, cs], vT[:, cs]]
                for q, src in enumerate(srcs):
                    fw.transpose(V(pbv[0:64, q * 128:(q + 1) * 128], pb_.trs), src, c.identb[:, :])
                fw.copy(V(tmT.h[:, ci, d, :, :].rearrange("p q e -> p (q e)"), tmT.trs), V(pbv[0:64, 0:512], pb_.trs),
                        e=("act" if (d == 0 or os.environ.get("B_TMT_ACT", "0") == "1") else "dve"))
            yield
        bN = [psB(c), psB(c)]
        for ci in range(GC):
            bA = [psA(c), psA(c)]
            hl_outer = os.environ.get("B_HLOUTER", "1") == "1"
            order = [(d, hl) for hl in range(2) for d in range(2)] if hl_outer else [(d, hl) for d in range(2) for hl in range(2)]
            for (d, hl) in order:
                ch = chunk_of(g, ci, d)
                cs = slice(ch * 64, ch * 64 + 64)
                pb = hl * 64
                for q in range(2):
                    o0 = d * 256 + q * 128
                    fw.mm(bA[hl][0:64, o0:o0 + 128], BK[d][pb:pb + 64, q, cs], V(AR[d].h[pb:pb + 64, :, cs], AR[d].trs))
                o1 = (ci * 2 + d) * 64
                fw.mm(bN[hl][0:64, o1:o1 + 64], AR[d][pb:pb + 64, 0, cs], BK[d][pb:pb + 64, 0, cs])
            for hl in range(2):
                m0 = ci * 4 + hl
                dst = V(Am.h[:, m0:m0 + 3:2, :].rearrange("p d (q e) -> p d q e", q=2), Am.trs)
                src = V(bA[hl].h[0:64, :].rearrange("p (d q e) -> p d q e", d=2, q=2), bA[hl].trs)
                msk = V(tri.h[:, :, 0:128].unsqueeze(2).to_broadcast([64, 2, 2, 128]), tri.trs)
                fw.tt(dst, src, msk, ALU.mult)
            yield
        for hl in range(2):
            dst = V(NTb[0].h[:, hl:NM:2, :].rearrange("p (ci d) e -> p ci d e", d=2), NTb[0].trs)
            src = V(bN[hl].h[0:64, :].rearrange("p (ci d e) -> p ci d e", ci=GC, d=2), bN[hl].trs)
            msk = V(tri.h[:, :, 128:192].unsqueeze(1).to_broadcast([64, GC, 2, 64]), tri.trs)
            fw.tt(dst, src, msk, ALU.mult)
        for (m0, m1) in halves:
            fw.tt(Rb[0][:, m0:m1, :], idb, Am[:, m0:m1, 0:64], ALU.subtract)
        yield
        Ncur = V(Am.h[:, :, 0:64], Am.trs)
        NTcur = NTb[0][:, :, :]
        rcur = 0
        nti = 0
        for lev in range(1, 6):
            ntn = 1 - nti
            nn = lev % 2
            pzs = []
            for (m0, m1) in halves:
                pz = psA(c)
                for m in range(m0, m1):
                    fw.mm(pz[0:64, (m - m0) * 64:(m - m0 + 1) * 64], V(Ncur.ap[:, m, :], Ncur.trs), V(NTcur.ap[:, m, :], NTcur.trs))
                pzs.append(pz)
            for hi, (m0, m1) in enumerate(halves):
                pz = pzs[hi]
                src3 = V(pz.h[0:64, 0:(m1 - m0) * 64].rearrange("p (m e) -> p m e", e=64), pz.trs)
                if USE_NTI:
                    fw.tt(NTI[:, m0:m1, :], src3, idb, ALU.add)
                if lev < 5 or not USE_NTI:
                    fw.copy(flat(NTb[ntn], m0, m1), pz[0:64, 0:(m1 - m0) * 64], e="act")
            yield
            if lev < 5:
                for (m0, m1) in halves:
                    pz = psA(c)
                    for m in range(m0, m1):
                        fw.mm(pz[0:64, (m - m0) * 64:(m - m0 + 1) * 64], V(NTcur.ap[:, m, :], NTcur.trs), V(Ncur.ap[:, m, :], Ncur.trs))
                    fw.copy(flat(Nb[nn], m0, m1), pz[0:64, 0:(m1 - m0) * 64], e="act")
                yield
            rn = 1 - rcur
            rdst = TT if lev == 5 else Rb[rn]
            for (m0, m1) in halves:
                pz = psB(c)
                for m in range(m0, m1):
                    o = pz[0:64, (m - m0) * 64:(m - m0 + 1) * 64]
                    if USE_NTI:
                        fw.mm(o, NTI[:, m, :], Rb[rcur][:, m, :])
                    else:
                        fw.mm(o, I64b, Rb[rcur][:, m, :], start=True, stop=False)
                        fw.mm(o, NTb[ntn][:, m, :], Rb[rcur][:, m, :], start=False, stop=True)
                fw.copy(flat(rdst, m0, m1), pz[0:64, 0:(m1 - m0) * 64])
            yield
            rcur = rn
            if lev < 5:
                Ncur = Nb[nn][:, :, :]
                NTcur = NTb[ntn][:, :, :]
                nti = ntn
        for (m0, m1) in halves:
            pz = psA(c)
            pq = psB(c)
            for m in range(m0, m1):
                ci, d, hl = m // 4, (m // 2) % 2, m % 2
                fw.mm(pz[0:64, (m - m0) * 64:(m - m0 + 1) * 64], tmT[:, ci, d, 0, hl * 64:(hl + 1) * 64], TT[:, m, :])
                fw.mm(pq[0:64, (m - m0) * 64:(m - m0 + 1) * 64], Am[:, m, 128:192], tmT[:, ci, d, 3, hl * 64:(hl + 1) * 64])
            fw.copy(flat(PT, m0, m1), pz[0:64, 0:(m1 - m0) * 64], e="act")
            fw.copy(flat(Zb, m0, m1), pq[0:64, 0:(m1 - m0) * 64])
            yield

    def drain(gen, n):
        if gen is None:
            return None
        for _ in range(n):
            try:
                next(gen)
            except StopIteration:
                return None
        return gen

    gen = phase1(0)
    drain(gen, 1000)
    slot = 0
    NY = 2 * GC + 2 + 14 + 2
    per_pt = -(-NY // (2 * GC))
    for g in range(ngroups):
        tmT, Am, TT, PT, Zb = tmT2[g % 2], Am2[g % 2], TT2[g % 2], PT2[g % 2], Zb2[g % 2]
        gen = phase1(g + 1) if g + 1 < ngroups else None
        if os.environ.get("B_NOPIPE", "0") == "1":
            gen = drain(gen, 1000)
        for ci in range(GC):
            scur = Sb.s(slot, (slice(None), slot, slice(None), slice(None)))
            snew = Sb.s(1 - slot, (slice(None), 1 - slot, slice(None), slice(None)))
            pu = psB(c)
            for inst in range(4):
                m = ci * 4 + inst
                o = pu[0:64, inst * 64:(inst + 1) * 64]
                fw.mm(o, PT[:, m, :], V(scur.ap[:, inst, :], scur.trs), start=True, stop=False)
                fw.mm(o, TT[:, m, :], Zb[:, m, :], start=False, stop=True)
            fw.act(V(Ub.h.rearrange("p i e -> p (i e)"), Ub.trs), pu[0:64, 0:256], AF.Copy, scale=-1.0)
            gen = drain(gen, per_pt)
            pS = psA(c)
            pY = psB(c)
            for inst in range(4):
                m = ci * 4 + inst
                d, hl = inst // 2, inst % 2
                ch = chunk_of(g, ci, d)
                cs = slice(ch * 64, ch * 64 + 64)
                hs = slice(hl * 64, (hl + 1) * 64)
                o = pS[0:64, inst * 64:(inst + 1) * 64]
                fw.mm(o, I64b, V(scur.ap[:, inst, :], scur.trs), start=True, stop=False)
                fw.mm(o, tmT[:, ci, d, 2, hs], tmT[:, ci, d, 3, hs], start=False, stop=False)
                fw.mm(o, tmT[:, ci, d, 1, hs], Ub[:, inst, :], start=False, stop=True)
            for d in range(2):
                ch = chunk_of(g, ci, d)
                esc = V(eLs.h[:, d, :, ch:ch + 1].to_broadcast([64, 2, 64]), eLs.trs)
                fw.tt(V(snew.ap[:, 2 * d:2 * d + 2, :], snew.trs),
                      V(pS.h[0:64, d * 128:(d + 1) * 128].rearrange("p (i e) -> p i e", i=2), pS.trs), esc, ALU.mult)
            for inst in range(4):
                m = ci * 4 + inst
                d, hl = inst // 2, inst % 2
                ch = chunk_of(g, ci, d)
                cs = slice(ch * 64, ch * 64 + 64)
                hs = slice(hl * 64, (hl + 1) * 64)
                oy = pY[0:64, inst * 64:(inst + 1) * 64]
                rt = AR[d][0:64, 1, cs] if hl == 0 else RLO[d][0:64, cs]
                fw.mm(oy, V(scur.ap[:, inst, :], scur.trs), rt, start=True, stop=False)
                fw.mm(oy, tmT[:, ci, d, 3, hs], Am[:, m, 192:256], start=False, stop=False)
                fw.mm(oy, Ub[:, inst, :], Am[:, m, 64:128], start=False, stop=True)
            for d in range(2):
                ch = chunk_of(g, ci, d)
                cs = slice(ch * 64, ch * 64 + 64)
                for hl in range(2):
                    inst = d * 2 + hl
                    fw.copy(ys[d][hl * 64:(hl + 1) * 64, cs], pY[0:64, inst * 64:(inst + 1) * 64], e="act")
            gen = drain(gen, per_pt)
            slot = 1 - slot
        drain(gen, 1000)

    arena_release(c, mark)
    pt_ = [alloc(c, [128, 512], F32)[:, :] for _ in range(6)]
    y_, sq, mean, var, t1, t2 = pt_
    for tb in range(NTB):
        tok = slice(tb * 512, (tb + 1) * 512)
        fw.tt(y_, ys[0][:, tok], ys[1][:, tok], ALU.add)
        fw.tt(sq, y_, y_, ALU.mult, e="pool")
        p1 = psA(c)
        fw.mm(p1[:, :], blk[:, :], y_)
        p2 = psA(c)
        fw.mm(p2[:, :], blk[:, :], sq)
        fw.ts(mean, p1[:, :], 1.0 / 64.0, None, ALU.mult)
        fw.tt(t1, mean, mean, ALU.mult, e="pool")
        fw.stt(var, p2[:, :], 1.0 / 64.0, t1, ALU.mult, ALU.subtract)
        fw.ts(var, var, 64e-5, None, ALU.add)
        fw.act(var, var, AF.Sqrt)
        fw.recip(var, var)
        fw.tt(t2, y_, mean, ALU.subtract, e="pool")
        fw.tt(t2, t2, var, ALU.mult)
        fw.ts(t2, t2, prm[:, 25 + hp:26 + hp], prm[:, 27 + hp:28 + hp], ALU.mult, ALU.add)
        fw.tt(t2, t2, bv[:, tok], ALU.add, e="pool")
        fw.tt(c.yT.s(1, (slice(None), 2 + hp, tok)), t2, gs[:, tok], ALU.mult)
```

```python
import math
import os
import numpy as np
import ml_dtypes
import concourse.bass as bass
import concourse.mybir as mybir
from concourse.bass_utils import run_bass_kernel_spmd

F32 = mybir.dt.float32
BF16 = mybir.dt.bfloat16
AF = mybir.ActivationFunctionType
ALU = mybir.AluOpType

T = 2048
DM = 1024
NTB = 4
IN_COLS = 9984
ROT_COLS = 2048
WCOLS = IN_COLS + ROT_COLS
ALPHA = (2 * 2) ** 0.25
LN_EPS = 1e-5
ARENA_F32 = 30208


class Tr:
    __slots__ = ("w", "r", "excl")

    def __init__(self, excl=False):
        self.w = None
        self.r = {}
        self.excl = excl


class V:
    __slots__ = ("ap", "trs")

    def __init__(self, ap, trs):
        self.ap = ap
        self.trs = trs


class Buf:
    def __init__(self, h, nslots=1):
        self.h = h
        self.trs = [Tr() for _ in range(nslots)]

    def __getitem__(self, idx):
        return V(self.h[idx], self.trs)

    def s(self, slot, idx):
        return V(self.h[idx], [self.trs[slot]])


class FW:
    LIMIT = 30000

    def __init__(self, nc, ndma=24):
        self.nc = nc
        self.eng = {"pe": nc.tensor, "act": nc.scalar, "dve": nc.vector, "pool": nc.gpsimd, "sp": nc.sync}
        self.sems = []
        self.cur = {}
        self.cnt = {}
        for e in self.eng:
            self.cur[e] = self._newsem("e_" + e)
            self.cnt[e] = 0
        self.known = {e: {} for e in self.eng}
        self.dma_sem = [self._newsem("dma%d" % i) for i in range(ndma)]
        self.dma_val = [0] * ndma
        self.n_hw = ndma
        self.dma_next = 0
        self.n_ins = 0
        self._uid = 0

    def _newsem(self, name):
        self._uid = getattr(self, "_uid", 0) + 1
        h = self.nc.alloc_semaphore("%s_%d" % (name, self._uid))
        self.sems.append(h)
        return len(self.sems) - 1

    def _wait(self, e, ev):
        si, val, src = ev
        if self.known[e].get(si, 0) >= val:
            return
        self.eng[e].wait_ge(self.sems[si], val)
        self.known[e][si] = val

    def _deps(self, e, reads, writes):
        for v in reads:
            for tr in v.trs:
                if tr.w is not None:
                    if tr.w[2] == e and e == "pe":
                        continue
                    self._wait(e, tr.w)
                if tr.excl:
                    for src, ev in tr.r.items():
                        if ev[2] != e:
                            self._wait(e, ev)
        for v in writes:
            for tr in v.trs:
                if tr.w is not None and not (tr.w[2] == e and e == "pe"):
                    self._wait(e, tr.w)
                for src, ev in tr.r.items():
                    if not (ev[2] == e and e == "pe"):
                        self._wait(e, ev)

    def _record(self, ev, reads, writes, key):
        for v in writes:
            for tr in v.trs:
                tr.w = ev
                tr.r = {}
        for v in reads:
            for tr in v.trs:
                tr.r[key] = ev

    def emit(self, e, fn, reads, writes):
        self._deps(e, reads, writes)
        ins = fn(self.eng[e])
        if self.cnt[e] >= self.LIMIT:
            self.cur[e] = self._newsem("e_" + e)
            self.cnt[e] = 0
        self.cnt[e] += 1
        ins.then_inc(self.sems[self.cur[e]], 1)
        ev = (self.cur[e], self.cnt[e], e)
        self._record(ev, reads, writes, e)
        self.n_ins += 1
        return ev

    def dma(self, out, in_, e="sp"):
        self._deps(e, [in_], [out])
        if e == "pool":
            self.dma_sem.append(self._newsem("swdma"))
            self.dma_val.append(0)
            slot = len(self.dma_sem) - 1
        else:
            slot = self.dma_next
            self.dma_next = (slot + 1) % self.n_hw
        si = self.dma_sem[slot]
        if self.dma_val[slot] > 0:
            self._wait(e, (si, self.dma_val[slot], "dma"))
        ins = self.eng[e].dma_start(out=out.ap, in_=in_.ap)
        self.dma_val[slot] += 16
        ins.then_inc(self.sems[si], 16)
        ev = (si, self.dma_val[slot], "dma%d" % slot)
        self._record(ev, [in_], [out], "dma%d" % slot)
        self.n_ins += 1
        return ev

    def barrier(self):
        evs = [(self.cur[f], self.cnt[f], f) for f in self.eng if self.cnt[f] > 0]
        evs += [(self.dma_sem[i], self.dma_val[i], "dma") for i in range(len(self.dma_sem)) if self.dma_val[i] > 0]
        for e in self.eng:
            for ev in evs:
                if not (ev[2] == e and e == "pe"):
                    self._wait(e, ev)

    def mm(self, out, lhsT, rhs, start=True, stop=True):
        return self.emit("pe", lambda E: E.matmul(out.ap, lhsT=lhsT.ap, rhs=rhs.ap, start=start, stop=stop),
                         [lhsT, rhs], [out])

    def transpose(self, out, in_, ident):
        return self.emit("pe", lambda E: E.transpose(out.ap, in_.ap, ident.ap), [in_, ident], [out])

    def act(self, out, in_, func, bias=None, scale=None, e="act"):
        kw = {}
        rd = [in_]
        if bias is not None:
            if isinstance(bias, V):
                kw["bias"] = bias.ap
                rd.append(bias)
            else:
                kw["bias"] = bias
        if scale is not None:
            if isinstance(scale, V):
                kw["scale"] = scale.ap
                rd.append(scale)
            else:
                kw["scale"] = scale
        return self.emit("act", lambda E: E.activation(out=out.ap, in_=in_.ap, func=func, **kw), rd, [out])

    def tt(self, out, in0, in1, op, e="dve"):
        return self.emit(e, lambda E: E.tensor_tensor(out=out.ap, in0=in0.ap, in1=in1.ap, op=op), [in0, in1], [out])

    def ts(self, out, in0, s1, s2, op0, op1=None, e="dve"):
        rd = [in0]
        a1 = s1
        a2 = s2
        if isinstance(s1, V):
            rd.append(s1)
            a1 = s1.ap
        if isinstance(s2, V):
            rd.append(s2)
            a2 = s2.ap
        if op1 is None:
            return self.emit(e, lambda E: E.tensor_scalar(out=out.ap, in0=in0.ap, scalar1=a1, scalar2=None, op0=op0),
                             rd, [out])
        return self.emit(e, lambda E: E.tensor_scalar(out=out.ap, in0=in0.ap, scalar1=a1, scalar2=a2, op0=op0, op1=op1),
                         rd, [out])

    def stt(self, out, in0, scalar, in1, op0, op1):
        rd = [in0, in1]
        a = scalar
        if isinstance(scalar, V):
            rd.append(scalar)
            a = scalar.ap
        return self.emit("dve", lambda E: E.scalar_tensor_tensor(out=out.ap, in0=in0.ap, scalar=a, in1=in1.ap,
                                                                 op0=op0, op1=op1), rd, [out])

    def copy(self, out, in_, e="dve"):
        if e == "act":
            return self.act(out, in_, AF.Copy)
        return self.emit(e, lambda E: E.tensor_copy(out=out.ap, in_=in_.ap), [in_], [out])

    def recip(self, out, in_):
        return self.emit("dve", lambda E: E.reciprocal(out=out.ap, in_=in_.ap), [in_], [out])

    def memset(self, out, val, e="dve"):
        return self.emit(e, lambda E: E.memset(out.ap, val), [], [out])


A_Q, A_K, A_V, A_G = 0, 256, 512, 768
B_R, B_K, B_V, B_GT, B_WL, B_AL = 1024, 1280, 1536, 1792, 2048, 2176
C_Q, C_K, C_V, C_G = 2304, 2560, 2816, 3072
D_BASE, D_G = 3328, 5632
MG = 5888
R_CQ, R_CK, R_D = 9984, 10240, 10496
NCH = WCOLS // 128


def _rot_perm():
    cols = []
    for base in (C_Q, C_K):
        for c in range(256):
            j = c % 32
            cols.append(base + c - j + (j + 16) % 32)
    for g in range(3):
        for part in (0, 256):
            base = D_BASE + g * 768 + part
            for c in range(256):
                j = c % 64
                cols.append(base + c - j + (j + 32) % 64)
    return np.asarray(cols, np.int64)


def _rope_tables():
    t = np.arange(T, dtype=np.float32)
    out = []
    for d in (32, 64):
        half = d // 2
        inv = np.power(np.float32(10000.0), -np.arange(half, dtype=np.float32) / np.float32(half)).astype(np.float32)
        ang = (t[:, None] * inv[None, :]).astype(np.float32)
        cos = np.cos(ang).astype(np.float32)
        sin = np.sin(ang).astype(np.float32)
        p = np.arange(128)
        j = p % d
        cosT = cos[:, j % half].T
        sgn = np.where(j < half, -1.0, 1.0).astype(np.float32)
        sinT = (sin[:, j % half].T * sgn[:, None]).astype(np.float32)
        out += [np.ascontiguousarray(cosT), np.ascontiguousarray(sinT)]
    return out


class Ctx:
    pass


def dram(nc, name, shape, dtype, kind):
    return nc.dram_tensor(name, list(shape), dtype, kind=kind).ap()


def build(nseq, parts=("A", "B", "C", "D"), dbg=False):
    nc = bass.Bass("TRN2", target_bir_lowering=False)
    fw = FW(nc)
    c = Ctx()
    c.nc, c.fw, c.parts, c.dbg = nc, fw, parts, dbg
    IN, OUT, INT = "ExternalInput", "ExternalOutput", "Internal"
    c.x = dram(nc, "x", [nseq, T, DM], F32, IN)
    c.y = dram(nc, "y", [nseq, T, DM], F32, OUT)
    c.w_in = dram(nc, "w_in", [2, DM, IN_COLS], F32, IN)
    c.w_rot = dram(nc, "w_rot", [2, DM, ROT_COLS], F32, IN)
    c.w_br = dram(nc, "w_branch", [2, 4, 256, DM], F32, IN)
    c.w_out = dram(nc, "w_out", [2, DM, DM], F32, IN)
    c.b_fm = dram(nc, "b_fm", [2, 128, NCH], F32, IN)
    c.b_cat = dram(nc, "b_cat", [2, WCOLS], F32, IN)
    c.vecs = dram(nc, "vecs", [8, DM], F32, IN)
    c.ident_b = dram(nc, "ident_b", [128, 128], BF16, IN)
    c.ident_f = dram(nc, "ident_f", [128, 128], F32, IN)
    c.rope = dram(nc, "rope", [4, 128, T], F32, IN)
    c.bandm = dram(nc, "bandm", [128, 3, 4, 128], BF16, IN)
    c.na_bias = dram(nc, "na_bias", [2, 128, 14 * 256], F32, IN)
    c.na_mask = dram(nc, "na_mask", [128, 14 * 256], F32, IN)
    c.rwp = dram(nc, "rwp", [2, 128, 64], F32, IN)
    c.rw_w2 = dram(nc, "rw_w2", [2, 128, 256], F32, IN)
    c.rw_a2 = dram(nc, "rw_a2", [2, 128, 256], F32, IN)
    c.df_lam = dram(nc, "df_lam", [2, 128], F32, IN)
    c.trimask = dram(nc, "trimask", [64, 2, 192], F32, IN)
    c.blk1 = dram(nc, "blk1", [128, 128], F32, IN)
    c.perms = dram(nc, "perms", [2, 128, 128], BF16, IN)
    c.wbf = dram(nc, "wbf", [2, DM, WCOLS], BF16, INT)
    c.wbr_bf = dram(nc, "wbr_bf", [2, 4, 256, DM], BF16, INT)
    c.wout_bf = dram(nc, "wout_bf", [2, DM, DM], BF16, INT)
    c.xres = dram(nc, "xres", [T, DM], F32, INT)
    c.tr_wbf = [[Tr() for _ in range(NCH)] for _ in range(2)]
    c.tr_wbr = [Tr(), Tr()]
    c.tr_wout = [Tr(), Tr()]
    c.tr_xres = [Tr() for _ in range(16)]
    c.tr_in = Tr()
    c.tr_y = Tr()
    if dbg:
        c.dbg_out = dram(nc, "dbg", [128, 8, T], BF16, OUT)
        c.tr_dbg = Tr()

    def sb(name, shape, dtype, nslots=1):
        return Buf(nc.alloc_sbuf_tensor(name, list(shape), dtype), nslots)

    c.sb = sb
    c.xT = sb("xT", [128, 8, T], BF16)
    c.yT = sb("yT", [128, 8, T], BF16, nslots=4)
    c.wbuf = sb("wbuf", [128, 2, 8, 512], BF16, nslots=2)
    c.identb = sb("identb", [128, 128], BF16)
    c.identf = sb("identf", [128, 128], F32)
    c.bfm = sb("bfm", [128, 2, NCH], F32)
    c.arena = nc.alloc_sbuf_tensor("arena", [128, ARENA_F32], F32)
    c.arena_off = 0
    c.arena_summ = {}
    c.arena_live = []
    c.ps = [Buf(nc.alloc_psum_tensor("ps%d" % i, [128, 512], F32)) for i in range(8)]
    for b_ in c.ps:
        b_.trs = [Tr(excl=True)]
    c.ps_i = [0, 0]

    fw.dma(c.identb[:, :], V(c.ident_b[:, :], [c.tr_in]))
    fw.dma(c.identf[:, :], V(c.ident_f[:, :], [c.tr_in]))
    for l in range(2):
        fw.dma(c.bfm[:, l, :], V(c.b_fm[l], [c.tr_in]))

    convert_weights(c)
    for s in range(nseq):
        stage0(c, s)
        for l in range(2):
            layer(c, s, l)
    for i in range(len(fw.dma_sem)):
        if fw.dma_val[i] > 0:
            fw._wait("sp", (fw.dma_sem[i], fw.dma_val[i], "dma"))
    return nc, fw


def _arena_retire(c):
    summ = c.arena_summ
    for buf in c.arena_live:
        for tr in buf.trs:
            evs = list(tr.r.values())
            if tr.w is not None:
                evs.append(tr.w)
            for ev in evs:
                key = ev[2]
                old = summ.get(key)
                if old is None or (ev[0], ev[1]) > (old[0], old[1]):
                    summ[key] = ev
    c.arena_live = []


def arena_reset(c):
    _arena_retire(c)
    c.arena_off = 0


def arena_release(c, mark):
    _arena_retire(c)
    c.arena_off = mark


def alloc(c, shape, dtype, nslots=1):
    n = int(np.prod(shape[1:]))
    n4 = (n + 1) // 2 if dtype == BF16 else n
    n4 = (n4 + 7) // 8 * 8
    assert c.arena_off + n4 <= ARENA_F32, ("arena overflow", c.arena_off, n4)
    ap = c.arena[0:shape[0], c.arena_off:c.arena_off + n4]
    c.arena_off += n4
    if dtype == BF16:
        ap = ap.bitcast(BF16)[:, 0:n]
    else:
        ap = ap[:, 0:n]
    if len(shape) > 2:
        names = " ".join("d%d" % i for i in range(len(shape) - 1))
        kw = {"d%d" % i: shape[i + 1] for i in range(len(shape) - 1)}
        ap = ap.rearrange("p (%s) -> p %s" % (names, names), **kw)
    b = Buf(ap, nslots)
    for tr in b.trs:
        tr.r = dict(c.arena_summ)
    c.arena_live.append(b)
    return b


def load_vecs(c, rows):
    for i, r in enumerate(rows):
        c.fw.dma(c.vec_bc[:, i, :], V(c.vecs[r:r + 1, :].partition_broadcast(128), [c.tr_in]))


def psA(c):
    i = c.ps_i[0]
    c.ps_i[0] = (i + 1) % 4
    return c.ps[i]


def psB(c):
    i = c.ps_i[1]
    c.ps_i[1] = (i + 1) % 4
    return c.ps[4 + i]


def convert_weights(c):
    fw = c.fw
    rd = [c.tr_in]
    for l in range(2):
        blocks = [(0, 512 * i, 512) for i in range(11)] + [(0, 5632, 256)]
        for (_, c0, n) in blocks:
            trs = c.tr_wbf[l][c0 // 128:(c0 + n) // 128]
            fw.dma(V(c.wbf[l, :, c0:c0 + n], trs), V(c.w_in[l, :, c0:c0 + n], rd), e="pool")
        dst = c.wbf[l, :, MG:IN_COLS].rearrange("k (dc b j) -> k dc b j", dc=8, b=4)
        for b in range(4):
            src = c.w_in[l, :, MG + b * 1024:MG + (b + 1) * 1024].rearrange("k (dc j) -> k dc j", dc=8)
            fw.dma(V(dst[:, :, b, :], c.tr_wbf[l][MG // 128:IN_COLS // 128]), V(src, rd), e="pool")
        for b in range(4):
            fw.dma(V(c.wbr_bf[l, b], [c.tr_wbr[l]]), V(c.w_br[l, b], rd), e="pool")
        for i in range(2):
            fw.dma(V(c.wout_bf[l, :, 512 * i:512 * (i + 1)], [c.tr_wout[l]]),
                   V(c.w_out[l, :, 512 * i:512 * (i + 1)], rd), e="pool")


def load_w(c, l, c0, n):
    fw = c.fw
    slot = getattr(c, "_wslot", 0)
    c._wslot = 1 - slot
    src = c.wbf[l, :, c0:c0 + n].rearrange("(kc p) n -> p kc n", p=128)
    trs = c.tr_wbf[l][c0 // 128:(c0 + n + 127) // 128]
    fw.dma(c.wbuf.s(slot, (slice(None), slot, slice(None), slice(0, n))), V(src, trs))
    return slot


def wv(c, slot, kc, j0, n):
    return c.wbuf.s(slot, (slice(None), slot, kc, slice(j0, j0 + n)))


def proj_fm(c, l, col_list, consume):
    fw = c.fw
    groups = []
    for i, c0 in enumerate(col_list):
        if groups and groups[-1][0] + groups[-1][1] == c0 and groups[-1][1] < 512:
            groups[-1][1] += 128
            groups[-1][2].append(i)
        else:
            groups.append([c0, 128, [i]])
    slots = [None] * len(groups)
    slots[0] = load_w(c, l, groups[0][0], groups[0][1])
    for gi, (g0, gn, idxs) in enumerate(groups):
        if gi + 1 < len(groups):
            slots[gi + 1] = load_w(c, l, groups[gi + 1][0], groups[gi + 1][1])
        for j, i in enumerate(idxs):
            for tb in range(NTB):
                ps = psA(c)
                for kc in range(8):
                    fw.mm(ps[:, :], wv(c, slots[gi], kc, j * 128, 128), c.xT[:, kc, tb * 512:(tb + 1) * 512],
                          start=(kc == 0), stop=(kc == 7))
                consume(i, tb, ps)


def ln_stats(c, z, sl):
    fw = c.fw
    st = c.ln_st.s(sl, (slice(None), sl, slice(None), slice(None)))
    mv = c.ln_mv
    tr = [mv.trs[sl]]
    fw.emit("dve", lambda E: E.bn_stats(out=st.ap[:, 0, :], in_=z.ap[:, 0:512]), [z], [st])
    fw.emit("dve", lambda E: E.bn_stats(out=st.ap[:, 1, :], in_=z.ap[:, 512:1024]), [z], [st])
    m = lambda a, b: V(mv.h[:, sl, a:b], tr)
    fw.emit("dve", lambda E: E.bn_aggr(out=mv.h[:, sl, 0:2], in_=st.ap), [st], [m(0, 2)])
    fw.ts(m(2, 3), m(1, 2), LN_EPS, None, ALU.add)
    fw.act(m(3, 4), m(2, 3), AF.Sqrt)
    fw.recip(m(4, 5), m(3, 4))
    fw.stt(m(5, 6), m(0, 1), -1.0, m(4, 5), ALU.mult, ALU.mult)


def ln_apply(c, z, gi, bi, out, sl):
    fw = c.fw
    tr = [c.ln_mv.trs[sl]]
    fw.act(out, z, AF.Identity, bias=V(c.ln_mv.h[:, sl, 5:6], tr), scale=V(c.ln_mv.h[:, sl, 4:5], tr))
    fw.tt(out, out, c.vec_bc[:, gi, :], ALU.mult)
    fw.tt(out, out, c.vec_bc[:, bi, :], ALU.add, e="pool")


def to_xT(c, rows, tt):
    fw = c.fw
    xb = c.xb_t
    fw.copy(xb[:, :], rows, e="act")
    ps = psB(c)
    psb = V(ps.h[:, :].bitcast(BF16), ps.trs)
    for k in range(8):
        fw.transpose(V(psb.ap[:, k * 128:(k + 1) * 128], ps.trs), xb[:, k * 128:(k + 1) * 128], c.identb[:, :])
    fw.copy(c.xT[:, :, tt * 128:(tt + 1) * 128], V(psb.ap.rearrange("p (k t) -> p k t", k=8), ps.trs))


def run_pipeline3(sa, sb, sc, n):
    for t in range(-2, n):
        if 0 <= t + 2 < n:
            sa(t + 2)
        if 0 <= t + 1 < n:
            sb(t + 1)
        if 0 <= t < n:
            sc(t)


def ln_allocs(c):
    c.ln_st = alloc(c, [128, 2, 2, 6], F32, nslots=2)
    c.ln_mv = alloc(c, [128, 2, 8], F32, nslots=2)
    c.xb_t = alloc(c, [128, DM], BF16)
    c.zt = alloc(c, [128, 3, DM], F32, nslots=3)
    c.ot = alloc(c, [128, 2, DM], F32, nslots=2)
    c.vec_bc = alloc(c, [128, 3, DM], F32)


def stage0(c, s):
    fw = c.fw
    arena_reset(c)
    ln_allocs(c)
    load_vecs(c, [0, 1])
    def zslot(tt):
        return c.zt.s(tt % 3, (slice(None), tt % 3, slice(None)))

    fw.dma(zslot(0), V(c.x[s, 0:128, :], [c.tr_in]))

    def stage_a(tt):
        if tt + 1 < 16:
            fw.dma(zslot(tt + 1), V(c.x[s, (tt + 1) * 128:(tt + 2) * 128, :], [c.tr_in]))
        ln_stats(c, zslot(tt), tt % 2)

    def stage_b1(tt):
        sl = tt % 2
        ln_apply(c, zslot(tt), 0, 1, c.ot.s(sl, (slice(None), sl, slice(None))), sl)

    def stage_b2(tt):
        sl = tt % 2
        o = c.ot.s(sl, (slice(None), sl, slice(None)))
        fw.dma(V(c.xres[tt * 128:(tt + 1) * 128, :], [c.tr_xres[tt]]), o)
        to_xT(c, o, tt)

    run_pipeline3(stage_a, stage_b1, stage_b2, 16)


def gate_branch(c, l, gcol, bi):
    fw = c.fw

    def consume(i, tb, ps):
        ch = (gcol // 128) + i
        sg = c.g_sig.s(tb % 2, (slice(None), tb % 2, slice(None)))
        fw.act(sg, ps[:, :], AF.Sigmoid, bias=c.bfm[:, l, ch:ch + 1])
        fw.stt(sg, ps[:, :], c.bfm[:, l, ch:ch + 1], sg, ALU.add, ALU.mult)
        yv = c.yT.s(bi, (slice(None), bi * 2 + i, slice(tb * 512, (tb + 1) * 512)))
        fw.tt(yv, yv, sg, ALU.mult, e="pool")

    proj_fm(c, l, [gcol, gcol + 128], consume)


def final_stage(c, s, l):
    fw = c.fw
    arena_reset(c)
    ln_allocs(c)
    c.mergedT = alloc(c, [128, 8, T], BF16)
    c.wbr_sb = alloc(c, [128, 4, 2, DM], BF16)
    c.wout_sb = alloc(c, [128, 8, DM], BF16)
    c.f_sig = alloc(c, [128, 2, 512], F32, nslots=2)
    c.f_acc = alloc(c, [128, 2, 512], F32, nslots=2)
    c.f_tmp = alloc(c, [128, 2, 512], F32, nslots=2)
    load_vecs(c, [2 + 3 * l, 3 + 3 * l, 4 + 3 * l])
    c.ones_row = alloc(c, [1, 128], BF16)
    c.bout_row = alloc(c, [1, DM], BF16)
    bstage = alloc(c, [1, DM], F32)
    fw.memset(c.ones_row[0:1, :], 1.0)
    fw.dma(bstage[0:1, :], V(c.vecs[2 + 3 * l:3 + 3 * l, :], [c.tr_in]))
    fw.copy(c.bout_row[0:1, :], bstage[0:1, :])
    for b in range(4):
        fw.dma(c.wbr_sb[:, b, :, :], V(c.wbr_bf[l, b].rearrange("(kc p) d -> p kc d", p=128), [c.tr_wbr[l]]))
    fw.dma(c.wout_sb[:, :, :], V(c.wout_bf[l].rearrange("(kc p) d -> p kc d", p=128), [c.tr_wout[l]]))
    slots = [None] * 8
    slots[0] = load_w(c, l, MG, 512)
    for dc in range(8):
        if dc + 1 < 8:
            slots[dc + 1] = load_w(c, l, MG + (dc + 1) * 512, 512)
        for tb in range(NTB):
            tok = slice(tb * 512, (tb + 1) * 512)
            ai = tb % 2
            acc = c.f_acc.s(ai, (slice(None), ai, slice(None)))
            for b in range(4):
                pg = psA(c)
                for kc in range(8):
                    fw.mm(pg[:, :], wv(c, slots[dc], kc, b * 128, 128), c.xT[:, kc, tok], start=(kc == 0), stop=(kc == 7))
                ch = MG // 128 + b * 8 + dc
                si = b % 2
                sg = c.f_sig.s(si, (slice(None), si, slice(None)))
                fw.act(sg, pg[:, :], AF.Sigmoid, bias=c.bfm[:, l, ch:ch + 1])
                pp = psB(c)
                for kc in range(2):
                    fw.mm(pp[:, :], c.wbr_sb[:, b, kc, dc * 128:(dc + 1) * 128],
                          c.yT.s(b, (slice(None), b * 2 + kc, tok)), start=(kc == 0), stop=(kc == 1))
                if b == 0:
                    fw.tt(acc, pp[:, :], sg, ALU.mult)
                else:
                    tm = c.f_tmp.s(si, (slice(None), si, slice(None)))
                    fw.tt(tm, pp[:, :], sg, ALU.mult)
                    if b < 3:
                        fw.tt(acc, acc, tm, ALU.add, e="pool")
                    else:
                        fw.tt(c.mergedT[:, dc, tok], acc, tm, ALU.add, e="pool")
    def zslot(tt):
        return c.zt.s(tt % 3, (slice(None), tt % 3, slice(None)))

    fw.dma(zslot(0), V(c.xres[0:128, :], [c.tr_xres[0]]))

    def stage_a(tt):
        z = zslot(tt)
        if tt + 1 < 16:
            fw.dma(zslot(tt + 1), V(c.xres[(tt + 1) * 128:(tt + 2) * 128, :], [c.tr_xres[tt + 1]]))
        for hf in range(2):
            po = psA(c)
            for kc in range(8):
                fw.mm(po[:, :], c.mergedT[:, kc, tt * 128:(tt + 1) * 128], c.wout_sb[:, kc, hf * 512:(hf + 1) * 512],
                      start=(kc == 0), stop=False)
            fw.mm(po[:, :], c.ones_row[0:1, :], c.bout_row[0:1, hf * 512:(hf + 1) * 512], start=False, stop=True)
            zh = V(z.ap[:, hf * 512:(hf + 1) * 512], z.trs)
            fw.stt(zh, zh, ALPHA, po[:, :], ALU.mult, ALU.add)
        ln_stats(c, z, tt % 2)

    def stage_b1(tt):
        sl = tt % 2
        ln_apply(c, zslot(tt), 1, 2, c.ot.s(sl, (slice(None), sl, slice(None))), sl)

    def stage_b2(tt):
        sl = tt % 2
        o = c.ot.s(sl, (slice(None), sl, slice(None)))
        if l == 0:
            fw.dma(V(c.xres[tt * 128:(tt + 1) * 128, :], [c.tr_xres[tt]]), o)
            to_xT(c, o, tt)
        else:
            fw.dma(V(c.y[s, tt * 128:(tt + 1) * 128, :], [c.tr_y]), o)

    run_pipeline3(stage_a, stage_b1, stage_b2, 16)


def layer(c, s, l):
    fw = c.fw
    if "B" in c.parts:
        branch_b(c, s, l)
    for bi, name in enumerate("ABCD"):
        if name not in c.parts:
            fw.memset(c.yT.s(bi, (slice(None), slice(bi * 2, bi * 2 + 2), slice(None))), 0.0, e="pool")
    if "A" in c.parts:
        branch_a(c, s, l)
    if "C" in c.parts:
        branch_c(c, s, l)
    if "D" in c.parts:
        branch_d(c, s, l)
    if c.dbg and l == 0 and s == 0:
        fw.dma(V(c.dbg_out[:, :, :], [c.tr_dbg]), c.yT[:, :, :])
    final_stage(c, s, l)


def rope_proj(c, l, qcol, perm, cosT, sinT, dst, tmp, t2b, qbb):
    fw = c.fw

    def consume(i, tb, ps):
        tok = slice(tb * 512, (tb + 1) * 512)
        ch = qcol // 128 + i
        sl = (i * NTB + tb) % 2
        qb = qbb.s(sl, (slice(None), sl, slice(None)))
        fw.act(qb, ps[:, :], AF.Identity, bias=c.bfm[:, l, ch:ch + 1])
        t1 = t2b.s(sl, (slice(None), sl, slice(None)))
        fw.stt(t1, ps[:, :], c.bfm[:, l, ch:ch + 1], cosT[:, tok], ALU.add, ALU.mult)
        pr = psB(c)
        fw.mm(pr[:, :], perm, qb)
        t2 = V(tmp.h[:, sl, :], [tmp.trs[0]])
        fw.tt(t2, pr[:, :], sinT[:, tok], ALU.mult)
        fw.tt(dst[:, i, tok], t1, t2, ALU.add, e="pool")

    proj_fm(c, l, [qcol, qcol + 128], consume)


def plain_proj(c, l, col, dst):
    fw = c.fw

    def consume(i, tb, ps):
        ch = col // 128 + i
        fw.ts(dst[:, i, tb * 512:(tb + 1) * 512], ps[:, :], c.bfm[:, l, ch:ch + 1], None, ALU.add)

    proj_fm(c, l, [col, col + 128], consume)


def vaug_init(c, vaug):
    v5 = vaug.h.rearrange("p j (hp par) e -> p j hp par e", par=2)
    c.fw.memset(V(v5[:, :, :, 0, 64:128], vaug.trs), 1.0, e="pool")
    c.fw.memset(V(v5[:, :, :, 1, 0:64], vaug.trs), 1.0, e="pool")


def v_proj(c, l, vcol, vaug, vbias, tok_sel, ntiles=16):
    fw = c.fw
    slot = load_w(c, l, vcol, 256)
    fw.dma(vbias[:, :], V(c.b_cat[l:l + 1, vcol:vcol + 256].partition_broadcast(128), [c.tr_in]))
    v5 = vaug.h.rearrange("p j (hp par) e -> p j hp par e", par=2)
    b4 = vbias.h.rearrange("p (hp par e) -> p hp par e", hp=2, par=2)
    for j in range(ntiles):
        ps = psA(c)
        for kc in range(8):
            fw.mm(ps[:, 0:256], V(c.xT.h[:, kc, tok_sel(j)], c.xT.trs), wv(c, slot, kc, 0, 256),
                  start=(kc == 0), stop=(kc == 7))
        p4 = ps.h[:, 0:256].rearrange("p (hp par e) -> p hp par e", hp=2, par=2)
        fw.tt(V(v5[:, j, :, 0, 0:64], vaug.trs), V(p4[:, :, 0, :], ps.trs), V(b4[:, :, 0, :], vbias.trs), ALU.add)
        fw.tt(V(v5[:, j, :, 1, 64:128], vaug.trs), V(p4[:, :, 1, :], ps.trs), V(b4[:, :, 1, :], vbias.trs), ALU.add)


def branch_c(c, s, l):
    fw = c.fw
    arena_reset(c)
    lam_init = 0.8 - 0.6 * math.exp(-0.3 * l)
    cosT = alloc(c, [128, T], F32)
    sinT = alloc(c, [128, T], F32)
    fw.dma(cosT[:, :], V(c.rope[0], [c.tr_in]))
    fw.dma(sinT[:, :], V(c.rope[1], [c.tr_in]))
    qT = alloc(c, [128, 2, T], BF16)
    kT = alloc(c, [128, 2, T], BF16)
    tmp = alloc(c, [128, 4, 512], F32)
    t2b = alloc(c, [128, 2, 512], F32, nslots=2)
    vaug = alloc(c, [128, 16, 4, 128], BF16)
    vbias = alloc(c, [128, 256], F32)
    c.g_sig = alloc(c, [128, 2, 512], F32, nslots=2)
    pt = alloc(c, [128, 4, 512], BF16, nslots=4)
    sm = alloc(c, [128, 16], F32)
    lamt = alloc(c, [128, 128], F32)
    prm = alloc(c, [128, 64], F32)
    ofull = alloc(c, [128, 512], F32)
    osq = alloc(c, [128, 512], F32)
    w1 = alloc(c, [128, 2, 512], F32, nslots=2)
    w2 = alloc(c, [128, 2, 512], F32, nslots=2)
    blk = alloc(c, [128, 128], F32)
    permC = alloc(c, [128, 128], BF16)
    qbb = alloc(c, [128, 2, 512], BF16, nslots=2)
    fw.dma(permC[:, :], V(c.perms[0], [c.tr_in]))
    fw.dma(blk[:, :], V(c.blk1[:, :], [c.tr_in]))
    fw.dma(prm[:, :], V(c.rwp[l], [c.tr_in]))
    fw.dma(lamt[:, :], V(c.df_lam[l:l + 1, :].partition_broadcast(128), [c.tr_in]))
    fw.tt(lamt[:, 0:32], lamt[:, 0:32], lamt[:, 32:64], ALU.mult)
    fw.tt(lamt[:, 64:96], lamt[:, 64:96], lamt[:, 96:128], ALU.mult)
    fw.emit("dve", lambda E: E.tensor_reduce(out=sm.h[:, 0:1], in_=lamt.h[:, 0:32], axis=mybir.AxisListType.X,
                                             op=ALU.add), [lamt[:, :]], [sm[:, :]])
    fw.emit("dve", lambda E: E.tensor_reduce(out=sm.h[:, 1:2], in_=lamt.h[:, 64:96], axis=mybir.AxisListType.X,
                                             op=ALU.add), [lamt[:, :]], [sm[:, :]])
    fw.act(sm[:, 2:4], sm[:, 0:2], AF.Exp)
    fw.tt(sm[:, 4:5], sm[:, 3:4], sm[:, 2:3], ALU.subtract)
    fw.ts(sm[:, 5:6], sm[:, 4:5], -lam_init, None, ALU.add)
    fw.ts(sm[:, 6:7], prm[:, 0:1], 1.0 - lam_init, None, ALU.mult)

    vaug_init(c, vaug)
    rope_proj(c, l, C_Q, permC[:, :], cosT, sinT, qT, tmp, t2b, qbb)
    rope_proj(c, l, C_K, permC[:, :], cosT, sinT, kT, tmp, t2b, qbb)
    v_proj(c, l, C_V, vaug, vbias, lambda j: slice(j * 128, (j + 1) * 128))

    qz = [alloc(c, [128, 2, T], BF16), alloc(c, [128, 2, T], BF16)]
    for i in range(2):
        fw.ts(qz[i][:, :, :], qT[:, :, :], prm[:, 29 + i:30 + i], None, ALU.mult, e="dve")
    scale = 32 ** -0.5
    items = [(hp, tb, i, kt, par) for hp in range(2) for tb in range(NTB) for i in range(2) for kt in range(16)
             for par in range(2)]
    pbuf = {}
    state = {}

    def stage1(n):
        hp, tb, i, kt, par = items[n]
        tok = slice(tb * 512, (tb + 1) * 512)
        pb = par * 64
        sc = psA(c)
        fw.mm(sc[:, :], kT[pb:pb + 64, hp, kt * 128:(kt + 1) * 128], qz[i][pb:pb + 64, hp, tok])
        p = pt.s(n % 4, (slice(None), n % 4, slice(None)))
        fw.act(p, sc[:, :], AF.Exp, scale=scale)
        pbuf[n] = p

    def stage2(n):
        hp, tb, i, kt, par = items[n]
        tok = slice(tb * 512, (tb + 1) * 512)
        h = hp * 2 + par
        if kt == 0:
            state[("acc", par)] = psB(c)
        acc = state[("acc", par)]
        fw.mm(acc[:, :], vaug[:, kt, h, :], pbuf.pop(n), start=(kt == 0), stop=(kt == 15))
        if kt == 15:
            olo, dlo = (0, 64) if par == 0 else (64, 0)
            o = slice(olo, olo + 64)
            d = slice(dlo, dlo + 64)
            r1 = w1.s(i, (o, i, slice(None)))
            fw.recip(r1, acc[d, :])
            t1 = w2.s(i, (o, i, slice(None)))
            fw.tt(t1, acc[o, :], r1, ALU.mult)
            if i == 1:
                t0 = w2.s(0, (o, 0, slice(None)))
                fw.stt(ofull[o, :], t1, sm[o, 5:6], t0, ALU.mult, ALU.add)
                if par == 1:
                    fw.tt(osq[:, :], ofull[:, :], ofull[:, :], ALU.mult)
                    ss = psB(c)
                    fw.mm(ss[:, :], blk[:, :], osq[:, :])
                    fw.ts(osq[:, :], ss[:, :], 1.0 / 64.0, 1e-5, ALU.mult, ALU.add)
                    fw.act(osq[:, :], osq[:, :], AF.Sqrt)
                    rs_ = w1.s(0, (slice(None), 0, slice(None)))
                    fw.recip(rs_, osq[:, :])
                    fw.stt(c.yT.s(2, (slice(None), 4 + hp, tok)), ofull[:, :], sm[:, 6:7], rs_, ALU.mult, ALU.mult)

    npair = len(items) // 2
    for m in range(npair + 1):
        if m < npair:
            stage1(2 * m)
            stage1(2 * m + 1)
        if m >= 1:
            stage2(2 * m - 2)
            stage2(2 * m - 1)
    gate_branch(c, l, C_G, 2)


def _host_inputs(inp):
    f32 = np.float32
    g = lambda k: np.asarray(inp[k], f32)
    w_in = g("w_in")
    b_in = g("b_in")
    perm = _rot_perm()
    w_rot = np.ascontiguousarray(w_in[:, :, perm])
    b_cat = np.ascontiguousarray(np.concatenate([b_in, b_in[:, perm]], axis=1))
    b_fm = np.ascontiguousarray(b_cat.reshape(2, NCH, 128).transpose(0, 2, 1))
    vecs = np.zeros((8, DM), f32)
    vecs[0], vecs[1] = g("ln0_g"), g("ln0_b")
    for l in range(2):
        vecs[2 + 3 * l], vecs[3 + 3 * l], vecs[4 + 3 * l] = g("b_out")[l], g("ln_g")[l], g("ln_b")[l]
    rope = np.stack(_rope_tables(), 0)
    i = np.arange(128)[:, None]
    j = np.arange(128)[None, :]
    band = np.stack([(i - j >= 64), (np.abs(i - j) <= 64), (j - i >= 64)], 1).astype(f32)
    bandm = np.ascontiguousarray(np.broadcast_to(band[:, :, None, :], (128, 3, 4, 128))).astype(ml_dtypes.bfloat16)
    rpb = g("na_rpb")
    kap = np.arange(2)[:, None, None, None, None]
    kc = np.arange(64)[None, :, None, None, None]
    oi = np.arange(14)[None, None, :, None, None]
    hh = np.asarray([0, 2, 1, 3])[None, None, None, :, None]
    qc = np.arange(64)[None, None, None, None, :]
    dr = (oi - 7) + kap
    dc = np.clip(kc - qc + 15, 0, 30)
    shp = (2, 64, 14, 4, 64)
    na_bias = np.stack([rpb[l][np.broadcast_to(hh, shp), np.broadcast_to(dr + 7, shp), np.broadcast_to(dc, shp)]
                        for l in range(2)], 0).reshape(2, 128, 14 * 256).astype(f32)
    cst = np.clip(qc - 8, 0, 48)
    ok = (kc >= cst) & (kc < cst + 16)
    na_mask = np.ascontiguousarray(np.broadcast_to(ok, shp)).reshape(128, 14 * 256).astype(f32)
    rwp = np.zeros((2, 128, 64), f32)
    p = np.arange(128)
    mu = g("rw_mu")
    for l in range(2):
        rwp[l, :, 0] = g("df_subln_g")[l][p % 64]
        for hp in range(2):
            ch = hp * 128 + p
            for q in range(4):
                rwp[l, :, 1 + 2 * q + hp] = mu[l, q * 256 + ch]
            for d in range(2):
                rwp[l, :, 11 + d * 2 + hp] = g("rw_w0")[l, d, ch]
                rwp[l, :, 15 + d * 2 + hp] = g("rw_a0")[l, d, ch]
            rwp[l, :, 19 + hp] = g("rw_kk")[l, ch]
            rwp[l, :, 21 + hp] = g("rw_ka")[l, ch]
            rwp[l, :, 23 + hp] = g("rw_rk")[l].reshape(256)[ch]
            rwp[l, :, 25 + hp] = g("rw_lnx_g")[l, ch]
            rwp[l, :, 27 + hp] = g("rw_lnx_b")[l, ch]
        rwp[l, :, 29] = ((p % 64) < 32)
        rwp[l, :, 30] = ((p % 64) >= 32)
        rwp[l, :, 9] = mu[l, 1024 + p]
        rwp[l, :, 10] = mu[l, 1152 + p]
    s_ = np.arange(64)[:, None]
    t_ = np.arange(64)[None, :]
    tri = np.zeros((64, 2, 2, 64), f32)
    tri[:, 0, 0], tri[:, 0, 1] = (s_ < t_), (s_ <= t_)
    tri[:, 1, 0], tri[:, 1, 1] = (s_ > t_), (s_ >= t_)
    blk1 = np.zeros((128, 128), f32)
    blk1[:64, :64] = 1
    blk1[64:, 64:] = 1
    perms = np.zeros((2, 128, 128), f32)
    pp = np.arange(128)
    for qi, dd in enumerate((32, 64)):
        jj = pp % dd
        partner = pp - jj + (jj + dd // 2) % dd
        perms[qi, partner, pp] = 1.0
    tri3 = np.zeros((64, 2, 192), f32)
    tri3[:, :, 0:128] = tri.reshape(64, 2, 128)
    tri3[:, 0, 128:192] = (s_ > t_)
    tri3[:, 1, 128:192] = (s_ < t_)
    return {
        "w_in": w_in, "w_rot": w_rot, "w_branch": g("w_branch"), "w_out": g("w_out"),
        "b_fm": b_fm, "b_cat": b_cat, "vecs": vecs,
        "ident_b": np.eye(128, dtype=f32).astype(ml_dtypes.bfloat16), "ident_f": np.eye(128, dtype=f32),
        "rope": np.ascontiguousarray(rope), "bandm": bandm, "na_bias": na_bias, "na_mask": na_mask,
        "rwp": rwp, "rw_w2": np.ascontiguousarray(g("rw_w2").reshape(2, 128, 256)),
        "rw_a2": np.ascontiguousarray(g("rw_a2").reshape(2, 128, 256)),
        "df_lam": np.ascontiguousarray(g("df_lam").reshape(2, 128)),
        "trimask": tri3, "blk1": blk1, "perms": perms.astype(ml_dtypes.bfloat16),
    }


_NC_CACHE = {}


def kernel(**inputs):
    xp = np.asarray(inputs["x_prompt"], np.float32)
    xs = np.asarray(inputs["x_sample"], np.float32)
    shared = _host_inputs(inputs)
    ncores = 8
    if "nc" not in _NC_CACHE:
        _NC_CACHE["nc"] = build(6)[0]
    nc = _NC_CACHE["nc"]
    in_maps = []
    for k in range(ncores):
        xk = np.concatenate([xp[4 * k:4 * k + 4], xs[2 * k:2 * k + 2]], axis=0)
        m = dict(shared)
        m["x"] = np.ascontiguousarray(xk)
        in_maps.append(m)
    res = run_bass_kernel_spmd(nc, in_maps, core_ids=list(range(ncores)))
    yp = np.empty_like(xp)
    ys = np.empty_like(xs)
    for k in range(ncores):
        yk = np.asarray(res.results[k]["y"], np.float32)
        yp[4 * k:4 * k + 4] = yk[0:4]
        ys[2 * k:2 * k + 2] = yk[4:6]
    return (yp, ys)


def branch_a(c, s, l):
    fw = c.fw
    arena_reset(c)
    qT = alloc(c, [128, 2, T], BF16)
    kT = alloc(c, [128, 2, T], BF16)
    vaug0 = alloc(c, [128, 16, 4, 128], BF16)
    vaug1 = alloc(c, [128, 15, 4, 128], BF16)
    vbias = alloc(c, [128, 256], F32)
    stg = alloc(c, [128, 14 * 256], F32)
    msk = alloc(c, [128, 14 * 256], F32)
    Mb = alloc(c, [128, 14, 256], BF16)
    pt = alloc(c, [128, 2, 4, 256], BF16, nslots=2)
    rd = alloc(c, [128, 2, 256], F32, nslots=2)
    c.g_sig = alloc(c, [128, 2, 512], F32, nslots=2)
    fw.dma(stg[:, :], V(c.na_bias[l], [c.tr_in]))
    fw.dma(msk[:, :], V(c.na_mask[:, :], [c.tr_in]))
    fw.act(stg[:, :], stg[:, :], AF.Exp)
    fw.tt(V(Mb.h.rearrange("p a b -> p (a b)"), Mb.trs), stg[:, :], msk[:, :], ALU.mult)
    vaug_init(c, vaug0)
    vaug_init(c, vaug1)
    plain_proj(c, l, A_Q, qT)
    plain_proj(c, l, A_K, kT)
    v_proj(c, l, A_V, vaug0, vbias, lambda j: slice(j * 128, (j + 1) * 128), 16)
    v_proj(c, l, A_V, vaug1, vbias, lambda j: slice(64 + j * 128, 64 + (j + 1) * 128), 15)
    rows = {}

    def stage1(r):
        rs = min(max(r - 4, 0), 24)
        sl = r % 2
        qs = slice(64 * r, 64 * r + 64)
        tiles = []
        for j in range(4):
            kr0 = rs + 2 * j
            oi = (rs - r + 2 * j) + 7
            p = pt.s(sl, (slice(None), sl, j, slice(None)))
            scs = [psA(c), psA(c)]
            for hp in range(2):
                for par in range(2):
                    pb = par * 64
                    fw.mm(scs[par][:, hp * 64:(hp + 1) * 64], kT[pb:pb + 64, hp, 64 * kr0:64 * kr0 + 128],
                          qT[pb:pb + 64, hp, qs])
            for par in range(2):
                fw.act(V(p.ap[:, par * 128:(par + 1) * 128], p.trs), scs[par][:, 0:128], AF.Exp, scale=0.125)
            fw.tt(p, p, Mb[:, oi, :], ALU.mult, e=("pool" if j % 2 == 0 else "dve"))
            tiles.append((p, (vaug0, kr0 // 2) if kr0 % 2 == 0 else (vaug1, (kr0 - 1) // 2)))
        rows[r] = tiles

    def stage2(r):
        tiles = rows.pop(r)
        sl = r % 2
        qs = slice(64 * r, 64 * r + 64)
        acc = psB(c)
        for h in range(4):
            for j, (p, (va, ti)) in enumerate(tiles):
                pc = (h % 2) * 128 + (h // 2) * 64
                fw.mm(acc[:, h * 64:(h + 1) * 64], va[:, ti, h, :], V(p.ap[:, pc:pc + 64], p.trs),
                      start=(j == 0), stop=(j == 3))
        a4 = acc.h[:, 0:256].rearrange("p (hp par q) -> p hp par q", hp=2, par=2)
        r4 = rd.h[:, sl, :].rearrange("p (hp par q) -> p hp par q", hp=2, par=2)
        rtr = [rd.trs[sl]]
        fw.recip(V(r4[0:64, :, 0, :], rtr), V(a4[64:128, :, 0, :], acc.trs))
        fw.recip(V(r4[64:128, :, 1, :], rtr), V(a4[0:64, :, 1, :], acc.trs))
        fw.tt(c.yT.s(0, (slice(0, 64), slice(0, 2), qs)), V(a4[0:64, :, 0, :], acc.trs), V(r4[0:64, :, 0, :], rtr), ALU.mult)
        fw.tt(c.yT.s(0, (slice(64, 128), slice(0, 2), qs)), V(a4[64:128, :, 1, :], acc.trs), V(r4[64:128, :, 1, :], rtr),
              ALU.mult)

    stage1(0)
    for r in range(32):
        if r + 1 < 32:
            stage1(r + 1)
        stage2(r)
    gate_branch(c, l, A_G, 0)


def branch_d(c, s, l):
    fw = c.fw
    arena_reset(c)
    cosT = alloc(c, [128, T], F32)
    sinT = alloc(c, [128, T], F32)
    fw.dma(cosT[:, :], V(c.rope[2], [c.tr_in]))
    fw.dma(sinT[:, :], V(c.rope[3], [c.tr_in]))
    acc = alloc(c, [128, 4, T], F32)
    qT = alloc(c, [128, 2, T], BF16)
    kT = alloc(c, [128, 2, T], BF16)
    tmp = alloc(c, [128, 4, 512], F32)
    t2b = alloc(c, [128, 2, 512], F32, nslots=2)
    c.g_sig = t2b
    vaug = alloc(c, [128, 16, 4, 128], BF16)
    vbias = alloc(c, [128, 256], F32)
    pt = alloc(c, [128, 2, 3, 512], BF16, nslots=2)
    bm = alloc(c, [128, 3, 512], BF16)
    permD = alloc(c, [128, 128], BF16)
    qbb = alloc(c, [128, 2, 512], BF16, nslots=2)
    fw.dma(permD[:, :], V(c.perms[1], [c.tr_in]))
    fw.dma(bm[:, :, :], V(c.bandm.rearrange("p a h q -> p a (h q)"), [c.tr_in]))
    vaug_init(c, vaug)
    unit = 0
    for g, dil in enumerate((1, 4, 16)):
        nqb = T // dil // 128
        base = D_BASE + g * 768
        rope_proj(c, l, base, permD[:, :], cosT, sinT, qT, tmp, t2b, qbb)
        rope_proj(c, l, base + 256, permD[:, :], cosT, sinT, kT, tmp, t2b, qbb)

        def tsel(j, dil=dil, nqb=nqb):
            rho, jb = divmod(j, nqb)
            st = 128 * jb * dil + rho
            return slice(st, st + 127 * dil + 1, dil)

        v_proj(c, l, base + 512, vaug, vbias, tsel, 16)
        units = [(rho, qb) for rho in range(dil) for qb in range(nqb)]
        ust = {}

        def stage1(ui, units=units, tsel=tsel, nqb=nqb, ust=ust):
            rho, qb = units[ui]
            qs = tsel(rho * nqb + qb)
            kts = [(jb, mt) for (jb, mt) in ((qb - 1, 0), (qb, 1), (qb + 1, 2)) if 0 <= jb < nqb]
            sl = (unit0 + ui) % 2
            ps_ = []
            for idx, (jb, mt) in enumerate(kts):
                ks = tsel(rho * nqb + jb)
                p = pt.s(sl, (slice(None), sl, idx, slice(None)))
                scs = [psA(c), psA(c)]
                for hp in range(2):
                    for par in range(2):
                        pb = par * 64
                        fw.mm(scs[par][:, hp * 128:(hp + 1) * 128], V(kT.h[pb:pb + 64, hp, ks], kT.trs),
                              V(qT.h[pb:pb + 64, hp, qs], qT.trs))
                for par in range(2):
                    fw.act(V(p.ap[:, par * 256:(par + 1) * 256], p.trs), scs[par][:, 0:256], AF.Exp, scale=0.125)
                fw.tt(p, p, bm[:, mt, :], ALU.mult, e=("pool" if idx % 2 == 0 else "dve"))
                ps_.append(p)
            ust[ui] = (qs, kts, ps_)

        def stage2(ui, units=units, nqb=nqb, ust=ust, g=g):
            rho, qb = units[ui]
            qs, kts, ps_ = ust.pop(ui)
            ob = psB(c)
            for h in range(4):
                for idx, (jb, mt) in enumerate(kts):
                    pc = (h % 2) * 256 + (h // 2) * 128
                    fw.mm(ob[:, h * 128:(h + 1) * 128], vaug[:, rho * nqb + jb, h, :],
                          V(ps_[idx].ap[:, pc:pc + 128], ps_[idx].trs),
                          start=(idx == 0), stop=(idx == len(kts) - 1))
            dst = V(acc.h[:, :, qs], acc.trs)
            src = V(ob.h[:, :].rearrange("p (h q) -> p h q", h=4), ob.trs)
            if g == 0:
                fw.copy(dst, src, e="act")
            else:
                fw.tt(dst, src, dst, ALU.add)

        unit0 = unit
        stage1(0)
        for ui in range(len(units)):
            if ui + 1 < len(units):
                stage1(ui + 1)
            stage2(ui)
        unit += len(units)
    a5 = acc.h.rearrange("p (hp par) t -> p hp par t", par=2)
    t5 = tmp.h.rearrange("p (hp par) t -> p hp par t", par=2)
    for tb in range(NTB):
        tok = slice(tb * 512, (tb + 1) * 512)
        fw.recip(V(t5[0:64, :, 0, :], tmp.trs), V(a5[64:128, :, 0, tok], acc.trs))
        fw.recip(V(t5[64:128, :, 1, :], tmp.trs), V(a5[0:64, :, 1, tok], acc.trs))
        fw.tt(c.yT.s(3, (slice(0, 64), slice(6, 8), tok)), V(a5[0:64, :, 0, tok], acc.trs), V(t5[0:64, :, 0, :], tmp.trs),
              ALU.mult)
        fw.tt(c.yT.s(3, (slice(64, 128), slice(6, 8), tok)), V(a5[64:128, :, 1, tok], acc.trs),
              V(t5[64:128, :, 1, :], tmp.trs), ALU.mult)
    gate_branch(c, l, D_G, 3)


USE_NTI = os.environ.get("B_NTI", "1") == "1"
GC = int(os.environ.get("B_GC", "4"))


def yslot_f32(c, slot):
    ap = c.yT.h[:, 2 * slot:2 * slot + 2, :].rearrange("p a t -> p (a t)").bitcast(F32)
    return Buf(ap, 1), c.yT.trs[slot]


def branch_b(c, s, l):
    for hp in range(2):
        rwkv_hp(c, l, hp)


def rwkv_hp(c, l, hp):
    fw = c.fw
    arena_reset(c)
    CD = -math.exp(-0.5)
    prm = alloc(c, [128, 64], F32)
    blk = alloc(c, [128, 128], F32)
    wst = alloc(c, [128, 2, 256], F32)
    w2sb = alloc(c, [128, 256], BF16)
    a2sb = alloc(c, [128, 256], BF16)
    der = alloc(c, [128, 16], F32)
    ones = alloc(c, [128, 64], F32)
    fw.dma(prm[:, :], V(c.rwp[l], [c.tr_in]))
    fw.dma(blk[:, :], V(c.blk1[:, :], [c.tr_in]))
    fw.dma(wst[:, 0, :], V(c.rw_w2[l], [c.tr_in]))
    fw.dma(wst[:, 1, :], V(c.rw_a2[l], [c.tr_in]))
    fw.copy(w2sb[:, :], wst[:, 0, :])
    fw.copy(a2sb[:, :], wst[:, 1, :])
    fw.memset(ones[:, :], 1.0)
    mucols = [1 + hp, 3 + hp, 5 + hp, 7 + hp, 9, 10]
    for i, mc in enumerate(mucols):
        fw.ts(der[:, i:i + 1], prm[:, mc:mc + 1], -1.0, 1.0, ALU.mult, ALU.add)
        fw.ts(der[:, 6 + i:7 + i], prm[:, mc:mc + 1], 0.5, None, ALU.mult)
    fw.ts(der[:, 12:13], prm[:, 21 + hp:22 + hp], -1.0, 1.0, ALU.mult, ALU.add)
    vT = alloc(c, [128, T], BF16)
    gs = alloc(c, [128, T], BF16)
    bv = alloc(c, [128, T], BF16)
    AR = [alloc(c, [128, 2, T], BF16) for _ in range(2)]
    BK = [alloc(c, [128, 2, T], BF16) for _ in range(2)]
    RLO = [alloc(c, [64, T], BF16) for _ in range(2)]
    eL = alloc(c, [128, 2, 32], F32)
    mark = c.arena_off

    uT = alloc(c, [128, 6, T + 2], BF16)
    fw.memset(uT[:, :, 0:1], 0.0)
    fw.memset(uT[:, :, T + 1:T + 2], 0.0)
    cols = [B_R + hp * 128, B_K + hp * 128, B_V + hp * 128, B_GT + hp * 128, B_WL, B_AL]

    def consume(i, tb, ps):
        ch = cols[i] // 128
        fw.act(uT[:, i, 1 + tb * 512:1 + (tb + 1) * 512], ps[:, :], AF.Identity, bias=c.bfm[:, l, ch:ch + 1])

    proj_fm(c, l, cols, consume)
    tbuf = [alloc(c, [128, 512], F32)[:, :] for _ in range(11)]
    slot_tmps = []
    for slot_ in (0, 2, 3):
        ex_ap = c.yT.h[:, 2 * slot_:2 * slot_ + 2, :].rearrange("p a t -> p (a t)").bitcast(F32)
        for i in range(4):
            t_ = Tr()
            t_.r = dict(c.yT.trs[slot_].r)
            t_.w = c.yT.trs[slot_].w
            slot_tmps.append((slot_, t_))
            tbuf.append(V(ex_ap[:, i * 512:(i + 1) * 512], [t_]))
    xr, xk, kk, tA, tB = tbuf[0:5]
    dtmp = [tbuf[5:14], tbuf[14:23]]
    rmask = alloc(c, [128, 512], F32)
    fw.memset(rmask[:, :], 1.0)
    fw.memset(V(rmask.h[:, 0:512:64], rmask.trs), 0.0)
    twl = alloc(c, [128, 512], BF16)
    tal = alloc(c, [128, 512], BF16)
    for tb in range(NTB):
        tok = slice(tb * 512, (tb + 1) * 512)

        def shift(i, out):
            fw.tt(tA, uT[:, i, tb * 512:tb * 512 + 512], uT[:, i, tb * 512 + 2:tb * 512 + 514], ALU.add, e="pool")
            fw.ts(tA, tA, der[:, 6 + i:7 + i], None, ALU.mult)
            fw.stt(out, uT[:, i, tb * 512 + 1:tb * 512 + 513], der[:, i:i + 1], tA, ALU.mult, ALU.add)

        shift(0, xr)
        shift(1, xk)
        shift(2, tB)
        fw.copy(vT[:, tok], tB, e="act")
        shift(3, tB)
        fw.act(kk, tB, AF.Sigmoid)
        fw.tt(gs[:, tok], tB, kk, ALU.mult, e="pool")
        shift(4, tB)
        fw.act(twl[:, :], tB, AF.Tanh)
        shift(5, tB)
        fw.copy(tal[:, :], tB, e="act")
        fw.ts(kk, xk, prm[:, 19 + hp:20 + hp], None, ALU.mult)
        fw.tt(tB, kk, kk, ALU.mult, e="pool")
        ps = psA(c)
        fw.mm(ps[:, :], blk[:, :], tB)
        fw.ts(tB, ps[:, :], 1e-24, None, ALU.max)
        fw.act(tB, tB, AF.Sqrt)
        fw.recip(tA, tB)
        fw.tt(kk, kk, tA, ALU.mult)

        def dir_chain(d):
            lw, L0, D_, eP, eM, eX, a_, kd, tD = dtmp[d]
            dsl = slice(d * 64, (d + 1) * 64)
            ps = psA(c)
            fw.mm(ps[:, :], w2sb[dsl, hp * 128:(hp + 1) * 128], twl[dsl, :])
            yield
            fw.act(lw, ps[:, :], AF.Sigmoid, bias=prm[:, 11 + d * 2 + hp:12 + d * 2 + hp])
            yield
            ps2 = psA(c)
            fw.mm(ps2[:, :], a2sb[dsl, hp * 128:(hp + 1) * 128], tal[dsl, :])
            fw.emit("dve", lambda E: E.tensor_tensor_scan(out=L0.ap, data0=rmask.h[:, :], data1=lw.ap,
                                                        initial=0.0, op0=ALU.mult, op1=ALU.add),
                    [rmask[:, :], lw], [L0])
            yield
            fw.act(a_, ps2[:, :], AF.Sigmoid, bias=prm[:, 15 + d * 2 + hp:16 + d * 2 + hp])
            yield
            ltot = V(L0.ap[:, 63:512:64], L0.trs)
            fw.act(eL[:, d, tb * 8:(tb + 1) * 8], ltot, AF.Exp, scale=CD)
            if d == 0:
                fw.tt(D_, L0, lw, ALU.subtract, e="pool")
                yield
                fw.act(eP, L0, AF.Exp, scale=CD)
                yield
                fw.act(eM, L0, AF.Exp, scale=-CD)
                yield
                fw.act(eX, D_, AF.Exp, scale=CD)
                yield
            else:
                l3 = V(L0.ap.rearrange("p (c t) -> p c t", t=64), L0.trs)
                lt3 = V(L0.ap[:, 63:512:64].unsqueeze(2).to_broadcast([128, 8, 64]), L0.trs)
                fw.tt(V(D_.ap.rearrange("p (c t) -> p c t", t=64), D_.trs), l3, lt3, ALU.subtract)
                yield
                fw.tt(lw, D_, lw, ALU.subtract, e="pool")
                yield
                fw.act(eX, D_, AF.Exp, scale=-CD)
                yield
                fw.act(eP, lw, AF.Exp, scale=-CD)
                yield
                fw.act(eM, lw, AF.Exp, scale=CD)
                yield
            fw.ts(tD, a_, prm[:, 21 + hp:22 + hp], der[:, 12:13], ALU.mult, ALU.add)
            yield
            fw.tt(kd, tD, xk, ALU.mult)
            yield
            fw.tt(tD, kk, a_, ALU.mult, e="pool")
            yield
            fw.tt(AR[d][:, 0, tok], kk, eX, ALU.mult)
            yield
            fw.tt(BK[d][:, 0, tok], tD, eM, ALU.mult)
            yield
            fw.tt(BK[d][:, 1, tok], kd, eM, ALU.mult, e="pool")
            yield
            fw.tt(AR[d][:, 1, tok], xr, eP, ALU.mult)
            yield
            fw.tt(RLO[d][0:64, tok], V(xr.ap[64:128, :], xr.trs), V(eP.ap[64:128, :], eP.trs), ALU.mult)
            yield

        gens = [dir_chain(0), dir_chain(1)]
        while gens:
            for g_ in list(gens):
                try:
                    next(g_)
                except StopIteration:
                    gens.remove(g_)
        fw.tt(tA, dtmp[0][7], dtmp[1][7], ALU.add, e="pool")
        fw.stt(tB, xr, prm[:, 23 + hp:24 + hp], tA, ALU.mult, ALU.mult)
        ps = psA(c)
        fw.mm(ps[:, :], blk[:, :], tB)
        fw.tt(bv[:, tok], ps[:, :], vT[:, tok], ALU.mult)

    arena_release(c, mark)
    for slot_, t_ in slot_tmps:
        dst = c.yT.trs[slot_]
        evs = list(t_.r.values()) + ([t_.w] if t_.w is not None else [])
        for ev in evs:
            old_ = dst.r.get(ev[2])
            if old_ is None or (ev[0], ev[1]) > (old_[0], old_[1]):
                dst.r[ev[2]] = ev
    ys = []
    for d in range(2):
        b_, tr_ = yslot_f32(c, (0, 2)[d])
        b_.trs = [tr_]
        ys.append(b_)
    eLs = alloc(c, [64, 2, 2, 32], F32)
    for d in range(2):
        for hl in range(2):
            fw.copy(eLs[:, d, hl, :], eL[hl * 64:(hl + 1) * 64, d, :], e="act")
    tri = alloc(c, [64, 2, 192], F32)
    fw.dma(tri[:, :, :], V(c.trimask[:, :, :], [c.tr_in]))
    NM = GC * 4
    tmT2 = [alloc(c, [64, GC, 2, 4, 128], BF16) for _ in range(2)]
    Am2 = [alloc(c, [64, NM, 256], BF16) for _ in range(2)]
    TT2 = [alloc(c, [64, NM, 64], BF16) for _ in range(2)]
    PT2 = [alloc(c, [64, NM, 64], BF16) for _ in range(2)]
    Zb2 = [alloc(c, [64, NM, 64], BF16) for _ in range(2)]
    Nb = [alloc(c, [64, NM, 64], BF16) for _ in range(2)]
    NTb = [alloc(c, [64, NM, 64], BF16) for _ in range(2)]
    NTI = alloc(c, [64, NM, 64], BF16)
    Rb = [alloc(c, [64, NM, 64], BF16) for _ in range(2)]
    Sb = alloc(c, [64, 2, 4, 64], BF16, nslots=2)
    Ub = alloc(c, [64, 4, 64], BF16)
    fw.memset(Sb[:, :, :, :], 0.0)
    I64b = c.identb[0:64, 0:64]
    nsteps = T // 64
    ngroups = nsteps // GC
    halves = [(0, NM // 2), (NM // 2, NM)]
    idb = V(c.identf.h[0:64, 0:64].unsqueeze(1).to_broadcast([64, NM // 2, 64]), c.identf.trs)

    def chunk_of(g, ci, d):
        it = g * GC + ci
        return it if d == 0 else nsteps - 1 - it

    def flat(buf, m0, m1):
        return V(buf.h[:, m0:m1, :].rearrange("p m e -> p (m e)"), buf.trs)

    def phase1(g):
        tmT, Am, TT, PT, Zb = tmT2[g % 2], Am2[g % 2], TT2[g % 2], PT2[g % 2], Zb2[g % 2]
        for ci in range(GC):
            for d in range(2):
                ch = chunk_of(g, ci, d)
                cs = slice(ch * 64, ch * 64 + 64)
                pb_ = psB(c)
                pbv = pb_.h[:, :].bitcast(BF16)
                srcs = [AR[d][:, 0, cs], BK[d][:, 0, cs], BK[d][:, 1, cs], vT[:, cs]]
                for q, src in enumerate(srcs):
                    fw.transpose(V(pbv[0:64, q * 128:(q + 1) * 128], pb_.trs), src, c.identb[:, :])
                fw.copy(V(tmT.h[:, ci, d, :, :].rearrange("p q e -> p (q e)"), tmT.trs), V(pbv[0:64, 0:512], pb_.trs),
                        e=("act" if (d == 0 or os.environ.get("B_TMT_ACT", "0") == "1") else "dve"))
            yield
        bN = [psB(c), psB(c)]
        for ci in range(GC):
            bA = [psA(c), psA(c)]
            hl_outer = os.environ.get("B_HLOUTER", "1") == "1"
            order = [(d, hl) for hl in range(2) for d in range(2)] if hl_outer else [(d, hl) for d in range(2) for hl in range(2)]
            for (d, hl) in order:
                ch = chunk_of(g, ci, d)
                cs = slice(ch * 64, ch * 64 + 64)
                pb = hl * 64
                for q in range(2):
                    o0 = d * 256 + q * 128
                    fw.mm(bA[hl][0:64, o0:o0 + 128], BK[d][pb:pb + 64, q, cs], V(AR[d].h[pb:pb + 64, :, cs], AR[d].trs))
                o1 = (ci * 2 + d) * 64
                fw.mm(bN[hl][0:64, o1:o1 + 64], AR[d][pb:pb + 64, 0, cs], BK[d][pb:pb + 64, 0, cs])
            for hl in range(2):
                m0 = ci * 4 + hl
                dst = V(Am.h[:, m0:m0 + 3:2, :].rearrange("p d (q e) -> p d q e", q=2), Am.trs)
                src = V(bA[hl].h[0:64, :].rearrange("p (d q e) -> p d q e", d=2, q=2), bA[hl].trs)
                msk = V(tri.h[:, :, 0:128].unsqueeze(2).to_broadcast([64, 2, 2, 128]), tri.trs)
                fw.tt(dst, src, msk, ALU.mult)
            yield
        for hl in range(2):
            dst = V(NTb[0].h[:, hl:NM:2, :].rearrange("p (ci d) e -> p ci d e", d=2), NTb[0].trs)
            src = V(bN[hl].h[0:64, :].rearrange("p (ci d e) -> p ci d e", ci=GC, d=2), bN[hl].trs)
            msk = V(tri.h[:, :, 128:192].unsqueeze(1).to_broadcast([64, GC, 2, 64]), tri.trs)
            fw.tt(dst, src, msk, ALU.mult)
        for (m0, m1) in halves:
            fw.tt(Rb[0][:, m0:m1, :], idb, Am[:, m0:m1, 0:64], ALU.subtract)
        yield
        Ncur = V(Am.h[:, :, 0:64], Am.trs)
        NTcur = NTb[0][:, :, :]
        rcur = 0
        nti = 0
        for lev in range(1, 6):
            ntn = 1 - nti
            nn = lev % 2
            pzs = []
            for (m0, m1) in halves:
                pz = psA(c)
                for m in range(m0, m1):
                    fw.mm(pz[0:64, (m - m0) * 64:(m - m0 + 1) * 64], V(Ncur.ap[:, m, :], Ncur.trs), V(NTcur.ap[:, m, :], NTcur.trs))
                pzs.append(pz)
            for hi, (m0, m1) in enumerate(halves):
                pz = pzs[hi]
                src3 = V(pz.h[0:64, 0:(m1 - m0) * 64].rearrange("p (m e) -> p m e", e=64), pz.trs)
                if USE_NTI:
                    fw.tt(NTI[:, m0:m1, :], src3, idb, ALU.add)
                if lev < 5 or not USE_NTI:
                    fw.copy(flat(NTb[ntn], m0, m1), pz[0:64, 0:(m1 - m0) * 64], e="act")
            yield
            if lev < 5:
                for (m0, m1) in halves:
                    pz = psA(c)
                    for m in range(m0, m1):
                        fw.mm(pz[0:64, (m - m0) * 64:(m - m0 + 1) * 64], V(NTcur.ap[:, m, :], NTcur.trs), V(Ncur.ap[:, m, :], Ncur.trs))
                    fw.copy(flat(Nb[nn], m0, m1), pz[0:64, 0:(m1 - m0) * 64], e="act")
                yield
            rn = 1 - rcur
            rdst = TT if lev == 5 else Rb[rn]
            for (m0, m1) in halves:
                pz = psB(c)
                for m in range(m0, m1):
                    o = pz[0:64, (m - m0) * 64:(m - m0 + 1) * 64]
                    if USE_NTI:
                        fw.mm(o, NTI[:, m, :], Rb[rcur][:, m, :])
                    else:
                        fw.mm(o, I64b, Rb[rcur][:, m, :], start=True, stop=False)
                        fw.mm(o, NTb[ntn][:, m, :], Rb[rcur][:, m, :], start=False, stop=True)
                fw.copy(flat(rdst, m0, m1), pz[0:64, 0:(m1 - m0) * 64])
            yield
            rcur = rn
            if lev < 5:
                Ncur = Nb[nn][:, :, :]
                NTcur = NTb[ntn][:, :, :]
                nti = ntn
        for (m0, m1) in halves:
            pz = psA(c)
            pq = psB(c)
            for m in range(m0, m1):
                ci, d, hl = m // 4, (m // 2) % 2, m % 2
                fw.mm(pz[0:64, (m - m0) * 64:(m - m0 + 1) * 64], tmT[:, ci, d, 0, hl * 64:(hl + 1) * 64], TT[:, m, :])
                fw.mm(pq[0:64, (m - m0) * 64:(m - m0 + 1) * 64], Am[:, m, 128:192], tmT[:, ci, d, 3, hl * 64:(hl + 1) * 64])
            fw.copy(flat(PT, m0, m1), pz[0:64, 0:(m1 - m0) * 64], e="act")
            fw.copy(flat(Zb, m0, m1), pq[0:64, 0:(m1 - m0) * 64])
            yield

    def drain(gen, n):
        if gen is None:
            return None
        for _ in range(n):
            try:
                next(gen)
            except StopIteration:
                return None
        return gen

    gen = phase1(0)
    drain(gen, 1000)
    slot = 0
    NY = 2 * GC + 2 + 14 + 2
    per_pt = -(-NY // (2 * GC))
    for g in range(ngroups):
        tmT, Am, TT, PT, Zb = tmT2[g % 2], Am2[g % 2], TT2[g % 2], PT2[g % 2], Zb2[g % 2]
        gen = phase1(g + 1) if g + 1 < ngroups else None
        if os.environ.get("B_NOPIPE", "0") == "1":
            gen = drain(gen, 1000)
        for ci in range(GC):
            scur = Sb.s(slot, (slice(None), slot, slice(None), slice(None)))
            snew = Sb.s(1 - slot, (slice(None), 1 - slot, slice(None), slice(None)))
            pu = psB(c)
            for inst in range(4):
                m = ci * 4 + inst
                o = pu[0:64, inst * 64:(inst + 1) * 64]
                fw.mm(o, PT[:, m, :], V(scur.ap[:, inst, :], scur.trs), start=True, stop=False)
                fw.mm(o, TT[:, m, :], Zb[:, m, :], start=False, stop=True)
            fw.act(V(Ub.h.rearrange("p i e -> p (i e)"), Ub.trs), pu[0:64, 0:256], AF.Copy, scale=-1.0)
            gen = drain(gen, per_pt)
            pS = psA(c)
            pY = psB(c)
            for inst in range(4):
                m = ci * 4 + inst
                d, hl = inst // 2, inst % 2
                ch = chunk_of(g, ci, d)
                cs = slice(ch * 64, ch * 64 + 64)
                hs = slice(hl * 64, (hl + 1) * 64)
                o = pS[0:64, inst * 64:(inst + 1) * 64]
                fw.mm(o, I64b, V(scur.ap[:, inst, :], scur.trs), start=True, stop=False)
                fw.mm(o, tmT[:, ci, d, 2, hs], tmT[:, ci, d, 3, hs], start=False, stop=False)
                fw.mm(o, tmT[:, ci, d, 1, hs], Ub[:, inst, :], start=False, stop=True)
            for d in range(2):
                ch = chunk_of(g, ci, d)
                esc = V(eLs.h[:, d, :, ch:ch + 1].to_broadcast([64, 2, 64]), eLs.trs)
                fw.tt(V(snew.ap[:, 2 * d:2 * d + 2, :], snew.trs),
                      V(pS.h[0:64, d * 128:(d + 1) * 128].rearrange("p (i e) -> p i e", i=2), pS.trs), esc, ALU.mult)
            for inst in range(4):
                m = ci * 4 + inst
                d, hl = inst // 2, inst % 2
                ch = chunk_of(g, ci, d)
                cs = slice(ch * 64, ch * 64 + 64)
                hs = slice(hl * 64, (hl + 1) * 64)
                oy = pY[0:64, inst * 64:(inst + 1) * 64]
                rt = AR[d][0:64, 1, cs] if hl == 0 else RLO[d][0:64, cs]
                fw.mm(oy, V(scur.ap[:, inst, :], scur.trs), rt, start=True, stop=False)
                fw.mm(oy, tmT[:, ci, d, 3, hs], Am[:, m, 192:256], start=False, stop=False)
                fw.mm(oy, Ub[:, inst, :], Am[:, m, 64:128], start=False, stop=True)
            for d in range(2):
                ch = chunk_of(g, ci, d)
                cs = slice(ch * 64, ch * 64 + 64)
                for hl in range(2):
                    inst = d * 2 + hl
                    fw.copy(ys[d][hl * 64:(hl + 1) * 64, cs], pY[0:64, inst * 64:(inst + 1) * 64], e="act")
            gen = drain(gen, per_pt)
            slot = 1 - slot
        drain(gen, 1000)

    arena_release(c, mark)
    pt_ = [alloc(c, [128, 512], F32)[:, :] for _ in range(6)]
    y_, sq, mean, var, t1, t2 = pt_
    for tb in range(NTB):
        tok = slice(tb * 512, (tb + 1) * 512)
        fw.tt(y_, ys[0][:, tok], ys[1][:, tok], ALU.add)
        fw.tt(sq, y_, y_, ALU.mult, e="pool")
        p1 = psA(c)
        fw.mm(p1[:, :], blk[:, :], y_)
        p2 = psA(c)
        fw.mm(p2[:, :], blk[:, :], sq)
        fw.ts(mean, p1[:, :], 1.0 / 64.0, None, ALU.mult)
        fw.tt(t1, mean, mean, ALU.mult, e="pool")
        fw.stt(var, p2[:, :], 1.0 / 64.0, t1, ALU.mult, ALU.subtract)
        fw.ts(var, var, 64e-5, None, ALU.add)
        fw.act(var, var, AF.Sqrt)
        fw.recip(t1, var)
        fw.tt(t2, y_, mean, ALU.subtract, e="pool")
        fw.tt(t2, t2, t1, ALU.mult)
        fw.ts(t2, t2, prm[:, 25 + hp:26 + hp], prm[:, 27 + hp:28 + hp], ALU.mult, ALU.add)
        fw.tt(t2, t2, bv[:, tok], ALU.add, e="pool")
        fw.tt(c.yT.s(1, (slice(None), 2 + hp, tok)), t2, gs[:, tok], ALU.mult)
```

```python
import math
import os
import numpy as np
import ml_dtypes
import concourse.bass as bass
import concourse.mybir as mybir
from concourse.bass_utils import run_bass_kernel_spmd

F32 = mybir.dt.float32
BF16 = mybir.dt.bfloat16
AF = mybir.ActivationFunctionType
ALU = mybir.AluOpType

T = 2048
DM = 1024
NTB = 4
IN_COLS = 9984
ROT_COLS = 2048
WCOLS = IN_COLS + ROT_COLS
ALPHA = (2 * 2) ** 0.25
LN_EPS = 1e-5
ARENA_F32 = 30208


class Tr:
    __slots__ = ("w", "r", "excl")

    def __init__(self, excl=False):
        self.w = None
        self.r = {}
        self.excl = excl


class V:
    __slots__ = ("ap", "trs")

    def __init__(self, ap, trs):
        self.ap = ap
        self.trs = trs


class Buf:
    def __init__(self, h, nslots=1):
        self.h = h
        self.trs = [Tr() for _ in range(nslots)]

    def __getitem__(self, idx):
        return V(self.h[idx], self.trs)

    def s(self, slot, idx):
        return V(self.h[idx], [self.trs[slot]])


class FW:
    LIMIT = 30000

    def __init__(self, nc, ndma=24):
        self.nc = nc
        self.eng = {"pe": nc.tensor, "act": nc.scalar, "dve": nc.vector, "pool": nc.gpsimd, "sp": nc.sync}
        self.sems = []
        self.cur = {}
        self.cnt = {}
        for e in self.eng:
            self.cur[e] = self._newsem("e_" + e)
            self.cnt[e] = 0
        self.known = {e: {} for e in self.eng}
        self.dma_sem = [self._newsem("dma%d" % i) for i in range(ndma)]
        self.dma_val = [0] * ndma
        self.n_hw = ndma
        self.dma_next = 0
        self.n_ins = 0
        self._uid = 0

    def _newsem(self, name):
        self._uid = getattr(self, "_uid", 0) + 1
        h = self.nc.alloc_semaphore("%s_%d" % (name, self._uid))
        self.sems.append(h)
        return len(self.sems) - 1

    def _wait(self, e, ev):
        si, val, src = ev
        if self.known[e].get(si, 0) >= val:
            return
        self.eng[e].wait_ge(self.sems[si], val)
        self.known[e][si] = val

    def _deps(self, e, reads, writes):
        for v in reads:
            for tr in v.trs:
                if tr.w is not None:
                    if tr.w[2] == e and e == "pe":
                        continue
                    self._wait(e, tr.w)
                if tr.excl:
                    for src, ev in tr.r.items():
                        if ev[2] != e:
                            self._wait(e, ev)
        for v in writes:
            for tr in v.trs:
                if tr.w is not None and not (tr.w[2] == e and e == "pe"):
                    self._wait(e, tr.w)
                for src, ev in tr.r.items():
                    if not (ev[2] == e and e == "pe"):
                        self._wait(e, ev)

    def _record(self, ev, reads, writes, key):
        for v in writes:
            for tr in v.trs:
                tr.w = ev
                tr.r = {}
        for v in reads:
            for tr in v.trs:
                tr.r[key] = ev

    def emit(self, e, fn, reads, writes):
        self._deps(e, reads, writes)
        ins = fn(self.eng[e])
        if self.cnt[e] >= self.LIMIT:
            self.cur[e] = self._newsem("e_" + e)
            self.cnt[e] = 0
        self.cnt[e] += 1
        ins.then_inc(self.sems[self.cur[e]], 1)
        ev = (self.cur[e], self.cnt[e], e)
        self._record(ev, reads, writes, e)
        self.n_ins += 1
        return ev

    def dma(self, out, in_, e="sp"):
        self._deps(e, [in_], [out])
        if e == "pool":
            self.dma_sem.append(self._newsem("swdma"))
            self.dma_val.append(0)
            slot = len(self.dma_sem) - 1
        else:
            slot = self.dma_next
            self.dma_next = (slot + 1) % self.n_hw
        si = self.dma_sem[slot]
        if self.dma_val[slot] > 0:
            self._wait(e, (si, self.dma_val[slot], "dma"))
        ins = self.eng[e].dma_start(out=out.ap, in_=in_.ap)
        self.dma_val[slot] += 16
        ins.then_inc(self.sems[si], 16)
        ev = (si, self.dma_val[slot], "dma%d" % slot)
        self._record(ev, [in_], [out], "dma%d" % slot)
        self.n_ins += 1
        return ev

    def barrier(self):
        evs = [(self.cur[f], self.cnt[f], f) for f in self.eng if self.cnt[f] > 0]
        evs += [(self.dma_sem[i], self.dma_val[i], "dma") for i in range(len(self.dma_sem)) if self.dma_val[i] > 0]
        for e in self.eng:
            for ev in evs:
                if not (ev[2] == e and e == "pe"):
                    self._wait(e, ev)

    def mm(self, out, lhsT, rhs, start=True, stop=True):
        return self.emit("pe", lambda E: E.matmul(out.ap, lhsT=lhsT.ap, rhs=rhs.ap, start=start, stop=stop),
                         [lhsT, rhs], [out])

    def transpose(self, out, in_, ident):
        return self.emit("pe", lambda E: E.transpose(out.ap, in_.ap, ident.ap), [in_, ident], [out])

    def act(self, out, in_, func, bias=None, scale=None, e="act"):
        kw = {}
        rd = [in_]
        if bias is not None:
            if isinstance(bias, V):
                kw["bias"] = bias.ap
                rd.append(bias)
            else:
                kw["bias"] = bias
        if scale is not None:
            if isinstance(scale, V):
                kw["scale"] = scale.ap
                rd.append(scale)
            else:
                kw["scale"] = scale
        return self.emit("act", lambda E: E.activation(out=out.ap, in_=in_.ap, func=func, **kw), rd, [out])

    def tt(self, out, in0, in1, op, e="dve"):
        return self.emit(e, lambda E: E.tensor_tensor(out=out.ap, in0=in0.ap, in1=in1.ap, op=op), [in0, in1], [out])

    def ts(self, out, in0, s1, s2, op0, op1=None, e="dve"):
        rd = [in0]
        a1 = s1
        a2 = s2
        if isinstance(s1, V):
            rd.append(s1)
            a1 = s1.ap
        if isinstance(s2, V):
            rd.append(s2)
            a2 = s2.ap
        if op1 is None:
            return self.emit(e, lambda E: E.tensor_scalar(out=out.ap, in0=in0.ap, scalar1=a1, scalar2=None, op0=op0),
                             rd, [out])
        return self.emit(e, lambda E: E.tensor_scalar(out=out.ap, in0=in0.ap, scalar1=a1, scalar2=a2, op0=op0, op1=op1),
                         rd, [out])

    def stt(self, out, in0, scalar, in1, op0, op1):
        rd = [in0, in1]
        a = scalar
        if isinstance(scalar, V):
            rd.append(scalar)
            a = scalar.ap
        return self.emit("dve", lambda E: E.scalar_tensor_tensor(out=out.ap, in0=in0.ap, scalar=a, in1=in1.ap,
                                                                 op0=op0, op1=op1), rd, [out])

    def copy(self, out, in_, e="dve"):
        if e == "act":
            return self.act(out, in_, AF.Copy)
        return self.emit(e, lambda E: E.tensor_copy(out=out.ap, in_=in_.ap), [in_], [out])

    def recip(self, out, in_):
        return self.emit("dve", lambda E: E.reciprocal(out=out.ap, in_=in_.ap), [in_], [out])

    def memset(self, out, val, e="dve"):
        return self.emit(e, lambda E: E.memset(out.ap, val), [], [out])


A_Q, A_K, A_V, A_G = 0, 256, 512, 768
B_R, B_K, B_V, B_GT, B_WL, B_AL = 1024, 1280, 1536, 1792, 2048, 2176
C_Q, C_K, C_V, C_G = 2304, 2560, 2816, 3072
D_BASE, D_G = 3328, 5632
MG = 5888
R_CQ, R_CK, R_D = 9984, 10240, 10496
NCH = WCOLS // 128


def _rot_perm():
    cols = []
    for base in (C_Q, C_K):
        for c in range(256):
            j = c % 32
            cols.append(base + c - j + (j + 16) % 32)
    for g in range(3):
        for part in (0, 256):
            base = D_BASE + g * 768 + part
            for c in range(256):
                j = c % 64
                cols.append(base + c - j + (j + 32) % 64)
    return np.asarray(cols, np.int64)


def _rope_tables():
    t = np.arange(T, dtype=np.float32)
    out = []
    for d in (32, 64):
        half = d // 2
        inv = np.power(np.float32(10000.0), -np.arange(half, dtype=np.float32) / np.float32(half)).astype(np.float32)
        ang = (t[:, None] * inv[None, :]).astype(np.float32)
        cos = np.cos(ang).astype(np.float32)
        sin = np.sin(ang).astype(np.float32)
        p = np.arange(128)
        j = p % d
        cosT = cos[:, j % half].T
        sgn = np.where(j < half, -1.0, 1.0).astype(np.float32)
        sinT = (sin[:, j % half].T * sgn[:, None]).astype(np.float32)
        out += [np.ascontiguousarray(cosT), np.ascontiguousarray(sinT)]
    return out


class Ctx:
    pass


def dram(nc, name, shape, dtype, kind):
    return nc.dram_tensor(name, list(shape), dtype, kind=kind).ap()


def build(nseq, parts=("A", "B", "C", "D"), dbg=False):
    nc = bass.Bass("TRN2", target_bir_lowering=False)
    fw = FW(nc)
    c = Ctx()
    c.nc, c.fw, c.parts, c.dbg = nc, fw, parts, dbg
    IN, OUT, INT = "ExternalInput", "ExternalOutput", "Internal"
    c.x = dram(nc, "x", [nseq, T, DM], F32, IN)
    c.y = dram(nc, "y", [nseq, T, DM], F32, OUT)
    c.w_in = dram(nc, "w_in", [2, DM, IN_COLS], F32, IN)
    c.w_br = dram(nc, "w_branch", [2, 4, 256, DM], F32, IN)
    c.w_out = dram(nc, "w_out", [2, DM, DM], F32, IN)
    c.b_fm = dram(nc, "b_fm", [2, 128, NCH], F32, IN)
    c.b_cat = dram(nc, "b_cat", [2, WCOLS], F32, IN)
    c.vecs = dram(nc, "vecs", [8, DM], F32, IN)
    c.ident_b = dram(nc, "ident_b", [128, 128], BF16, IN)
    c.ident_f = dram(nc, "ident_f", [128, 128], F32, IN)
    c.rope = dram(nc, "rope", [4, 128, T], F32, IN)
    c.bandm = dram(nc, "bandm", [128, 3, 4, 128], BF16, IN)
    c.na_bias = dram(nc, "na_bias", [2, 128, 14 * 256], F32, IN)
    c.na_mask = dram(nc, "na_mask", [128, 14 * 256], F32, IN)
    c.rwp = dram(nc, "rwp", [2, 128, 64], F32, IN)
    c.rw_w2 = dram(nc, "rw_w2", [2, 128, 256], F32, IN)
    c.rw_a2 = dram(nc, "rw_a2", [2, 128, 256], F32, IN)
    c.df_lam = dram(nc, "df_lam", [2, 128], F32, IN)
    c.trimask = dram(nc, "trimask", [64, 2, 192], F32, IN)
    c.blk1 = dram(nc, "blk1", [128, 128], F32, IN)
    c.perms = dram(nc, "perms", [2, 128, 128], BF16, IN)
    c.wbf = dram(nc, "wbf", [2, DM, WCOLS], BF16, INT)
    c.wbr_bf = dram(nc, "wbr_bf", [2, 4, 256, DM], BF16, INT)
    c.wout_bf = dram(nc, "wout_bf", [2, DM, DM], BF16, INT)
    c.xres = dram(nc, "xres", [T, DM], F32, INT)
    c.tr_wbf = [[Tr() for _ in range(NCH)] for _ in range(2)]
    c.tr_wbr = [Tr(), Tr()]
    c.tr_wout = [Tr(), Tr()]
    c.tr_xres = [Tr() for _ in range(16)]
    c.tr_in = Tr()
    c.tr_y = Tr()
    if dbg:
        c.dbg_out = dram(nc, "dbg", [128, 8, T], BF16, OUT)
        c.tr_dbg = Tr()

    def sb(name, shape, dtype, nslots=1):
        return Buf(nc.alloc_sbuf_tensor(name, list(shape), dtype), nslots)

    c.sb = sb
    c.xT = sb("xT", [128, 8, T], BF16)
    c.yT = sb("yT", [128, 8, T], BF16, nslots=4)
    c.wbuf = sb("wbuf", [128, 2, 8, 512], BF16, nslots=2)
    c.identb = sb("identb", [128, 128], BF16)
    c.identf = sb("identf", [128, 128], F32)
    c.bfm = sb("bfm", [128, 2, NCH], F32)
    c.arena = nc.alloc_sbuf_tensor("arena", [128, ARENA_F32], F32)
    c.arena_off = 0
    c.arena_summ = {}
    c.arena_live = []
    c.ps = [Buf(nc.alloc_psum_tensor("ps%d" % i, [128, 512], F32)) for i in range(8)]
    for b_ in c.ps:
        b_.trs = [Tr(excl=True)]
    c.ps_i = [0, 0]

    fw.dma(c.identb[:, :], V(c.ident_b[:, :], [c.tr_in]))
    fw.dma(c.identf[:, :], V(c.ident_f[:, :], [c.tr_in]))
    for l in range(2):
        fw.dma(c.bfm[:, l, :], V(c.b_fm[l], [c.tr_in]))

    convert_weights(c)
    for s in range(nseq):
        stage0(c, s)
        for l in range(2):
            layer(c, s, l)
    for i in range(len(fw.dma_sem)):
        if fw.dma_val[i] > 0:
            fw._wait("sp", (fw.dma_sem[i], fw.dma_val[i], "dma"))
    return nc, fw


def _arena_retire(c):
    summ = c.arena_summ
    for buf in c.arena_live:
        for tr in buf.trs:
            evs = list(tr.r.values())
            if tr.w is not None:
                evs.append(tr.w)
            for ev in evs:
                key = ev[2]
                old = summ.get(key)
                if old is None or (ev[0], ev[1]) > (old[0], old[1]):
                    summ[key] = ev
    c.arena_live = []


def arena_reset(c):
    _arena_retire(c)
    c.arena_off = 0


def arena_release(c, mark):
    _arena_retire(c)
    c.arena_off = mark


def alloc(c, shape, dtype, nslots=1):
    n = int(np.prod(shape[1:]))
    n4 = (n + 1) // 2 if dtype == BF16 else n
    n4 = (n4 + 7) // 8 * 8
    assert c.arena_off + n4 <= ARENA_F32, ("arena overflow", c.arena_off, n4)
    ap = c.arena[0:shape[0], c.arena_off:c.arena_off + n4]
    c.arena_off += n4
    if dtype == BF16:
        ap = ap.bitcast(BF16)[:, 0:n]
    else:
        ap = ap[:, 0:n]
    if len(shape) > 2:
        names = " ".join("d%d" % i for i in range(len(shape) - 1))
        kw = {"d%d" % i: shape[i + 1] for i in range(len(shape) - 1)}
        ap = ap.rearrange("p (%s) -> p %s" % (names, names), **kw)
    b = Buf(ap, nslots)
    for tr in b.trs:
        tr.r = dict(c.arena_summ)
    c.arena_live.append(b)
    return b


def load_vecs(c, rows):
    for i, r in enumerate(rows):
        c.fw.dma(c.vec_bc[:, i, :], V(c.vecs[r:r + 1, :].partition_broadcast(128), [c.tr_in]))


def psA(c):
    i = c.ps_i[0]
    c.ps_i[0] = (i + 1) % 4
    return c.ps[i]


def psB(c):
    i = c.ps_i[1]
    c.ps_i[1] = (i + 1) % 4
    return c.ps[4 + i]


def convert_weights(c):
    fw = c.fw
    rd = [c.tr_in]
    for l in range(2):
        blocks = [(0, 512 * i, 512) for i in range(11)] + [(0, 5632, 256)]
        for (_, c0, n) in blocks:
            trs = c.tr_wbf[l][c0 // 128:(c0 + n) // 128]
            fw.dma(V(c.wbf[l, :, c0:c0 + n], trs), V(c.w_in[l, :, c0:c0 + n], rd), e="pool")
        dst = c.wbf[l, :, MG:IN_COLS].rearrange("k (dc b j) -> k dc b j", dc=8, b=4)
        for b in range(4):
            src = c.w_in[l, :, MG + b * 1024:MG + (b + 1) * 1024].rearrange("k (dc j) -> k dc j", dc=8)
            fw.dma(V(dst[:, :, b, :], c.tr_wbf[l][MG // 128:IN_COLS // 128]), V(src, rd), e="pool")
        for b in range(4):
            fw.dma(V(c.wbr_bf[l, b], [c.tr_wbr[l]]), V(c.w_br[l, b], rd), e="pool")
        for i in range(2):
            fw.dma(V(c.wout_bf[l, :, 512 * i:512 * (i + 1)], [c.tr_wout[l]]),
                   V(c.w_out[l, :, 512 * i:512 * (i + 1)], rd), e="pool")


def load_w(c, l, c0, n):
    fw = c.fw
    slot = getattr(c, "_wslot", 0)
    c._wslot = 1 - slot
    src = c.wbf[l, :, c0:c0 + n].rearrange("(kc p) n -> p kc n", p=128)
    trs = c.tr_wbf[l][c0 // 128:(c0 + n + 127) // 128]
    fw.dma(c.wbuf.s(slot, (slice(None), slot, slice(None), slice(0, n))), V(src, trs))
    return slot


def wv(c, slot, kc, j0, n):
    return c.wbuf.s(slot, (slice(None), slot, kc, slice(j0, j0 + n)))


def proj_fm(c, l, col_list, consume):
    fw = c.fw
    groups = []
    for i, c0 in enumerate(col_list):
        if groups and groups[-1][0] + groups[-1][1] == c0 and groups[-1][1] < 512:
            groups[-1][1] += 128
            groups[-1][2].append(i)
        else:
            groups.append([c0, 128, [i]])
    slots = [None] * len(groups)
    slots[0] = load_w(c, l, groups[0][0], groups[0][1])
    for gi, (g0, gn, idxs) in enumerate(groups):
        if gi + 1 < len(groups):
            slots[gi + 1] = load_w(c, l, groups[gi + 1][0], groups[gi + 1][1])
        for j, i in enumerate(idxs):
            for tb in range(NTB):
                ps = psA(c)
                for kc in range(8):
                    fw.mm(ps[:, :], wv(c, slots[gi], kc, j * 128, 128), c.xT[:, kc, tb * 512:(tb + 1) * 512],
                          start=(kc == 0), stop=(kc == 7))
                consume(i, tb, ps)


def ln_stats(c, z, sl):
    fw = c.fw
    st = c.ln_st.s(sl, (slice(None), sl, slice(None), slice(None)))
    mv = c.ln_mv
    tr = [mv.trs[sl]]
    fw.emit("dve", lambda E: E.bn_stats(out=st.ap[:, 0, :], in_=z.ap[:, 0:512]), [z], [st])
    fw.emit("dve", lambda E: E.bn_stats(out=st.ap[:, 1, :], in_=z.ap[:, 512:1024]), [z], [st])
    m = lambda a, b: V(mv.h[:, sl, a:b], tr)
    fw.emit("dve", lambda E: E.bn_aggr(out=mv.h[:, sl, 0:2], in_=st.ap), [st], [m(0, 2)])
    fw.ts(m(2, 3), m(1, 2), LN_EPS, None, ALU.add)
    fw.act(m(3, 4), m(2, 3), AF.Sqrt)
    fw.recip(m(4, 5), m(3, 4))
    fw.stt(m(5, 6), m(0, 1), -1.0, m(4, 5), ALU.mult, ALU.mult)


def ln_apply(c, z, gi, bi, out, sl):
    fw = c.fw
    tr = [c.ln_mv.trs[sl]]
    fw.act(out, z, AF.Identity, bias=V(c.ln_mv.h[:, sl, 5:6], tr), scale=V(c.ln_mv.h[:, sl, 4:5], tr))
    fw.tt(out, out, c.vec_bc[:, gi, :], ALU.mult)
    fw.tt(out, out, c.vec_bc[:, bi, :], ALU.add, e="pool")


def to_xT(c, rows, tt):
    fw = c.fw
    xb = c.xb_t
    fw.copy(xb[:, :], rows, e="act")
    ps = psB(c)
    psb = V(ps.h[:, :].bitcast(BF16), ps.trs)
    for k in range(8):
        fw.transpose(V(psb.ap[:, k * 128:(k + 1) * 128], ps.trs), xb[:, k * 128:(k + 1) * 128], c.identb[:, :])
    fw.copy(c.xT[:, :, tt * 128:(tt + 1) * 128], V(psb.ap.rearrange("p (k t) -> p k t", k=8), ps.trs))


def run_pipeline3(sa, sb, sc, n):
    for t in range(-2, n):
        if 0 <= t + 2 < n:
            sa(t + 2)
        if 0 <= t + 1 < n:
            sb(t + 1)
        if 0 <= t < n:
            sc(t)


def ln_allocs(c):
    c.ln_st = alloc(c, [128, 2, 2, 6], F32, nslots=2)
    c.ln_mv = alloc(c, [128, 2, 8], F32, nslots=2)
    c.xb_t = alloc(c, [128, DM], BF16)
    c.zt = alloc(c, [128, 3, DM], F32, nslots=3)
    c.ot = alloc(c, [128, 2, DM], F32, nslots=2)
    c.vec_bc = alloc(c, [128, 3, DM], F32)


def stage0(c, s):
    fw = c.fw
    arena_reset(c)
    ln_allocs(c)
    load_vecs(c, [0, 1])
    def zslot(tt):
        return c.zt.s(tt % 3, (slice(None), tt % 3, slice(None)))

    fw.dma(zslot(0), V(c.x[s, 0:128, :], [c.tr_in]))

    def stage_a(tt):
        if tt + 1 < 16:
            fw.dma(zslot(tt + 1), V(c.x[s, (tt + 1) * 128:(tt + 2) * 128, :], [c.tr_in]))
        ln_stats(c, zslot(tt), tt % 2)

    def stage_b1(tt):
        sl = tt % 2
        ln_apply(c, zslot(tt), 0, 1, c.ot.s(sl, (slice(None), sl, slice(None))), sl)

    def stage_b2(tt):
        sl = tt % 2
        o = c.ot.s(sl, (slice(None), sl, slice(None)))
        fw.dma(V(c.xres[tt * 128:(tt + 1) * 128, :], [c.tr_xres[tt]]), o)
        to_xT(c, o, tt)

    run_pipeline3(stage_a, stage_b1, stage_b2, 16)


def gate_branch(c, l, gcol, bi):
    fw = c.fw

    def consume(i, tb, ps):
        ch = (gcol // 128) + i
        sg = c.g_sig.s(tb % 2, (slice(None), tb % 2, slice(None)))
        fw.act(sg, ps[:, :], AF.Sigmoid, bias=c.bfm[:, l, ch:ch + 1])
        fw.stt(sg, ps[:, :], c.bfm[:, l, ch:ch + 1], sg, ALU.add, ALU.mult)
        yv = c.yT.s(bi, (slice(None), bi * 2 + i, slice(tb * 512, (tb + 1) * 512)))
        fw.tt(yv, yv, sg, ALU.mult, e="pool")

    proj_fm(c, l, [gcol, gcol + 128], consume)


def final_stage(c, s, l):
    fw = c.fw
    arena_reset(c)
    ln_allocs(c)
    c.mergedT = alloc(c, [128, 8, T], BF16)
    c.wbr_sb = alloc(c, [128, 4, 2, DM], BF16)
    c.wout_sb = alloc(c, [128, 8, DM], BF16)
    c.f_sig = alloc(c, [128, 2, 512], F32, nslots=2)
    c.f_acc = alloc(c, [128, 2, 512], F32, nslots=2)
    c.f_tmp = alloc(c, [128, 2, 512], F32, nslots=2)
    load_vecs(c, [2 + 3 * l, 3 + 3 * l, 4 + 3 * l])
    c.ones_row = alloc(c, [1, 128], BF16)
    c.bout_row = alloc(c, [1, DM], BF16)
    bstage = alloc(c, [1, DM], F32)
    fw.memset(c.ones_row[0:1, :], 1.0)
    fw.dma(bstage[0:1, :], V(c.vecs[2 + 3 * l:3 + 3 * l, :], [c.tr_in]))
    fw.copy(c.bout_row[0:1, :], bstage[0:1, :])
    for b in range(4):
        fw.dma(c.wbr_sb[:, b, :, :], V(c.wbr_bf[l, b].rearrange("(kc p) d -> p kc d", p=128), [c.tr_wbr[l]]))
    fw.dma(c.wout_sb[:, :, :], V(c.wout_bf[l].rearrange("(kc p) d -> p kc d", p=128), [c.tr_wout[l]]))
    slots = [None] * 8
    slots[0] = load_w(c, l, MG, 512)
    for dc in range(8):
        if dc + 1 < 8:
            slots[dc + 1] = load_w(c, l, MG + (dc + 1) * 512, 512)
        for tb in range(NTB):
            tok = slice(tb * 512, (tb + 1) * 512)
            ai = tb % 2
            acc = c.f_acc.s(ai, (slice(None), ai, slice(None)))
            for b in range(4):
                pg = psA(c)
                for kc in range(8):
                    fw.mm(pg[:, :], wv(c, slots[dc], kc, b * 128, 128), c.xT[:, kc, tok], start=(kc == 0), stop=(kc == 7))
                ch = MG // 128 + b * 8 + dc
                si = b % 2
                sg = c.f_sig.s(si, (slice(None), si, slice(None)))
                fw.act(sg, pg[:, :], AF.Sigmoid, bias=c.bfm[:, l, ch:ch + 1])
                pp = psB(c)
                for kc in range(2):
                    fw.mm(pp[:, :], c.wbr_sb[:, b, kc, dc * 128:(dc + 1) * 128],
                          c.yT.s(b, (slice(None), b * 2 + kc, tok)), start=(kc == 0), stop=(kc == 1))
                if b == 0:
                    fw.tt(acc, pp[:, :], sg, ALU.mult)
                else:
                    tm = c.f_tmp.s(si, (slice(None), si, slice(None)))
                    fw.tt(tm, pp[:, :], sg, ALU.mult)
                    if b < 3:
                        fw.tt(acc, acc, tm, ALU.add, e="pool")
                    else:
                        fw.tt(c.mergedT[:, dc, tok], acc, tm, ALU.add, e="pool")
    def zslot(tt):
        return c.zt.s(tt % 3, (slice(None), tt % 3, slice(None)))

    fw.dma(zslot(0), V(c.xres[0:128, :], [c.tr_xres[0]]))

    def stage_a(tt):
        z = zslot(tt)
        if tt + 1 < 16:
            fw.dma(zslot(tt + 1), V(c.xres[(tt + 1) * 128:(tt + 2) * 128, :], [c.tr_xres[tt + 1]]))
        for hf in range(2):
            po = psA(c)
            for kc in range(8):
                fw.mm(po[:, :], c.mergedT[:, kc, tt * 128:(tt + 1) * 128], c.wout_sb[:, kc, hf * 512:(hf + 1) * 512],
                      start=(kc == 0), stop=False)
            fw.mm(po[:, :], c.ones_row[0:1, :], c.bout_row[0:1, hf * 512:(hf + 1) * 512], start=False, stop=True)
            zh = V(z.ap[:, hf * 512:(hf + 1) * 512], z.trs)
            fw.stt(zh, zh, ALPHA, po[:, :], ALU.mult, ALU.add)
        ln_stats(c, z, tt % 2)

    def stage_b1(tt):
        sl = tt % 2
        ln_apply(c, zslot(tt), 1, 2, c.ot.s(sl, (slice(None), sl, slice(None))), sl)

    def stage_b2(tt):
        sl = tt % 2
        o = c.ot.s(sl, (slice(None), sl, slice(None)))
        if l == 0:
            fw.dma(V(c.xres[tt * 128:(tt + 1) * 128, :], [c.tr_xres[tt]]), o)
            to_xT(c, o, tt)
        else:
            fw.dma(V(c.y[s, tt * 128:(tt + 1) * 128, :], [c.tr_y]), o)

    run_pipeline3(stage_a, stage_b1, stage_b2, 16)


def layer(c, s, l):
    fw = c.fw
    if "B" in c.parts:
        branch_b(c, s, l)
    for bi, name in enumerate("ABCD"):
        if name not in c.parts:
            fw.memset(c.yT.s(bi, (slice(None), slice(bi * 2, bi * 2 + 2), slice(None))), 0.0, e="pool")
    if "A" in c.parts:
        branch_a(c, s, l)
    if "C" in c.parts:
        branch_c(c, s, l)
    if "D" in c.parts:
        branch_d(c, s, l)
    if c.dbg and l == 0 and s == 0:
        fw.dma(V(c.dbg_out[:, :, :], [c.tr_dbg]), c.yT[:, :, :])
    final_stage(c, s, l)


def rope_proj(c, l, qcol, perm, cosT, sinT, dst, tmp, t2b, qbb):
    fw = c.fw

    def consume(i, tb, ps):
        tok = slice(tb * 512, (tb + 1) * 512)
        ch = qcol // 128 + i
        sl = (i * NTB + tb) % 2
        qb = qbb.s(sl, (slice(None), sl, slice(None)))
        fw.act(qb, ps[:, :], AF.Identity, bias=c.bfm[:, l, ch:ch + 1])
        t1 = t2b.s(sl, (slice(None), sl, slice(None)))
        fw.stt(t1, ps[:, :], c.bfm[:, l, ch:ch + 1], cosT[:, tok], ALU.add, ALU.mult)
        pr = psB(c)
        fw.mm(pr[:, :], perm, qb)
        t2 = V(tmp.h[:, sl, :], [tmp.trs[0]])
        fw.tt(t2, pr[:, :], sinT[:, tok], ALU.mult)
        fw.tt(dst[:, i, tok], t1, t2, ALU.add, e="pool")

    proj_fm(c, l, [qcol, qcol + 128], consume)


def plain_proj(c, l, col, dst):
    fw = c.fw

    def consume(i, tb, ps):
        ch = col // 128 + i
        fw.ts(dst[:, i, tb * 512:(tb + 1) * 512], ps[:, :], c.bfm[:, l, ch:ch + 1], None, ALU.add)

    proj_fm(c, l, [col, col + 128], consume)


def vaug_init(c, vaug):
    v5 = vaug.h.rearrange("p j (hp par) e -> p j hp par e", par=2)
    c.fw.memset(V(v5[:, :, :, 0, 64:128], vaug.trs), 1.0, e="pool")
    c.fw.memset(V(v5[:, :, :, 1, 0:64], vaug.trs), 1.0, e="pool")


def v_proj(c, l, vcol, vaug, vbias, tok_sel, ntiles=16):
    fw = c.fw
    slot = load_w(c, l, vcol, 256)
    fw.dma(vbias[:, :], V(c.b_cat[l:l + 1, vcol:vcol + 256].partition_broadcast(128), [c.tr_in]))
    v5 = vaug.h.rearrange("p j (hp par) e -> p j hp par e", par=2)
    b4 = vbias.h.rearrange("p (hp par e) -> p hp par e", hp=2, par=2)
    for j in range(ntiles):
        ps = psA(c)
        for kc in range(8):
            fw.mm(ps[:, 0:256], V(c.xT.h[:, kc, tok_sel(j)], c.xT.trs), wv(c, slot, kc, 0, 256),
                  start=(kc == 0), stop=(kc == 7))
        p4 = ps.h[:, 0:256].rearrange("p (hp par e) -> p hp par e", hp=2, par=2)
        fw.tt(V(v5[:, j, :, 0, 0:64], vaug.trs), V(p4[:, :, 0, :], ps.trs), V(b4[:, :, 0, :], vbias.trs), ALU.add)
        fw.tt(V(v5[:, j, :, 1, 64:128], vaug.trs), V(p4[:, :, 1, :], ps.trs), V(b4[:, :, 1, :], vbias.trs), ALU.add)


def branch_c(c, s, l):
    fw = c.fw
    arena_reset(c)
    lam_init = 0.8 - 0.6 * math.exp(-0.3 * l)
    cosT = alloc(c, [128, T], F32)
    sinT = alloc(c, [128, T], F32)
    fw.dma(cosT[:, :], V(c.rope[0], [c.tr_in]))
    fw.dma(sinT[:, :], V(c.rope[1], [c.tr_in]))
    qT = alloc(c, [128, 2, T], BF16)
    kT = alloc(c, [128, 2, T], BF16)
    tmp = alloc(c, [128, 4, 512], F32)
    t2b = alloc(c, [128, 2, 512], F32, nslots=2)
    vaug = alloc(c, [128, 16, 4, 128], BF16)
    vbias = alloc(c, [128, 256], F32)
    c.g_sig = alloc(c, [128, 2, 512], F32, nslots=2)
    pt = alloc(c, [128, 4, 512], BF16, nslots=4)
    sm = alloc(c, [128, 16], F32)
    lamt = alloc(c, [128, 128], F32)
    prm = alloc(c, [128, 64], F32)
    ofull = alloc(c, [128, 512], F32)
    osq = alloc(c, [128, 512], F32)
    w1 = alloc(c, [128, 2, 512], F32, nslots=2)
    w2 = alloc(c, [128, 2, 512], F32, nslots=2)
    blk = alloc(c, [128, 128], F32)
    permC = alloc(c, [128, 128], BF16)
    qbb = alloc(c, [128, 2, 512], BF16, nslots=2)
    fw.dma(permC[:, :], V(c.perms[0], [c.tr_in]))
    fw.dma(blk[:, :], V(c.blk1[:, :], [c.tr_in]))
    fw.dma(prm[:, :], V(c.rwp[l], [c.tr_in]))
    fw.dma(lamt[:, :], V(c.df_lam[l:l + 1, :].partition_broadcast(128), [c.tr_in]))
    fw.tt(lamt[:, 0:32], lamt[:, 0:32], lamt[:, 32:64], ALU.mult)
    fw.tt(lamt[:, 64:96], lamt[:, 64:96], lamt[:, 96:128], ALU.mult)
    fw.emit("dve", lambda E: E.tensor_reduce(out=sm.h[:, 0:1], in_=lamt.h[:, 0:32], axis=mybir.AxisListType.X,
                                             op=ALU.add), [lamt[:, :]], [sm[:, :]])
    fw.emit("dve", lambda E: E.tensor_reduce(out=sm.h[:, 1:2], in_=lamt.h[:, 64:96], axis=mybir.AxisListType.X,
                                             op=ALU.add), [lamt[:, :]], [sm[:, :]])
    fw.act(sm[:, 2:4], sm[:, 0:2], AF.Exp)
    fw.tt(sm[:, 4:5], sm[:, 3:4], sm[:, 2:3], ALU.subtract)
    fw.ts(sm[:, 5:6], sm[:, 4:5], -lam_init, None, ALU.add)
    fw.ts(sm[:, 6:7], prm[:, 0:1], 1.0 - lam_init, None, ALU.mult)

    vaug_init(c, vaug)
    rope_proj(c, l, C_Q, permC[:, :], cosT, sinT, qT, tmp, t2b, qbb)
    rope_proj(c, l, C_K, permC[:, :], cosT, sinT, kT, tmp, t2b, qbb)
    v_proj(c, l, C_V, vaug, vbias, lambda j: slice(j * 128, (j + 1) * 128))

    qz = [alloc(c, [128, 2, T], BF16), alloc(c, [128, 2, T], BF16)]
    for i in range(2):
        fw.ts(qz[i][:, :, :], qT[:, :, :], prm[:, 29 + i:30 + i], None, ALU.mult, e="dve")
    scale = 32 ** -0.5
    items = [(hp, tb, i, kt, par) for hp in range(2) for tb in range(NTB) for i in range(2) for kt in range(16)
             for par in range(2)]
    pbuf = {}
    state = {}

    def stage1(n):
        hp, tb, i, kt, par = items[n]
        tok = slice(tb * 512, (tb + 1) * 512)
        pb = par * 64
        sc = psA(c)
        fw.mm(sc[:, :], kT[pb:pb + 64, hp, kt * 128:(kt + 1) * 128], qz[i][pb:pb + 64, hp, tok])
        p = pt.s(n % 4, (slice(None), n % 4, slice(None)))
        fw.act(p, sc[:, :], AF.Exp, scale=scale)
        pbuf[n] = p

    def stage2(n):
        hp, tb, i, kt, par = items[n]
        tok = slice(tb * 512, (tb + 1) * 512)
        h = hp * 2 + par
        if kt == 0:
            state[("acc", par)] = psB(c)
        acc = state[("acc", par)]
        fw.mm(acc[:, :], vaug[:, kt, h, :], pbuf.pop(n), start=(kt == 0), stop=(kt == 15))
        if kt == 15:
            olo, dlo = (0, 64) if par == 0 else (64, 0)
            o = slice(olo, olo + 64)
            d = slice(dlo, dlo + 64)
            r1 = w1.s(i, (o, i, slice(None)))
            fw.recip(r1, acc[d, :])
            t1 = w2.s(i, (o, i, slice(None)))
            fw.tt(t1, acc[o, :], r1, ALU.mult)
            if i == 1:
                t0 = w2.s(0, (o, 0, slice(None)))
                fw.stt(ofull[o, :], t1, sm[o, 5:6], t0, ALU.mult, ALU.add)
                if par == 1:
                    fw.tt(osq[:, :], ofull[:, :], ofull[:, :], ALU.mult)
                    ss = psB(c)
                    fw.mm(ss[:, :], blk[:, :], osq[:, :])
                    fw.ts(osq[:, :], ss[:, :], 1.0 / 64.0, 1e-5, ALU.mult, ALU.add)
                    fw.act(osq[:, :], osq[:, :], AF.Sqrt)
                    rs_ = w1.s(0, (slice(None), 0, slice(None)))
                    fw.recip(rs_, osq[:, :])
                    fw.stt(c.yT.s(2, (slice(None), 4 + hp, tok)), ofull[:, :], sm[:, 6:7], rs_, ALU.mult, ALU.mult)

    npair = len(items) // 2
    for m in range(npair + 1):
        if m < npair:
            stage1(2 * m)
            stage1(2 * m + 1)
        if m >= 1:
            stage2(2 * m - 2)
            stage2(2 * m - 1)
    gate_branch(c, l, C_G, 2)


def _host_inputs(inp):
    f32 = np.float32
    g = lambda k: np.asarray(inp[k], f32)
    w_in = g("w_in")
    b_in = g("b_in")
    perm = _rot_perm()
    b_cat = np.ascontiguousarray(np.concatenate([b_in, b_in[:, perm]], axis=1))
    b_fm = np.ascontiguousarray(b_cat.reshape(2, NCH, 128).transpose(0, 2, 1))
    vecs = np.zeros((8, DM), f32)
    vecs[0], vecs[1] = g("ln0_g"), g("ln0_b")
    for l in range(2):
        vecs[2 + 3 * l], vecs[3 + 3 * l], vecs[4 + 3 * l] = g("b_out")[l], g("ln_g")[l], g("ln_b")[l]
    rope = np.stack(_rope_tables(), 0)
    i = np.arange(128)[:, None]
    j = np.arange(128)[None, :]
    band = np.stack([(i - j >= 64), (np.abs(i - j) <= 64), (j - i >= 64)], 1).astype(f32)
    bandm = np.ascontiguousarray(np.broadcast_to(band[:, :, None, :], (128, 3, 4, 128))).astype(ml_dtypes.bfloat16)
    rpb = g("na_rpb")
    kap = np.arange(2)[:, None, None, None, None]
    kc = np.arange(64)[None, :, None, None, None]
    oi = np.arange(14)[None, None, :, None, None]
    hh = np.asarray([0, 2, 1, 3])[None, None, None, :, None]
    qc = np.arange(64)[None, None, None, None, :]
    dr = (oi - 7) + kap
    dc = np.clip(kc - qc + 15, 0, 30)
    shp = (2, 64, 14, 4, 64)
    na_bias = np.stack([rpb[l][np.broadcast_to(hh, shp), np.broadcast_to(dr + 7, shp), np.broadcast_to(dc, shp)]
                        for l in range(2)], 0).reshape(2, 128, 14 * 256).astype(f32)
    cst = np.clip(qc - 8, 0, 48)
    ok = (kc >= cst) & (kc < cst + 16)
    na_mask = np.ascontiguousarray(np.broadcast_to(ok, shp)).reshape(128, 14 * 256).astype(f32)
    rwp = np.zeros((2, 128, 64), f32)
    p = np.arange(128)
    mu = g("rw_mu")
    for l in range(2):
        rwp[l, :, 0] = g("df_subln_g")[l][p % 64]
        for hp in range(2):
            ch = hp * 128 + p
            for q in range(4):
                rwp[l, :, 1 + 2 * q + hp] = mu[l, q * 256 + ch]
            for d in range(2):
                rwp[l, :, 11 + d * 2 + hp] = g("rw_w0")[l, d, ch]
                rwp[l, :, 15 + d * 2 + hp] = g("rw_a0")[l, d, ch]
            rwp[l, :, 19 + hp] = g("rw_kk")[l, ch]
            rwp[l, :, 21 + hp] = g("rw_ka")[l, ch]
            rwp[l, :, 23 + hp] = g("rw_rk")[l].reshape(256)[ch]
            rwp[l, :, 25 + hp] = g("rw_lnx_g")[l, ch]
            rwp[l, :, 27 + hp] = g("rw_lnx_b")[l, ch]
        rwp[l, :, 29] = ((p % 64) < 32)
        rwp[l, :, 30] = ((p % 64) >= 32)
        rwp[l, :, 9] = mu[l, 1024 + p]
        rwp[l, :, 10] = mu[l, 1152 + p]
    s_ = np.arange(64)[:, None]
    t_ = np.arange(64)[None, :]
    tri = np.zeros((64, 2, 2, 64), f32)
    tri[:, 0, 0], tri[:, 0, 1] = (s_ < t_), (s_ <= t_)
    tri[:, 1, 0], tri[:, 1, 1] = (s_ > t_), (s_ >= t_)
    blk1 = np.zeros((128, 128), f32)
    blk1[:64, :64] = 1
    blk1[64:, 64:] = 1
    perms = np.zeros((2, 128, 128), f32)
    pp = np.arange(128)
    for qi, dd in enumerate((32, 64)):
        jj = pp % dd
        partner = pp - jj + (jj + dd // 2) % dd
        perms[qi, partner, pp] = 1.0
    tri3 = np.zeros((64, 2, 192), f32)
    tri3[:, :, 0:128] = tri.reshape(64, 2, 128)
    tri3[:, 0, 128:192] = (s_ > t_)
    tri3[:, 1, 128:192] = (s_ < t_)
    return {
        "w_in": w_in, "w_branch": g("w_branch"), "w_out": g("w_out"),
        "b_fm": b_fm, "b_cat": b_cat, "vecs": vecs,
        "ident_b": np.eye(128, dtype=f32).astype(ml_dtypes.bfloat16), "ident_f": np.eye(128, dtype=f32),
        "rope": np.ascontiguousarray(rope), "bandm": bandm, "na_bias": na_bias, "na_mask": na_mask,
        "rwp": rwp, "rw_w2": np.ascontiguousarray(g("rw_w2").reshape(2, 128, 256)),
        "rw_a2": np.ascontiguousarray(g("rw_a2").reshape(2, 128, 256)),
        "df_lam": np.ascontiguousarray(g("df_lam").reshape(2, 128)),
        "trimask": tri3, "blk1": blk1, "perms": perms.astype(ml_dtypes.bfloat16),
    }


_NC_CACHE = {}


def kernel(**inputs):
    xp = np.asarray(inputs["x_prompt"], np.float32)
    xs = np.asarray(inputs["x_sample"], np.float32)
    shared = _host_inputs(inputs)
    ncores = 8
    if "nc" not in _NC_CACHE:
        _NC_CACHE["nc"] = build(6)[0]
    nc = _NC_CACHE["nc"]
    in_maps = []
    for k in range(ncores):
        xk = np.concatenate([xp[4 * k:4 * k + 4], xs[2 * k:2 * k + 2]], axis=0)
        m = dict(shared)
        m["x"] = np.ascontiguousarray(xk)
        in_maps.append(m)
    res = run_bass_kernel_spmd(nc, in_maps, core_ids=list(range(ncores)))
    yp = np.empty_like(xp)
    ys = np.empty_like(xs)
    for k in range(ncores):
        yk = np.asarray(res.results[k]["y"], np.float32)
        yp[4 * k:4 * k + 4] = yk[0:4]
        ys[2 * k:2 * k + 2] = yk[4:6]
    return (yp, ys)


def branch_a(c, s, l):
    fw = c.fw
    arena_reset(c)
    qT = alloc(c, [128, 2, T], BF16)
    kT = alloc(c, [128, 2, T], BF16)
    vaug0 = alloc(c, [128, 16, 4, 128], BF16)
    vaug1 = alloc(c, [128, 15, 4, 128], BF16)
    vbias = alloc(c, [128, 256], F32)
    stg = alloc(c, [128, 14 * 256], F32)
    msk = alloc(c, [128, 14 * 256], F32)
    Mb = alloc(c, [128, 14, 256], BF16)
    pt = alloc(c, [128, 2, 4, 256], BF16, nslots=2)
    rd = alloc(c, [128, 2, 256], F32, nslots=2)
    c.g_sig = alloc(c, [128, 2, 512], F32, nslots=2)
    fw.dma(stg[:, :], V(c.na_bias[l], [c.tr_in]))
    fw.dma(msk[:, :], V(c.na_mask[:, :], [c.tr_in]))
    fw.act(stg[:, :], stg[:, :], AF.Exp)
    fw.tt(V(Mb.h.rearrange("p a b -> p (a b)"), Mb.trs), stg[:, :], msk[:, :], ALU.mult)
    vaug_init(c, vaug0)
    vaug_init(c, vaug1)
    plain_proj(c, l, A_Q, qT)
    plain_proj(c, l, A_K, kT)
    v_proj(c, l, A_V, vaug0, vbias, lambda j: slice(j * 128, (j + 1) * 128), 16)
    v_proj(c, l, A_V, vaug1, vbias, lambda j: slice(64 + j * 128, 64 + (j + 1) * 128), 15)
    rows = {}

    def stage1(r):
        rs = min(max(r - 4, 0), 24)
        sl = r % 2
        qs = slice(64 * r, 64 * r + 64)
        tiles = []
        for j in range(4):
            kr0 = rs + 2 * j
            oi = (rs - r + 2 * j) + 7
            p = pt.s(sl, (slice(None), sl, j, slice(None)))
            scs = [psA(c), psA(c)]
            for hp in range(2):
                for par in range(2):
                    pb = par * 64
                    fw.mm(scs[par][:, hp * 64:(hp + 1) * 64], kT[pb:pb + 64, hp, 64 * kr0:64 * kr0 + 128],
                          qT[pb:pb + 64, hp, qs])
            for par in range(2):
                fw.act(V(p.ap[:, par * 128:(par + 1) * 128], p.trs), scs[par][:, 0:128], AF.Exp, scale=0.125)
            fw.tt(p, p, Mb[:, oi, :], ALU.mult, e=("pool" if j % 2 == 0 else "dve"))
            tiles.append((p, (vaug0, kr0 // 2) if kr0 % 2 == 0 else (vaug1, (kr0 - 1) // 2)))
        rows[r] = tiles

    def stage2(r):
        tiles = rows.pop(r)
        sl = r % 2
        qs = slice(64 * r, 64 * r + 64)
        acc = psB(c)
        for h in range(4):
            for j, (p, (va, ti)) in enumerate(tiles):
                pc = (h % 2) * 128 + (h // 2) * 64
                fw.mm(acc[:, h * 64:(h + 1) * 64], va[:, ti, h, :], V(p.ap[:, pc:pc + 64], p.trs),
                      start=(j == 0), stop=(j == 3))
        a4 = acc.h[:, 0:256].rearrange("p (hp par q) -> p hp par q", hp=2, par=2)
        r4 = rd.h[:, sl, :].rearrange("p (hp par q) -> p hp par q", hp=2, par=2)
        rtr = [rd.trs[sl]]
        fw.recip(V(r4[0:64, :, 0, :], rtr), V(a4[64:128, :, 0, :], acc.trs))
        fw.recip(V(r4[64:128, :, 1, :], rtr), V(a4[0:64, :, 1, :], acc.trs))
        fw.tt(c.yT.s(0, (slice(0, 64), slice(0, 2), qs)), V(a4[0:64, :, 0, :], acc.trs), V(r4[0:64, :, 0, :], rtr), ALU.mult)
        fw.tt(c.yT.s(0, (slice(64, 128), slice(0, 2), qs)), V(a4[64:128, :, 1, :], acc.trs), V(r4[64:128, :, 1, :], rtr),
              ALU.mult)

    stage1(0)
    for r in range(32):
        if r + 1 < 32:
            stage1(r + 1)
        stage2(r)
    gate_branch(c, l, A_G, 0)


def branch_d(c, s, l):
    fw = c.fw
    arena_reset(c)
    cosT = alloc(c, [128, T], F32)
    sinT = alloc(c, [128, T], F32)
    fw.dma(cosT[:, :], V(c.rope[2], [c.tr_in]))
    fw.dma(sinT[:, :], V(c.rope[3], [c.tr_in]))
    acc = alloc(c, [128, 4, T], F32)
    qT = alloc(c, [128, 2, T], BF16)
    kT = alloc(c, [128, 2, T], BF16)
    tmp = alloc(c, [128, 4, 512], F32)
    t2b = alloc(c, [128, 2, 512], F32, nslots=2)
    c.g_sig = t2b
    vaug = alloc(c, [128, 16, 4, 128], BF16)
    vbias = alloc(c, [128, 256], F32)
    pt = alloc(c, [128, 2, 3, 512], BF16, nslots=2)
    bm = alloc(c, [128, 3, 512], BF16)
    permD = alloc(c, [128, 128], BF16)
    qbb = alloc(c, [128, 2, 512], BF16, nslots=2)
    fw.dma(permD[:, :], V(c.perms[1], [c.tr_in]))
    fw.dma(bm[:, :, :], V(c.bandm.rearrange("p a h q -> p a (h q)"), [c.tr_in]))
    vaug_init(c, vaug)
    unit = 0
    for g, dil in enumerate((1, 4, 16)):
        nqb = T // dil // 128
        base = D_BASE + g * 768
        rope_proj(c, l, base, permD[:, :], cosT, sinT, qT, tmp, t2b, qbb)
        rope_proj(c, l, base + 256, permD[:, :], cosT, sinT, kT, tmp, t2b, qbb)

        def tsel(j, dil=dil, nqb=nqb):
            rho, jb = divmod(j, nqb)
            st = 128 * jb * dil + rho
            return slice(st, st + 127 * dil + 1, dil)

        v_proj(c, l, base + 512, vaug, vbias, tsel, 16)
        units = [(rho, qb) for rho in range(dil) for qb in range(nqb)]
        ust = {}

        def stage1(ui, units=units, tsel=tsel, nqb=nqb, ust=ust):
            rho, qb = units[ui]
            qs = tsel(rho * nqb + qb)
            kts = [(jb, mt) for (jb, mt) in ((qb - 1, 0), (qb, 1), (qb + 1, 2)) if 0 <= jb < nqb]
            sl = (unit0 + ui) % 2
            ps_ = []
            for idx, (jb, mt) in enumerate(kts):
                ks = tsel(rho * nqb + jb)
                p = pt.s(sl, (slice(None), sl, idx, slice(None)))
                scs = [psA(c), psA(c)]
                for hp in range(2):
                    for par in range(2):
                        pb = par * 64
                        fw.mm(scs[par][:, hp * 128:(hp + 1) * 128], V(kT.h[pb:pb + 64, hp, ks], kT.trs),
                              V(qT.h[pb:pb + 64, hp, qs], qT.trs))
                for par in range(2):
                    fw.act(V(p.ap[:, par * 256:(par + 1) * 256], p.trs), scs[par][:, 0:256], AF.Exp, scale=0.125)
                fw.tt(p, p, bm[:, mt, :], ALU.mult, e=("pool" if idx % 2 == 0 else "dve"))
                ps_.append(p)
            ust[ui] = (qs, kts, ps_)

        def stage2(ui, units=units, nqb=nqb, ust=ust, g=g):
            rho, qb = units[ui]
            qs, kts, ps_ = ust.pop(ui)
            ob = psB(c)
            for h in range(4):
                for idx, (jb, mt) in enumerate(kts):
                    pc = (h % 2) * 256 + (h // 2) * 128
                    fw.mm(ob[:, h * 128:(h + 1) * 128], vaug[:, rho * nqb + jb, h, :],
                          V(ps_[idx].ap[:, pc:pc + 128], ps_[idx].trs),
                          start=(idx == 0), stop=(idx == len(kts) - 1))
            dst = V(acc.h[:, :, qs], acc.trs)
            src = V(ob.h[:, :].rearrange("p (h q) -> p h q", h=4), ob.trs)
            if g == 0:
                fw.copy(dst, src, e="act")
            else:
                fw.tt(dst, src, dst, ALU.add)

        unit0 = unit
        stage1(0)
        for ui in range(len(units)):
            if ui + 1 < len(units):
                stage1(ui + 1)
            stage2(ui)
        unit += len(units)
    a5 = acc.h.rearrange("p (hp par) t -> p hp par t", par=2)
    t5 = tmp.h.rearrange("p (hp par) t -> p hp par t", par=2)
    for tb in range(NTB):
        tok = slice(tb * 512, (tb + 1) * 512)
        fw.recip(V(t5[0:64, :, 0, :], tmp.trs), V(a5[64:128, :, 0, tok], acc.trs))
        fw.recip(V(t5[64:128, :, 1, :], tmp.trs), V(a5[0:64, :, 1, tok], acc.trs))
        fw.tt(c.yT.s(3, (slice(0, 64), slice(6, 8), tok)), V(a5[0:64, :, 0, tok], acc.trs), V(t5[0:64, :, 0, :], tmp.trs),
              ALU.mult)
        fw.tt(c.yT.s(3, (slice(64, 128), slice(6, 8), tok)), V(a5[64:128, :, 1, tok], acc.trs),
              V(t5[64:128, :, 1, :], tmp.trs), ALU.mult)
    gate_branch(c, l, D_G, 3)


USE_NTI = True
GC = 4


def yslot_f32(c, slot):
    ap = c.yT.h[:, 2 * slot:2 * slot + 2, :].rearrange("p a t -> p (a t)").bitcast(F32)
    return Buf(ap, 1), c.yT.trs[slot]


def branch_b(c, s, l):
    for hp in range(2):
        rwkv_hp(c, l, hp)


def rwkv_hp(c, l, hp):
    fw = c.fw
    arena_reset(c)
    CD = -math.exp(-0.5)
    prm = alloc(c, [128, 64], F32)
    blk = alloc(c, [128, 128], F32)
    wst = alloc(c, [128, 2, 256], F32)
    w2sb = alloc(c, [128, 256], BF16)
    a2sb = alloc(c, [128, 256], BF16)
    der = alloc(c, [128, 16], F32)
    ones = alloc(c, [128, 64], F32)
    fw.dma(prm[:, :], V(c.rwp[l], [c.tr_in]))
    fw.dma(blk[:, :], V(c.blk1[:, :], [c.tr_in]))
    fw.dma(wst[:, 0, :], V(c.rw_w2[l], [c.tr_in]))
    fw.dma(wst[:, 1, :], V(c.rw_a2[l], [c.tr_in]))
    fw.copy(w2sb[:, :], wst[:, 0, :])
    fw.copy(a2sb[:, :], wst[:, 1, :])
    fw.memset(ones[:, :], 1.0)
    mucols = [1 + hp, 3 + hp, 5 + hp, 7 + hp, 9, 10]
    for i, mc in enumerate(mucols):
        fw.ts(der[:, i:i + 1], prm[:, mc:mc + 1], -1.0, 1.0, ALU.mult, ALU.add)
        fw.ts(der[:, 6 + i:7 + i], prm[:, mc:mc + 1], 0.5, None, ALU.mult)
    fw.ts(der[:, 12:13], prm[:, 21 + hp:22 + hp], -1.0, 1.0, ALU.mult, ALU.add)
    vT = alloc(c, [128, T], BF16)
    gs = alloc(c, [128, T], BF16)
    bv = alloc(c, [128, T], BF16)
    AR = [alloc(c, [128, 2, T], BF16) for _ in range(2)]
    BK = [alloc(c, [128, 2, T], BF16) for _ in range(2)]
    RLO = [alloc(c, [64, T], BF16) for _ in range(2)]
    eL = alloc(c, [128, 2, 32], F32)
    mark = c.arena_off

    uT = alloc(c, [128, 6, T + 2], BF16)
    fw.memset(uT[:, :, 0:1], 0.0)
    fw.memset(uT[:, :, T + 1:T + 2], 0.0)
    cols = [B_R + hp * 128, B_K + hp * 128, B_V + hp * 128, B_GT + hp * 128, B_WL, B_AL]

    def consume(i, tb, ps):
        ch = cols[i] // 128
        fw.act(uT[:, i, 1 + tb * 512:1 + (tb + 1) * 512], ps[:, :], AF.Identity, bias=c.bfm[:, l, ch:ch + 1])

    proj_fm(c, l, cols, consume)
    tbuf = [alloc(c, [128, 512], F32)[:, :] for _ in range(11)]
    slot_tmps = []
    for slot_ in (0, 2, 3):
        ex_ap = c.yT.h[:, 2 * slot_:2 * slot_ + 2, :].rearrange("p a t -> p (a t)").bitcast(F32)
        for i in range(4):
            t_ = Tr()
            t_.r = dict(c.yT.trs[slot_].r)
            t_.w = c.yT.trs[slot_].w
            slot_tmps.append((slot_, t_))
            tbuf.append(V(ex_ap[:, i * 512:(i + 1) * 512], [t_]))
    xr, xk, kk, tA, tB = tbuf[0:5]
    dtmp = [tbuf[5:14], tbuf[14:23]]
    rmask = alloc(c, [128, 512], F32)
    fw.memset(rmask[:, :], 1.0)
    fw.memset(V(rmask.h[:, 0:512:64], rmask.trs), 0.0)
    twl = alloc(c, [128, 512], BF16)
    tal = alloc(c, [128, 512], BF16)
    for tb in range(NTB):
        tok = slice(tb * 512, (tb + 1) * 512)

        def shift(i, out):
            fw.tt(tA, uT[:, i, tb * 512:tb * 512 + 512], uT[:, i, tb * 512 + 2:tb * 512 + 514], ALU.add, e="pool")
            fw.ts(tA, tA, der[:, 6 + i:7 + i], None, ALU.mult)
            fw.stt(out, uT[:, i, tb * 512 + 1:tb * 512 + 513], der[:, i:i + 1], tA, ALU.mult, ALU.add)

        shift(0, xr)
        shift(1, xk)
        shift(2, tB)
        fw.copy(vT[:, tok], tB, e="act")
        shift(3, tB)
        fw.act(kk, tB, AF.Sigmoid)
        fw.tt(gs[:, tok], tB, kk, ALU.mult, e="pool")
        shift(4, tB)
        fw.act(twl[:, :], tB, AF.Tanh)
        shift(5, tB)
        fw.copy(tal[:, :], tB, e="act")
        fw.ts(kk, xk, prm[:, 19 + hp:20 + hp], None, ALU.mult)
        fw.tt(tB, kk, kk, ALU.mult, e="pool")
        ps = psA(c)
        fw.mm(ps[:, :], blk[:, :], tB)
        fw.ts(tB, ps[:, :], 1e-24, None, ALU.max)
        fw.act(tB, tB, AF.Sqrt)
        fw.recip(tA, tB)
        fw.tt(kk, kk, tA, ALU.mult)

        def dir_chain(d):
            lw, L0, D_, eP, eM, eX, a_, kd, tD = dtmp[d]
            dsl = slice(d * 64, (d + 1) * 64)
            ps = psA(c)
            fw.mm(ps[:, :], w2sb[dsl, hp * 128:(hp + 1) * 128], twl[dsl, :])
            yield
            fw.act(lw, ps[:, :], AF.Sigmoid, bias=prm[:, 11 + d * 2 + hp:12 + d * 2 + hp])
            yield
            ps2 = psA(c)
            fw.mm(ps2[:, :], a2sb[dsl, hp * 128:(hp + 1) * 128], tal[dsl, :])
            fw.emit("dve", lambda E: E.tensor_tensor_scan(out=L0.ap, data0=rmask.h[:, :], data1=lw.ap,
                                                        initial=0.0, op0=ALU.mult, op1=ALU.add),
                    [rmask[:, :], lw], [L0])
            yield
            fw.act(a_, ps2[:, :], AF.Sigmoid, bias=prm[:, 15 + d * 2 + hp:16 + d * 2 + hp])
            yield
            ltot = V(L0.ap[:, 63:512:64], L0.trs)
            fw.act(eL[:, d, tb * 8:(tb + 1) * 8], ltot, AF.Exp, scale=CD)
            if d == 0:
                fw.tt(D_, L0, lw, ALU.subtract, e="pool")
                yield
                fw.act(eP, L0, AF.Exp, scale=CD)
                yield
                fw.act(eM, L0, AF.Exp, scale=-CD)
                yield
                fw.act(eX, D_, AF.Exp, scale=CD)
                yield
            else:
                l3 = V(L0.ap.rearrange("p (c t) -> p c t", t=64), L0.trs)
                lt3 = V(L0.ap[:, 63:512:64].unsqueeze(2).to_broadcast([128, 8, 64]), L0.trs)
                fw.tt(V(D_.ap.rearrange("p (c t) -> p c t", t=64), D_.trs), l3, lt3, ALU.subtract)
                yield
                fw.tt(lw, D_, lw, ALU.subtract, e="pool")
                yield
                fw.act(eX, D_, AF.Exp, scale=-CD)
                yield
                fw.act(eP, lw, AF.Exp, scale=-CD)
                yield
                fw.act(eM, lw, AF.Exp, scale=CD)
                yield
            fw.ts(tD, a_, prm[:, 21 + hp:22 + hp], der[:, 12:13], ALU.mult, ALU.add)
            yield
            fw.tt(kd, tD, xk, ALU.mult)
            yield
            fw.tt(tD, kk, a_, ALU.mult, e="pool")
            yield
            fw.tt(AR[d][:, 0, tok], kk, eX, ALU.mult)
            yield
            fw.tt(BK[d][:, 0, tok], tD, eM, ALU.mult)
            yield
            fw.tt(BK[d][:, 1, tok], kd, eM, ALU.mult, e="pool")
            yield
            fw.tt(AR[d][:, 1, tok], xr, eP, ALU.mult)
            yield
            fw.tt(RLO[d][0:64, tok], V(xr.ap[64:128, :], xr.trs), V(eP.ap[64:128, :], eP.trs), ALU.mult)
            yield

        gens = [dir_chain(0), dir_chain(1)]
        while gens:
            for g_ in list(gens):
                try:
                    next(g_)
                except StopIteration:
                    gens.remove(g_)
        fw.tt(tA, dtmp[0][7], dtmp[1][7], ALU.add, e="pool")
        fw.stt(tB, xr, prm[:, 23 + hp:24 + hp], tA, ALU.mult, ALU.mult)
        ps = psA(c)
        fw.mm(ps[:, :], blk[:, :], tB)
        fw.tt(bv[:, tok], ps[:, :], vT[:, tok], ALU.mult)

    arena_release(c, mark)
    for slot_, t_ in slot_tmps:
        dst = c.yT.trs[slot_]
        evs = list(t_.r.values()) + ([t_.w] if t_.w is not None else [])
        for ev in evs:
            old_ = dst.r.get(ev[2])
            if old_ is None or (ev[0], ev[1]) > (old_[0], old_[1]):
                dst.r[ev[2]] = ev
    ys = []
    for d in range(2):
        b_, tr_ = yslot_f32(c, (0, 2)[d])
        b_.trs = [tr_]
        ys.append(b_)
    eLs = alloc(c, [64, 2, 2, 32], F32)
    for d in range(2):
        for hl in range(2):
            fw.copy(eLs[:, d, hl, :], eL[hl * 64:(hl + 1) * 64, d, :], e="act")
    tri = alloc(c, [64, 2, 192], F32)
    fw.dma(tri[:, :, :], V(c.trimask[:, :, :], [c.tr_in]))
    NM = GC * 4
    tmT2 = [alloc(c, [64, GC, 2, 4, 128], BF16) for _ in range(2)]
    Am2 = [alloc(c, [64, NM, 256], BF16) for _ in range(2)]
    TT2 = [alloc(c, [64, NM, 64], BF16) for _ in range(2)]
    PT2 = [alloc(c, [64, NM, 64], BF16) for _ in range(2)]
    Zb2 = [alloc(c, [64, NM, 64], BF16) for _ in range(2)]
    Nb = [alloc(c, [64, NM, 64], BF16) for _ in range(2)]
    NTb = [alloc(c, [64, NM, 64], BF16) for _ in range(2)]
    NTI = alloc(c, [64, NM, 64], BF16)
    Rb = [alloc(c, [64, NM, 64], BF16) for _ in range(2)]
    Sb = alloc(c, [64, 2, 4, 64], BF16, nslots=2)
    Ub = alloc(c, [64, 4, 64], BF16)
    fw.memset(Sb[:, :, :, :], 0.0)
    I64b = c.identb[0:64, 0:64]
    nsteps = T // 64
    ngroups = nsteps // GC
    halves = [(0, NM // 2), (NM // 2, NM)]
    idb = V(c.identf.h[0:64, 0:64].unsqueeze(1).to_broadcast([64, NM // 2, 64]), c.identf.trs)

    def chunk_of(g, ci, d):
        it = g * GC + ci
        return it if d == 0 else nsteps - 1 - it

    def flat(buf, m0, m1):
        return V(buf.h[:, m0:m1, :].rearrange("p m e -> p (m e)"), buf.trs)

    def phase1(g):
        tmT, Am, TT, PT, Zb = tmT2[g % 2], Am2[g % 2], TT2[g % 2], PT2[g % 2], Zb2[g % 2]
        for ci in range(GC):
            for d in range(2):
                ch = chunk_of(g, ci, d)
                cs = slice(ch * 64, ch * 64 + 64)
                pb_ = psB(c)
                pbv = pb_.h[:, :].bitcast(BF16)
                srcs = [AR[d][:, 0, cs], BK[d][:, 0, cs], BK[d][:, 1, cs], vT[:, cs]]
                for q, src in enumerate(srcs):
                    fw.transpose(V(pbv[0:64, q * 128:(q + 1) * 128], pb_.trs), src, c.identb[:, :])
                fw.copy(V(tmT.h[:, ci, d, :, :].rearrange("p q e -> p (q e)"), tmT.trs), V(pbv[0:64, 0:512], pb_.trs),
                        e=("act" if d == 0 else "dve"))
            yield
        bN = [psB(c), psB(c)]
        for ci in range(GC):
            bA = [psA(c), psA(c)]
            hl_outer = True
            order = [(d, hl) for hl in range(2) for d in range(2)] if hl_outer else [(d, hl) for d in range(2) for hl in range(2)]
            for (d, hl) in order:
                ch = chunk_of(g, ci, d)
                cs = slice(ch * 64, ch * 64 + 64)
                pb = hl * 64
                for q in range(2):
                    o0 = d * 256 + q * 128
                    fw.mm(bA[hl][0:64, o0:o0 + 128], BK[d][pb:pb + 64, q, cs], V(AR[d].h[pb:pb + 64, :, cs], AR[d].trs))
                o1 = (ci * 2 + d) * 64
                fw.mm(bN[hl][0:64, o1:o1 + 64], AR[d][pb:pb + 64, 0, cs], BK[d][pb:pb + 64, 0, cs])
            for hl in range(2):
                m0 = ci * 4 + hl
                dst = V(Am.h[:, m0:m0 + 3:2, :].rearrange("p d (q e) -> p d q e", q=2), Am.trs)
                src = V(bA[hl].h[0:64, :].rearrange("p (d q e) -> p d q e", d=2, q=2), bA[hl].trs)
                msk = V(tri.h[:, :, 0:128].unsqueeze(2).to_broadcast([64, 2, 2, 128]), tri.trs)
                fw.tt(dst, src, msk, ALU.mult)
            yield
        for hl in range(2):
            dst = V(NTb[0].h[:, hl:NM:2, :].rearrange("p (ci d) e -> p ci d e", d=2), NTb[0].trs)
            src = V(bN[hl].h[0:64, :].rearrange("p (ci d e) -> p ci d e", ci=GC, d=2), bN[hl].trs)
            msk = V(tri.h[:, :, 128:192].unsqueeze(1).to_broadcast([64, GC, 2, 64]), tri.trs)
            fw.tt(dst, src, msk, ALU.mult)
        for (m0, m1) in halves:
            fw.tt(Rb[0][:, m0:m1, :], idb, Am[:, m0:m1, 0:64], ALU.subtract)
        yield
        Ncur = V(Am.h[:, :, 0:64], Am.trs)
        NTcur = NTb[0][:, :, :]
        rcur = 0
        nti = 0
        for lev in range(1, 6):
            ntn = 1 - nti
            nn = lev % 2
            pzs = []
            for (m0, m1) in halves:
                pz = psA(c)
                for m in range(m0, m1):
                    fw.mm(pz[0:64, (m - m0) * 64:(m - m0 + 1) * 64], V(Ncur.ap[:, m, :], Ncur.trs), V(NTcur.ap[:, m, :], NTcur.trs))
                pzs.append(pz)
            for hi, (m0, m1) in enumerate(halves):
                pz = pzs[hi]
                src3 = V(pz.h[0:64, 0:(m1 - m0) * 64].rearrange("p (m e) -> p m e", e=64), pz.trs)
                if USE_NTI:
                    fw.tt(NTI[:, m0:m1, :], src3, idb, ALU.add)
                if lev < 5 or not USE_NTI:
                    fw.copy(flat(NTb[ntn], m0, m1), pz[0:64, 0:(m1 - m0) * 64], e="act")
            yield
            if lev < 5:
                for (m0, m1) in halves:
                    pz = psA(c)
                    for m in range(m0, m1):
                        fw.mm(pz[0:64, (m - m0) * 64:(m - m0 + 1) * 64], V(NTcur.ap[:, m, :], NTcur.trs), V(Ncur.ap[:, m, :], Ncur.trs))
                    fw.copy(flat(Nb[nn], m0, m1), pz[0:64, 0:(m1 - m0) * 64], e="act")
                yield
            rn = 1 - rcur
            rdst = TT if lev == 5 else Rb[rn]
            for (m0, m1) in halves:
                pz = psB(c)
                for m in range(m0, m1):
                    o = pz[0:64, (m - m0) * 64:(m - m0 + 1) * 64]
                    if USE_NTI:
                        fw.mm(o, NTI[:, m, :], Rb[rcur][:, m, :])
                    else:
                        fw.mm(o, I64b, Rb[rcur][:, m, :], start=True, stop=False)
                        fw.mm(o, NTb[ntn][:, m, :], Rb[rcur][:, m, :], start=False, stop=True)
                fw.copy(flat(rdst, m0, m1), pz[0:64, 0:(m1 - m0) * 64])
            yield
            rcur = rn
            if lev < 5:
                Ncur = Nb[nn][:, :, :]
                NTcur = NTb[ntn][:, :, :]
                nti = ntn
        for (m0, m1) in halves:
            pz = psA(c)
            pq = psB(c)
            for m in range(m0, m1):
                ci, d, hl = m // 4, (m // 2) % 2, m % 2
                fw.mm(pz[0:64, (m - m0) * 64:(m - m0 + 1) * 64], tmT[:, ci, d, 0, hl * 64:(hl + 1) * 64], TT[:, m, :])
                fw.mm(pq[0:64, (m - m0) * 64:(m - m0 + 1) * 64], Am[:, m, 128:192], tmT[:, ci, d, 3, hl * 64:(hl + 1) * 64])
            fw.copy(flat(PT, m0, m1), pz[0:64, 0:(m1 - m0) * 64], e="act")
            fw.copy(flat(Zb, m0, m1), pq[0:64, 0:(m1 - m0) * 64])
            yield

    def drain(gen, n):
        if gen is None:
            return None
        for _ in range(n):
            try:
                next(gen)
            except StopIteration:
                return None
        return gen

    gen = phase1(0)
    drain(gen, 1000)
    slot = 0
    NY = 2 * GC + 2 + 14 + 2
    per_pt = -(-NY // (2 * GC))
    for g in range(ngroups):
        tmT, Am, TT, PT, Zb = tmT2[g % 2], Am2[g % 2], TT2[g % 2], PT2[g % 2], Zb2[g % 2]
        gen = phase1(g + 1) if g + 1 < ngroups else None
        for ci in range(GC):
            scur = Sb.s(slot, (slice(None), slot, slice(None), slice(None)))
            snew = Sb.s(1 - slot, (slice(None), 1 - slot, slice(None), slice(None)))
            pu = psB(c)
            for inst in range(4):
                m = ci * 4 + inst
                o = pu[0:64, inst * 64:(inst + 1) * 64]
                fw.mm(o, PT[:, m, :], V(scur.ap[:, inst, :], scur.trs), start=True, stop=False)
                fw.mm(o, TT[:, m, :], Zb[:, m, :], start=False, stop=True)
            fw.act(V(Ub.h.rearrange("p i e -> p (i e)"), Ub.trs), pu[0:64, 0:256], AF.Copy, scale=-1.0)
            gen = drain(gen, per_pt)
            pS = psA(c)
            pY = psB(c)
            for inst in range(4):
                m = ci * 4 + inst
                d, hl = inst // 2, inst % 2
                ch = chunk_of(g, ci, d)
                cs = slice(ch * 64, ch * 64 + 64)
                hs = slice(hl * 64, (hl + 1) * 64)
                o = pS[0:64, inst * 64:(inst + 1) * 64]
                fw.mm(o, I64b, V(scur.ap[:, inst, :], scur.trs), start=True, stop=False)
                fw.mm(o, tmT[:, ci, d, 2, hs], tmT[:, ci, d, 3, hs], start=False, stop=False)
                fw.mm(o, tmT[:, ci, d, 1, hs], Ub[:, inst, :], start=False, stop=True)
            for d in range(2):
                ch = chunk_of(g, ci, d)
                esc = V(eLs.h[:, d, :, ch:ch + 1].to_broadcast([64, 2, 64]), eLs.trs)
                fw.tt(V(snew.ap[:, 2 * d:2 * d + 2, :], snew.trs),
                      V(pS.h[0:64, d * 128:(d + 1) * 128].rearrange("p (i e) -> p i e", i=2), pS.trs), esc, ALU.mult)
            for inst in range(4):
                m = ci * 4 + inst
                d, hl = inst // 2, inst % 2
                ch = chunk_of(g, ci, d)
                cs = slice(ch * 64, ch * 64 + 64)
                hs = slice(hl * 64, (hl + 1) * 64)
                oy = pY[0:64, inst * 64:(inst + 1) * 64]
                rt = AR[d][0:64, 1, cs] if hl == 0 else RLO[d][0:64, cs]
                fw.mm(oy, V(scur.ap[:, inst, :], scur.trs), rt, start=True, stop=False)
                fw.mm(oy, tmT[:, ci, d, 3, hs], Am[:, m, 192:256], start=False, stop=False)
                fw.mm(oy, Ub[:, inst, :], Am[:, m, 64:128], start=False, stop=True)
            for d in range(2):
                ch = chunk_of(g, ci, d)
                cs = slice(ch * 64, ch * 64 + 64)
                for hl in range(2):
                    inst = d * 2 + hl
                    fw.copy(ys[d][hl * 64:(hl + 1) * 64, cs], pY[0:64, inst * 64:(inst + 1) * 64], e="act")
            gen = drain(gen, per_pt)
            slot = 1 - slot
        drain(gen, 1000)

    arena_release(c, mark)
    pt_ = [alloc(c, [128, 512], F32)[:, :] for _ in range(6)]
    y_, sq, mean, var, t1, t2 = pt_
    for tb in range(NTB):
        tok = slice(tb * 512, (tb + 1) * 512)
        fw.tt(y_, ys[0][:, tok], ys[1][:, tok], ALU.add)
        fw.tt(sq, y_, y_, ALU.mult, e="pool")
        p1 = psA(c)
        fw.mm(p1[:, :], blk[:, :], y_)
        p2 = psA(c)
        fw.mm(p2[:, :], blk[:, :], sq)
        fw.ts(mean, p1[:, :], 1.0 / 64.0, None, ALU.mult)
        fw.tt(t1, mean, mean, ALU.mult, e="pool")
        fw.stt(var, p2[:, :], 1.0 / 64.0, t1, ALU.mult, ALU.subtract)
        fw.ts(var, var, 64e-5, None, ALU.add)
        fw.act(var, var, AF.Sqrt)
        fw.recip(t1, var)
        fw.tt(t2, y_, mean, ALU.subtract, e="pool")
        fw.tt(t2, t2, t1, ALU.mult)
        fw.ts(t2, t2, prm[:, 25 + hp:26 + hp], prm[:, 27 + hp:28 + hp], ALU.mult, ALU.add)
        fw.tt(t2, t2, bv[:, tok], ALU.add, e="pool")
        fw.tt(c.yT.s(1, (slice(None), 2 + hp, tok)), t2, gs[:, tok], ALU.mult)
```

```python
import math
import os
import numpy as np
import ml_dtypes
import concourse.bass as bass
import concourse.mybir as mybir
from concourse.bass_utils import run_bass_kernel_spmd

F32 = mybir.dt.float32
BF16 = mybir.dt.bfloat16
AF = mybir.ActivationFunctionType
ALU = mybir.AluOpType

T = 2048
DM = 1024
NTB = 4
IN_COLS = 9984
ROT_COLS = 2048
WCOLS = IN_COLS + ROT_COLS
ALPHA = (2 * 2) ** 0.25
LN_EPS = 1e-5
ARENA_F32 = 30208


class Tr:
    __slots__ = ("w", "r", "excl")

    def __init__(self, excl=False):
        self.w = None
        self.r = {}
        self.excl = excl


class V:
    __slots__ = ("ap", "trs")

    def __init__(self, ap, trs):
        self.ap = ap
        self.trs = trs


class Buf:
    def __init__(self, h, nslots=1):
        self.h = h
        self.trs = [Tr() for _ in range(nslots)]

    def __getitem__(self, idx):
        return V(self.h[idx], self.trs)

    def s(self, slot, idx):
        return V(self.h[idx], [self.trs[slot]])


class FW:
    LIMIT = 30000

    def __init__(self, nc, ndma=24):
        self.nc = nc
        self.eng = {"pe": nc.tensor, "act": nc.scalar, "dve": nc.vector, "pool": nc.gpsimd, "sp": nc.sync}
        self.sems = []
        self.cur = {}
        self.cnt = {}
        for e in self.eng:
            self.cur[e] = self._newsem("e_" + e)
            self.cnt[e] = 0
        self.known = {e: {} for e in self.eng}
        self.dma_sem = [self._newsem("dma%d" % i) for i in range(ndma)]
        self.dma_val = [0] * ndma
        self.n_hw = ndma
        self.dma_next = 0
        self.n_ins = 0
        self._uid = 0

    def _newsem(self, name):
        self._uid = getattr(self, "_uid", 0) + 1
        h = self.nc.alloc_semaphore("%s_%d" % (name, self._uid))
        self.sems.append(h)
        return len(self.sems) - 1

    def _wait(self, e, ev):
        si, val, src = ev
        if self.known[e].get(si, 0) >= val:
            return
        self.eng[e].wait_ge(self.sems[si], val)
        self.known[e][si] = val

    def _deps(self, e, reads, writes):
        for v in reads:
            for tr in v.trs:
                if tr.w is not None:
                    if tr.w[2] == e and e == "pe":
                        continue
                    self._wait(e, tr.w)
                if tr.excl:
                    for src, ev in tr.r.items():
                        if ev[2] != e:
                            self._wait(e, ev)
        for v in writes:
            for tr in v.trs:
                if tr.w is not None and not (tr.w[2] == e and e == "pe"):
                    self._wait(e, tr.w)
                for src, ev in tr.r.items():
                    if not (ev[2] == e and e == "pe"):
                        self._wait(e, ev)

    def _record(self, ev, reads, writes, key):
        for v in writes:
            for tr in v.trs:
                tr.w = ev
                tr.r = {}
        for v in reads:
            for tr in v.trs:
                tr.r[key] = ev

    def emit(self, e, fn, reads, writes):
        self._deps(e, reads, writes)
        ins = fn(self.eng[e])
        if self.cnt[e] >= self.LIMIT:
            self.cur[e] = self._newsem("e_" + e)
            self.cnt[e] = 0
        self.cnt[e] += 1
        ins.then_inc(self.sems[self.cur[e]], 1)
        ev = (self.cur[e], self.cnt[e], e)
        self._record(ev, reads, writes, e)
        self.n_ins += 1
        return ev

    def dma(self, out, in_, e="sp"):
        self._deps(e, [in_], [out])
        if e == "pool":
            self.dma_sem.append(self._newsem("swdma"))
            self.dma_val.append(0)
            slot = len(self.dma_sem) - 1
        else:
            slot = self.dma_next
            self.dma_next = (slot + 1) % self.n_hw
        si = self.dma_sem[slot]
        if self.dma_val[slot] > 0:
            self._wait(e, (si, self.dma_val[slot], "dma"))
        ins = self.eng[e].dma_start(out=out.ap, in_=in_.ap)
        self.dma_val[slot] += 16
        ins.then_inc(self.sems[si], 16)
        ev = (si, self.dma_val[slot], "dma%d" % slot)
        self._record(ev, [in_], [out], "dma%d" % slot)
        self.n_ins += 1
        return ev

    def barrier(self):
        evs = [(self.cur[f], self.cnt[f], f) for f in self.eng if self.cnt[f] > 0]
        evs += [(self.dma_sem[i], self.dma_val[i], "dma") for i in range(len(self.dma_sem)) if self.dma_val[i] > 0]
        for e in self.eng:
            for ev in evs:
                if not (ev[2] == e and e == "pe"):
                    self._wait(e, ev)

    def mm(self, out, lhsT, rhs, start=True, stop=True):
        return self.emit("pe", lambda E: E.matmul(out.ap, lhsT=lhsT.ap, rhs=rhs.ap, start=start, stop=stop),
                         [lhsT, rhs], [out])

    def transpose(self, out, in_, ident):
        return self.emit("pe", lambda E: E.transpose(out.ap, in_.ap, ident.ap), [in_, ident], [out])

    def act(self, out, in_, func, bias=None, scale=None, e="act"):
        kw = {}
        rd = [in_]
        if bias is not None:
            if isinstance(bias, V):
                kw["bias"] = bias.ap
                rd.append(bias)
            else:
                kw["bias"] = bias
        if scale is not None:
            if isinstance(scale, V):
                kw["scale"] = scale.ap
                rd.append(scale)
            else:
                kw["scale"] = scale
        return self.emit("act", lambda E: E.activation(out=out.ap, in_=in_.ap, func=func, **kw), rd, [out])

    def tt(self, out, in0, in1, op, e="dve"):
        return self.emit(e, lambda E: E.tensor_tensor(out=out.ap, in0=in0.ap, in1=in1.ap, op=op), [in0, in1], [out])

    def ts(self, out, in0, s1, s2, op0, op1=None, e="dve"):
        rd = [in0]
        a1 = s1
        a2 = s2
        if isinstance(s1, V):
            rd.append(s1)
            a1 = s1.ap
        if isinstance(s2, V):
            rd.append(s2)
            a2 = s2.ap
        if op1 is None:
            return self.emit(e, lambda E: E.tensor_scalar(out=out.ap, in0=in0.ap, scalar1=a1, scalar2=None, op0=op0),
                             rd, [out])
        return self.emit(e, lambda E: E.tensor_scalar(out=out.ap, in0=in0.ap, scalar1=a1, scalar2=a2, op0=op0, op1=op1),
                         rd, [out])

    def stt(self, out, in0, scalar, in1, op0, op1):
        rd = [in0, in1]
        a = scalar
        if isinstance(scalar, V):
            rd.append(scalar)
            a = scalar.ap
        return self.emit("dve", lambda E: E.scalar_tensor_tensor(out=out.ap, in0=in0.ap, scalar=a, in1=in1.ap,
                                                                 op0=op0, op1=op1), rd, [out])

    def copy(self, out, in_, e="dve"):
        if e == "act":
            return self.act(out, in_, AF.Copy)
        return self.emit(e, lambda E: E.tensor_copy(out=out.ap, in_=in_.ap), [in_], [out])

    def recip(self, out, in_):
        return self.emit("dve", lambda E: E.reciprocal(out=out.ap, in_=in_.ap), [in_], [out])

    def memset(self, out, val, e="dve"):
        return self.emit(e, lambda E: E.memset(out.ap, val), [], [out])


A_Q, A_K, A_V, A_G = 0, 256, 512, 768
B_R, B_K, B_V, B_GT, B_WL, B_AL = 1024, 1280, 1536, 1792, 2048, 2176
C_Q, C_K, C_V, C_G = 2304, 2560, 2816, 3072
D_BASE, D_G = 3328, 5632
MG = 5888
R_CQ, R_CK, R_D = 9984, 10240, 10496
NCH = WCOLS // 128


def _rot_perm():
    cols = []
    for base in (C_Q, C_K):
        for c in range(256):
            j = c % 32
            cols.append(base + c - j + (j + 16) % 32)
    for g in range(3):
        for part in (0, 256):
            base = D_BASE + g * 768 + part
            for c in range(256):
                j = c % 64
                cols.append(base + c - j + (j + 32) % 64)
    return np.asarray(cols, np.int64)


def _rope_tables():
    t = np.arange(T, dtype=np.float32)
    out = []
    for d in (32, 64):
        half = d // 2
        inv = np.power(np.float32(10000.0), -np.arange(half, dtype=np.float32) / np.float32(half)).astype(np.float32)
        ang = (t[:, None] * inv[None, :]).astype(np.float32)
        cos = np.cos(ang).astype(np.float32)
        sin = np.sin(ang).astype(np.float32)
        p = np.arange(128)
        j = p % d
        cosT = cos[:, j % half].T
        sgn = np.where(j < half, -1.0, 1.0).astype(np.float32)
        sinT = (sin[:, j % half].T * sgn[:, None]).astype(np.float32)
        out += [np.ascontiguousarray(cosT), np.ascontiguousarray(sinT)]
    return out


class Ctx:
    pass


def dram(nc, name, shape, dtype, kind):
    return nc.dram_tensor(name, list(shape), dtype, kind=kind).ap()


def build(nseq, parts=("A", "B", "C", "D"), dbg=False):
    nc = bass.Bass("TRN2", target_bir_lowering=False)
    fw = FW(nc)
    c = Ctx()
    c.nc, c.fw, c.parts, c.dbg = nc, fw, parts, dbg
    IN, OUT, INT = "ExternalInput", "ExternalOutput", "Internal"
    c.x = dram(nc, "x", [nseq, T, DM], F32, IN)
    c.y = dram(nc, "y", [nseq, T, DM], F32, OUT)
    c.w_in = dram(nc, "w_in", [2, DM, IN_COLS], F32, IN)
    c.w_br = dram(nc, "w_branch", [2, 4, 256, DM], F32, IN)
    c.w_out = dram(nc, "w_out", [2, DM, DM], F32, IN)
    c.b_fm = dram(nc, "b_fm", [2, 128, NCH], F32, IN)
    c.b_cat = dram(nc, "b_cat", [2, WCOLS], F32, IN)
    c.vecs = dram(nc, "vecs", [8, DM], F32, IN)
    c.ident_b = dram(nc, "ident_b", [128, 128], BF16, IN)
    c.ident_f = dram(nc, "ident_f", [128, 128], F32, IN)
    c.rope = dram(nc, "rope", [4, 128, T], F32, IN)
    c.bandm = dram(nc, "bandm", [128, 3, 4, 128], BF16, IN)
    c.na_bias = dram(nc, "na_bias", [2, 128, 14 * 256], F32, IN)
    c.na_mask = dram(nc, "na_mask", [128, 14 * 256], F32, IN)
    c.rwp = dram(nc, "rwp", [2, 128, 64], F32, IN)
    c.rw_w2 = dram(nc, "rw_w2", [2, 128, 256], F32, IN)
    c.rw_a2 = dram(nc, "rw_a2", [2, 128, 256], F32, IN)
    c.df_lam = dram(nc, "df_lam", [2, 128], F32, IN)
    c.trimask = dram(nc, "trimask", [64, 2, 192], F32, IN)
    c.blk1 = dram(nc, "blk1", [128, 128], F32, IN)
    c.perms = dram(nc, "perms", [2, 128, 128], BF16, IN)
    c.wbf = dram(nc, "wbf", [2, DM, WCOLS], BF16, INT)
    c.wbr_bf = dram(nc, "wbr_bf", [2, 4, 256, DM], BF16, INT)
    c.wout_bf = dram(nc, "wout_bf", [2, DM, DM], BF16, INT)
    c.xres = dram(nc, "xres", [T, DM], F32, INT)
    c.tr_wbf = [[Tr() for _ in range(NCH)] for _ in range(2)]
    c.tr_wbr = [Tr(), Tr()]
    c.tr_wout = [Tr(), Tr()]
    c.tr_xres = [Tr() for _ in range(16)]
    c.tr_in = Tr()
    c.tr_y = Tr()
    if dbg:
        c.dbg_out = dram(nc, "dbg", [128, 8, T], BF16, OUT)
        c.tr_dbg = Tr()

    def sb(name, shape, dtype, nslots=1):
        return Buf(nc.alloc_sbuf_tensor(name, list(shape), dtype), nslots)

    c.sb = sb
    c.xT = sb("xT", [128, 8, T], BF16)
    c.yT = sb("yT", [128, 8, T], BF16, nslots=4)
    c.wbuf = sb("wbuf", [128, 2, 8, 512], BF16, nslots=2)
    c.identb = sb("identb", [128, 128], BF16)
    c.identf = sb("identf", [128, 128], F32)
    c.bfm = sb("bfm", [128, 2, NCH], F32)
    c.arena = nc.alloc_sbuf_tensor("arena", [128, ARENA_F32], F32)
    c.arena_off = 0
    c.arena_summ = {}
    c.arena_live = []
    c.ps = [Buf(nc.alloc_psum_tensor("ps%d" % i, [128, 512], F32)) for i in range(8)]
    for b_ in c.ps:
        b_.trs = [Tr(excl=True)]
    c.ps_i = [0, 0]

    fw.dma(c.identb[:, :], V(c.ident_b[:, :], [c.tr_in]))
    fw.dma(c.identf[:, :], V(c.ident_f[:, :], [c.tr_in]))
    for l in range(2):
        fw.dma(c.bfm[:, l, :], V(c.b_fm[l], [c.tr_in]))

    convert_weights(c)
    for s in range(nseq):
        stage0(c, s)
        for l in range(2):
            layer(c, s, l)
    for i in range(len(fw.dma_sem)):
        if fw.dma_val[i] > 0:
            fw._wait("sp", (fw.dma_sem[i], fw.dma_val[i], "dma"))
    return nc, fw


def _arena_retire(c):
    summ = c.arena_summ
    for buf in c.arena_live:
        for tr in buf.trs:
            evs = list(tr.r.values())
            if tr.w is not None:
                evs.append(tr.w)
            for ev in evs:
                key = ev[2]
                old = summ.get(key)
                if old is None or (ev[0], ev[1]) > (old[0], old[1]):
                    summ[key] = ev
    c.arena_live = []


def arena_reset(c):
    _arena_retire(c)
    c.arena_off = 0


def arena_release(c, mark):
    _arena_retire(c)
    c.arena_off = mark


def alloc(c, shape, dtype, nslots=1):
    n = int(np.prod(shape[1:]))
    n4 = (n + 1) // 2 if dtype == BF16 else n
    n4 = (n4 + 7) // 8 * 8
    assert c.arena_off + n4 <= ARENA_F32, ("arena overflow", c.arena_off, n4)
    ap = c.arena[0:shape[0], c.arena_off:c.arena_off + n4]
    c.arena_off += n4
    if dtype == BF16:
        ap = ap.bitcast(BF16)[:, 0:n]
    else:
        ap = ap[:, 0:n]
    if len(shape) > 2:
        names = " ".join("d%d" % i for i in range(len(shape) - 1))
        kw = {"d%d" % i: shape[i + 1] for i in range(len(shape) - 1)}
        ap = ap.rearrange("p (%s) -> p %s" % (names, names), **kw)
    b = Buf(ap, nslots)
    for tr in b.trs:
        tr.r = dict(c.arena_summ)
    c.arena_live.append(b)
    return b


def load_vecs(c, rows):
    for i, r in enumerate(rows):
        c.fw.dma(c.vec_bc[:, i, :], V(c.vecs[r:r + 1, :].partition_broadcast(128), [c.tr_in]))


def psA(c):
    i = c.ps_i[0]
    c.ps_i[0] = (i + 1) % 4
    return c.ps[i]


def psB(c):
    i = c.ps_i[1]
    c.ps_i[1] = (i + 1) % 4
    return c.ps[4 + i]


def convert_weights(c):
    fw = c.fw
    rd = [c.tr_in]
    for l in range(2):
        blocks = [(0, 512 * i, 512) for i in range(11)] + [(0, 5632, 256)]
        for (_, c0, n) in blocks:
            trs = c.tr_wbf[l][c0 // 128:(c0 + n) // 128]
            fw.dma(V(c.wbf[l, :, c0:c0 + n], trs), V(c.w_in[l, :, c0:c0 + n], rd), e="pool")
        dst = c.wbf[l, :, MG:IN_COLS].rearrange("k (dc b j) -> k dc b j", dc=8, b=4)
        for b in range(4):
            src = c.w_in[l, :, MG + b * 1024:MG + (b + 1) * 1024].rearrange("k (dc j) -> k dc j", dc=8)
            fw.dma(V(dst[:, :, b, :], c.tr_wbf[l][MG // 128:IN_COLS // 128]), V(src, rd), e="pool")
        for b in range(4):
            fw.dma(V(c.wbr_bf[l, b], [c.tr_wbr[l]]), V(c.w_br[l, b], rd), e="pool")
        for i in range(2):
            fw.dma(V(c.wout_bf[l, :, 512 * i:512 * (i + 1)], [c.tr_wout[l]]),
                   V(c.w_out[l, :, 512 * i:512 * (i + 1)], rd), e="pool")


def load_w(c, l, c0, n):
    fw = c.fw
    slot = getattr(c, "_wslot", 0)
    c._wslot = 1 - slot
    src = c.wbf[l, :, c0:c0 + n].rearrange("(kc p) n -> p kc n", p=128)
    trs = c.tr_wbf[l][c0 // 128:(c0 + n + 127) // 128]
    fw.dma(c.wbuf.s(slot, (slice(None), slot, slice(None), slice(0, n))), V(src, trs))
    return slot


def wv(c, slot, kc, j0, n):
    return c.wbuf.s(slot, (slice(None), slot, kc, slice(j0, j0 + n)))


def proj_fm(c, l, col_list, consume):
    fw = c.fw
    groups = []
    for i, c0 in enumerate(col_list):
        if groups and groups[-1][0] + groups[-1][1] == c0 and groups[-1][1] < 512:
            groups[-1][1] += 128
            groups[-1][2].append(i)
        else:
            groups.append([c0, 128, [i]])
    slots = [None] * len(groups)
    slots[0] = load_w(c, l, groups[0][0], groups[0][1])
    for gi, (g0, gn, idxs) in enumerate(groups):
        if gi + 1 < len(groups):
            slots[gi + 1] = load_w(c, l, groups[gi + 1][0], groups[gi + 1][1])
        for j, i in enumerate(idxs):
            for tb in range(NTB):
                ps = psA(c)
                for kc in range(8):
                    fw.mm(ps[:, :], wv(c, slots[gi], kc, j * 128, 128), c.xT[:, kc, tb * 512:(tb + 1) * 512],
                          start=(kc == 0), stop=(kc == 7))
                consume(i, tb, ps)


def ln_stats(c, z, sl):
    fw = c.fw
    st = c.ln_st.s(sl, (slice(None), sl, slice(None), slice(None)))
    mv = c.ln_mv
    tr = [mv.trs[sl]]
    fw.emit("dve", lambda E: E.bn_stats(out=st.ap[:, 0, :], in_=z.ap[:, 0:512]), [z], [st])
    fw.emit("dve", lambda E: E.bn_stats(out=st.ap[:, 1, :], in_=z.ap[:, 512:1024]), [z], [st])
    m = lambda a, b: V(mv.h[:, sl, a:b], tr)
    fw.emit("dve", lambda E: E.bn_aggr(out=mv.h[:, sl, 0:2], in_=st.ap), [st], [m(0, 2)])
    fw.ts(m(2, 3), m(1, 2), LN_EPS, None, ALU.add)
    fw.tt(m(4, 5), m(2, 3), c.ln_mhalf[:, 0:1], ALU.pow, e="pool")
    fw.stt(m(5, 6), m(0, 1), -1.0, m(4, 5), ALU.mult, ALU.mult)


def ln_apply(c, z, gi, bi, out, sl):
    fw = c.fw
    tr = [c.ln_mv.trs[sl]]
    fw.act(out, z, AF.Identity, bias=V(c.ln_mv.h[:, sl, 5:6], tr), scale=V(c.ln_mv.h[:, sl, 4:5], tr))
    fw.tt(out, out, c.vec_bc[:, gi, :], ALU.mult)
    fw.tt(out, out, c.vec_bc[:, bi, :], ALU.add, e="pool")


def to_xT(c, rows, tt):
    fw = c.fw
    xb = c.xb_t
    fw.copy(xb[:, :], rows, e="act")
    ps = psB(c)
    psb = V(ps.h[:, :].bitcast(BF16), ps.trs)
    for k in range(8):
        fw.transpose(V(psb.ap[:, k * 128:(k + 1) * 128], ps.trs), xb[:, k * 128:(k + 1) * 128], c.identb[:, :])
    fw.copy(c.xT[:, :, tt * 128:(tt + 1) * 128], V(psb.ap.rearrange("p (k t) -> p k t", k=8), ps.trs))


def run_pipeline3(sa, sb, sc, n):
    for t in range(-2, n):
        if 0 <= t + 2 < n:
            sa(t + 2)
        if 0 <= t + 1 < n:
            sb(t + 1)
        if 0 <= t < n:
            sc(t)


def ln_allocs(c):
    c.ln_st = alloc(c, [128, 2, 2, 6], F32, nslots=2)
    c.ln_mv = alloc(c, [128, 2, 8], F32, nslots=2)
    c.xb_t = alloc(c, [128, DM], BF16)
    c.zt = alloc(c, [128, 3, DM], F32, nslots=3)
    c.ot = alloc(c, [128, 2, DM], F32, nslots=2)
    c.vec_bc = alloc(c, [128, 3, DM], F32)
    c.ln_mhalf = alloc(c, [128, 8], F32)
    c.fw.memset(c.ln_mhalf[:, :], -0.5)


def stage0(c, s):
    fw = c.fw
    arena_reset(c)
    ln_allocs(c)
    load_vecs(c, [0, 1])
    def zslot(tt):
        return c.zt.s(tt % 3, (slice(None), tt % 3, slice(None)))

    fw.dma(zslot(0), V(c.x[s, 0:128, :], [c.tr_in]))

    def stage_a(tt):
        if tt + 1 < 16:
            fw.dma(zslot(tt + 1), V(c.x[s, (tt + 1) * 128:(tt + 2) * 128, :], [c.tr_in]))
        ln_stats(c, zslot(tt), tt % 2)

    def stage_b1(tt):
        sl = tt % 2
        ln_apply(c, zslot(tt), 0, 1, c.ot.s(sl, (slice(None), sl, slice(None))), sl)

    def stage_b2(tt):
        sl = tt % 2
        o = c.ot.s(sl, (slice(None), sl, slice(None)))
        fw.dma(V(c.xres[tt * 128:(tt + 1) * 128, :], [c.tr_xres[tt]]), o)
        to_xT(c, o, tt)

    run_pipeline3(stage_a, stage_b1, stage_b2, 16)


def gate_branch(c, l, gcol, bi):
    fw = c.fw

    def consume(i, tb, ps):
        ch = (gcol // 128) + i
        sg = c.g_sig.s(tb % 2, (slice(None), tb % 2, slice(None)))
        fw.act(sg, ps[:, :], AF.Sigmoid, bias=c.bfm[:, l, ch:ch + 1])
        fw.stt(sg, ps[:, :], c.bfm[:, l, ch:ch + 1], sg, ALU.add, ALU.mult)
        yv = c.yT.s(bi, (slice(None), bi * 2 + i, slice(tb * 512, (tb + 1) * 512)))
        fw.tt(yv, yv, sg, ALU.mult, e="pool")

    proj_fm(c, l, [gcol, gcol + 128], consume)


def final_stage(c, s, l):
    fw = c.fw
    arena_reset(c)
    ln_allocs(c)
    c.mergedT = alloc(c, [128, 8, T], BF16)
    c.wbr_sb = alloc(c, [128, 4, 2, DM], BF16)
    c.wout_sb = alloc(c, [128, 8, DM], BF16)
    c.f_sig = alloc(c, [128, 2, 512], F32, nslots=2)
    c.f_acc = alloc(c, [128, 2, 512], F32, nslots=2)
    c.f_tmp = alloc(c, [128, 2, 512], F32, nslots=2)
    load_vecs(c, [2 + 3 * l, 3 + 3 * l, 4 + 3 * l])
    c.ones_row = alloc(c, [1, 128], BF16)
    c.bout_row = alloc(c, [1, DM], BF16)
    bstage = alloc(c, [1, DM], F32)
    fw.memset(c.ones_row[0:1, :], 1.0)
    fw.dma(bstage[0:1, :], V(c.vecs[2 + 3 * l:3 + 3 * l, :], [c.tr_in]))
    fw.copy(c.bout_row[0:1, :], bstage[0:1, :])
    for b in range(4):
        fw.dma(c.wbr_sb[:, b, :, :], V(c.wbr_bf[l, b].rearrange("(kc p) d -> p kc d", p=128), [c.tr_wbr[l]]))
    fw.dma(c.wout_sb[:, :, :], V(c.wout_bf[l].rearrange("(kc p) d -> p kc d", p=128), [c.tr_wout[l]]))
    slots = [None] * 8
    slots[0] = load_w(c, l, MG, 512)
    for dc in range(8):
        if dc + 1 < 8:
            slots[dc + 1] = load_w(c, l, MG + (dc + 1) * 512, 512)
        for tb in range(NTB):
            tok = slice(tb * 512, (tb + 1) * 512)
            ai = tb % 2
            acc = c.f_acc.s(ai, (slice(None), ai, slice(None)))
            for b in range(4):
                pg = psA(c)
                for kc in range(8):
                    fw.mm(pg[:, :], wv(c, slots[dc], kc, b * 128, 128), c.xT[:, kc, tok], start=(kc == 0), stop=(kc == 7))
                ch = MG // 128 + b * 8 + dc
                si = b % 2
                sg = c.f_sig.s(si, (slice(None), si, slice(None)))
                fw.act(sg, pg[:, :], AF.Sigmoid, bias=c.bfm[:, l, ch:ch + 1])
                pp = psB(c)
                for kc in range(2):
                    fw.mm(pp[:, :], c.wbr_sb[:, b, kc, dc * 128:(dc + 1) * 128],
                          c.yT.s(b, (slice(None), b * 2 + kc, tok)), start=(kc == 0), stop=(kc == 1))
                if b == 0:
                    fw.tt(acc, pp[:, :], sg, ALU.mult)
                else:
                    tm = c.f_tmp.s(si, (slice(None), si, slice(None)))
                    fw.tt(tm, pp[:, :], sg, ALU.mult)
                    if b < 3:
                        fw.tt(acc, acc, tm, ALU.add, e="pool")
                    else:
                        fw.tt(c.mergedT[:, dc, tok], acc, tm, ALU.add, e="pool")
    def zslot(tt):
        return c.zt.s(tt % 3, (slice(None), tt % 3, slice(None)))

    fw.dma(zslot(0), V(c.xres[0:128, :], [c.tr_xres[0]]))

    def stage_a(tt):
        z = zslot(tt)
        if tt + 1 < 16:
            fw.dma(zslot(tt + 1), V(c.xres[(tt + 1) * 128:(tt + 2) * 128, :], [c.tr_xres[tt + 1]]))
        for hf in range(2):
            po = psA(c)
            for kc in range(8):
                fw.mm(po[:, :], c.mergedT[:, kc, tt * 128:(tt + 1) * 128], c.wout_sb[:, kc, hf * 512:(hf + 1) * 512],
                      start=(kc == 0), stop=False)
            fw.mm(po[:, :], c.ones_row[0:1, :], c.bout_row[0:1, hf * 512:(hf + 1) * 512], start=False, stop=True)
            zh = V(z.ap[:, hf * 512:(hf + 1) * 512], z.trs)
            fw.stt(zh, zh, ALPHA, po[:, :], ALU.mult, ALU.add)
        ln_stats(c, z, tt % 2)

    def stage_b1(tt):
        sl = tt % 2
        ln_apply(c, zslot(tt), 1, 2, c.ot.s(sl, (slice(None), sl, slice(None))), sl)

    def stage_b2(tt):
        sl = tt % 2
        o = c.ot.s(sl, (slice(None), sl, slice(None)))
        if l == 0:
            fw.dma(V(c.xres[tt * 128:(tt + 1) * 128, :], [c.tr_xres[tt]]), o)
            to_xT(c, o, tt)
        else:
            fw.dma(V(c.y[s, tt * 128:(tt + 1) * 128, :], [c.tr_y]), o)

    run_pipeline3(stage_a, stage_b1, stage_b2, 16)


def layer(c, s, l):
    fw = c.fw
    if "B" in c.parts:
        branch_b(c, s, l)
    for bi, name in enumerate("ABCD"):
        if name not in c.parts:
            fw.memset(c.yT.s(bi, (slice(None), slice(bi * 2, bi * 2 + 2), slice(None))), 0.0, e="pool")
    if "A" in c.parts:
        branch_a(c, s, l)
    if "C" in c.parts:
        branch_c(c, s, l)
    if "D" in c.parts:
        branch_d(c, s, l)
    if c.dbg and l == 0 and s == 0:
        fw.dma(V(c.dbg_out[:, :, :], [c.tr_dbg]), c.yT[:, :, :])
    final_stage(c, s, l)


def rope_proj(c, l, qcol, perm, cosT, sinT, dst, tmp, t2b, qbb):
    fw = c.fw

    def consume(i, tb, ps):
        tok = slice(tb * 512, (tb + 1) * 512)
        ch = qcol // 128 + i
        sl = (i * NTB + tb) % 2
        qb = qbb.s(sl, (slice(None), sl, slice(None)))
        fw.act(qb, ps[:, :], AF.Identity, bias=c.bfm[:, l, ch:ch + 1])
        t1 = t2b.s(sl, (slice(None), sl, slice(None)))
        fw.stt(t1, ps[:, :], c.bfm[:, l, ch:ch + 1], cosT[:, tok], ALU.add, ALU.mult)
        pr = psB(c)
        fw.mm(pr[:, :], perm, qb)
        t2 = V(tmp.h[:, sl, :], [tmp.trs[0]])
        fw.tt(t2, pr[:, :], sinT[:, tok], ALU.mult)
        fw.tt(dst[:, i, tok], t1, t2, ALU.add, e="pool")

    proj_fm(c, l, [qcol, qcol + 128], consume)


def plain_proj(c, l, col, dst):
    fw = c.fw

    def consume(i, tb, ps):
        ch = col // 128 + i
        fw.ts(dst[:, i, tb * 512:(tb + 1) * 512], ps[:, :], c.bfm[:, l, ch:ch + 1], None, ALU.add)

    proj_fm(c, l, [col, col + 128], consume)


def vaug_init(c, vaug):
    v5 = vaug.h.rearrange("p j (hp par) e -> p j hp par e", par=2)
    c.fw.memset(V(v5[:, :, :, 0, 64:128], vaug.trs), 1.0, e="pool")
    c.fw.memset(V(v5[:, :, :, 1, 0:64], vaug.trs), 1.0, e="pool")


def v_proj(c, l, vcol, vaug, vbias, tok_sel, ntiles=16):
    fw = c.fw
    slot = load_w(c, l, vcol, 256)
    fw.dma(vbias[:, :], V(c.b_cat[l:l + 1, vcol:vcol + 256].partition_broadcast(128), [c.tr_in]))
    v5 = vaug.h.rearrange("p j (hp par) e -> p j hp par e", par=2)
    b4 = vbias.h.rearrange("p (hp par e) -> p hp par e", hp=2, par=2)
    for j in range(ntiles):
        ps = psA(c)
        for kc in range(8):
            fw.mm(ps[:, 0:256], V(c.xT.h[:, kc, tok_sel(j)], c.xT.trs), wv(c, slot, kc, 0, 256),
                  start=(kc == 0), stop=(kc == 7))
        p4 = ps.h[:, 0:256].rearrange("p (hp par e) -> p hp par e", hp=2, par=2)
        fw.tt(V(v5[:, j, :, 0, 0:64], vaug.trs), V(p4[:, :, 0, :], ps.trs), V(b4[:, :, 0, :], vbias.trs), ALU.add)
        fw.tt(V(v5[:, j, :, 1, 64:128], vaug.trs), V(p4[:, :, 1, :], ps.trs), V(b4[:, :, 1, :], vbias.trs), ALU.add)


def branch_c(c, s, l):
    fw = c.fw
    arena_reset(c)
    lam_init = 0.8 - 0.6 * math.exp(-0.3 * l)
    cosT = alloc(c, [128, T], F32)
    sinT = alloc(c, [128, T], F32)
    fw.dma(cosT[:, :], V(c.rope[0], [c.tr_in]))
    fw.dma(sinT[:, :], V(c.rope[1], [c.tr_in]))
    qT = alloc(c, [128, 2, T], BF16)
    kT = alloc(c, [128, 2, T], BF16)
    tmp = alloc(c, [128, 4, 512], F32)
    t2b = alloc(c, [128, 2, 512], F32, nslots=2)
    vaug = alloc(c, [128, 16, 4, 128], BF16)
    vbias = alloc(c, [128, 256], F32)
    c.g_sig = alloc(c, [128, 2, 512], F32, nslots=2)
    pt = alloc(c, [128, 4, 512], BF16, nslots=4)
    sm = alloc(c, [128, 16], F32)
    lamt = alloc(c, [128, 128], F32)
    prm = alloc(c, [128, 64], F32)
    ofull = alloc(c, [128, 512], F32)
    osq = alloc(c, [128, 512], F32)
    w1 = alloc(c, [128, 2, 512], F32, nslots=2)
    w2 = alloc(c, [128, 2, 512], F32, nslots=2)
    blk = alloc(c, [128, 128], F32)
    permC = alloc(c, [128, 128], BF16)
    qbb = alloc(c, [128, 2, 512], BF16, nslots=2)
    fw.dma(permC[:, :], V(c.perms[0], [c.tr_in]))
    fw.dma(blk[:, :], V(c.blk1[:, :], [c.tr_in]))
    fw.dma(prm[:, :], V(c.rwp[l], [c.tr_in]))
    fw.dma(lamt[:, :], V(c.df_lam[l:l + 1, :].partition_broadcast(128), [c.tr_in]))
    fw.tt(lamt[:, 0:32], lamt[:, 0:32], lamt[:, 32:64], ALU.mult)
    fw.tt(lamt[:, 64:96], lamt[:, 64:96], lamt[:, 96:128], ALU.mult)
    fw.emit("dve", lambda E: E.tensor_reduce(out=sm.h[:, 0:1], in_=lamt.h[:, 0:32], axis=mybir.AxisListType.X,
                                             op=ALU.add), [lamt[:, :]], [sm[:, :]])
    fw.emit("dve", lambda E: E.tensor_reduce(out=sm.h[:, 1:2], in_=lamt.h[:, 64:96], axis=mybir.AxisListType.X,
                                             op=ALU.add), [lamt[:, :]], [sm[:, :]])
    fw.act(sm[:, 2:4], sm[:, 0:2], AF.Exp)
    fw.tt(sm[:, 4:5], sm[:, 3:4], sm[:, 2:3], ALU.subtract)
    fw.ts(sm[:, 5:6], sm[:, 4:5], -lam_init, None, ALU.add)
    fw.ts(sm[:, 6:7], prm[:, 0:1], 1.0 - lam_init, None, ALU.mult)

    vaug_init(c, vaug)
    rope_proj(c, l, C_Q, permC[:, :], cosT, sinT, qT, tmp, t2b, qbb)
    rope_proj(c, l, C_K, permC[:, :], cosT, sinT, kT, tmp, t2b, qbb)
    v_proj(c, l, C_V, vaug, vbias, lambda j: slice(j * 128, (j + 1) * 128))

    qz = [alloc(c, [128, 2, T], BF16), alloc(c, [128, 2, T], BF16)]
    for i in range(2):
        fw.ts(qz[i][:, :, :], qT[:, :, :], prm[:, 29 + i:30 + i], None, ALU.mult, e="dve")
    scale = 32 ** -0.5
    items = [(hp, tb, i, kt, par) for hp in range(2) for tb in range(NTB) for i in range(2) for kt in range(16)
             for par in range(2)]
    pbuf = {}
    state = {}

    def stage1(n):
        hp, tb, i, kt, par = items[n]
        tok = slice(tb * 512, (tb + 1) * 512)
        pb = par * 64
        sc = psA(c)
        fw.mm(sc[:, :], kT[pb:pb + 64, hp, kt * 128:(kt + 1) * 128], qz[i][pb:pb + 64, hp, tok])
        p = pt.s(n % 4, (slice(None), n % 4, slice(None)))
        fw.act(p, sc[:, :], AF.Exp, scale=scale)
        pbuf[n] = p

    def stage2(n):
        hp, tb, i, kt, par = items[n]
        tok = slice(tb * 512, (tb + 1) * 512)
        h = hp * 2 + par
        if kt == 0:
            state[("acc", par)] = psB(c)
        acc = state[("acc", par)]
        fw.mm(acc[:, :], vaug[:, kt, h, :], pbuf.pop(n), start=(kt == 0), stop=(kt == 15))
        if kt == 15:
            olo, dlo = (0, 64) if par == 0 else (64, 0)
            o = slice(olo, olo + 64)
            d = slice(dlo, dlo + 64)
            r1 = w1.s(i, (o, i, slice(None)))
            fw.recip(r1, acc[d, :])
            t1 = w2.s(i, (o, i, slice(None)))
            fw.tt(t1, acc[o, :], r1, ALU.mult)
            if i == 1:
                t0 = w2.s(0, (o, 0, slice(None)))
                fw.stt(ofull[o, :], t1, sm[o, 5:6], t0, ALU.mult, ALU.add)
                if par == 1:
                    fw.tt(osq[:, :], ofull[:, :], ofull[:, :], ALU.mult)
                    ss = psB(c)
                    fw.mm(ss[:, :], blk[:, :], osq[:, :])
                    fw.ts(osq[:, :], ss[:, :], 1.0 / 64.0, 1e-5, ALU.mult, ALU.add)
                    fw.act(osq[:, :], osq[:, :], AF.Sqrt)
                    rs_ = w1.s(0, (slice(None), 0, slice(None)))
                    fw.recip(rs_, osq[:, :])
                    fw.stt(c.yT.s(2, (slice(None), 4 + hp, tok)), ofull[:, :], sm[:, 6:7], rs_, ALU.mult, ALU.mult)

    npair = len(items) // 2
    for m in range(npair + 1):
        if m < npair:
            stage1(2 * m)
            stage1(2 * m + 1)
        if m >= 1:
            stage2(2 * m - 2)
            stage2(2 * m - 1)
    gate_branch(c, l, C_G, 2)


def _host_inputs(inp):
    f32 = np.float32
    g = lambda k: np.asarray(inp[k], f32)
    w_in = g("w_in")
    b_in = g("b_in")
    perm = _rot_perm()
    b_cat = np.ascontiguousarray(np.concatenate([b_in, b_in[:, perm]], axis=1))
    b_fm = np.ascontiguousarray(b_cat.reshape(2, NCH, 128).transpose(0, 2, 1))
    vecs = np.zeros((8, DM), f32)
    vecs[0], vecs[1] = g("ln0_g"), g("ln0_b")
    for l in range(2):
        vecs[2 + 3 * l], vecs[3 + 3 * l], vecs[4 + 3 * l] = g("b_out")[l], g("ln_g")[l], g("ln_b")[l]
    rope = np.stack(_rope_tables(), 0)
    i = np.arange(128)[:, None]
    j = np.arange(128)[None, :]
    band = np.stack([(i - j >= 64), (np.abs(i - j) <= 64), (j - i >= 64)], 1).astype(f32)
    bandm = np.ascontiguousarray(np.broadcast_to(band[:, :, None, :], (128, 3, 4, 128))).astype(ml_dtypes.bfloat16)
    rpb = g("na_rpb")
    kap = np.arange(2)[:, None, None, None, None]
    kc = np.arange(64)[None, :, None, None, None]
    oi = np.arange(14)[None, None, :, None, None]
    hh = np.asarray([0, 2, 1, 3])[None, None, None, :, None]
    qc = np.arange(64)[None, None, None, None, :]
    dr = (oi - 7) + kap
    dc = np.clip(kc - qc + 15, 0, 30)
    shp = (2, 64, 14, 4, 64)
    na_bias = np.stack([rpb[l][np.broadcast_to(hh, shp), np.broadcast_to(dr + 7, shp), np.broadcast_to(dc, shp)]
                        for l in range(2)], 0).reshape(2, 128, 14 * 256).astype(f32)
    cst = np.clip(qc - 8, 0, 48)
    ok = (kc >= cst) & (kc < cst + 16)
    na_mask = np.ascontiguousarray(np.broadcast_to(ok, shp)).reshape(128, 14 * 256).astype(f32)
    rwp = np.zeros((2, 128, 64), f32)
    p = np.arange(128)
    mu = g("rw_mu")
    for l in range(2):
        rwp[l, :, 0] = g("df_subln_g")[l][p % 64]
        for hp in range(2):
            ch = hp * 128 + p
            for q in range(4):
                rwp[l, :, 1 + 2 * q + hp] = mu[l, q * 256 + ch]
            for d in range(2):
                rwp[l, :, 11 + d * 2 + hp] = g("rw_w0")[l, d, ch]
                rwp[l, :, 15 + d * 2 + hp] = g("rw_a0")[l, d, ch]
            rwp[l, :, 19 + hp] = g("rw_kk")[l, ch]
            rwp[l, :, 21 + hp] = g("rw_ka")[l, ch]
            rwp[l, :, 23 + hp] = g("rw_rk")[l].reshape(256)[ch]
            rwp[l, :, 25 + hp] = g("rw_lnx_g")[l, ch]
            rwp[l, :, 27 + hp] = g("rw_lnx_b")[l, ch]
        rwp[l, :, 29] = ((p % 64) < 32)
        rwp[l, :, 30] = ((p % 64) >= 32)
        rwp[l, :, 9] = mu[l, 1024 + p]
        rwp[l, :, 10] = mu[l, 1152 + p]
    s_ = np.arange(64)[:, None]
    t_ = np.arange(64)[None, :]
    tri = np.zeros((64, 2, 2, 64), f32)
    tri[:, 0, 0], tri[:, 0, 1] = (s_ < t_), (s_ <= t_)
    tri[:, 1, 0], tri[:, 1, 1] = (s_ > t_), (s_ >= t_)
    blk1 = np.zeros((128, 128), f32)
    blk1[:64, :64] = 1
    blk1[64:, 64:] = 1
    perms = np.zeros((2, 128, 128), f32)
    pp = np.arange(128)
    for qi, dd in enumerate((32, 64)):
        jj = pp % dd
        partner = pp - jj + (jj + dd // 2) % dd
        perms[qi, partner, pp] = 1.0
    tri3 = np.zeros((64, 2, 192), f32)
    tri3[:, :, 0:128] = tri.reshape(64, 2, 128)
    tri3[:, 0, 128:192] = (s_ > t_)
    tri3[:, 1, 128:192] = (s_ < t_)
    return {
        "w_in": w_in, "w_branch": g("w_branch"), "w_out": g("w_out"),
        "b_fm": b_fm, "b_cat": b_cat, "vecs": vecs,
        "ident_b": np.eye(128, dtype=f32).astype(ml_dtypes.bfloat16), "ident_f": np.eye(128, dtype=f32),
        "rope": np.ascontiguousarray(rope), "bandm": bandm, "na_bias": na_bias, "na_mask": na_mask,
        "rwp": rwp, "rw_w2": np.ascontiguousarray(g("rw_w2").reshape(2, 128, 256)),
        "rw_a2": np.ascontiguousarray(g("rw_a2").reshape(2, 128, 256)),
        "df_lam": np.ascontiguousarray(g("df_lam").reshape(2, 128)),
        "trimask": tri3, "blk1": blk1, "perms": perms.astype(ml_dtypes.bfloat16),
    }


_NC_CACHE = {}


def kernel(**inputs):
    xp = np.asarray(inputs["x_prompt"], np.float32)
    xs = np.asarray(inputs["x_sample"], np.float32)
    shared = _host_inputs(inputs)
    ncores = 8
    if "nc" not in _NC_CACHE:
        _NC_CACHE["nc"] = build(6)[0]
    nc = _NC_CACHE["nc"]
    in_maps = []
    for k in range(ncores):
        xk = np.concatenate([xp[4 * k:4 * k + 4], xs[2 * k:2 * k + 2]], axis=0)
        m = dict(shared)
        m["x"] = np.ascontiguousarray(xk)
        in_maps.append(m)
    res = run_bass_kernel_spmd(nc, in_maps, core_ids=list(range(ncores)))
    yp = np.empty_like(xp)
    ys = np.empty_like(xs)
    for k in range(ncores):
        yk = np.asarray(res.results[k]["y"], np.float32)
        yp[4 * k:4 * k + 4] = yk[0:4]
        ys[2 * k:2 * k + 2] = yk[4:6]
    return (yp, ys)


def branch_a(c, s, l):
    fw = c.fw
    arena_reset(c)
    qT = alloc(c, [128, 2, T], BF16)
    kT = alloc(c, [128, 2, T], BF16)
    vaug0 = alloc(c, [128, 16, 4, 128], BF16)
    vaug1 = alloc(c, [128, 15, 4, 128], BF16)
    vbias = alloc(c, [128, 256], F32)
    stg = alloc(c, [128, 14 * 256], F32)
    msk = alloc(c, [128, 14 * 256], F32)
    Mb = alloc(c, [128, 14, 256], BF16)
    pt = alloc(c, [128, 2, 4, 256], BF16, nslots=2)
    rd = alloc(c, [128, 2, 256], F32, nslots=2)
    c.g_sig = alloc(c, [128, 2, 512], F32, nslots=2)
    fw.dma(stg[:, :], V(c.na_bias[l], [c.tr_in]))
    fw.dma(msk[:, :], V(c.na_mask[:, :], [c.tr_in]))
    fw.act(stg[:, :], stg[:, :], AF.Exp)
    fw.tt(V(Mb.h.rearrange("p a b -> p (a b)"), Mb.trs), stg[:, :], msk[:, :], ALU.mult)
    vaug_init(c, vaug0)
    vaug_init(c, vaug1)
    plain_proj(c, l, A_Q, qT)
    plain_proj(c, l, A_K, kT)
    v_proj(c, l, A_V, vaug0, vbias, lambda j: slice(j * 128, (j + 1) * 128), 16)
    v_proj(c, l, A_V, vaug1, vbias, lambda j: slice(64 + j * 128, 64 + (j + 1) * 128), 15)
    rows = {}

    def stage1(r):
        rs = min(max(r - 4, 0), 24)
        sl = r % 2
        qs = slice(64 * r, 64 * r + 64)
        tiles = []
        for j in range(4):
            kr0 = rs + 2 * j
            oi = (rs - r + 2 * j) + 7
            p = pt.s(sl, (slice(None), sl, j, slice(None)))
            scs = [psA(c), psA(c)]
            for hp in range(2):
                for par in range(2):
                    pb = par * 64
                    fw.mm(scs[par][:, hp * 64:(hp + 1) * 64], kT[pb:pb + 64, hp, 64 * kr0:64 * kr0 + 128],
                          qT[pb:pb + 64, hp, qs])
            for par in range(2):
                fw.act(V(p.ap[:, par * 128:(par + 1) * 128], p.trs), scs[par][:, 0:128], AF.Exp, scale=0.125)
            fw.tt(p, p, Mb[:, oi, :], ALU.mult, e=("pool" if j % 2 == 0 else "dve"))
            tiles.append((p, (vaug0, kr0 // 2) if kr0 % 2 == 0 else (vaug1, (kr0 - 1) // 2)))
        rows[r] = tiles

    def stage2(r):
        tiles = rows.pop(r)
        sl = r % 2
        qs = slice(64 * r, 64 * r + 64)
        acc = psB(c)
        for h in range(4):
            for j, (p, (va, ti)) in enumerate(tiles):
                pc = (h % 2) * 128 + (h // 2) * 64
                fw.mm(acc[:, h * 64:(h + 1) * 64], va[:, ti, h, :], V(p.ap[:, pc:pc + 64], p.trs),
                      start=(j == 0), stop=(j == 3))
        a4 = acc.h[:, 0:256].rearrange("p (hp par q) -> p hp par q", hp=2, par=2)
        r4 = rd.h[:, sl, :].rearrange("p (hp par q) -> p hp par q", hp=2, par=2)
        rtr = [rd.trs[sl]]
        fw.recip(V(r4[0:64, :, 0, :], rtr), V(a4[64:128, :, 0, :], acc.trs))
        fw.recip(V(r4[64:128, :, 1, :], rtr), V(a4[0:64, :, 1, :], acc.trs))
        fw.tt(c.yT.s(0, (slice(0, 64), slice(0, 2), qs)), V(a4[0:64, :, 0, :], acc.trs), V(r4[0:64, :, 0, :], rtr), ALU.mult)
        fw.tt(c.yT.s(0, (slice(64, 128), slice(0, 2), qs)), V(a4[64:128, :, 1, :], acc.trs), V(r4[64:128, :, 1, :], rtr),
              ALU.mult)

    stage1(0)
    for r in range(32):
        if r + 1 < 32:
            stage1(r + 1)
        stage2(r)
    gate_branch(c, l, A_G, 0)


def branch_d(c, s, l):
    fw = c.fw
    arena_reset(c)
    cosT = alloc(c, [128, T], F32)
    sinT = alloc(c, [128, T], F32)
    fw.dma(cosT[:, :], V(c.rope[2], [c.tr_in]))
    fw.dma(sinT[:, :], V(c.rope[3], [c.tr_in]))
    acc = alloc(c, [128, 4, T], F32)
    qT = alloc(c, [128, 2, T], BF16)
    kT = alloc(c, [128, 2, T], BF16)
    tmp = alloc(c, [128, 4, 512], F32)
    t2b = alloc(c, [128, 2, 512], F32, nslots=2)
    c.g_sig = t2b
    vaug = alloc(c, [128, 16, 4, 128], BF16)
    vbias = alloc(c, [128, 256], F32)
    pt = alloc(c, [128, 2, 3, 512], BF16, nslots=2)
    bm = alloc(c, [128, 3, 512], BF16)
    permD = alloc(c, [128, 128], BF16)
    qbb = alloc(c, [128, 2, 512], BF16, nslots=2)
    fw.dma(permD[:, :], V(c.perms[1], [c.tr_in]))
    fw.dma(bm[:, :, :], V(c.bandm.rearrange("p a h q -> p a (h q)"), [c.tr_in]))
    vaug_init(c, vaug)
    unit = 0
    for g, dil in enumerate((1, 4, 16)):
        nqb = T // dil // 128
        base = D_BASE + g * 768
        rope_proj(c, l, base, permD[:, :], cosT, sinT, qT, tmp, t2b, qbb)
        rope_proj(c, l, base + 256, permD[:, :], cosT, sinT, kT, tmp, t2b, qbb)

        def tsel(j, dil=dil, nqb=nqb):
            rho, jb = divmod(j, nqb)
            st = 128 * jb * dil + rho
            return slice(st, st + 127 * dil + 1, dil)

        v_proj(c, l, base + 512, vaug, vbias, tsel, 16)
        units = [(rho, qb) for rho in range(dil) for qb in range(nqb)]
        ust = {}

        def stage1(ui, units=units, tsel=tsel, nqb=nqb, ust=ust):
            rho, qb = units[ui]
            qs = tsel(rho * nqb + qb)
            kts = [(jb, mt) for (jb, mt) in ((qb - 1, 0), (qb, 1), (qb + 1, 2)) if 0 <= jb < nqb]
            sl = (unit0 + ui) % 2
            ps_ = []
            for idx, (jb, mt) in enumerate(kts):
                ks = tsel(rho * nqb + jb)
                p = pt.s(sl, (slice(None), sl, idx, slice(None)))
                scs = [psA(c), psA(c)]
                for hp in range(2):
                    for par in range(2):
                        pb = par * 64
                        fw.mm(scs[par][:, hp * 128:(hp + 1) * 128], V(kT.h[pb:pb + 64, hp, ks], kT.trs),
                              V(qT.h[pb:pb + 64, hp, qs], qT.trs))
                for par in range(2):
                    fw.act(V(p.ap[:, par * 256:(par + 1) * 256], p.trs), scs[par][:, 0:256], AF.Exp, scale=0.125)
                fw.tt(p, p, bm[:, mt, :], ALU.mult, e=("pool" if idx % 2 == 0 else "dve"))
                ps_.append(p)
            ust[ui] = (qs, kts, ps_)

        def stage2(ui, units=units, nqb=nqb, ust=ust, g=g):
            rho, qb = units[ui]
            qs, kts, ps_ = ust.pop(ui)
            ob = psB(c)
            for h in range(4):
                for idx, (jb, mt) in enumerate(kts):
                    pc = (h % 2) * 256 + (h // 2) * 128
                    fw.mm(ob[:, h * 128:(h + 1) * 128], vaug[:, rho * nqb + jb, h, :],
                          V(ps_[idx].ap[:, pc:pc + 128], ps_[idx].trs),
                          start=(idx == 0), stop=(idx == len(kts) - 1))
            dst = V(acc.h[:, :, qs], acc.trs)
            src = V(ob.h[:, :].rearrange("p (h q) -> p h q", h=4), ob.trs)
            if g == 0:
                fw.copy(dst, src, e="act")
            else:
                fw.tt(dst, src, dst, ALU.add)

        unit0 = unit
        stage1(0)
        for ui in range(len(units)):
            if ui + 1 < len(units):
                stage1(ui + 1)
            stage2(ui)
        unit += len(units)
    a5 = acc.h.rearrange("p (hp par) t -> p hp par t", par=2)
    t5 = tmp.h.rearrange("p (hp par) t -> p hp par t", par=2)
    for tb in range(NTB):
        tok = slice(tb * 512, (tb + 1) * 512)
        fw.recip(V(t5[0:64, :, 0, :], tmp.trs), V(a5[64:128, :, 0, tok], acc.trs))
        fw.recip(V(t5[64:128, :, 1, :], tmp.trs), V(a5[0:64, :, 1, tok], acc.trs))
        fw.tt(c.yT.s(3, (slice(0, 64), slice(6, 8), tok)), V(a5[0:64, :, 0, tok], acc.trs), V(t5[0:64, :, 0, :], tmp.trs),
              ALU.mult)
        fw.tt(c.yT.s(3, (slice(64, 128), slice(6, 8), tok)), V(a5[64:128, :, 1, tok], acc.trs),
              V(t5[64:128, :, 1, :], tmp.trs), ALU.mult)
    gate_branch(c, l, D_G, 3)


USE_NTI = True
GC = 4


def yslot_f32(c, slot):
    ap = c.yT.h[:, 2 * slot:2 * slot + 2, :].rearrange("p a t -> p (a t)").bitcast(F32)
    return Buf(ap, 1), c.yT.trs[slot]


def branch_b(c, s, l):
    for hp in range(2):
        rwkv_hp(c, l, hp)


def rwkv_hp(c, l, hp):
    fw = c.fw
    arena_reset(c)
    CD = -math.exp(-0.5)
    prm = alloc(c, [128, 64], F32)
    blk = alloc(c, [128, 128], F32)
    wst = alloc(c, [128, 2, 256], F32)
    w2sb = alloc(c, [128, 256], BF16)
    a2sb = alloc(c, [128, 256], BF16)
    der = alloc(c, [128, 16], F32)
    ones = alloc(c, [128, 64], F32)
    fw.dma(prm[:, :], V(c.rwp[l], [c.tr_in]))
    fw.dma(blk[:, :], V(c.blk1[:, :], [c.tr_in]))
    fw.dma(wst[:, 0, :], V(c.rw_w2[l], [c.tr_in]))
    fw.dma(wst[:, 1, :], V(c.rw_a2[l], [c.tr_in]))
    fw.copy(w2sb[:, :], wst[:, 0, :])
    fw.copy(a2sb[:, :], wst[:, 1, :])
    fw.memset(ones[:, :], 1.0)
    mucols = [1 + hp, 3 + hp, 5 + hp, 7 + hp, 9, 10]
    for i, mc in enumerate(mucols):
        fw.ts(der[:, i:i + 1], prm[:, mc:mc + 1], -1.0, 1.0, ALU.mult, ALU.add)
        fw.ts(der[:, 6 + i:7 + i], prm[:, mc:mc + 1], 0.5, None, ALU.mult)
    fw.ts(der[:, 12:13], prm[:, 21 + hp:22 + hp], -1.0, 1.0, ALU.mult, ALU.add)
    vT = alloc(c, [128, T], BF16)
    gs = alloc(c, [128, T], BF16)
    bv = alloc(c, [128, T], BF16)
    AR = [alloc(c, [128, 2, T], BF16) for _ in range(2)]
    BK = [alloc(c, [128, 2, T], BF16) for _ in range(2)]
    RLO = [alloc(c, [64, T], BF16) for _ in range(2)]
    eL = alloc(c, [128, 2, 32], F32)
    mark = c.arena_off

    uT = alloc(c, [128, 6, T + 2], BF16)
    fw.memset(uT[:, :, 0:1], 0.0)
    fw.memset(uT[:, :, T + 1:T + 2], 0.0)
    cols = [B_R + hp * 128, B_K + hp * 128, B_V + hp * 128, B_GT + hp * 128, B_WL, B_AL]

    def consume(i, tb, ps):
        ch = cols[i] // 128
        fw.act(uT[:, i, 1 + tb * 512:1 + (tb + 1) * 512], ps[:, :], AF.Identity, bias=c.bfm[:, l, ch:ch + 1])

    proj_fm(c, l, cols, consume)
    tbuf = [alloc(c, [128, 512], F32)[:, :] for _ in range(11)]
    slot_tmps = []
    for slot_ in (0, 2, 3):
        ex_ap = c.yT.h[:, 2 * slot_:2 * slot_ + 2, :].rearrange("p a t -> p (a t)").bitcast(F32)
        for i in range(4):
            t_ = Tr()
            t_.r = dict(c.yT.trs[slot_].r)
            t_.w = c.yT.trs[slot_].w
            slot_tmps.append((slot_, t_))
            tbuf.append(V(ex_ap[:, i * 512:(i + 1) * 512], [t_]))
    xr, xk, kk, tA, tB = tbuf[0:5]
    dtmp = [tbuf[5:14], tbuf[14:23]]
    rmask = alloc(c, [128, 512], F32)
    fw.memset(rmask[:, :], 1.0)
    fw.memset(V(rmask.h[:, 0:512:64], rmask.trs), 0.0)
    twl = alloc(c, [128, 512], BF16)
    tal = alloc(c, [128, 512], BF16)
    for tb in range(NTB):
        tok = slice(tb * 512, (tb + 1) * 512)

        def shift(i, out):
            fw.tt(tA, uT[:, i, tb * 512:tb * 512 + 512], uT[:, i, tb * 512 + 2:tb * 512 + 514], ALU.add, e="pool")
            fw.ts(tA, tA, der[:, 6 + i:7 + i], None, ALU.mult)
            fw.stt(out, uT[:, i, tb * 512 + 1:tb * 512 + 513], der[:, i:i + 1], tA, ALU.mult, ALU.add)

        shift(0, xr)
        shift(1, xk)
        shift(2, tB)
        fw.copy(vT[:, tok], tB, e="act")
        shift(3, tB)
        fw.act(kk, tB, AF.Sigmoid)
        fw.tt(gs[:, tok], tB, kk, ALU.mult, e="pool")
        shift(4, tB)
        fw.act(twl[:, :], tB, AF.Tanh)
        shift(5, tB)
        fw.copy(tal[:, :], tB, e="act")
        fw.ts(kk, xk, prm[:, 19 + hp:20 + hp], None, ALU.mult)
        fw.tt(tB, kk, kk, ALU.mult, e="pool")
        ps = psA(c)
        fw.mm(ps[:, :], blk[:, :], tB)
        fw.ts(tB, ps[:, :], 1e-24, None, ALU.max)
        fw.act(tB, tB, AF.Sqrt)
        fw.recip(tA, tB)
        fw.tt(kk, kk, tA, ALU.mult)

        def dir_chain(d):
            lw, L0, D_, eP, eM, eX, a_, kd, tD = dtmp[d]
            dsl = slice(d * 64, (d + 1) * 64)
            ps = psA(c)
            fw.mm(ps[:, :], w2sb[dsl, hp * 128:(hp + 1) * 128], twl[dsl, :])
            yield
            fw.act(lw, ps[:, :], AF.Sigmoid, bias=prm[:, 11 + d * 2 + hp:12 + d * 2 + hp])
            yield
            ps2 = psA(c)
            fw.mm(ps2[:, :], a2sb[dsl, hp * 128:(hp + 1) * 128], tal[dsl, :])
            fw.emit("dve", lambda E: E.tensor_tensor_scan(out=L0.ap, data0=rmask.h[:, :], data1=lw.ap,
                                                        initial=0.0, op0=ALU.mult, op1=ALU.add),
                    [rmask[:, :], lw], [L0])
            yield
            fw.act(a_, ps2[:, :], AF.Sigmoid, bias=prm[:, 15 + d * 2 + hp:16 + d * 2 + hp])
            yield
            ltot = V(L0.ap[:, 63:512:64], L0.trs)
            fw.act(eL[:, d, tb * 8:(tb + 1) * 8], ltot, AF.Exp, scale=CD)
            if d == 0:
                fw.tt(D_, L0, lw, ALU.subtract, e="pool")
                yield
                fw.act(eP, L0, AF.Exp, scale=CD)
                yield
                fw.act(eM, L0, AF.Exp, scale=-CD)
                yield
                fw.act(eX, D_, AF.Exp, scale=CD)
                yield
            else:
                l3 = V(L0.ap.rearrange("p (c t) -> p c t", t=64), L0.trs)
                lt3 = V(L0.ap[:, 63:512:64].unsqueeze(2).to_broadcast([128, 8, 64]), L0.trs)
                fw.tt(V(D_.ap.rearrange("p (c t) -> p c t", t=64), D_.trs), l3, lt3, ALU.subtract)
                yield
                fw.tt(lw, D_, lw, ALU.subtract, e="pool")
                yield
                fw.act(eX, D_, AF.Exp, scale=-CD)
                yield
                fw.act(eP, lw, AF.Exp, scale=-CD)
                yield
                fw.act(eM, lw, AF.Exp, scale=CD)
                yield
            fw.ts(tD, a_, prm[:, 21 + hp:22 + hp], der[:, 12:13], ALU.mult, ALU.add)
            yield
            fw.tt(kd, tD, xk, ALU.mult)
            yield
            fw.tt(tD, kk, a_, ALU.mult, e="pool")
            yield
            fw.tt(AR[d][:, 0, tok], kk, eX, ALU.mult)
            yield
            fw.tt(BK[d][:, 0, tok], tD, eM, ALU.mult)
            yield
            fw.tt(BK[d][:, 1, tok], kd, eM, ALU.mult, e="pool")
            yield
            fw.tt(AR[d][:, 1, tok], xr, eP, ALU.mult)
            yield
            fw.tt(RLO[d][0:64, tok], V(xr.ap[64:128, :], xr.trs), V(eP.ap[64:128, :], eP.trs), ALU.mult)
            yield

        gens = [dir_chain(0), dir_chain(1)]
        while gens:
            for g_ in list(gens):
                try:
                    next(g_)
                except StopIteration:
                    gens.remove(g_)
        fw.tt(tA, dtmp[0][7], dtmp[1][7], ALU.add, e="pool")
        fw.stt(tB, xr, prm[:, 23 + hp:24 + hp], tA, ALU.mult, ALU.mult)
        ps = psA(c)
        fw.mm(ps[:, :], blk[:, :], tB)
        fw.tt(bv[:, tok], ps[:, :], vT[:, tok], ALU.mult)

    arena_release(c, mark)
    for slot_, t_ in slot_tmps:
        dst = c.yT.trs[slot_]
        evs = list(t_.r.values()) + ([t_.w] if t_.w is not None else [])
        for ev in evs:
            old_ = dst.r.get(ev[2])
            if old_ is None or (ev[0], ev[1]) > (old_[0], old_[1]):
                dst.r[ev[2]] = ev
    ys = []
    for d in range(2):
        b_, tr_ = yslot_f32(c, (0, 2)[d])
        b_.trs = [tr_]
        ys.append(b_)
    eLs = alloc(c, [64, 2, 2, 32], F32)
    for d in range(2):
        for hl in range(2):
            fw.copy(eLs[:, d, hl, :], eL[hl * 64:(hl + 1) * 64, d, :], e="act")
    tri = alloc(c, [64, 2, 192], F32)
    fw.dma(tri[:, :, :], V(c.trimask[:, :, :], [c.tr_in]))
    NM = GC * 4
    tmT2 = [alloc(c, [64, GC, 2, 4, 128], BF16) for _ in range(2)]
    Am2 = [alloc(c, [64, NM, 256], BF16) for _ in range(2)]
    TT2 = [alloc(c, [64, NM, 64], BF16) for _ in range(2)]
    PT2 = [alloc(c, [64, NM, 64], BF16) for _ in range(2)]
    Zb2 = [alloc(c, [64, NM, 64], BF16) for _ in range(2)]
    Nb = [alloc(c, [64, NM, 64], BF16) for _ in range(2)]
    NTb = [alloc(c, [64, NM, 64], BF16) for _ in range(2)]
    NTI = alloc(c, [64, NM, 64], BF16)
    Rb = [alloc(c, [64, NM, 64], BF16) for _ in range(2)]
    Sb = alloc(c, [64, 2, 4, 64], BF16, nslots=2)
    Ub = alloc(c, [64, 4, 64], BF16)
    fw.memset(Sb[:, :, :, :], 0.0)
    I64b = c.identb[0:64, 0:64]
    nsteps = T // 64
    ngroups = nsteps // GC
    halves = [(0, NM // 2), (NM // 2, NM)]
    idb = V(c.identf.h[0:64, 0:64].unsqueeze(1).to_broadcast([64, NM // 2, 64]), c.identf.trs)

    def chunk_of(g, ci, d):
        it = g * GC + ci
        return it if d == 0 else nsteps - 1 - it

    def flat(buf, m0, m1):
        return V(buf.h[:, m0:m1, :].rearrange("p m e -> p (m e)"), buf.trs)

    def phase1(g):
        tmT, Am, TT, PT, Zb = tmT2[g % 2], Am2[g % 2], TT2[g % 2], PT2[g % 2], Zb2[g % 2]
        for ci in range(GC):
            for d in range(2):
                ch = chunk_of(g, ci, d)
                cs = slice(ch * 64, ch * 64 + 64)
                pb_ = psB(c)
                pbv = pb_.h[:, :].bitcast(BF16)
                srcs = [AR[d][:, 0, cs], BK[d][:, 0, cs], BK[d][:, 1, cs], vT[:, cs]]
                for q, src in enumerate(srcs):
                    fw.transpose(V(pbv[0:64, q * 128:(q + 1) * 128], pb_.trs), src, c.identb[:, :])
                fw.copy(V(tmT.h[:, ci, d, :, :].rearrange("p q e -> p (q e)"), tmT.trs), V(pbv[0:64, 0:512], pb_.trs),
                        e=("act" if d == 0 else "dve"))
            yield
        bN = [psB(c), psB(c)]
        for ci in range(GC):
            bA = [psA(c), psA(c)]
            hl_outer = True
            order = [(d, hl) for hl in range(2) for d in range(2)] if hl_outer else [(d, hl) for d in range(2) for hl in range(2)]
            for (d, hl) in order:
                ch = chunk_of(g, ci, d)
                cs = slice(ch * 64, ch * 64 + 64)
                pb = hl * 64
                for q in range(2):
                    o0 = d * 256 + q * 128
                    fw.mm(bA[hl][0:64, o0:o0 + 128], BK[d][pb:pb + 64, q, cs], V(AR[d].h[pb:pb + 64, :, cs], AR[d].trs))
                o1 = (ci * 2 + d) * 64
                fw.mm(bN[hl][0:64, o1:o1 + 64], AR[d][pb:pb + 64, 0, cs], BK[d][pb:pb + 64, 0, cs])
            for hl in range(2):
                m0 = ci * 4 + hl
                dst = V(Am.h[:, m0:m0 + 3:2, :].rearrange("p d (q e) -> p d q e", q=2), Am.trs)
                src = V(bA[hl].h[0:64, :].rearrange("p (d q e) -> p d q e", d=2, q=2), bA[hl].trs)
                msk = V(tri.h[:, :, 0:128].unsqueeze(2).to_broadcast([64, 2, 2, 128]), tri.trs)
                fw.tt(dst, src, msk, ALU.mult)
            yield
        for hl in range(2):
            dst = V(NTb[0].h[:, hl:NM:2, :].rearrange("p (ci d) e -> p ci d e", d=2), NTb[0].trs)
            src = V(bN[hl].h[0:64, :].rearrange("p (ci d e) -> p ci d e", ci=GC, d=2), bN[hl].trs)
            msk = V(tri.h[:, :, 128:192].unsqueeze(1).to_broadcast([64, GC, 2, 64]), tri.trs)
            fw.tt(dst, src, msk, ALU.mult)
        for (m0, m1) in halves:
            fw.tt(Rb[0][:, m0:m1, :], idb, Am[:, m0:m1, 0:64], ALU.subtract)
        yield
        Ncur = V(Am.h[:, :, 0:64], Am.trs)
        NTcur = NTb[0][:, :, :]
        rcur = 0
        nti = 0
        for lev in range(1, 6):
            ntn = 1 - nti
            nn = lev % 2
            pzs = []
            for (m0, m1) in halves:
                pz = psA(c)
                for m in range(m0, m1):
                    fw.mm(pz[0:64, (m - m0) * 64:(m - m0 + 1) * 64], V(Ncur.ap[:, m, :], Ncur.trs), V(NTcur.ap[:, m, :], NTcur.trs))
                pzs.append(pz)
            for hi, (m0, m1) in enumerate(halves):
                pz = pzs[hi]
                src3 = V(pz.h[0:64, 0:(m1 - m0) * 64].rearrange("p (m e) -> p m e", e=64), pz.trs)
                if USE_NTI:
                    fw.tt(NTI[:, m0:m1, :], src3, idb, ALU.add)
                if lev < 5 or not USE_NTI:
                    fw.copy(flat(NTb[ntn], m0, m1), pz[0:64, 0:(m1 - m0) * 64], e="act")
            yield
            if lev < 5:
                for (m0, m1) in halves:
                    pz = psA(c)
                    for m in range(m0, m1):
                        fw.mm(pz[0:64, (m - m0) * 64:(m - m0 + 1) * 64], V(NTcur.ap[:, m, :], NTcur.trs), V(Ncur.ap[:, m, :], Ncur.trs))
                    fw.copy(flat(Nb[nn], m0, m1), pz[0:64, 0:(m1 - m0) * 64], e="act")
                yield
            rn = 1 - rcur
            rdst = TT if lev == 5 else Rb[rn]
            for (m0, m1) in halves:
                pz = psB(c)
                for m in range(m0, m1):
                    o = pz[0:64, (m - m0) * 64:(m - m0 + 1) * 64]
                    if USE_NTI:
                        fw.mm(o, NTI[:, m, :], Rb[rcur][:, m, :])
                    else:
                        fw.mm(o, I64b, Rb[rcur][:, m, :], start=True, stop=False)
                        fw.mm(o, NTb[ntn][:, m, :], Rb[rcur][:, m, :], start=False, stop=True)
                fw.copy(flat(rdst, m0, m1), pz[0:64, 0:(m1 - m0) * 64])
            yield
            rcur = rn
            if lev < 5:
                Ncur = Nb[nn][:, :, :]
                NTcur = NTb[ntn][:, :, :]
                nti = ntn
        for (m0, m1) in halves:
            pz = psA(c)
            pq = psB(c)
            for m in range(m0, m1):
                ci, d, hl = m // 4, (m // 2) % 2, m % 2
                fw.mm(pz[0:64, (m - m0) * 64:(m - m0 + 1) * 64], tmT[:, ci, d, 0, hl * 64:(hl + 1) * 64], TT[:, m, :])
                fw.mm(pq[0:64, (m - m0) * 64:(m - m0 + 1) * 64], Am[:, m, 128:192], tmT[:, ci, d, 3, hl * 64:(hl + 1) * 64])
            fw.copy(flat(PT, m0, m1), pz[0:64, 0:(m1 - m0) * 64], e="act")
            fw.copy(flat(Zb, m0, m1), pq[0:64, 0:(m1 - m0) * 64])
            yield

    def drain(gen, n):
        if gen is None:
            return None
        for _ in range(n):
            try:
                next(gen)
            except StopIteration:
                return None
        return gen

    gen = phase1(0)
    drain(gen, 1000)
    slot = 0
    NY = 2 * GC + 2 + 14 + 2
    per_pt = -(-NY // (2 * GC))
    for g in range(ngroups):
        tmT, Am, TT, PT, Zb = tmT2[g % 2], Am2[g % 2], TT2[g % 2], PT2[g % 2], Zb2[g % 2]
        gen = phase1(g + 1) if g + 1 < ngroups else None
        for ci in range(GC):
            scur = Sb.s(slot, (slice(None), slot, slice(None), slice(None)))
            snew = Sb.s(1 - slot, (slice(None), 1 - slot, slice(None), slice(None)))
            pu = psB(c)
            for inst in range(4):
                m = ci * 4 + inst
                o = pu[0:64, inst * 64:(inst + 1) * 64]
                fw.mm(o, PT[:, m, :], V(scur.ap[:, inst, :], scur.trs), start=True, stop=False)
                fw.mm(o, TT[:, m, :], Zb[:, m, :], start=False, stop=True)
            fw.act(V(Ub.h.rearrange("p i e -> p (i e)"), Ub.trs), pu[0:64, 0:256], AF.Copy, scale=-1.0)
            gen = drain(gen, per_pt)
            pS = psA(c)
            pY = psB(c)
            for inst in range(4):
                m = ci * 4 + inst
                d, hl = inst // 2, inst % 2
                ch = chunk_of(g, ci, d)
                cs = slice(ch * 64, ch * 64 + 64)
                hs = slice(hl * 64, (hl + 1) * 64)
                o = pS[0:64, inst * 64:(inst + 1) * 64]
                fw.mm(o, I64b, V(scur.ap[:, inst, :], scur.trs), start=True, stop=False)
                fw.mm(o, tmT[:, ci, d, 2, hs], tmT[:, ci, d, 3, hs], start=False, stop=False)
                fw.mm(o, tmT[:, ci, d, 1, hs], Ub[:, inst, :], start=False, stop=True)
            for d in range(2):
                ch = chunk_of(g, ci, d)
                esc = V(eLs.h[:, d, :, ch:ch + 1].to_broadcast([64, 2, 64]), eLs.trs)
                fw.tt(V(snew.ap[:, 2 * d:2 * d + 2, :], snew.trs),
                      V(pS.h[0:64, d * 128:(d + 1) * 128].rearrange("p (i e) -> p i e", i=2), pS.trs), esc, ALU.mult)
            for inst in range(4):
                m = ci * 4 + inst
                d, hl = inst // 2, inst % 2
                ch = chunk_of(g, ci, d)
                cs = slice(ch * 64, ch * 64 + 64)
                hs = slice(hl * 64, (hl + 1) * 64)
                oy = pY[0:64, inst * 64:(inst + 1) * 64]
                rt = AR[d][0:64, 1, cs] if hl == 0 else RLO[d][0:64, cs]
                fw.mm(oy, V(scur.ap[:, inst, :], scur.trs), rt, start=True, stop=False)
                fw.mm(oy, tmT[:, ci, d, 3, hs], Am[:, m, 192:256], start=False, stop=False)
                fw.mm(oy, Ub[:, inst, :], Am[:, m, 64:128], start=False, stop=True)
            for d in range(2):
                ch = chunk_of(g, ci, d)
                cs = slice(ch * 64, ch * 64 + 64)
                for hl in range(2):
                    inst = d * 2 + hl
                    fw.copy(ys[d][hl * 64:(hl + 1) * 64, cs], pY[0:64, inst * 64:(inst + 1) * 64], e="act")
            gen = drain(gen, per_pt)
            slot = 1 - slot
        drain(gen, 1000)

    arena_release(c, mark)
    pt_ = [alloc(c, [128, 512], F32)[:, :] for _ in range(6)]
    y_, sq, mean, var, t1, t2 = pt_
    for tb in range(NTB):
        tok = slice(tb * 512, (tb + 1) * 512)
        fw.tt(y_, ys[0][:, tok], ys[1][:, tok], ALU.add)
        fw.tt(sq, y_, y_, ALU.mult, e="pool")
        p1 = psA(c)
        fw.mm(p1[:, :], blk[:, :], y_)
        p2 = psA(c)
        fw.mm(p2[:, :], blk[:, :], sq)
        fw.ts(mean, p1[:, :], 1.0 / 64.0, None, ALU.mult)
        fw.tt(t1, mean, mean, ALU.mult, e="pool")
        fw.stt(var, p2[:, :], 1.0 / 64.0, t1, ALU.mult, ALU.subtract)
        fw.ts(var, var, 64e-5, None, ALU.add)
        fw.act(var, var, AF.Sqrt)
        fw.recip(t1, var)
        fw.tt(t2, y_, mean, ALU.subtract, e="pool")
        fw.tt(t2, t2, t1, ALU.mult)
        fw.ts(t2, t2, prm[:, 25 + hp:26 + hp], prm[:, 27 + hp:28 + hp], ALU.mult, ALU.add)
        fw.tt(t2, t2, bv[:, tok], ALU.add, e="pool")
        fw.tt(c.yT.s(1, (slice(None), 2 + hp, tok)), t2, gs[:, tok], ALU.mult)
```
